# Optimizing a Trainium2 kernel written in Bass

```python
import jax
import jax.numpy as jnp
from jax import lax
import numpy as np

D_MODEL = 1024
BATCH = 4
SEQ = 4096
DEPTH = 4

GRID_W = 64
CTX_LEN = 256

BRANCH_WIDTH = 512
NA_HEADS = 8
NA_HEAD_DIM = 64
NA_WIDTH = NA_HEADS * NA_HEAD_DIM
NA_WIN_H = 8
NA_WIN_W = 16
RWKV_HEADS = 8
RWKV_HEAD_DIM = 64
RWKV_WIDTH = RWKV_HEADS * RWKV_HEAD_DIM
DECAY_LORA = 64
AAA_LORA = 64
GATE_LORA = 128
RWKV_IN = 3 * RWKV_WIDTH + 2 * DECAY_LORA + 2 * AAA_LORA + GATE_LORA
RWKV_GN_EPS = 64e-5
SGU_GROUPS = 8
SGU_WIDTH = 512
SGU_CHUNK = 128
SGU_IN = 2 * SGU_WIDTH
N_BRANCH = 3
D_FF = -(-8 * D_MODEL // (3 * 256)) * 256
LN_EPS = 1e-5
ALPHA = (2 * DEPTH) ** 0.25
BETA = (8 * DEPTH) ** -0.25

_IN_SIZES = (N_BRANCH * D_MODEL, NA_WIDTH, NA_WIDTH, NA_WIDTH, RWKV_IN, SGU_IN)
IN_SPLITS = tuple(int(s) for s in np.cumsum(_IN_SIZES)[:-1])
D_IN = int(sum(_IN_SIZES))
_RWKV_SIZES = (RWKV_WIDTH, RWKV_WIDTH, RWKV_WIDTH, 2 * DECAY_LORA, 2 * AAA_LORA, GATE_LORA)
RWKV_SPLITS = tuple(int(s) for s in np.cumsum(_RWKV_SIZES)[:-1])

kernel_name = 'hybrid_na_rwkv7_gmlp_deepnorm_block'


def layer_norm(x, g, b, eps=LN_EPS):
    xf = x.astype(jnp.float32)
    mu = jnp.mean(xf, axis=-1, keepdims=True)
    var = jnp.mean(jnp.square(xf - mu), axis=-1, keepdims=True)
    return ((xf - mu) * lax.rsqrt(var + eps) * g + b).astype(x.dtype)


def _heads(t):
    return t.reshape(t.shape[0], t.shape[1], NA_HEADS, NA_HEAD_DIM)


def neighbourhood_attention(q, k, v, kc, vc, rpb):
    B, S, H, Dh = q.shape
    rows = S // GRID_W
    wh = min(NA_WIN_H, rows)
    n_loc = wh * NA_WIN_W
    scale = Dh ** -0.5
    qg = q.reshape(B, rows, GRID_W, H, Dh)
    kg = k.reshape(B, rows, GRID_W, H, Dh)
    vg = v.reshape(B, rows, GRID_W, H, Dh)
    row0 = jnp.clip(jnp.arange(rows) - wh // 2, 0, rows - wh)
    cols = jnp.arange(GRID_W)
    key_cols = jnp.clip(cols - NA_WIN_W // 2, 0, GRID_W - NA_WIN_W)[:, None] + jnp.arange(NA_WIN_W)
    bias_c = rpb[:, :, key_cols - cols[:, None] + NA_WIN_W - 1]

    def row_block(r):
        key_rows = row0[r] + jnp.arange(wh)
        kb = jnp.take(kg, key_rows, axis=1)[:, :, key_cols]
        vb = jnp.take(vg, key_rows, axis=1)[:, :, key_cols]
        qb = lax.dynamic_index_in_dim(qg, r, axis=1, keepdims=False)
        bias = jnp.take(bias_c, key_rows - r + NA_WIN_H - 1, axis=1)
        s_loc = jnp.einsum('bchd,bicjhd->bhcij', qb, kb) * scale + jnp.transpose(bias, (0, 2, 1, 3))
        s_ctx = jnp.einsum('bchd,blhd->bhcl', qb, kc) * scale
        s = jnp.concatenate([s_loc.reshape(B, H, GRID_W, n_loc).astype(jnp.float32),
                             s_ctx.astype(jnp.float32)], axis=-1)
        p = jax.nn.softmax(s, axis=-1).astype(v.dtype)
        p_loc = p[..., :n_loc].reshape(B, H, GRID_W, wh, NA_WIN_W)
        return (jnp.einsum('bhcij,bicjhd->bchd', p_loc, vb)
                + jnp.einsum('bhcl,blhd->bchd', p[..., n_loc:], vc))

    out = lax.map(row_block, jnp.arange(rows))
    return jnp.transpose(out, (1, 0, 2, 3, 4)).reshape(B, S, H * Dh)


def context_attention(qc, kc, vc):
    s = jnp.einsum('blhd,bmhd->bhlm', qc, kc) * qc.shape[-1] ** -0.5
    p = jax.nn.softmax(s.astype(jnp.float32), axis=-1).astype(vc.dtype)
    return jnp.einsum('bhlm,bmhd->blhd', p, vc).reshape(qc.shape[0], qc.shape[1], -1)


def centred_shift(p, mu_prev, mu_next):
    zero = jnp.zeros_like(p[:, :1])
    prev = jnp.concatenate([zero, p[:, :-1]], axis=1)
    nxt = jnp.concatenate([p[:, 1:], zero], axis=1)
    return p + mu_prev * (prev - p) + mu_next * (nxt - p)


def _dir_time_major(t):
    B, T, _, _ = t.shape
    t = jnp.stack([t[:, :, 0], jnp.flip(t[:, :, 1], axis=1)], axis=0)
    return jnp.transpose(t.reshape(2, B, T, RWKV_HEADS, RWKV_HEAD_DIM), (2, 0, 1, 3, 4))


def rwkv_prep(p, mu_prev, mu_next, w0, w2, a0, a2, g2, k_k, k_a):
    f32 = jnp.float32
    p = centred_shift(p, mu_prev, mu_next)
    r, k, v, wc, ac, gc = jnp.split(p, RWKV_SPLITS, axis=-1)
    B, T, C = r.shape
    wc = wc.reshape(B, T, 2, DECAY_LORA)
    ac = ac.reshape(B, T, 2, AAA_LORA)
    w_log = -jax.nn.softplus(-(w0 + jnp.einsum('btdr,drc->btdc', jnp.tanh(wc), w2)).astype(f32)) - 0.5
    decay = jnp.exp(-jnp.exp(w_log))
    a = jax.nn.sigmoid((a0 + jnp.einsum('btdr,drc->btdc', ac, a2)).astype(f32))
    g = jax.nn.sigmoid(gc) @ g2
    kk = (k * k_k).astype(f32).reshape(B, T, RWKV_HEADS, RWKV_HEAD_DIM)
    kk = (kk * lax.rsqrt(jnp.sum(kk * kk, axis=-1, keepdims=True) + 1e-12)).reshape(B, T, C)
    kd = k.astype(f32)[:, :, None] * (1.0 + (a - 1.0) * k_a)
    shape = kd.shape
    rd = jnp.broadcast_to(r.astype(f32)[:, :, None], shape)
    vd = jnp.broadcast_to(v.astype(f32)[:, :, None], shape)
    kkd = jnp.broadcast_to(kk[:, :, None], shape)
    scan_in = (_dir_time_major(rd), _dir_time_major(decay), _dir_time_major(kd),
               _dir_time_major(vd), _dir_time_major(kkd), _dir_time_major(a))
    return scan_in, (r, kd, v, g)


def delta_scan(state0, inputs, emit):
    def step(S, inp):
        r_t, w_t, k_t, v_t, kk_t, a_t = inp
        s_kk = jnp.einsum('...vk,...k->...v', S, kk_t)
        S = (S * w_t[..., None, :] - s_kk[..., :, None] * (kk_t * a_t)[..., None, :]
             + v_t[..., :, None] * k_t[..., None, :])
        return S, (jnp.einsum('...vk,...k->...v', S, r_t) if emit else None)
    return lax.scan(step, state0, inputs)


def rwkv_readout(ys, r, kd, v, g, r_k, gn_g, gn_b):
    B, T, C = r.shape
    H, N = RWKV_HEADS, RWKV_HEAD_DIM
    y = jnp.transpose(ys[:, 0] + jnp.flip(ys[:, 1], axis=0), (1, 0, 2, 3))
    mu = jnp.mean(y, axis=-1, keepdims=True)
    var = jnp.mean(jnp.square(y - mu), axis=-1, keepdims=True)
    y = (y - mu) * lax.rsqrt(var + RWKV_GN_EPS) * gn_g.reshape(H, N) + gn_b.reshape(H, N)
    bonus = jnp.sum(r.astype(jnp.float32).reshape(B, T, 1, H, N) * kd.reshape(B, T, 2, H, N) * r_k,
                    axis=(2, 4))
    y = y + bonus[..., None] * v.reshape(B, T, H, N)
    return (y.reshape(B, T, C) * g).astype(r.dtype)


def spatial_gating(p, ln_g, ln_b, w_s, b_s):
    u, v = jnp.split(jax.nn.gelu(p), 2, axis=-1)
    v = layer_norm(v, ln_g, ln_b)
    B, T, C = v.shape
    vc = v.reshape(B, T // SGU_CHUNK, SGU_CHUNK, SGU_GROUPS, C // SGU_GROUPS)
    vm = jnp.einsum('gpq,bnqgc->bnpgc', w_s, vc) + jnp.transpose(b_s)[:, :, None]
    return u * vm.reshape(B, T, C)


def merge_branches(gates, o_na, o_rwkv, o_sgu, w_branch, w_out):
    g_na, g_rwkv, g_sgu = jnp.split(jax.nn.sigmoid(gates), N_BRANCH, axis=-1)
    y = g_na * (o_na @ w_branch[0]) + g_rwkv * (o_rwkv @ w_branch[1]) + g_sgu * (o_sgu @ w_branch[2])
    return y @ w_out


def token_mixer(h, hc, w_in, rpb, mu_prev, mu_next, w0, w2, a0, a2, g2, k_k, k_a, r_k, gn_g, gn_b,
                sgu_ln_g, sgu_ln_b, sgu_w, sgu_b, w_branch, w_out, ctx_out):
    B = h.shape[0]
    gates, q, k, v, p_rwkv, p_sgu = jnp.split(h @ w_in, IN_SPLITS, axis=-1)
    gates_c, q_c, k_c, v_c, p_rwkv_c, p_sgu_c = jnp.split(hc @ w_in, IN_SPLITS, axis=-1)
    kc_h, vc_h = _heads(k_c), _heads(v_c)
    o_na = neighbourhood_attention(_heads(q), _heads(k), _heads(v), kc_h, vc_h, rpb)
    rw = (mu_prev, mu_next, w0, w2, a0, a2, g2, k_k, k_a)
    scan_c, read_c = rwkv_prep(p_rwkv_c, *rw)
    scan_l, read_l = rwkv_prep(p_rwkv, *rw)
    state0 = jnp.zeros((2, B, RWKV_HEADS, RWKV_HEAD_DIM, RWKV_HEAD_DIM), jnp.float32)
    state_ctx, ys_c = delta_scan(state0, scan_c, ctx_out)
    _, ys_l = delta_scan(state_ctx, scan_l, True)
    o_rwkv = rwkv_readout(ys_l, *read_l, r_k, gn_g, gn_b)
    o_sgu = spatial_gating(p_sgu, sgu_ln_g, sgu_ln_b, sgu_w, sgu_b)
    y = merge_branches(gates, o_na, o_rwkv, o_sgu, w_branch, w_out)
    if not ctx_out:
        return y, None
    o_na_c = context_attention(_heads(q_c), kc_h, vc_h)
    o_rwkv_c = rwkv_readout(ys_c, *read_c, r_k, gn_g, gn_b)
    o_sgu_c = spatial_gating(p_sgu_c, sgu_ln_g, sgu_ln_b, sgu_w, sgu_b)
    y_c = merge_branches(gates_c, o_na_c, o_rwkv_c, o_sgu_c, w_branch, w_out)
    return y, y_c


def swiglu(h, w_gu, w_down):
    gate, up = jnp.split(h @ w_gu, 2, axis=-1)
    return (jax.nn.silu(gate) * up) @ w_down


def setup_inputs(seed: int = 0) -> dict:
    key = jax.random.key(seed)
    ks = iter(jax.random.split(key, 32))
    L, D, W = DEPTH, D_MODEL, BRANCH_WIDTH

    def nrm(shape, s):
        return s * jax.random.normal(next(ks), shape, jnp.float32)

    def uni(shape, lo, hi):
        return jax.random.uniform(next(ks), shape, jnp.float32, minval=lo, maxval=hi)

    return {
        'x': nrm((BATCH, SEQ, D), 1.0),
        'c': nrm((BATCH, D), 1.0),
        'ctx': nrm((BATCH, CTX_LEN, D), 1.0),
        'c_ctx': nrm((D,), 1.0),
        'w_ada': nrm((L, D, 6 * D), 0.5 * D ** -0.5),
        'b_ada': nrm((L, 6 * D), 0.02),
        'w_in': nrm((L, D, D_IN), D ** -0.5),
        'na_rpb': nrm((L, NA_HEADS, 2 * NA_WIN_H - 1, 2 * NA_WIN_W - 1), 0.5),
        'rwkv_mu_prev': uni((L, RWKV_IN), 0.0, 0.5),
        'rwkv_mu_next': uni((L, RWKV_IN), 0.0, 0.5),
        'rwkv_w0': uni((L, 2, RWKV_WIDTH), -6.0, 1.0),
        'rwkv_w2': nrm((L, 2, DECAY_LORA, RWKV_WIDTH), 0.5 * DECAY_LORA ** -0.5),
        'rwkv_a0': nrm((L, 2, RWKV_WIDTH), 0.1),
        'rwkv_a2': nrm((L, 2, AAA_LORA, RWKV_WIDTH), 0.5 * AAA_LORA ** -0.5),
        'rwkv_g2': nrm((L, GATE_LORA, RWKV_WIDTH), GATE_LORA ** -0.5),
        'rwkv_k_k': 0.85 + nrm((L, RWKV_WIDTH), 0.05),
        'rwkv_k_a': 1.0 + nrm((L, RWKV_WIDTH), 0.05),
        'rwkv_r_k': nrm((L, RWKV_HEADS, RWKV_HEAD_DIM), 0.1),
        'rwkv_gn_g': 1.0 + nrm((L, RWKV_WIDTH), 0.05),
        'rwkv_gn_b': nrm((L, RWKV_WIDTH), 0.02),
        'sgu_ln_g': 1.0 + nrm((L, SGU_WIDTH), 0.05),
        'sgu_ln_b': nrm((L, SGU_WIDTH), 0.02),
        'sgu_w': nrm((L, SGU_GROUPS, SGU_CHUNK, SGU_CHUNK), SGU_CHUNK ** -0.5),
        'sgu_b': 1.0 + nrm((L, SGU_GROUPS, SGU_CHUNK), 0.1),
        'w_branch': nrm((L, N_BRANCH, W, D), BETA * W ** -0.5),
        'w_out': nrm((L, D, D), BETA * D ** -0.5),
        'ln1_g': 1.0 + nrm((L, D), 0.05),
        'ln1_b': nrm((L, D), 0.02),
        'ln2_g': 1.0 + nrm((L, D), 0.05),
        'ln2_b': nrm((L, D), 0.02),
        'ffn_w_gu': nrm((L, D, 2 * D_FF), D ** -0.5),
        'ffn_w_down': nrm((L, D_FF, D), BETA * D_FF ** -0.5),
    }


def reference(x, c, ctx, c_ctx, w_ada, b_ada, w_in, na_rpb, rwkv_mu_prev, rwkv_mu_next, rwkv_w0, rwkv_w2,
              rwkv_a0, rwkv_a2, rwkv_g2, rwkv_k_k, rwkv_k_a, rwkv_r_k, rwkv_gn_g, rwkv_gn_b, sgu_ln_g, sgu_ln_b,
              sgu_w, sgu_b, w_branch, w_out, ln1_g, ln1_b, ln2_g, ln2_b, ffn_w_gu, ffn_w_down):
    silu_c = jax.nn.silu(c)
    silu_cc = jax.nn.silu(c_ctx)
    xc = ctx
    for l in range(DEPTH):
        last = l == DEPTH - 1
        sh_m, sc_m, g_m, sh_f, sc_f, g_f = jnp.split((silu_c @ w_ada[l] + b_ada[l])[:, None, :], 6, axis=-1)
        csh_m, csc_m, cg_m, csh_f, csc_f, cg_f = jnp.split(silu_cc @ w_ada[l] + b_ada[l], 6, axis=-1)
        y, y_c = token_mixer(x * (1.0 + sc_m) + sh_m, xc * (1.0 + csc_m) + csh_m, w_in[l], na_rpb[l],
                             rwkv_mu_prev[l], rwkv_mu_next[l], rwkv_w0[l], rwkv_w2[l], rwkv_a0[l], rwkv_a2[l],
                             rwkv_g2[l], rwkv_k_k[l], rwkv_k_a[l], rwkv_r_k[l], rwkv_gn_g[l], rwkv_gn_b[l],
                             sgu_ln_g[l], sgu_ln_b[l], sgu_w[l], sgu_b[l], w_branch[l], w_out[l], not last)
        x = layer_norm(ALPHA * x + g_m * y, ln1_g[l], ln1_b[l])
        x = layer_norm(ALPHA * x + g_f * swiglu(x * (1.0 + sc_f) + sh_f, ffn_w_gu[l], ffn_w_down[l]),
                       ln2_g[l], ln2_b[l])
        if not last:
            xc = layer_norm(ALPHA * xc + cg_m * y_c, ln1_g[l], ln1_b[l])
            xc = layer_norm(ALPHA * xc + cg_f * swiglu(xc * (1.0 + csc_f) + csh_f, ffn_w_gu[l], ffn_w_down[l]),
                            ln2_g[l], ln2_b[l])
    return x
```

```python
import numpy as np
import concourse.bass as bass
import concourse.mybir as mybir
from concourse.bass_utils import run_bass_kernel_spmd

F32 = mybir.dt.float32
BF16 = mybir.dt.bfloat16
ALU = mybir.AluOpType
AF = mybir.ActivationFunctionType
AX = mybir.AxisListType

D = 1024
DEPTH = 4
LCTX = 256
SEQ = 4096
T = LCTX + SEQ
NT = T // 128
GROUPS = [(0, 256)] + [(256 + 512 * i, 512) for i in range(8)]
D_IN = 7552
D_FF = 2816
ALPHA = (2 * DEPTH) ** 0.25


class Prog:
    ENGS = ("tensor", "vector", "scalar", "gpsimd", "sync")

    def __init__(self):
        self.nc = bass.Bass("TRN2", target_bir_lowering=False)
        self.ops = []
        self.n_dma_sems = 32
        arena = self.nc.alloc_sbuf_tensor("arena", [128, 212000], mybir.dt.uint8)
        self.arena_base = self.nc.lookup_mloc(arena).addr
        self.arena_size = 212000
        self.sp = 0
        self.nalloc = 0

    def dram(self, name, shape, dtype, kind="Internal"):
        return self.nc.dram_tensor(name, list(shape), dtype, kind=kind).ap()

    def sbuf(self, name, shape, dtype=F32):
        esz = {F32: 4, BF16: 2}[dtype]
        nbytes = int(np.prod(shape[1:])) * esz
        off = (self.sp + 63) // 64 * 64
        assert off + nbytes <= self.arena_size, "SBUF arena overflow at %s: %d + %d" % (name, off, nbytes)
        self.sp = off + nbytes
        self.nalloc += 1
        return self.nc.alloc_sbuf_tensor_at("%s_%d" % (name, self.nalloc), list(shape), dtype, offset=self.arena_base + off)

    def mark(self):
        return self.sp

    def release(self, m):
        self.sp = m

    def barrier(self, new_epoch=False):
        self.ops.append(("barrier", new_epoch, (), (), False))

    def psum(self, name, shape, dtype=F32):
        return self.nc.alloc_psum_tensor(name, list(shape), dtype)

    def op(self, eng, fn, reads=(), writes=(), dma=False):
        self.ops.append((eng, fn, tuple(reads), tuple(writes), dma))

    def dma(self, eng, out, in_, reads=(), writes=(), slow=False):
        if slow:
            self.op(eng, lambda e: e.dma_start(out=out, in_=in_, allow_slow_non_contiguous=True), reads, writes, dma=True)
        else:
            self.op(eng, lambda e: e.dma_start(out=out, in_=in_), reads, writes, dma=True)

    def finalize(self):
        nc = self.nc
        ops = self.ops
        last_w = {}
        readers = {}
        deps = []
        force_sig = set()
        last_on = {}
        for i, (eng, fn, rd, wr, isdma) in enumerate(ops):
            if eng == "barrier":
                force_sig.update(last_on.values())
                last_w = {}
                readers = {}
                deps.append(set())
                continue
            if not isdma:
                last_on[eng] = i
            d = set()
            for k in rd:
                if k in last_w:
                    d.add(last_w[k])
            for k in wr:
                if k in last_w:
                    d.add(last_w[k])
                d.update(readers.get(k, {}).values())
            for k in rd:
                readers.setdefault(k, {})[(eng, i) if isdma else eng] = i
            for k in wr:
                last_w[k] = i
                readers[k] = {}
            d.discard(i)
            if eng == "tensor" and not isdma:
                d = {j for j in d if not (ops[j][0] == "tensor" and not ops[j][4])}
            deps.append(d)
        has_dep = [False] * len(ops)
        for d in deps:
            for j in d:
                has_dep[j] = True
        for j in force_sig:
            has_dep[j] = True
        sems = {e: nc.alloc_semaphore("s_" + e) for e in self.ENGS}
        dsems = [nc.alloc_semaphore("d%d" % j) for j in range(self.n_dma_sems)]
        cnt = {e: 0 for e in self.ENGS}
        sig = [None] * len(ops)
        waited = {e: {} for e in self.ENGS}
        ndma = 0
        final = {}
        dlast = {}
        for i, (e, fn, rd, wr, isdma) in enumerate(ops):
            if e == "barrier":
                for e1 in self.ENGS:
                    eng1 = getattr(nc, e1)
                    for e2 in self.ENGS:
                        if cnt[e2] > waited[e1].get(sems[e2].num, 0):
                            waited[e1][sems[e2].num] = cnt[e2]
                            eng1.wait_ge(sems[e2], cnt[e2])
                    for jn, (sd, vd) in dlast.items():
                        if vd > waited[e1].get(jn, 0):
                            waited[e1][jn] = vd
                            eng1.wait_ge(sd, vd)
                if fn:
                    nep = getattr(self, "_nep", 0) + 1
                    self._nep = nep
                    sems = {e_: nc.alloc_semaphore("s%d_%s" % (nep, e_)) for e_ in self.ENGS}
                    cnt = {e_: 0 for e_ in self.ENGS}
                continue
            eng = getattr(nc, e)
            need = {}
            for j in deps[i]:
                s, v = sig[j]
                if need.get(s.num, (None, 0))[1] < v:
                    need[s.num] = (s, v)
            if isdma:
                js = ndma % self.n_dma_sems
                v = 16 * (ndma // self.n_dma_sems + 1)
                if v > 16 and need.get(dsems[js].num, (None, 0))[1] < v - 16:
                    need[dsems[js].num] = (dsems[js], v - 16)
            for sn, (s, val) in need.items():
                if waited[e].get(sn, 0) >= val:
                    continue
                waited[e][sn] = val
                eng.wait_ge(s, val)
            ins = fn(eng)
            if isdma:
                ins.then_inc(dsems[js], 16)
                sig[i] = (dsems[js], v)
                dlast[dsems[js].num] = (dsems[js], v)
                ndma += 1
                if any(str(k).startswith("OUT") for k in wr):
                    if final.get(dsems[js].num, (None, 0))[1] < v:
                        final[dsems[js].num] = (dsems[js], v)
            elif has_dep[i]:
                cnt[e] += 1
                ins.then_inc(sems[e], 1)
                sig[i] = (sems[e], cnt[e])
            else:
                sig[i] = (sems[e], cnt[e])
        for sn, (s, v) in final.items():
            nc.sync.wait_ge(s, v)
        self.counts = dict(cnt, ndma=ndma, nops=len(ops))
        return nc


def build(n_layers=DEPTH, stop=None, dbg=(), mode="full"):
    NL = n_layers
    P = Prog()
    nc = P.nc
    if mode == "full":
        xin = P.dram("xin", [T, D], F32, "ExternalInput")
        out_d = P.dram("out", [SEQ, D], F32, "ExternalOutput")
    else:
        xT_in = P.dram("xT_in", [D, T], F32, "ExternalInput")
        xT_out = P.dram("xT_out", [D, T], F32, "ExternalOutput")
    ccT = P.dram("ccT", [D, 2], F32, "ExternalInput")
    ident_d = P.dram("ident", [128, 128], F32, "ExternalInput")
    w_ada = P.dram("w_ada", [NL, D, 6 * D], F32, "ExternalInput")
    b_adaT = P.dram("b_adaT", [NL, 128, 48], F32, "ExternalInput")
    w_in = P.dram("w_in", [NL, D, D_IN], F32, "ExternalInput")
    dbg_t = {}

    def dbg_out(name, shape, dtype=F32):
        dbg_t[name] = P.dram("dbg_" + name, shape, dtype, "ExternalOutput")
        return dbg_t[name]

    xT = P.dram("xT", [D, T], F32)
    qT = P.dram("qT", [512, T], BF16)
    kT = P.dram("kT", [512, T], BF16)
    vtok = P.dram("vtok", [T, 512], BF16)
    prw = P.dram("prw", [1920, T], F32)
    sguU = P.dram("sguU", [512, T], F32)
    sguV = P.dram("sguV", [T, 512], F32)

    ident = P.sbuf("ident_s", [128, 128])
    PS = [P.psum("ps%d" % i, [128, 512]) for i in range(6)]
    PSB = [P.psum("psb%d" % i, [128, 1024], BF16) for i in range(2)]
    mod = P.sbuf("mod", [128, NL, 48, 2])
    mod1 = P.sbuf("mod1", [128, NL, 48, 2])
    P.dma("sync", ident[:], ident_d, writes=["ident"])

    cc = P.sbuf("cc", [128, 8, 2])
    scc = P.sbuf("scc", [128, 8, 2])
    badaT = P.sbuf("badaT", [128, NL, 48])
    P.dma("sync", cc[:], ccT.rearrange("(k p) c -> p k c", p=128), writes=["cc"])
    P.dma("sync", badaT[:], b_adaT.rearrange("l p j -> p l j"), writes=["badaT"])
    P.op("scalar", lambda e: e.activation(out=scc[:], in_=cc[:], func=AF.Silu), reads=["cc"], writes=["scc"])
    identb = P.sbuf("identb", [128, 128], BF16)
    P.op("vector", lambda e: e.tensor_copy(out=identb[:], in_=ident[:]), reads=["ident"], writes=["identb"])
    lnp = P.sbuf("lnp", [128, NL, 4, 8])
    ones = P.sbuf("ones", [128, 128])
    eps5 = P.sbuf("eps5", [128, 1])
    P.op("vector", lambda e: e.memset(eps5[:], 1e-5), writes=["eps"])
    m0 = P.mark()
    wblk0 = [P.sbuf("wblk0%d" % i, [128, 8, 512]) for i in range(2)]
    nb = 0
    for l in range(n_layers):
        for cb in range(12):
            s = nb % 2
            P.dma("sync", wblk0[s][:], w_ada[l, :, cb * 512:(cb + 1) * 512].rearrange("(k p) m -> p k m", p=128),
                  writes=[("wblk0", s)])
            for mi in range(4):
                j = cb * 4 + mi
                for k in range(8):
                    P.op("tensor", lambda e, s=s, mi=mi, k=k, j=j: e.matmul(
                        PS[0][:, 2 * j:2 * j + 2], lhsT=wblk0[s][:, k, mi * 128:(mi + 1) * 128], rhs=scc[:, k, :],
                        start=(k == 0), stop=(k == 7)), reads=[("wblk0", s), "scc"], writes=[("ps", 0)])
            nb += 1
        for c in range(2):
            P.op("vector", lambda e, l=l, c=c: e.tensor_tensor(
                out=mod[:, l, :, c], in0=PS[0][:, 0:96].rearrange("p (j c) -> p j c", c=2)[:, :, c], in1=badaT[:, l, :], op=ALU.add),
                reads=[("ps", 0), "badaT"], writes=["mod"])
        P.op("vector", lambda e, l=l: e.tensor_scalar_add(out=mod1[:, l], in0=mod[:, l], scalar1=1.0), reads=["mod"], writes=["mod1"])

    xtile = [P.sbuf("xtile%d" % i, [128, D]) for i in range(2)]
    xTst = [P.sbuf("xTst%d" % i, [128, 8, 128]) for i in range(2)]
    if mode != "full":
        for gi, (t0, n) in enumerate(GROUPS):
            P.dma("sync", xT[:, t0:t0 + n], xT_in[:, t0:t0 + n], writes=[("xT", gi)])
    for t in range(NT if mode == "full" else 0):
        s = t % 2
        P.dma("sync", xtile[s][:], xin[t * 128:(t + 1) * 128, :], writes=[("xtile", s)])
        for half in range(2):
            pb = 1 + half
            for kk in range(4):
                k = half * 4 + kk
                P.op("tensor", lambda e, s=s, k=k, kk=kk, pb=pb: e.transpose(
                    out=PS[pb][:, kk * 128:(kk + 1) * 128], in_=xtile[s][:, k * 128:(k + 1) * 128], identity=ident[:]),
                    reads=[("xtile", s), "ident"], writes=[("ps", pb)])
            P.op("scalar" if half == 0 else "vector", (lambda e, s=s, half=half, pb=pb: e.copy(
                out=xTst[s][:, half * 4:(half + 1) * 4, :], in_=PS[pb][:].rearrange("p (k t) -> p k t", k=4)))
                if half == 0 else (lambda e, s=s, half=half, pb=pb: e.tensor_copy(
                    out=xTst[s][:, half * 4:(half + 1) * 4, :], in_=PS[pb][:].rearrange("p (k t) -> p k t", k=4))),
                reads=[("ps", pb)], writes=[("xTst", s, half)])
        P.dma("gpsimd", xT.rearrange("(k p) t -> p k t", p=128)[:, :, t * 128:(t + 1) * 128], xTst[s][:],
              reads=[("xTst", s, 0), ("xTst", s, 1)], writes=[("xT", t // 4)])

    if "mod" in dbg:
        d = dbg_out("mod", [128, NL * 48 * 2])
        P.dma("sync", d, mod[:].rearrange("p l j c -> p (l j c)"), reads=["mod"], writes=["OUT_dbgmod"])
    if "xT" in dbg:
        d = dbg_out("xT", [D, T])
        P.dma("sync", d, xT, reads=[("xT", g) for g in range(9)], writes=["OUT_dbgxT"])
    if stop == "s0":
        return P, dbg_t
    P.barrier()
    P.release(m0)


    sgu_lng = P.dram("sgu_lng", [NL, 128, 512], F32, "ExternalInput")
    sgu_lnb = P.dram("sgu_lnb", [NL, 128, 512], F32, "ExternalInput")
    sgu_wT = P.dram("sgu_wT", [NL, 128, 8, 128], F32, "ExternalInput")
    sgu_bB = P.dram("sgu_bB", [NL, 64, 8, 128], F32, "ExternalInput")
    na_bias = P.dram("na_bias", [NL, 5, 128, 8, 896], F32, "ExternalInput")
    osgu_d = P.dram("osgu", [512, T], BF16)
    ona_d = P.dram("ona", [512, T], BF16)
    def gelu(src, dst, ta, tb, npart, rk, wk, pfx):
        P.op("gpsimd", lambda e: e.tensor_tensor(out=ta, in0=src, in1=src, op=ALU.mult), reads=rk, writes=[(pfx, "ta")])
        P.op("gpsimd", lambda e: e.tensor_scalar(out=ta, in0=ta, scalar1=0.044715, scalar2=1.0, op0=ALU.mult, op1=ALU.add),
             reads=[(pfx, "ta")], writes=[(pfx, "ta")])
        P.op("gpsimd", lambda e: e.tensor_tensor(out=ta, in0=ta, in1=src, op=ALU.mult), reads=[(pfx, "ta")] + list(rk), writes=[(pfx, "ta")])
        P.op("scalar", lambda e: e.activation(out=tb, in_=ta, func=AF.Sigmoid, scale=1.5957691216057308),
             reads=[(pfx, "ta")], writes=[(pfx, "tb")])
        P.op("vector", lambda e: e.tensor_tensor(out=dst, in0=tb, in1=src, op=ALU.mult), reads=[(pfx, "tb")] + list(rk), writes=wk)

    def stage_sgu(l):
        sg = {}
        if True:
            sg["lng"] = P.sbuf("sg_lng", [128, 512]); sg["lnb"] = P.sbuf("sg_lnb", [128, 512])
            sg["wT"] = P.sbuf("sg_wT", [128, 8, 128]); sg["bB"] = P.sbuf("sg_bB", [64, 8, 128])
            for nm, shp, dt in (("sv", [128, 512], F32), ("gv", [128, 512], F32), ("vn", [128, 512], F32), ("tva", [128, 512], F32),
                                ("tvb", [128, 512], F32), ("su", [64, 8, 128], F32), ("gu", [64, 8, 128], F32), ("tua", [64, 8, 128], F32),
                                ("tub", [64, 8, 128], F32), ("st6", [128, 6], F32), ("mv", [128, 2], F32), ("rstd", [128, 1], F32),
                                ("tmp", [64, 8, 128], F32), ("osg", [64, 8, 128], BF16)):
                sg[nm] = [P.sbuf("sg_%s%d" % (nm, i), shp, dt) for i in range(2)]
        P.dma("sync", sg["lng"][:], sgu_lng[l], writes=["sg_par"])
        P.dma("sync", sg["lnb"][:], sgu_lnb[l], writes=["sg_par"])
        P.dma("sync", sg["wT"][:], sgu_wT[l], writes=["sg_par"])
        P.dma("sync", sg["bB"][:], sgu_bB[l], writes=["sg_par"])
        sguU_v = sguU.rearrange("(g c) t -> c g t", c=64)
        osgu_v = osgu_d.rearrange("(g c) t -> c g t", c=64)
        for t in range(NT):
            s = t % 2
            gi = 0 if t < 2 else 1 + (t - 2) // 4
            sv, gv, vn, su, gu, tmp, osg = (sg[k][s] for k in ("sv", "gv", "vn", "su", "gu", "tmp", "osg"))
            st6, mv, rstd = sg["st6"][s], sg["mv"][s], sg["rstd"][s]
            P.dma("sync", sv[:], sguV[t * 128:(t + 1) * 128, :], reads=[("sguV", "tm", t)], writes=[("sv", s)])
            P.dma("sync", su[:], sguU_v[:, :, t * 128:(t + 1) * 128], reads=[("sguU", "fm", gi)], writes=[("su", s)])
            gelu(sv[:], gv[:], sg["tva"][s][:], sg["tvb"][s][:], 128, [("sv", s)], [("gv", s)], ("gv", s))
            P.op("vector", lambda e, st6=st6, gv=gv: e.bn_stats(out=st6[:], in_=gv[:]), reads=[("gv", s)], writes=[("st6", s)])
            P.op("vector", lambda e, st6=st6, mv=mv: e.bn_aggr(out=mv[:], in_=st6[:]), reads=[("st6", s)], writes=[("mv", s)])
            P.op("scalar", lambda e, mv=mv, rstd=rstd: e.activation(out=rstd[:], in_=mv[:, 1:2], func=AF.Sqrt, bias=eps5[:, 0:1], scale=1.0),
                 reads=[("mv", s), "eps"], writes=[("rstd", s)])
            P.op("vector", lambda e, rstd=rstd: e.reciprocal(out=rstd[:], in_=rstd[:]), reads=[("rstd", s)], writes=[("rstd", s)])
            P.op("vector", lambda e, vn=vn, gv=gv, mv=mv, rstd=rstd: e.tensor_scalar(
                out=vn[:], in0=gv[:], scalar1=mv[:, 0:1], scalar2=rstd[:, 0:1], op0=ALU.subtract, op1=ALU.mult),
                reads=[("gv", s), ("mv", s), ("rstd", s)], writes=[("vn", s)])
            P.op("gpsimd", lambda e, vn=vn: e.tensor_tensor(out=vn[:], in0=vn[:], in1=sg["lng"][:], op=ALU.mult),
                 reads=[("vn", s), "sg_par"], writes=[("vn", s)])
            P.op("gpsimd", lambda e, vn=vn: e.tensor_tensor(out=vn[:], in0=vn[:], in1=sg["lnb"][:], op=ALU.add),
                 reads=[("vn", s), "sg_par"], writes=[("vn", s)])
            gelu(su[:], gu[:], sg["tua"][s][:], sg["tub"][s][:], 64, [("su", s)], [("gu", s)], ("gu", s))
            for half in range(2):
                pb = 2 * s + half
                for gg in range(4):
                    g = half * 4 + gg
                    P.op("tensor", lambda e, vn=vn, g=g, gg=gg, pb=pb: e.matmul(
                        PS[pb][0:64, gg * 128:(gg + 1) * 128], lhsT=vn[:, g * 64:(g + 1) * 64], rhs=sg["wT"][:, g, :],
                        start=True, stop=True), reads=[("vn", s), "sg_par"], writes=[("ps", pb)])
                P.op("vector", lambda e, tmp=tmp, pb=pb, half=half: e.tensor_tensor(
                    out=tmp[:, half * 4:(half + 1) * 4, :], in0=PS[pb][0:64, :].rearrange("p (g t) -> p g t", g=4),
                    in1=sg["bB"][:, half * 4:(half + 1) * 4, :], op=ALU.add), reads=[("ps", pb), "sg_par"], writes=[("sgtmp", s, half)])
                P.op("gpsimd", lambda e, tmp=tmp, gu=gu, osg=osg, half=half: e.tensor_tensor(
                    out=osg[:, half * 4:(half + 1) * 4, :], in0=tmp[:, half * 4:(half + 1) * 4, :], in1=gu[:, half * 4:(half + 1) * 4, :],
                    op=ALU.mult), reads=[("sgtmp", s, half), ("gu", s)], writes=[("osg", s, half)])
            P.dma("gpsimd", osgu_v[:, :, t * 128:(t + 1) * 128], osg[:], reads=[("osg", s, 0), ("osg", s, 1)], writes=[("osgu", gi)])
        if "osgu" in dbg and l == 0:
            d = dbg_out("osgu", [512, T], BF16)
            P.dma("sync", d, osgu_d, reads=[("osgu", g) for g in range(9)], writes=["OUT_dbgosgu"])

    def stage_na(l):
        na = {}
        if True:
            na["q"] = P.sbuf("na_q", [128, 4, T], BF16); na["k"] = P.sbuf("na_k", [128, 4, T], BF16)
            na["v"] = P.sbuf("na_v", [128, NT, 512], BF16)
            na["bI"] = P.sbuf("na_bI", [128, 8, 896]); na["bE"] = P.sbuf("na_bE", [128, 8, 896])
            for nm, shp, dt in (("s", [128, 896], F32), ("p", [128, 896], BF16), ("pT", [128, 896], BF16), ("nmx", [128, 1], F32)):
                na[nm] = [P.sbuf("na_%s%d" % (nm, i), shp, dt) for i in range(2)]
            for nm, shp, dt in (("rs", [128, 8], F32), ("rinv", [128, 8], F32), ("o", [128, 512], BF16), ("oT", [128, 4, 128], BF16)):
                na[nm] = [P.sbuf("na_%s%d" % (nm, i), shp, dt) for i in range(2)]
        qT_v = qT.rearrange("(k p) t -> p k t", p=128); kT_v = kT.rearrange("(k p) t -> p k t", p=128)
        vt_v = vtok.rearrange("(t p) f -> p t f", p=128)
        for gi, (t0, n) in enumerate(GROUPS):
            P.dma("sync", na["q"][:, :, t0:t0 + n], qT_v[:, :, t0:t0 + n], reads=[("qT", "fm", gi)], writes=[("na_q", gi)])
            P.dma("sync", na["k"][:, :, t0:t0 + n], kT_v[:, :, t0:t0 + n], reads=[("kT", "fm", gi)], writes=[("na_k", gi)])
        for t in range(NT):
            P.dma("sync", na["v"][:, t, :], vt_v[:, t, :], reads=[("vtok", "tm", t)], writes=[("na_v", t)])
        P.dma("sync", na["bI"][:], na_bias[l, 2], writes=["na_bI"])
        ona_v = ona_d.rearrange("(k p) t -> p k t", p=128)
        grp_of_tok = lambda tok: 0 if tok < 256 else 1 + (tok - 256) // 512
        hcnt = 0
        for t in range(NT):
            isctx = t < 2
            so = t % 2
            if isctx:
                nk = 256; pat = None; vtiles = [0, 1]
                kgroups = [0]
            else:
                qt = t - 2
                kb = min(max(2 * qt - 4, 0), 54)
                ktok0 = 256 + kb * 64
                nk = 896
                pat = {0: 0, 1: 1, 30: 3, 31: 4}.get(qt, 2)
                vtiles = [2 + kb // 2 + j for j in range(5)] + [0, 1]
                kgroups = sorted({0, grp_of_tok(ktok0), grp_of_tok(ktok0 + 639)})
                if pat != 2:
                    P.dma("sync", na["bE"][:], na_bias[l, pat], writes=["na_bE"])
            bias = None if isctx else (na["bI"] if pat == 2 else na["bE"])
            bkey = "na_bI" if pat == 2 else "na_bE"
            psO = PS[4 + so]
            qg = grp_of_tok(t * 128)
            for h in range(8):
                ch, p0 = h // 2, (h % 2) * 64
                x = hcnt % 2
                hcnt += 1
                psA, psBk = PS[x], PS[2 + x]
                s_t, p_t, pT_t, nmx = na["s"][x], na["p"][x], na["pT"][x], na["nmx"][x]
                lhsT = na["q"][p0:p0 + 64, ch, t * 128:(t + 1) * 128]
                krd = [("na_q", qg)] + [("na_k", g) for g in kgroups]
                if isctx:
                    P.op("tensor", lambda e, lhsT=lhsT, psA=psA, ch=ch, p0=p0: e.matmul(
                        psA[:, 0:256], lhsT=lhsT, rhs=na["k"][p0:p0 + 64, ch, 0:256], start=True, stop=True),
                        reads=krd, writes=[("ps", x)])
                    P.op("vector", lambda e, s_t=s_t, psA=psA: e.tensor_scalar(out=s_t[:, 0:256], in0=psA[:, 0:256], scalar1=0.125,
                                                                               scalar2=None, op0=ALU.mult),
                         reads=[("ps", x)], writes=[("na_s", x)])
                else:
                    P.op("tensor", lambda e, lhsT=lhsT, psA=psA, ch=ch, p0=p0, ktok0=ktok0: e.matmul(
                        psA[:, 0:512], lhsT=lhsT, rhs=na["k"][p0:p0 + 64, ch, ktok0:ktok0 + 512], start=True, stop=True),
                        reads=krd, writes=[("ps", x)])
                    P.op("tensor", lambda e, lhsT=lhsT, psBk=psBk, ch=ch, p0=p0, ktok0=ktok0: e.matmul(
                        psBk[:, 0:128], lhsT=lhsT, rhs=na["k"][p0:p0 + 64, ch, ktok0 + 512:ktok0 + 640], start=True, stop=True),
                        reads=krd, writes=[("ps", 2 + x)])
                    P.op("tensor", lambda e, lhsT=lhsT, psBk=psBk, ch=ch, p0=p0: e.matmul(
                        psBk[:, 128:384], lhsT=lhsT, rhs=na["k"][p0:p0 + 64, ch, 0:256], start=True, stop=True),
                        reads=krd, writes=[("ps", 2 + x)])
                    P.op("vector", lambda e, s_t=s_t, psA=psA, bias=bias, h=h: e.scalar_tensor_tensor(
                        out=s_t[:, 0:512], in0=psA[:, 0:512], scalar=0.125, in1=bias[:, h, 0:512], op0=ALU.mult, op1=ALU.add),
                        reads=[("ps", x), bkey], writes=[("na_s", x)])
                    P.op("vector", lambda e, s_t=s_t, psBk=psBk, bias=bias, h=h: e.scalar_tensor_tensor(
                        out=s_t[:, 512:896], in0=psBk[:, 0:384], scalar=0.125, in1=bias[:, h, 512:896], op0=ALU.mult, op1=ALU.add),
                        reads=[("ps", 2 + x), bkey], writes=[("na_s", x)])
                P.op("vector", lambda e, s_t=s_t, nmx=nmx, nk=nk: e.tensor_reduce(out=nmx[:, 0:1], in_=s_t[:, 0:nk], axis=AX.X, op=ALU.max,
                                                                                 negate=True),
                     reads=[("na_s", x)], writes=[("na_nmx", x)])
                P.op("scalar", lambda e, s_t=s_t, p_t=p_t, nmx=nmx, nk=nk, h=h, so=so: e.activation(
                    out=p_t[:, 0:nk], in_=s_t[:, 0:nk], func=AF.Exp, bias=nmx[:, 0:1], scale=1.0, accum_out=na["rs"][so][:, h:h + 1]),
                    reads=[("na_s", x), ("na_nmx", x)], writes=[("na_p", x), ("na_rs", so)])
                for j in range(nk // 128):
                    P.op("tensor", lambda e, p_t=p_t, j=j, x=x: e.transpose(out=PSB[x][:, j * 128:(j + 1) * 128],
                                                                          in_=p_t[:, j * 128:(j + 1) * 128], identity=identb[:]),
                         reads=[("na_p", x), "identb"], writes=[("psb", x)])
                if h % 2 == 0:
                    P.op("scalar", lambda e, pT_t=pT_t, x=x, nk=nk: e.copy(out=pT_t[:, 0:nk], in_=PSB[x][:, 0:nk]),
                         reads=[("psb", x)], writes=[("na_pT", x)])
                else:
                    P.op("vector", lambda e, pT_t=pT_t, x=x, nk=nk: e.tensor_copy(out=pT_t[:, 0:nk], in_=PSB[x][:, 0:nk]),
                         reads=[("psb", x)], writes=[("na_pT", x)])
                for j, vt in enumerate(vtiles):
                    P.op("tensor", lambda e, pT_t=pT_t, j=j, vt=vt, h=h, psO=psO, last=(j == len(vtiles) - 1): e.matmul(
                        psO[:, h * 64:(h + 1) * 64], lhsT=pT_t[:, j * 128:(j + 1) * 128], rhs=na["v"][:, vt, h * 64:(h + 1) * 64],
                        start=(j == 0), stop=last), reads=[("na_pT", x), ("na_v", vt)], writes=[("ps", 4 + so)])
            rs, rinv, o_t, oT = na["rs"][so], na["rinv"][so], na["o"][so], na["oT"][so]
            P.op("vector", lambda e, rs=rs, rinv=rinv: e.reciprocal(out=rinv[:], in_=rs[:]), reads=[("na_rs", so)], writes=[("na_rinv", so)])
            P.op("vector", lambda e, o_t=o_t, psO=psO, rinv=rinv: e.tensor_tensor(
                out=o_t[:].rearrange("p (h d) -> p h d", h=8), in0=psO[:].rearrange("p (h d) -> p h d", h=8),
                in1=rinv[:].unsqueeze(2).to_broadcast([128, 8, 64]), op=ALU.mult),
                reads=[("ps", 4 + so), ("na_rinv", so)], writes=[("na_o", so)])
            xx = hcnt % 2
            for c4 in range(4):
                P.op("tensor", lambda e, o_t=o_t, c4=c4, xx=xx: e.transpose(out=PSB[xx][:, c4 * 128:(c4 + 1) * 128],
                                                                          in_=o_t[:, c4 * 128:(c4 + 1) * 128], identity=identb[:]),
                     reads=[("na_o", so), "identb"], writes=[("psb", xx)])
            P.op("scalar", lambda e, oT=oT, xx=xx: e.copy(out=oT[:], in_=PSB[xx][:, 0:512].rearrange("p (c t) -> p c t", c=4)),
                 reads=[("psb", xx)], writes=[("na_oT", so)])
            P.dma("gpsimd", ona_v[:, :, t * 128:(t + 1) * 128], oT[:], reads=[("na_oT", so)], writes=[("ona", qg)])
        if "ona" in dbg and l == 0:
            d = dbg_out("ona", [512, T], BF16)
            P.dma("sync", d, ona_d, reads=[("ona", g) for g in range(9)], writes=["OUT_dbgona"])


    w_branch = P.dram("w_branch", [NL, 3, 512, D], F32, "ExternalInput")
    w_out = P.dram("w_out", [NL, D, D], F32, "ExternalInput")
    w_gu = P.dram("ffn_w_gu", [NL, D, 2 * D_FF], F32, "ExternalInput")
    w_down = P.dram("ffn_w_down", [NL, D_FF, D], F32, "ExternalInput")
    lnp_d = P.dram("lnp", [128, NL, 4, 8], F32, "ExternalInput")
    orw_d = P.dram("orw", [512, T], BF16)
    P.dma("sync", lnp[:], lnp_d, writes=["lnp"])
    P.op("vector", lambda e: e.memset(ones[:], 1.0), writes=["ones"])

    def load_w_bf16(dst_view, src_view, stg, npart, nfree, tag):
        a, b = dst_view.shape[1], dst_view.shape[2]
        rows = max(1, 4096 // b)
        i = 0
        for a0 in range(0, a, rows):
            a1 = min(a, a0 + rows)
            st = stg[i % 2]
            P.dma("sync", st[0:npart, 0:(a1 - a0) * b].rearrange("p (a b) -> p a b", b=b), src_view[:, a0:a1, :], writes=[("stg", i % 2)])
            P.op("gpsimd" if i % 2 == 0 else "vector", lambda e, st=st, a0=a0, a1=a1: e.tensor_copy(
                out=dst_view[:, a0:a1, :], in_=st[0:npart, 0:(a1 - a0) * b].rearrange("p (a b) -> p a b", b=b)),
                reads=[("stg", i % 2)], writes=[tag])
            i += 1

    def layer_norm_T(r, n, l, gi_, bi_, out_view, tagr, tagout, sq, stat):
        for k in range(8):
            P.op("tensor", lambda e, k=k: e.matmul(PS[0][:, :n], lhsT=ones[:], rhs=r[:, k, :n], start=(k == 0), stop=(k == 7)),
                 reads=[tagr, "ones"], writes=[("ps", 0)])
        for k in range(8):
            sqk = sq[k % 2]
            P.op("scalar", lambda e, k=k, sqk=sqk: e.activation(out=sqk[:, :n], in_=r[:, k, :n], func=AF.Square),
                 reads=[tagr], writes=[("lnsq", k % 2)])
            P.op("tensor", lambda e, k=k, sqk=sqk: e.matmul(PS[1][:, :n], lhsT=ones[:], rhs=sqk[:, :n], start=(k == 0), stop=(k == 7)),
                 reads=[("lnsq", k % 2), "ones"], writes=[("ps", 1)])
        mean, var, rstd = stat
        P.op("vector", lambda e: e.tensor_scalar(out=mean[:, :n], in0=PS[0][:, :n], scalar1=1.0 / 1024, scalar2=None, op0=ALU.mult),
             reads=[("ps", 0)], writes=["ln_mean"])
        P.op("vector", lambda e: e.tensor_tensor(out=var[:, :n], in0=mean[:, :n], in1=mean[:, :n], op=ALU.mult),
             reads=["ln_mean"], writes=["ln_var"])
        P.op("vector", lambda e: e.scalar_tensor_tensor(out=var[:, :n], in0=PS[1][:, :n], scalar=1.0 / 1024, in1=var[:, :n],
                                                        op0=ALU.mult, op1=ALU.subtract), reads=[("ps", 1), "ln_var"], writes=["ln_var"])
        P.op("scalar", lambda e: e.activation(out=rstd[:, :n], in_=var[:, :n], func=AF.Sqrt, bias=eps5[:, 0:1], scale=1.0),
             reads=["ln_var", "eps"], writes=["ln_rstd"])
        P.op("vector", lambda e: e.reciprocal(out=rstd[:, :n], in_=rstd[:, :n]), reads=["ln_rstd"], writes=["ln_rstd"])
        for k in range(8):
            eng = "vector" if k % 2 == 0 else "gpsimd"
            P.op(eng, lambda e, k=k: e.tensor_tensor(out=r[:, k, :n], in0=r[:, k, :n], in1=mean[:, :n], op=ALU.subtract),
                 reads=[tagr, "ln_mean", "ln_rstd"], writes=[(tagr, "c", k)])
            P.op(eng, lambda e, k=k: e.tensor_tensor(out=r[:, k, :n], in0=r[:, k, :n], in1=rstd[:, :n], op=ALU.mult),
                 reads=[tagr, (tagr, "c", k), "ln_rstd"], writes=[(tagr, "c", k)])
            P.op(eng, lambda e, k=k: e.tensor_scalar(out=out_view[:, k, :n], in0=r[:, k, :n], scalar1=lnp[:, l, gi_, k:k + 1],
                                                     scalar2=lnp[:, l, bi_, k:k + 1], op0=ALU.mult, op1=ALU.add),
                 reads=[tagr, (tagr, "c", k), "lnp"], writes=[(tagout, k)])

    def stage_merge(l):
        wg = P.sbuf("m_wg", [128, 8, 3072], BF16)
        wbn = P.sbuf("m_wbn", [128, 4, 1024], BF16); wbr = P.sbuf("m_wbr", [128, 4, 1024], BF16)
        wbs = P.sbuf("m_wbs", [64, 8, 1024], BF16); wo = P.sbuf("m_wo", [128, 8, 1024], BF16)
        stg = [P.sbuf("m_stg%d" % i, [128, 4096]) for i in range(2)]
        xr = P.sbuf("m_xr", [128, 8, 512]); hh = P.sbuf("m_h", [128, 8, 512], BF16)
        o_n = P.sbuf("m_on", [128, 4, 512], BF16); o_r = P.sbuf("m_or", [128, 4, 512], BF16); o_s = P.sbuf("m_os", [64, 8, 512], BF16)
        sig = [P.sbuf("m_sig%d" % i, [128, 512]) for i in range(3)]
        ypre = P.sbuf("m_ypre", [128, 8, 512], BF16); yacc = P.sbuf("m_yacc", [128, 512]); ytmp = P.sbuf("m_ytmp", [128, 512])
        sq = [P.sbuf("m_sq%d" % i, [128, 512]) for i in range(2)]
        stat = [P.sbuf("m_st%d" % i, [128, 512]) for i in range(3)]
        load_w_bf16(wg[:], w_in[l, :, 0:3072].rearrange("(k p) m -> p k m", p=128), stg, 128, 0, "m_wg")
        load_w_bf16(wbn[:], w_branch[l, 0].rearrange("(k p) m -> p k m", p=128), stg, 128, 0, "m_wbn")
        load_w_bf16(wbr[:], w_branch[l, 1].rearrange("(k p) m -> p k m", p=128), stg, 128, 0, "m_wbr")
        load_w_bf16(wbs[:], w_branch[l, 2].rearrange("(g c) m -> c g m", c=64), stg, 64, 0, "m_wbs")
        load_w_bf16(wo[:], w_out[l].rearrange("(k p) m -> p k m", p=128), stg, 128, 0, "m_wo")
        ona_v = ona_d.rearrange("(k p) t -> p k t", p=128); orw_v = orw_d.rearrange("(k p) t -> p k t", p=128)
        osgu_v = osgu_d.rearrange("(g c) t -> c g t", c=64)
        pr = 0
        for gi, (t0, n) in enumerate(GROUPS):
            c = 1 if gi == 0 else 0
            P.dma("sync", xr[:, :, :n], xT_v[:, :, t0:t0 + n], reads=[("xT", gi)], writes=["m_xr"])
            P.dma("sync", o_n[:, :, :n], ona_v[:, :, t0:t0 + n], reads=[("ona", gi)], writes=["m_on"])
            P.dma("sync", o_r[:, :, :n], orw_v[:, :, t0:t0 + n], reads=[("orw", gi)], writes=["m_or"])
            P.dma("sync", o_s[:, :, :n], osgu_v[:, :, t0:t0 + n], reads=[("osgu", gi)], writes=["m_os"])
            for k in range(8):
                P.op("vector" if k % 2 == 0 else "gpsimd", lambda e, k=k, c=c: e.tensor_scalar(
                    out=hh[:, k, :n], in0=xr[:, k, :n], scalar1=mod1[:, l, 8 + k, c:c + 1], scalar2=mod[:, l, k, c:c + 1],
                    op0=ALU.mult, op1=ALU.add), reads=["m_xr", "mod", "mod1"], writes=["m_h"])
            for m in range(8):
                for i in range(3):
                    pg, pb = PS[2 * (pr % 3)], PS[2 * (pr % 3) + 1]
                    kg, kb_ = ("ps", 2 * (pr % 3)), ("ps", 2 * (pr % 3) + 1)
                    pr += 1
                    mc = i * 8 + m
                    for k in range(8):
                        P.op("tensor", lambda e, k=k, mc=mc, pg=pg: e.matmul(pg[:, :n], lhsT=wg[:, k, mc * 128:(mc + 1) * 128], rhs=hh[:, k, :n],
                                                                             start=(k == 0), stop=(k == 7)), reads=["m_wg", "m_h"], writes=[kg])
                    P.op("scalar", lambda e, i=i, pg=pg: e.activation(out=sig[i][:, :n], in_=pg[:, :n], func=AF.Sigmoid),
                         reads=[kg], writes=[("m_sig", i)])
                    if i < 2:
                        wb_, ob_, ok_ = (wbn, o_n, "m_on") if i == 0 else (wbr, o_r, "m_or")
                        for k in range(4):
                            P.op("tensor", lambda e, k=k, m=m, pb=pb, wb_=wb_, ob_=ob_: e.matmul(
                                pb[:, :n], lhsT=wb_[:, k, m * 128:(m + 1) * 128], rhs=ob_[:, k, :n], start=(k == 0), stop=(k == 3)),
                                reads=["m_wbn", "m_wbr", ok_], writes=[kb_])
                    else:
                        for g in range(8):
                            P.op("tensor", lambda e, g=g, m=m, pb=pb: e.matmul(
                                pb[:, :n], lhsT=wbs[:, g, m * 128:(m + 1) * 128], rhs=o_s[:, g, :n], start=(g == 0), stop=(g == 7)),
                                reads=["m_wbs", "m_os"], writes=[kb_])
                    if i == 0:
                        P.op("vector", lambda e, pb=pb: e.tensor_tensor(out=yacc[:, :n], in0=pb[:, :n], in1=sig[0][:, :n], op=ALU.mult),
                             reads=[kb_, ("m_sig", 0)], writes=["m_yacc"])
                    else:
                        P.op("vector", lambda e, pb=pb, i=i: e.tensor_tensor(out=ytmp[:, :n], in0=pb[:, :n], in1=sig[i][:, :n], op=ALU.mult),
                             reads=[kb_, ("m_sig", i)], writes=["m_ytmp"])
                        if i == 1:
                            P.op("gpsimd", lambda e: e.tensor_tensor(out=yacc[:, :n], in0=yacc[:, :n], in1=ytmp[:, :n], op=ALU.add),
                                 reads=["m_yacc", "m_ytmp"], writes=["m_yacc"])
                        else:
                            P.op("gpsimd", lambda e, m=m: e.tensor_tensor(out=ypre[:, m, :n], in0=yacc[:, :n], in1=ytmp[:, :n], op=ALU.add),
                                 reads=["m_yacc", "m_ytmp"], writes=["m_ypre"])
            for m in range(8):
                pg = PS[2 + m % 4]; kg = ("ps", 2 + m % 4)
                for k in range(8):
                    P.op("tensor", lambda e, k=k, m=m, pg=pg: e.matmul(pg[:, :n], lhsT=wo[:, k, m * 128:(m + 1) * 128], rhs=ypre[:, k, :n],
                                                                         start=(k == 0), stop=(k == 7)), reads=["m_wo", "m_ypre"], writes=[kg])
                P.op("scalar", lambda e, m=m, pg=pg, c=c: e.activation(out=ytmp[:, :n], in_=pg[:, :n], func=AF.Copy,
                                                                      scale=mod[:, l, 16 + m, c:c + 1]), reads=[kg, "mod"], writes=["m_ytmp"])
                P.op("vector", lambda e, m=m: e.scalar_tensor_tensor(out=xr[:, m, :n], in0=xr[:, m, :n], scalar=ALPHA, in1=ytmp[:, :n],
                                                                    op0=ALU.mult, op1=ALU.add), reads=["m_xr", "m_ytmp"], writes=["m_xr"])
            layer_norm_T(xr, n, l, 0, 1, xr, "m_xr", "m_x1", sq, stat)
            P.dma("gpsimd", xT_v[:, :, t0:t0 + n], xr[:, :, :n], reads=["m_xr"] + [("m_x1", k) for k in range(8)], writes=[("xT", gi)])
        if "x1" in dbg and l == 0:
            d = dbg_out("x1", [D, T])
            P.dma("sync", d, xT, reads=[("xT", g) for g in range(9)], writes=["OUT_dbgx1"])

    FG = [(t0, 256) for t0 in range(0, T, 256)]

    def stage_ffn(l):
        wgu = P.sbuf("f_wgu", [128, 8, 2 * D_FF], BF16)
        wd = P.sbuf("f_wd", [128, 22, 1024], BF16)
        stg = [P.sbuf("f_stg%d" % i, [128, 2048]) for i in range(2)]
        xr = P.sbuf("f_xr", [128, 8, 256]); ff = P.sbuf("f_f", [128, 8, 256], BF16)
        act = P.sbuf("f_a", [128, 22, 256], BF16)
        sgt = [P.sbuf("f_sg%d" % i, [128, 256]) for i in range(2)]
        ytmp = P.sbuf("f_ytmp", [128, 256])
        sq = [P.sbuf("f_sq%d" % i, [128, 256]) for i in range(2)]
        stat = [P.sbuf("f_st%d" % i, [128, 256]) for i in range(3)]

        def load2(dst_view, src_view, tag):
            a, b = dst_view.shape[1], dst_view.shape[2]
            i = 0
            for a0 in range(a):
                for b0 in range(0, b, 2048):
                    b1 = min(b, b0 + 2048)
                    st = stg[i % 2]
                    P.dma("sync", st[:, 0:b1 - b0], src_view[:, a0, b0:b1], writes=[("fstg", i % 2)])
                    P.op("gpsimd" if i % 2 == 0 else "vector", lambda e, st=st, a0=a0, b0=b0, b1=b1: e.tensor_copy(
                        out=dst_view[:, a0, b0:b1], in_=st[:, 0:b1 - b0]), reads=[("fstg", i % 2)], writes=[tag])
                    i += 1
        load2(wgu[:], w_gu[l].rearrange("(k p) m -> p k m", p=128), "f_wgu")
        load2(wd[:], w_down[l].rearrange("(k p) m -> p k m", p=128), "f_wd")
        for fi, (t0, n) in enumerate(FG):
            gi = 0 if t0 < 256 else 1 + (t0 - 256) // 512
            c = 1 if fi == 0 else 0
            P.dma("sync", xr[:, :, :n], xT_v[:, :, t0:t0 + n], reads=[("xT", gi)], writes=["f_xr"])
            for k in range(8):
                P.op("vector" if k % 2 == 0 else "gpsimd", lambda e, k=k, c=c: e.tensor_scalar(
                    out=ff[:, k, :n], in0=xr[:, k, :n], scalar1=mod1[:, l, 32 + k, c:c + 1], scalar2=mod[:, l, 24 + k, c:c + 1],
                    op0=ALU.mult, op1=ALU.add), reads=["f_xr", "mod", "mod1"], writes=["f_f"])
            for j in range(22):
                x2 = j % 2
                pg, pu = PS[2 + 2 * x2], PS[3 + 2 * x2]
                kg, ku = ("ps", 2 + 2 * x2), ("ps", 3 + 2 * x2)
                for k in range(8):
                    P.op("tensor", lambda e, k=k, j=j, pg=pg: e.matmul(pg[:, :n], lhsT=wgu[:, k, j * 128:(j + 1) * 128], rhs=ff[:, k, :n],
                                                                         start=(k == 0), stop=(k == 7)), reads=["f_wgu", "f_f"], writes=[kg])
                for k in range(8):
                    P.op("tensor", lambda e, k=k, j=j, pu=pu: e.matmul(pu[:, :n], lhsT=wgu[:, k, D_FF + j * 128:D_FF + (j + 1) * 128],
                                                                         rhs=ff[:, k, :n], start=(k == 0), stop=(k == 7)),
                         reads=["f_wgu", "f_f"], writes=[ku])
                P.op("scalar", lambda e, pg=pg, x2=x2: e.activation(out=sgt[x2][:, :n], in_=pg[:, :n], func=AF.Silu),
                     reads=[kg], writes=[("f_sg", x2)])
                P.op("vector", lambda e, pu=pu, x2=x2, j=j: e.tensor_tensor(out=act[:, j, :n], in0=pu[:, :n], in1=sgt[x2][:, :n], op=ALU.mult),
                     reads=[ku, ("f_sg", x2)], writes=["f_a"])
            for m in range(8):
                pg = PS[2 + m % 4]; kg = ("ps", 2 + m % 4)
                for j in range(22):
                    P.op("tensor", lambda e, j=j, m=m, pg=pg: e.matmul(pg[:, :n], lhsT=wd[:, j, m * 128:(m + 1) * 128], rhs=act[:, j, :n],
                                                                         start=(j == 0), stop=(j == 21)), reads=["f_wd", "f_a"], writes=[kg])
                P.op("scalar", lambda e, m=m, pg=pg, c=c: e.activation(out=ytmp[:, :n], in_=pg[:, :n], func=AF.Copy,
                                                                      scale=mod[:, l, 40 + m, c:c + 1]), reads=[kg, "mod"], writes=["f_ytmp"])
                P.op("vector", lambda e, m=m: e.scalar_tensor_tensor(out=xr[:, m, :n], in0=xr[:, m, :n], scalar=ALPHA, in1=ytmp[:, :n],
                                                                    op0=ALU.mult, op1=ALU.add), reads=["f_xr", "f_ytmp"], writes=["f_xr"])
            layer_norm_T(xr, n, l, 2, 3, xr, "f_xr", "f_x2", sq, stat)
            P.dma("gpsimd", xT_v[:, :, t0:t0 + n], xr[:, :, :n], reads=["f_xr"] + [("f_x2", k) for k in range(8)], writes=[("xT", gi)])
        if "x2" in dbg and l == 0:
            d = dbg_out("x2", [D, T])
            P.dma("sync", d, xT, reads=[("xT", g) for g in range(9)], writes=["OUT_dbgx2"])

    def stage_final():
        xg_ = [P.sbuf("o_xg%d" % i, [128, 8, 128]) for i in range(2)]
        ot = [P.sbuf("o_t%d" % i, [128, 1024]) for i in range(2)]
        for t in range(2, NT):
            s_ = t % 2
            gi = 1 + (t - 2) // 4
            P.dma("sync", xg_[s_][:], xT_v[:, :, t * 128:(t + 1) * 128], reads=[("xT", gi)], writes=[("o_xg", s_)])
            for half in range(2):
                pb = 2 + 2 * s_ + half
                for kk in range(4):
                    k = half * 4 + kk
                    P.op("tensor", lambda e, s_=s_, k=k, kk=kk, pb=pb: e.transpose(
                        out=PS[pb][:, kk * 128:(kk + 1) * 128], in_=xg_[s_][:, k, :], identity=ident[:]),
                        reads=[("o_xg", s_), "ident"], writes=[("ps", pb)])
                if half == 0:
                    P.op("scalar", lambda e, s_=s_, pb=pb: e.copy(out=ot[s_][:, 0:512], in_=PS[pb][:]), reads=[("ps", pb)], writes=[("o_t", s_, 0)])
                else:
                    P.op("vector", lambda e, s_=s_, pb=pb: e.tensor_copy(out=ot[s_][:, 512:1024], in_=PS[pb][:]), reads=[("ps", pb)],
                         writes=[("o_t", s_, 1)])
            P.dma("gpsimd", out_d[(t - 2) * 128:(t - 1) * 128, :], ot[s_][:], reads=[("o_t", s_, 0), ("o_t", s_, 1)], writes=["OUT_%d" % t])

    rw_mu_d = P.dram("rw_mu", [128, NL, 2, 15], F32, "ExternalInput")
    rw_w0a0_d = P.dram("rw_w0a0", [128, NL, 2, 2, 4], F32, "ExternalInput")
    rw_w2_d = P.dram("rw_w2", [NL, 128, 512], F32, "ExternalInput")
    rw_a2_d = P.dram("rw_a2", [NL, 128, 512], F32, "ExternalInput")
    rw_g2_d = P.dram("rw_g2", [NL, 128, 512], F32, "ExternalInput")
    rw_vec_d = P.dram("rw_vec", [128, NL, 5, 4], F32, "ExternalInput")
    mask64_d = P.dram("mask64", [64, 4, 64], F32, "ExternalInput")
    bones_d = P.dram("bones", [128, 128], F32, "ExternalInput")
    g_d = P.dram("rw_g", [512, T], F32)
    bv_d = P.dram("rw_bv", [512, T], F32)
    NCH = T // 64
    summ_d = P.dram("rw_summ", [2, NCH, 64, 2056], F32)
    etot_d = P.dram("rw_etot", [2, NCH, 512], F32)
    y_d = P.dram("rw_y", [2, T, 512], F32)
    CDEC = float(np.exp(-0.5))
    psctr = [0]

    def psn():
        i = psctr[0] % 6
        psctr[0] += 1
        return PS[i], ("ps", i)

    def stage_rwkv(l):
        import os
        RW_NG = int(os.environ.get("RW_NGROUPS", "17")); RW_CH = int(os.environ.get("RW_CHUNKS", "1")); RW_PH = int(os.environ.get("RW_PHASES", "3"))
        RW_RND = int(os.environ.get("RW_ROUNDS", "6")); RW_ST = int(os.environ.get("RW_STEPS", "99"))
        NG = 256
        prw_v = prw.rearrange("(k p) t -> p k t", p=128)
        g_v = g_d.rearrange("(k p) t -> p k t", p=128)
        bv_v = bv_d.rearrange("(k p) t -> p k t", p=128)
        grp512 = lambda tok: 0 if tok < 256 else 1 + (tok - 256) // 512
        mu = P.sbuf("rw_mu", [128, 2, 15]); c0 = P.sbuf("rw_c0", [128, 15])
        w0a0 = P.sbuf("rw_w0a0", [128, 2, 2, 4]); w2s = P.sbuf("rw_w2", [128, 512]); a2s = P.sbuf("rw_a2", [128, 512])
        g2s = P.sbuf("rw_g2", [128, 512]); vec = P.sbuf("rw_vec", [128, 5, 4]); omka = P.sbuf("rw_omka", [128, 4])
        mask = P.sbuf("rw_mask", [64, 4, 64]); bones = P.sbuf("rw_bones", [128, 128]); eps12 = P.sbuf("rw_eps12", [128, 1])
        eps_gn = P.sbuf("rw_epsgn", [128, 1]); rmask = P.sbuf("rw_rmask", [128, 16, 64])
        P.dma("sync", mu[:], rw_mu_d[:, l], writes=["rw_par"]); P.dma("sync", w0a0[:], rw_w0a0_d[:, l], writes=["rw_par"])
        P.dma("sync", w2s[:], rw_w2_d[l], writes=["rw_par"]); P.dma("sync", a2s[:], rw_a2_d[l], writes=["rw_par"])
        P.dma("sync", g2s[:], rw_g2_d[l], writes=["rw_par"]); P.dma("sync", vec[:], rw_vec_d[:, l], writes=["rw_par"])
        P.dma("sync", mask[:], mask64_d, writes=["rw_par"]); P.dma("sync", bones[:], bones_d, writes=["rw_par"])
        P.op("vector", lambda e: e.tensor_tensor(out=c0[:], in0=mu[:, 0, :], in1=mu[:, 1, :], op=ALU.add), reads=["rw_par"], writes=["rw_c0"])
        P.op("vector", lambda e: e.tensor_scalar(out=c0[:], in0=c0[:], scalar1=-1.0, scalar2=1.0, op0=ALU.mult, op1=ALU.add),
             reads=["rw_c0"], writes=["rw_c0"])
        P.op("vector", lambda e: e.tensor_scalar(out=omka[:], in0=vec[:, 1, :], scalar1=-1.0, scalar2=1.0, op0=ALU.mult, op1=ALU.add),
             reads=["rw_par"], writes=["rw_omka"])
        P.op("vector", lambda e: e.memset(eps12[:], 1e-12), writes=["rw_eps"])
        P.op("vector", lambda e: e.memset(eps_gn[:], 64e-5), writes=["rw_eps"])
        P.op("vector", lambda e: e.memset(rmask[:], 1.0), writes=["rw_rmask"])
        P.op("vector", lambda e: e.memset(rmask[:, :, 0:1], 0.0), reads=["rw_rmask"], writes=["rw_rmask"])
        m1 = P.mark()
        pin = P.sbuf("rw_pin", [128, 15, NG + 2]); psh = P.sbuf("rw_psh", [128, 15, NG])
        sgw = [P.sbuf("rw_sgw%d" % d, [128, 4, NG]) for d in range(2)]
        aa = [P.sbuf("rw_a%d" % d, [128, 4, NG]) for d in range(2)]
        kd = [P.sbuf("rw_kd%d" % d, [128, 4, NG]) for d in range(2)]
        kk = P.sbuf("rw_kk", [128, 4, NG]); tA = P.sbuf("rw_tA", [128, 4, NG]); tB = P.sbuf("rw_tB", [128, 4, NG])
        Lp = P.sbuf("rw_Lp", [128, 4, NG]); Li = P.sbuf("rw_Li", [128, 4, NG]); Le = P.sbuf("rw_Le", [128, 4, NG])
        rt = P.sbuf("rw_rt", [128, 4, NG]); at = P.sbuf("rw_at", [128, 4, NG]); kt = P.sbuf("rw_kt", [128, 4, NG])
        bt = P.sbuf("rw_bt", [128, 4, NG]); Kh = P.sbuf("rw_Kh", [128, 4, NG]); Bh = P.sbuf("rw_Bh", [128, 4, NG])
        etot = P.sbuf("rw_etot", [128, 4, 4]); gst = P.sbuf("rw_gst", [128, 4, NG])
        mk = {nm: [P.sbuf("rw_%s%s" % (nm, eo), [128, 4, NG]) for eo in "EO"] for nm in ("at", "kt", "bt")}
        Vt = [P.sbuf("rw_Vt%d" % c, [64, 512]) for c in range(4)]
        KhT = P.sbuf("rw_KhT", [64, 512]); BhT = P.sbuf("rw_BhT", [64, 512])
        Mka = P.sbuf("rw_Mka", [64, 8, 64]); Mkr = P.sbuf("rw_Mkr", [64, 8, 64]); Mbr = P.sbuf("rw_Mbr", [64, 8, 64])
        Nn = [P.sbuf("rw_N%d" % i, [64, 8, 64]) for i in range(2)]; NTr = [P.sbuf("rw_NT%d" % i, [64, 8, 64]) for i in range(2)]
        Z = P.sbuf("rw_Z", [64, 8, 128]); Zn = P.sbuf("rw_Zn", [64, 8, 128])
        summ = [P.sbuf("rw_summ%d" % i, [64, 2056]) for i in range(2)]
        r_ = psh[:, 0:4, :]; k_ = psh[:, 4:8, :]; v_ = psh[:, 8:12, :]
        TA = [("rw_tA", fc) for fc in range(4)]; TB = [("rw_tB", fc) for fc in range(4)]
        nsum = 0
        for gx in range(min(T // NG, RW_NG)):
            t0 = gx * NG
            isctx = gx == 0
            rdk = sorted({grp512(max(t0 - 1, 0)), grp512(t0), grp512(min(t0 + NG, T - 1))})
            rdk = [("prw", "fm", g) for g in rdk]
            hasL = not (isctx or gx == 1)
            hasR = not (isctx or gx == T // NG - 1)
            lo = t0 - 1 if hasL else t0
            hi = t0 + NG + 1 if hasR else t0 + NG
            P.dma("sync", pin[:, :, lo - (t0 - 1):hi - (t0 - 1)], prw_v[:, :, lo:hi], reads=rdk, writes=["rw_pin"])
            if not hasL:
                P.op("gpsimd", lambda e: e.memset(pin[:, :, 0:1], 0.0), reads=["rw_pin"], writes=["rw_pinL"])
            if not hasR:
                P.op("gpsimd", lambda e: e.memset(pin[:, :, NG + 1:NG + 2], 0.0), reads=["rw_pin"], writes=["rw_pinR"])
            pk = ["rw_pin", "rw_pinL", "rw_pinR"]
            for ch in range(15):
                P.op("gpsimd", lambda e, ch=ch: e.tensor_scalar(out=psh[:, ch, :], in0=pin[:, ch, 1:NG + 1], scalar1=c0[:, ch:ch + 1],
                                                               scalar2=None, op0=ALU.mult), reads=pk + ["rw_c0"], writes=[("rw_psh", ch)])
                P.op("vector", lambda e, ch=ch: e.scalar_tensor_tensor(out=psh[:, ch, :], in0=pin[:, ch, 0:NG], scalar=mu[:, 0, ch:ch + 1],
                                                                      in1=psh[:, ch, :], op0=ALU.mult, op1=ALU.add),
                     reads=pk + ["rw_par", ("rw_psh", ch)], writes=[("rw_psh", ch)])
                P.op("vector", lambda e, ch=ch: e.scalar_tensor_tensor(out=psh[:, ch, :], in0=pin[:, ch, 2:NG + 2], scalar=mu[:, 1, ch:ch + 1],
                                                                      in1=psh[:, ch, :], op0=ALU.mult, op1=ALU.add),
                     reads=pk + ["rw_par", ("rw_psh", ch)], writes=[("rw_psh", ch)])
            RK = [("rw_psh", c) for c in range(0, 4)]; KK = [("rw_psh", c) for c in range(4, 8)]; VK = [("rw_psh", c) for c in range(8, 12)]
            P.op("scalar", lambda e: e.activation(out=psh[:, 12, :], in_=psh[:, 12, :], func=AF.Tanh), reads=[("rw_psh", 12)], writes=[("rw_psh", 12)])
            P.op("scalar", lambda e: e.activation(out=psh[:, 14, :], in_=psh[:, 14, :], func=AF.Sigmoid), reads=[("rw_psh", 14)],
                 writes=[("rw_psh", 14)])
            for which, (wsrc, srcch, dst) in enumerate(((w2s, 12, sgw), (a2s, 13, aa))):
                for d in range(2):
                    for fc in range(4):
                        ps, pk_ = psn()
                        P.op("tensor", lambda e, d=d, fc=fc, ps=ps, wsrc=wsrc, srcch=srcch: e.matmul(
                            ps[:, 0:NG], lhsT=wsrc[d * 64:(d + 1) * 64, fc * 128:(fc + 1) * 128], rhs=psh[d * 64:(d + 1) * 64, srcch, :],
                            start=True, stop=True), reads=["rw_par", ("rw_psh", srcch)], writes=[pk_])
                        P.op("scalar", lambda e, d=d, fc=fc, ps=ps, dst=dst, which=which: e.activation(
                            out=dst[d][:, fc, :], in_=ps[:, 0:NG], func=AF.Sigmoid, bias=w0a0[:, which, d, fc:fc + 1], scale=1.0),
                            reads=[pk_, "rw_par"], writes=[("rw_sa", which, d)])
            for fc in range(4):
                ps, pk_ = psn()
                P.op("tensor", lambda e, fc=fc, ps=ps: e.matmul(ps[:, 0:NG], lhsT=g2s[:, fc * 128:(fc + 1) * 128], rhs=psh[:, 14, :],
                                                              start=True, stop=True), reads=["rw_par", ("rw_psh", 14)], writes=[pk_])
                P.op("scalar", lambda e, fc=fc, ps=ps: e.copy(out=gst[:, fc, :], in_=ps[:, 0:NG]), reads=[pk_], writes=["rw_gst"])
            P.dma("gpsimd", g_v[:, :, t0:t0 + NG], gst[:], reads=["rw_gst"], writes=[("rw_g", gx // 2)])
            for fc in range(4):
                P.op("gpsimd", lambda e, fc=fc: e.tensor_scalar(out=kk[:, fc, :], in0=psh[:, 4 + fc, :], scalar1=vec[:, 0, fc:fc + 1], scalar2=None,
                                                               op0=ALU.mult), reads=KK + ["rw_par"], writes=[("rw_kk", fc)])
                P.op("scalar", lambda e, fc=fc: e.activation(out=tA[:, fc, :], in_=kk[:, fc, :], func=AF.Square), reads=[("rw_kk", fc)],
                     writes=[("rw_tA", fc)])
                ps, pk_ = psn()
                P.op("tensor", lambda e, fc=fc, ps=ps: e.matmul(ps[:, 0:NG], lhsT=bones[:], rhs=tA[:, fc, :], start=True, stop=True),
                     reads=["rw_par", ("rw_tA", fc)], writes=[pk_])
                P.op("scalar", lambda e, fc=fc, ps=ps: e.activation(out=tB[:, fc, :], in_=ps[:, 0:NG], func=AF.Sqrt, bias=eps12[:, 0:1], scale=1.0),
                     reads=[pk_, "rw_eps"], writes=[("rw_tB", fc)])
                P.op("vector", lambda e, fc=fc: e.reciprocal(out=tB[:, fc, :], in_=tB[:, fc, :]), reads=[("rw_tB", fc)], writes=[("rw_tB", fc)])
                P.op("vector", lambda e, fc=fc: e.tensor_tensor(out=kk[:, fc, :], in0=kk[:, fc, :], in1=tB[:, fc, :], op=ALU.mult),
                     reads=[("rw_kk", fc), ("rw_tB", fc)], writes=[("rw_kk", fc)])
            KKN = [("rw_kk", fc) for fc in range(4)]
            for d in range(2):
                for fc in range(4):
                    P.op("gpsimd", lambda e, d=d, fc=fc: e.tensor_scalar(out=kd[d][:, fc, :], in0=aa[d][:, fc, :], scalar1=vec[:, 1, fc:fc + 1],
                                                                        scalar2=omka[:, fc:fc + 1], op0=ALU.mult, op1=ALU.add),
                         reads=[("rw_sa", 1, d), "rw_par", "rw_omka"], writes=[("rw_kd", d)])
                P.op("gpsimd", lambda e, d=d: e.tensor_tensor(out=kd[d][:], in0=kd[d][:], in1=k_, op=ALU.mult),
                     reads=[("rw_kd", d)] + KK, writes=[("rw_kd", d)])
                P.op("vector", lambda e, d=d: e.tensor_tensor(out=aa[d][:], in0=aa[d][:], in1=kk[:], op=ALU.mult),
                     reads=[("rw_sa", 1, d), ("rw_kd", d)] + KKN, writes=[("rw_sa", 1, d)])
            P.op("gpsimd", lambda e: e.tensor_tensor(out=tA[:], in0=kd[0][:], in1=kd[1][:], op=ALU.add),
                 reads=[("rw_kd", 0), ("rw_kd", 1)], writes=TA)
            P.op("gpsimd", lambda e: e.tensor_tensor(out=tA[:], in0=tA[:], in1=r_, op=ALU.mult), reads=TA + RK, writes=TA)
            for fc in range(4):
                P.op("gpsimd", lambda e, fc=fc: e.tensor_scalar(out=tA[:, fc, :], in0=tA[:, fc, :], scalar1=vec[:, 2, fc:fc + 1], scalar2=None,
                                                               op0=ALU.mult), reads=[("rw_tA", fc), "rw_par"], writes=[("rw_tA", fc)])
                ps, pk_ = psn()
                P.op("tensor", lambda e, fc=fc, ps=ps: e.matmul(ps[:, 0:NG], lhsT=bones[:], rhs=tA[:, fc, :], start=True, stop=True),
                     reads=["rw_par", ("rw_tA", fc)], writes=[pk_])
                P.op("vector", lambda e, fc=fc, ps=ps: e.tensor_tensor(out=gst[:, fc, :], in0=ps[:, 0:NG], in1=psh[:, 8 + fc, :], op=ALU.mult),
                     reads=[pk_, "rw_gst"] + VK, writes=["rw_gst"])
            P.dma("gpsimd", bv_v[:, :, t0:t0 + NG], gst[:], reads=["rw_gst"], writes=[("rw_bv", gx // 2)])
            for c in range(4):
                ps, pk_ = psn()
                for fc in range(4):
                    P.op("tensor", lambda e, c=c, fc=fc, ps=ps: e.transpose(out=ps[0:64, fc * 128:(fc + 1) * 128],
                                                                          in_=psh[:, 8 + fc, c * 64:(c + 1) * 64], identity=ident[:]),
                         reads=VK + ["ident"], writes=[pk_])
                P.op("scalar", lambda e, c=c, ps=ps: e.copy(out=Vt[c][:], in_=ps[0:64, :]), reads=[pk_], writes=[("rw_Vt", c)])
            for d in range(2):
                P.op("vector", lambda e, d=d: e.tensor_tensor_scan(out=Lp[:].rearrange("p a b -> p (a b)"),
                                                                  data0=rmask[:].rearrange("p a b -> p (a b)"),
                                                                  data1=sgw[d][:].rearrange("p a b -> p (a b)"), initial=0.0,
                                                                  op0=ALU.mult, op1=ALU.add),
                     reads=[("rw_sa", 0, d), "rw_rmask"], writes=["rw_Lp"])
                Lp3 = Lp[:].rearrange("p a (c t) -> p (a c) t", t=64)
                Li3 = Li[:].rearrange("p a (c t) -> p (a c) t", t=64); Le3 = Le[:].rearrange("p a (c t) -> p (a c) t", t=64)
                tot_b = Lp3[:, :, 63:64].to_broadcast([128, 16, 64])
                if d == 0:
                    P.op("gpsimd", lambda e: e.tensor_copy(out=Li[:], in_=Lp[:]), reads=["rw_Lp"], writes=["rw_Li"])
                    P.op("vector", lambda e, d=d: e.tensor_tensor(out=Le[:], in0=Lp[:], in1=sgw[d][:], op=ALU.subtract),
                         reads=["rw_Lp", ("rw_sa", 0, d)], writes=["rw_Le"])
                else:
                    P.op("vector", lambda e, d=d: e.tensor_tensor(out=Le[:], in0=Lp[:], in1=sgw[d][:], op=ALU.subtract),
                         reads=["rw_Lp", ("rw_sa", 0, d)], writes=["rw_Le"])
                    P.op("vector", lambda e: e.tensor_tensor(out=Li3, in0=tot_b, in1=Le3, op=ALU.subtract), reads=["rw_Lp", "rw_Le"],
                         writes=["rw_Li"])
                    P.op("vector", lambda e: e.tensor_tensor(out=Le3, in0=tot_b, in1=Lp3, op=ALU.subtract), reads=["rw_Lp", "rw_Li"],
                         writes=["rw_Le"])
                P.op("scalar", lambda e: e.activation(out=etot[:].rearrange("p a c -> p (a c)"), in_=Lp3[:, :, 63], func=AF.Exp, scale=-CDEC),
                     reads=["rw_Lp"], writes=["rw_etot"])
                P.op("scalar", lambda e: e.activation(out=tA[:], in_=Li[:], func=AF.Exp, scale=-CDEC), reads=["rw_Li"], writes=TA)
                P.op("vector", lambda e: e.tensor_tensor(out=rt[:], in0=r_, in1=tA[:], op=ALU.mult), reads=RK + TA, writes=["rw_rt"])
                P.op("scalar", lambda e: e.activation(out=tB[:], in_=Le[:], func=AF.Exp, scale=-CDEC), reads=["rw_Le"], writes=TB)
                P.op("gpsimd", lambda e: e.tensor_tensor(out=at[:], in0=kk[:], in1=tB[:], op=ALU.mult), reads=KKN + TB, writes=["rw_at"])
                P.op("scalar", lambda e: e.activation(out=tA[:], in_=Li[:], func=AF.Exp, scale=CDEC), reads=["rw_Li"], writes=TA)
                P.op("vector", lambda e, d=d: e.tensor_tensor(out=kt[:], in0=kd[d][:], in1=tA[:], op=ALU.mult), reads=[("rw_kd", d)] + TA, writes=["rw_kt"])
                P.op("gpsimd", lambda e, d=d: e.tensor_tensor(out=bt[:], in0=aa[d][:], in1=tA[:], op=ALU.mult), reads=[("rw_sa", 1, d)] + TA, writes=["rw_bt"])
                et_b = etot[:].rearrange("p a c -> p (a c)").unsqueeze(2).to_broadcast([128, 16, 64])
                P.op("vector", lambda e: e.tensor_tensor(out=Kh[:].rearrange("p a (c t) -> p (a c) t", t=64),
                                                         in0=kt[:].rearrange("p a (c t) -> p (a c) t", t=64), in1=et_b, op=ALU.mult),
                     reads=["rw_kt", "rw_etot"], writes=["rw_Kh"])
                P.op("gpsimd", lambda e: e.tensor_tensor(out=Bh[:].rearrange("p a (c t) -> p (a c) t", t=64),
                                                         in0=bt[:].rearrange("p a (c t) -> p (a c) t", t=64), in1=et_b, op=ALU.mult),
                     reads=["rw_bt", "rw_etot"], writes=["rw_Bh"])
                for xi, (nm, X) in enumerate((("at", at), ("kt", kt), ("bt", bt))):
                    for eo in range(2):
                        P.op("gpsimd" if (xi + eo) % 2 == 0 else "vector", lambda e, nm=nm, X=X, eo=eo: e.tensor_scalar(
                            out=mk[nm][eo][:], in0=X[:], scalar1=bones[:, eo * 64:eo * 64 + 1], scalar2=None, op0=ALU.mult),
                            reads=["rw_" + nm, "rw_par"], writes=[("rw_mk", nm)])
                cg0 = gx * 4
                mS = 0 if d == 0 else 2
                mST = 2 if d == 0 else 0
                mI = 1 if d == 0 else 3
                for c in range(4 if RW_CH else 0):
                    sl = slice(c * 64, (c + 1) * 64)
                    cg = cg0 + c
                    sm = summ[nsum % 2]; smk = ("rw_summ", nsum % 2)
                    nsum += 1
                    hd = lambda X, h, sl=sl: X[:, h // 2, sl]
                    hdL = lambda nm, h, sl=sl: mk[nm][h % 2][:, h // 2, sl]
                    for src, skey, dst, dkey in ((at, "rw_at", None, "rw_Z"), (Kh, "rw_Kh", KhT, "rw_KhT"), (Bh, "rw_Bh", BhT, "rw_BhT")):
                        ps, pk_ = psn()
                        for fc in range(4):
                            P.op("tensor", lambda e, fc=fc, ps=ps, src=src, sl=sl: e.transpose(out=ps[0:64, fc * 128:(fc + 1) * 128],
                                                                                            in_=src[:, fc, sl], identity=ident[:]),
                                 reads=[skey, "ident"], writes=[pk_])
                        if dst is None:
                            P.op("scalar", lambda e, ps=ps: e.copy(out=Z[:, :, 0:64], in_=ps[0:64, :].rearrange("p (h k) -> p h k", h=8)),
                                 reads=[pk_], writes=["rw_Z"])
                        else:
                            P.op("scalar", lambda e, ps=ps, dst=dst: e.copy(out=dst[:], in_=ps[0:64, :]), reads=[pk_], writes=[dkey])
                    if RW_ST <= 1:
                        continue
                    specs = (("bt", at, ("rw_mk", "bt"), "rw_at", Nn[0], ("rw_N", 0), mS), ("at", bt, ("rw_mk", "at"), "rw_bt", NTr[0], ("rw_NT", 0), mST),
                             ("kt", at, ("rw_mk", "kt"), "rw_at", Mka, "rw_Mka", mS), ("kt", rt, ("rw_mk", "kt"), "rw_rt", Mkr, "rw_Mkr", mI),
                             ("bt", rt, ("rw_mk", "bt"), "rw_rt", Mbr, "rw_Mbr", mI))
                    for si, (L_, R_, lk, rk_, dst, dkey, mi) in enumerate(specs):
                        ps, pk_ = psn()
                        for h in range(8):
                            P.op("tensor", lambda e, h=h, ps=ps, la=hdL(L_, h), ra=hd(R_, h): e.matmul(ps[0:64, h * 64:(h + 1) * 64], lhsT=la, rhs=ra,
                                                                                                     start=True, stop=True), reads=[lk, rk_], writes=[pk_])
                        P.op("vector" if si % 2 == 0 else "gpsimd" if False else "vector", lambda e, ps=ps, dst=dst, mi=mi: e.tensor_tensor(
                            out=dst[:], in0=ps[0:64, :].rearrange("p (h i) -> p h i", h=8), in1=mask[:, mi:mi + 1, :].to_broadcast([64, 8, 64]),
                            op=ALU.mult), reads=[pk_, "rw_par"], writes=[dkey])
                    if RW_ST <= 2:
                        continue
                    ps, pk_ = psn()
                    for h in range(8):
                        P.op("tensor", lambda e, h=h, ps=ps, c=c: e.matmul(ps[0:64, h * 64:(h + 1) * 64], lhsT=Mka[:, h, :],
                                                                          rhs=Vt[c][:, h * 64:(h + 1) * 64], start=True, stop=True),
                             reads=["rw_Mka", ("rw_Vt", c)], writes=[pk_])
                    P.op("scalar", lambda e, ps=ps: e.copy(out=Z[:, :, 64:128], in_=ps[0:64, :].rearrange("p (h k) -> p h k", h=8)),
                         reads=[pk_], writes=["rw_Z2"])
                    if RW_ST <= 3:
                        continue
                    cur = 0
                    for rnd in range(RW_RND):
                        N_, NT_ = Nn[cur], NTr[cur]
                        nk, ntk = ("rw_N", cur), ("rw_NT", cur)
                        pz = [psn(), psn()]
                        for h in range(8):
                            ps, pk_ = pz[h // 4]
                            P.op("tensor", lambda e, h=h, ps=ps, N_=N_: e.matmul(ps[0:64, (h % 4) * 128:(h % 4 + 1) * 128], lhsT=N_[:, h, :],
                                                                                rhs=Z[:, h, :], start=True, stop=True),
                                 reads=[nk, "rw_Z", "rw_Z2"], writes=[pk_])
                        for half in range(2):
                            ps, pk_ = pz[half]
                            P.op("vector", lambda e, half=half, ps=ps, rnd=rnd: e.tensor_tensor(
                                out=Z[:, half * 4:(half + 1) * 4, :], in0=Z[:, half * 4:(half + 1) * 4, :],
                                in1=ps[0:64, :].rearrange("p (h k) -> p h k", h=4), op=(ALU.subtract if rnd == 0 else ALU.add)),
                                reads=[pk_, "rw_Z", "rw_Z2"], writes=["rw_Z", "rw_Z2"])
                        if rnd < 5:
                            nxt = 1 - cur
                            ps, pk_ = psn()
                            for h in range(8):
                                P.op("tensor", lambda e, h=h, ps=ps, N_=N_, NT_=NT_: e.matmul(ps[0:64, h * 64:(h + 1) * 64], lhsT=NT_[:, h, :],
                                                                                             rhs=N_[:, h, :], start=True, stop=True),
                                     reads=[nk, ntk], writes=[pk_])
                            P.op("scalar", lambda e, ps=ps, nxt=nxt: e.copy(out=Nn[nxt][:], in_=ps[0:64, :].rearrange("p (h k) -> p h k", h=8)),
                                 reads=[pk_], writes=[("rw_N", nxt)])
                            if rnd < 4:
                                ps, pk_ = psn()
                                for h in range(8):
                                    P.op("tensor", lambda e, h=h, ps=ps, N_=N_, NT_=NT_: e.matmul(ps[0:64, h * 64:(h + 1) * 64], lhsT=N_[:, h, :],
                                                                                                 rhs=NT_[:, h, :], start=True, stop=True),
                                         reads=[nk, ntk], writes=[pk_])
                                P.op("gpsimd" if False else "scalar", lambda e, ps=ps, nxt=nxt: e.copy(
                                    out=NTr[nxt][:], in_=ps[0:64, :].rearrange("p (h k) -> p h k", h=8)), reads=[pk_], writes=[("rw_NT", nxt)])
                            cur = nxt
                    P.op("gpsimd", lambda e: e.tensor_scalar(out=Zn[:], in0=Z[:], scalar1=-1.0, scalar2=None, op0=ALU.mult),
                         reads=["rw_Z", "rw_Z2"], writes=["rw_Zn"])
                    if RW_ST <= 4:
                        continue
                    ps, pk_ = psn()
                    for h in range(8):
                        P.op("tensor", lambda e, h=h, ps=ps: e.matmul(ps[0:64, h * 64:(h + 1) * 64], lhsT=Zn[:, h, 0:64],
                                                                     rhs=BhT[:, h * 64:(h + 1) * 64], start=True, stop=True),
                             reads=["rw_Zn", "rw_BhT"], writes=[pk_])
                    P.op("scalar", lambda e, ps=ps, sm=sm: e.copy(out=sm[:, 0:512], in_=ps[0:64, :]), reads=[pk_], writes=[(smk, 0)])
                    if RW_ST <= 5:
                        continue
                    ps, pk_ = psn()
                    for h in range(8):
                        P.op("tensor", lambda e, h=h, ps=ps, c=c: e.matmul(ps[0:64, h * 64:(h + 1) * 64], lhsT=KhT[:, h * 64:(h + 1) * 64],
                                                                          rhs=Vt[c][:, h * 64:(h + 1) * 64], start=True, stop=False),
                             reads=["rw_KhT", ("rw_Vt", c)], writes=[pk_])
                        P.op("tensor", lambda e, h=h, ps=ps: e.matmul(ps[0:64, h * 64:(h + 1) * 64], lhsT=BhT[:, h * 64:(h + 1) * 64],
                                                                     rhs=Zn[:, h, 64:128], start=False, stop=True),
                             reads=["rw_BhT", "rw_Zn"], writes=[pk_])
                    P.op("vector", lambda e, ps=ps, sm=sm: e.tensor_copy(out=sm[:, 512:1024], in_=ps[0:64, :]), reads=[pk_], writes=[(smk, 1)])
                    if RW_ST <= 6:
                        continue
                    ps, pk_ = psn()
                    for h in range(8):
                        p0 = (h % 2) * 64
                        P.op("tensor", lambda e, h=h, ps=ps, p0=p0, ra=hd(rt, h): e.matmul(ps[0:64, h * 64:(h + 1) * 64],
                                                                                          lhsT=ident[:, p0:p0 + 64], rhs=ra,
                                                                                          start=True, stop=False),
                             reads=["ident", "rw_rt"], writes=[pk_])
                        P.op("tensor", lambda e, h=h, ps=ps: e.matmul(ps[0:64, h * 64:(h + 1) * 64], lhsT=Zn[:, h, 0:64], rhs=Mbr[:, h, :],
                                                                     start=False, stop=True), reads=["rw_Zn", "rw_Mbr"], writes=[pk_])
                    P.op("scalar", lambda e, ps=ps, sm=sm: e.copy(out=sm[:, 1024:1536], in_=ps[0:64, :]), reads=[pk_], writes=[(smk, 2)])
                    if RW_ST <= 7:
                        continue
                    ps, pk_ = psn()
                    for h in range(8):
                        P.op("tensor", lambda e, h=h, ps=ps, c=c: e.matmul(ps[0:64, h * 64:(h + 1) * 64], lhsT=Mkr[:, h, :],
                                                                          rhs=Vt[c][:, h * 64:(h + 1) * 64], start=True, stop=False),
                             reads=["rw_Mkr", ("rw_Vt", c)], writes=[pk_])
                        P.op("tensor", lambda e, h=h, ps=ps: e.matmul(ps[0:64, h * 64:(h + 1) * 64], lhsT=Mbr[:, h, :], rhs=Zn[:, h, 64:128],
                                                                     start=False, stop=True), reads=["rw_Mbr", "rw_Zn"], writes=[pk_])
                    P.op("vector", lambda e, ps=ps, sm=sm: e.tensor_copy(out=sm[:, 1536:2048], in_=ps[0:64, :]), reads=[pk_], writes=[(smk, 3)])
                    if RW_ST <= 8:
                        continue
                    ps, pk_ = psn()
                    for h in range(8):
                        p0 = (h % 2) * 64
                        P.op("tensor", lambda e, h=h, ps=ps, p0=p0, c=c: e.matmul(ps[0:64, h:h + 1], lhsT=ident[:, p0:p0 + 64],
                                                                                 rhs=etot[:, h // 2, c:c + 1], start=True, stop=True),
                             reads=["ident", "rw_etot"], writes=[pk_])
                    P.op("vector", lambda e, ps=ps, sm=sm: e.tensor_copy(out=sm[:, 2048:2056], in_=ps[0:64, 0:8]), reads=[pk_], writes=[(smk, 4)])
                    P.dma("gpsimd", summ_d[d, cg], sm[:], reads=[(smk, i) for i in range(5)], writes=[("rw_summd", d, cg)])
        if "rwprep" in dbg and l == 0:
            for nm, src in (("g", g_d), ("bv", bv_d)):
                d_ = dbg_out("rw_" + nm, [512, T])
                P.dma("sync", d_, src, reads=[("rw_" + nm, i) for i in range(9)], writes=["OUT_dbgrw" + nm])
        P.barrier(); P.release(m1)
        if RW_PH < 2:
            return
        ST = [P.sbuf("rw_ST%d" % i, [64, 8, 64]) for i in range(2)]
        sm2 = [P.sbuf("rw_sm2_%d" % i, [64, 2056]) for i in range(3)]
        yt = [P.sbuf("rw_yt%d" % i, [64, 512]) for i in range(2)]
        stt = P.sbuf("rw_stt", [64, 8, 64])
        step = 0
        for d in range(2):
            order = list(range(NCH)) if d == 0 else [3, 2, 1, 0] + list(range(NCH - 1, 3, -1))
            cur = 0
            P.op("vector", lambda e: e.memset(ST[0][:], 0.0), reads=[("rw_ST", 0)], writes=[("rw_ST", 0)])
            for cg in order:
                b3 = step % 3
                b2 = step % 2
                step += 1
                P.dma("sync", sm2[b3][:], summ_d[d, cg], reads=[], writes=[("rw_sm2", b3)])
                S_, Sn_ = ST[cur], ST[1 - cur]
                psy, pyk = psn(); pss, psk = psn()
                for h in range(8):
                    P.op("tensor", lambda e, h=h, psy=psy, S_=S_, b3=b3: e.matmul(psy[0:64, h * 64:(h + 1) * 64],
                                                                               lhsT=sm2[b3][:, 1024 + h * 64:1024 + (h + 1) * 64], rhs=S_[:, h, :],
                                                                               start=True, stop=True),
                         reads=[("rw_sm2", b3), ("rw_ST", cur)], writes=[pyk])
                for h in range(8):
                    P.op("tensor", lambda e, h=h, pss=pss, S_=S_, b3=b3: e.matmul(pss[0:64, h * 64:(h + 1) * 64],
                                                                               lhsT=sm2[b3][:, h * 64:(h + 1) * 64], rhs=S_[:, h, :],
                                                                               start=True, stop=True),
                         reads=[("rw_sm2", b3), ("rw_ST", cur)], writes=[psk])
                P.op("gpsimd", lambda e, S_=S_, b3=b3: e.tensor_tensor(out=stt[:], in0=S_[:], in1=sm2[b3][:, 2048:2056].unsqueeze(2).to_broadcast([64, 8, 64]),
                                                                      op=ALU.mult), reads=[("rw_ST", cur), ("rw_sm2", b3)], writes=["rw_stt"])
                P.op("gpsimd", lambda e, b3=b3: e.tensor_tensor(out=stt[:], in0=stt[:], in1=sm2[b3][:, 512:1024].rearrange("p (h k) -> p h k", h=8),
                                                               op=ALU.add), reads=["rw_stt", ("rw_sm2", b3)], writes=["rw_stt"])
                P.op("vector", lambda e, pss=pss, Sn_=Sn_: e.tensor_tensor(out=Sn_[:], in0=pss[0:64, :].rearrange("p (h k) -> p h k", h=8),
                                                                          in1=stt[:], op=ALU.add), reads=[psk, "rw_stt"], writes=[("rw_ST", 1 - cur)])
                P.op("vector", lambda e, psy=psy, b2=b2, b3=b3: e.tensor_tensor(out=yt[b2][:], in0=psy[0:64, :], in1=sm2[b3][:, 1536:2048], op=ALU.add),
                     reads=[pyk, ("rw_sm2", b3)], writes=[("rw_yt", b2)])
                P.dma("gpsimd", y_d[d, cg * 64:(cg + 1) * 64, :], yt[b2][:], reads=[("rw_yt", b2)], writes=[("rw_yd", d, cg // 2)])
                cur = 1 - cur
        if "rwy" in dbg and l == 0:
            d_ = dbg_out("rw_y", [2, T, 512])
            P.dma("sync", d_, y_d, reads=[("rw_yd", d, t) for d in range(2) for t in range(NT)], writes=["OUT_dbgrwy"])
        P.barrier(); P.release(m1)
        if RW_PH < 3:
            return
        y0 = [P.sbuf("rw_y0_%d" % i, [128, 512]) for i in range(2)]; y1 = [P.sbuf("rw_y1_%d" % i, [128, 512]) for i in range(2)]
        sqt = [P.sbuf("rw_sq%d" % i, [128, 512]) for i in range(2)]
        st8 = [P.sbuf("rw_st8_%d" % i, [128, 4, 8]) for i in range(2)]
        bvt = [P.sbuf("rw_bvt%d" % i, [128, 4, 128]) for i in range(2)]; gt = [P.sbuf("rw_gt%d" % i, [128, 4, 128]) for i in range(2)]
        of = [P.sbuf("rw_of%d" % i, [128, 4, 128]) for i in range(2)]; ob = [P.sbuf("rw_ob%d" % i, [128, 4, 128], BF16) for i in range(2)]
        orw_v = orw_d.rearrange("(k p) t -> p k t", p=128)
        for t in range(NT):
            s_ = t % 2
            gi = 0 if t < 2 else 1 + (t - 2) // 4
            P.dma("sync", y0[s_][:], y_d[0, t * 128:(t + 1) * 128, :], reads=[("rw_yd", 0, t)], writes=[("rw_y0", s_)])
            P.dma("sync", y1[s_][:], y_d[1, t * 128:(t + 1) * 128, :], reads=[("rw_yd", 1, t)], writes=[("rw_y1", s_)])
            P.dma("sync", bvt[s_][:], bv_v[:, :, t * 128:(t + 1) * 128], reads=[("rw_bv", i) for i in range(9)], writes=[("rw_bvt", s_)])
            P.dma("sync", gt[s_][:], g_v[:, :, t * 128:(t + 1) * 128], reads=[("rw_g", i) for i in range(9)], writes=[("rw_gt", s_)])
            ys = y0[s_]; st = st8[s_]
            P.op("gpsimd", lambda e, s_=s_: e.tensor_tensor(out=y0[s_][:], in0=y0[s_][:], in1=y1[s_][:], op=ALU.add),
                 reads=[("rw_y0", s_), ("rw_y1", s_)], writes=[("rw_y0", s_)])
            y3 = ys[:].rearrange("p (h n) -> p h n", h=8)
            P.op("vector", lambda e, st=st, y3=y3: e.tensor_reduce(out=st[:, 0, :], in_=y3, axis=AX.X, op=ALU.add), reads=[("rw_y0", s_)],
                 writes=[("rw_st8", s_)])
            P.op("scalar", lambda e, s_=s_, ys=ys: e.activation(out=sqt[s_][:], in_=ys[:], func=AF.Square), reads=[("rw_y0", s_)],
                 writes=[("rw_sq", s_)])
            P.op("vector", lambda e, st=st, s_=s_: e.tensor_reduce(out=st[:, 1, :], in_=sqt[s_][:].rearrange("p (h n) -> p h n", h=8), axis=AX.X,
                                                                  op=ALU.add), reads=[("rw_sq", s_), ("rw_st8", s_)], writes=[("rw_st8", s_)])
            P.op("vector", lambda e, st=st: e.tensor_scalar(out=st[:, 0, :], in0=st[:, 0, :], scalar1=1.0 / 64, scalar2=None, op0=ALU.mult),
                 reads=[("rw_st8", s_)], writes=[("rw_st8", s_)])
            P.op("vector", lambda e, st=st: e.tensor_tensor(out=st[:, 2, :], in0=st[:, 0, :], in1=st[:, 0, :], op=ALU.mult),
                 reads=[("rw_st8", s_)], writes=[("rw_st8", s_)])
            P.op("vector", lambda e, st=st: e.scalar_tensor_tensor(out=st[:, 2, :], in0=st[:, 1, :], scalar=1.0 / 64, in1=st[:, 2, :],
                                                                  op0=ALU.mult, op1=ALU.subtract), reads=[("rw_st8", s_)], writes=[("rw_st8", s_)])
            P.op("scalar", lambda e, st=st: e.activation(out=st[:, 3, :], in_=st[:, 2, :], func=AF.Sqrt, bias=eps_gn[:, 0:1], scale=1.0),
                 reads=[("rw_st8", s_), "rw_eps"], writes=[("rw_st8", s_)])
            P.op("vector", lambda e, st=st: e.reciprocal(out=st[:, 3, :], in_=st[:, 3, :]), reads=[("rw_st8", s_)], writes=[("rw_st8", s_)])
            P.op("vector", lambda e, st=st, y3=y3: e.tensor_tensor(out=y3, in0=y3, in1=st[:, 0, :].unsqueeze(2).to_broadcast([128, 8, 64]),
                                                                  op=ALU.subtract), reads=[("rw_y0", s_), ("rw_st8", s_), ("rw_sq", s_)],
                 writes=[("rw_y0", s_)])
            P.op("vector", lambda e, st=st, y3=y3: e.tensor_tensor(out=y3, in0=y3, in1=st[:, 3, :].unsqueeze(2).to_broadcast([128, 8, 64]),
                                                                  op=ALU.mult), reads=[("rw_y0", s_), ("rw_st8", s_)], writes=[("rw_y0", s_)])
            ps, pk_ = psn()
            for fc in range(4):
                P.op("tensor", lambda e, fc=fc, ps=ps, ys=ys: e.transpose(out=ps[:, fc * 128:(fc + 1) * 128], in_=ys[:, fc * 128:(fc + 1) * 128],
                                                                        identity=ident[:]), reads=[("rw_y0", s_), "ident"], writes=[pk_])
            for fc in range(4):
                P.op("vector", lambda e, fc=fc, ps=ps, s_=s_: e.tensor_scalar(out=of[s_][:, fc, :], in0=ps[:, fc * 128:(fc + 1) * 128],
                                                                            scalar1=vec[:, 3, fc:fc + 1], scalar2=vec[:, 4, fc:fc + 1],
                                                                            op0=ALU.mult, op1=ALU.add),
                     reads=[pk_, "rw_par"], writes=[("rw_of", s_, fc)])
            P.op("gpsimd", lambda e, s_=s_: e.tensor_tensor(out=of[s_][:], in0=of[s_][:], in1=bvt[s_][:], op=ALU.add),
                 reads=[("rw_of", s_, fc) for fc in range(4)] + [("rw_bvt", s_)], writes=[("rw_of2", s_)])
            P.op("gpsimd", lambda e, s_=s_: e.tensor_tensor(out=ob[s_][:], in0=of[s_][:], in1=gt[s_][:], op=ALU.mult),
                 reads=[("rw_of2", s_), ("rw_gt", s_)], writes=[("rw_ob", s_)])
            P.dma("gpsimd", orw_v[:, :, t * 128:(t + 1) * 128], ob[s_][:], reads=[("rw_ob", s_)], writes=[("orw", gi)])
        if "orw" in dbg and l == 0:
            d_ = dbg_out("orw", [512, T], BF16)
            P.dma("sync", d_, orw_d, reads=[("orw", g) for g in range(9)], writes=["OUT_dbgorw"])

    xT_v = xT.rearrange("(k p) t -> p k t", p=128)
    def stage_inproj(l):
        hT = P.sbuf("hT", [128, 8, T], BF16)
        xg = [P.sbuf("xg%d" % i, [128, 8, 512]) for i in range(2)]
        wblk = [P.sbuf("wblk%d" % i, [128, 8, 512]) for i in range(2)]
        wbf = [P.sbuf("wbf%d" % i, [128, 8, 512], BF16) for i in range(2)]
        ost = [P.sbuf("ost%d" % i, [128, 512]) for i in range(4)]
        ostb = [P.sbuf("ostb%d" % i, [128, 512], BF16) for i in range(4)]
        for gi, (t0, n) in enumerate(GROUPS):
            s = gi % 2
            c = 1 if gi == 0 else 0
            P.dma("sync", xg[s][:, :, :n], xT_v[:, :, t0:t0 + n], reads=[("xT", gi)], writes=[("xg", s)])
            for k in range(8):
                eng = "vector" if k % 2 == 0 else "gpsimd"
                P.op(eng, lambda e, s=s, k=k, c=c, l=l, t0=t0, n=n: e.tensor_scalar(
                    out=hT[:, k, t0:t0 + n], in0=xg[s][:, k, :n], scalar1=mod1[:, l, 8 + k, c:c + 1],
                    scalar2=mod[:, l, k, c:c + 1], op0=ALU.mult, op1=ALU.add),
                    reads=[("xg", s), "mod", "mod1"], writes=[("hT", gi)])
        if "hT" in dbg and l == 0:
            d = dbg_out("hT", [128, 8 * T], BF16)
            P.dma("sync", d, hT[:].rearrange("p k t -> p (k t)"), reads=[("hT", g) for g in range(9)], writes=["OUT_dbghT"])
        blocks = [(3072, 512, "fm_bf", qT, 0), (3584, 512, "fm_bf", kT, 0), (4096, 512, "tm_bf", vtok, 0),
                  (4608, 512, "fm", prw, 0), (5120, 512, "fm", prw, 512), (5632, 512, "fm", prw, 1024),
                  (6144, 384, "fm", prw, 1536), (6528, 512, "fm", sguU, 0), (7040, 512, "tm", sguV, 0)]
        nev = 0
        for bi, (c0, ncol, kind, dst, r0) in enumerate(blocks):
            s = bi % 2
            P.dma("sync", wblk[s][:, :, :ncol], w_in[l, :, c0:c0 + ncol].rearrange("(k p) m -> p k m", p=128),
                  writes=[("wblk", s)])
            for k in range(8):
                eng = "gpsimd" if k % 2 == 0 else "vector"
                P.op(eng, lambda e, s=s, k=k, ncol=ncol: e.tensor_copy(out=wbf[s][:, k, :ncol], in_=wblk[s][:, k, :ncol]),
                     reads=[("wblk", s)], writes=[("wbf", s)])
            if kind.startswith("fm"):
                for gi, (t0, n) in enumerate(GROUPS):
                    for mi in range(ncol // 128):
                        pb = 2 + nev % 4
                        for k in range(8):
                            P.op("tensor", lambda e, s=s, k=k, mi=mi, t0=t0, n=n, pb=pb: e.matmul(
                                PS[pb][:, :n], lhsT=wbf[s][:, k, mi * 128:(mi + 1) * 128], rhs=hT[:, k, t0:t0 + n],
                                start=(k == 0), stop=(k == 7)), reads=[("wbf", s), ("hT", gi)], writes=[("ps", pb)])
                        so = nev % 4
                        o_t = ostb[so] if kind == "fm_bf" else ost[so]
                        okey = ("ostb", so) if kind == "fm_bf" else ("ost", so)
                        if nev % 2 == 0:
                            P.op("scalar", lambda e, o_t=o_t, pb=pb, n=n: e.copy(out=o_t[:, :n], in_=PS[pb][:, :n]),
                                 reads=[("ps", pb)], writes=[okey])
                        else:
                            P.op("vector", lambda e, o_t=o_t, pb=pb, n=n: e.tensor_copy(out=o_t[:, :n], in_=PS[pb][:, :n]),
                                 reads=[("ps", pb)], writes=[okey])
                        rr = r0 + mi * 128
                        P.dma("gpsimd", dst[rr:rr + 128, t0:t0 + n], o_t[:, :n], reads=[okey], writes=[(dst.name, "fm", gi)])
                        nev += 1
            else:
                for t in range(NT):
                    pb = 2 + nev % 4
                    gi = 0 if t < 2 else 1 + (t - 2) // 4
                    for k in range(8):
                        P.op("tensor", lambda e, s=s, k=k, t=t, pb=pb: e.matmul(
                            PS[pb][:, :], lhsT=hT[:, k, t * 128:(t + 1) * 128], rhs=wbf[s][:, k, :],
                            start=(k == 0), stop=(k == 7)), reads=[("wbf", s), ("hT", gi)], writes=[("ps", pb)])
                    so = nev % 4
                    o_t = ostb[so] if kind == "tm_bf" else ost[so]
                    okey = ("ostb", so) if kind == "tm_bf" else ("ost", so)
                    if nev % 2 == 0:
                        P.op("scalar", lambda e, o_t=o_t, pb=pb: e.copy(out=o_t[:], in_=PS[pb][:]), reads=[("ps", pb)], writes=[okey])
                    else:
                        P.op("vector", lambda e, o_t=o_t, pb=pb: e.tensor_copy(out=o_t[:], in_=PS[pb][:]), reads=[("ps", pb)], writes=[okey])
                    P.dma("gpsimd", dst[t * 128:(t + 1) * 128, :], o_t[:], reads=[okey], writes=[(dst.name, "tm", t)])
                    nev += 1
        if "p" in dbg and l == 0:
            for nm, src, shp, dt in (("qT", qT, [512, T], BF16), ("kT", kT, [512, T], BF16), ("vtok", vtok, [T, 512], BF16),
                                     ("prw", prw, [1920, T], F32), ("sguU", sguU, [512, T], F32), ("sguV", sguV, [T, 512], F32)):
                d = dbg_out(nm, shp, dt)
                rk = [(src.name, "fm", g) for g in range(9)] + [(src.name, "tm", t) for t in range(NT)]
                P.dma("sync", d, src, reads=rk, writes=["OUT_dbg" + nm])
    for l in range(n_layers):
        stage_inproj(l)
        if stop == "s1":
            return P, dbg_t
        P.barrier(); P.release(m0)
        stage_sgu(l)
        if stop == "s2":
            return P, dbg_t
        P.barrier(); P.release(m0)
        stage_na(l)
        if stop == "s3":
            return P, dbg_t
        P.barrier(); P.release(m0)
        stage_rwkv(l)
        if stop == "s4":
            return P, dbg_t
        P.barrier(); P.release(m0)
        stage_merge(l)
        if stop == "s5":
            return P, dbg_t
        P.barrier(); P.release(m0)
        stage_ffn(l)
        if stop == "s6":
            return P, dbg_t
        P.barrier(new_epoch=True); P.release(m0)
    if mode == "full":
        stage_final()
    else:
        for gi, (t0, n) in enumerate(GROUPS):
            P.dma("sync", xT_out[:, t0:t0 + n], xT[:, t0:t0 + n], reads=[("xT", gi)], writes=["OUT_x%d" % gi])
    return P, dbg_t


def host_inputs(inputs, b, l0=0, nl=DEPTH, xT=None):
    m = {}
    if xT is None:
        x = np.asarray(inputs["x"], np.float32)
        ctx = np.asarray(inputs["ctx"], np.float32)
        m["xin"] = np.ascontiguousarray(np.concatenate([ctx[b], x[b]], axis=0))
    else:
        m["xT_in"] = xT
    m["ccT"] = np.ascontiguousarray(np.stack([np.asarray(inputs["c"], np.float32)[b], np.asarray(inputs["c_ctx"], np.float32)], axis=1))
    m["ident"] = np.eye(128, dtype=np.float32)
    f = lambda k: np.asarray(inputs[k], np.float32)[l0:l0 + nl]
    m["w_ada"] = f("w_ada")
    m["b_adaT"] = np.ascontiguousarray(f("b_ada").reshape(nl, 48, 128).transpose(0, 2, 1))
    m["w_in"] = f("w_in")
    m["sgu_lng"] = np.ascontiguousarray(np.broadcast_to(f("sgu_ln_g")[:, None, :], (nl, 128, 512)))
    m["sgu_lnb"] = np.ascontiguousarray(np.broadcast_to(f("sgu_ln_b")[:, None, :], (nl, 128, 512)))
    m["sgu_wT"] = np.ascontiguousarray(f("sgu_w").transpose(0, 3, 1, 2))
    m["sgu_bB"] = np.ascontiguousarray(np.broadcast_to(f("sgu_b")[:, None, :, :], (nl, 64, 8, 128)))
    m["na_bias"] = na_bias_layout(f("na_rpb"))
    fmN = lambda a, n: a.reshape(a.shape[0], n, 128).transpose(2, 0, 1)
    m["rw_mu"] = np.ascontiguousarray(np.stack([fmN(f("rwkv_mu_prev"), 15), fmN(f("rwkv_mu_next"), 15)], axis=2))
    w0 = f("rwkv_w0").reshape(nl, 2, 4, 128).transpose(3, 0, 1, 2); a0 = f("rwkv_a0").reshape(nl, 2, 4, 128).transpose(3, 0, 1, 2)
    m["rw_w0a0"] = np.ascontiguousarray(np.stack([w0, a0], axis=2))
    m["rw_w2"] = np.ascontiguousarray(f("rwkv_w2").reshape(nl, 128, 512)); m["rw_a2"] = np.ascontiguousarray(f("rwkv_a2").reshape(nl, 128, 512))
    m["rw_g2"] = f("rwkv_g2")
    m["rw_vec"] = np.ascontiguousarray(np.stack([fmN(f(k).reshape(nl, 512), 4) for k in
                                                 ("rwkv_k_k", "rwkv_k_a", "rwkv_r_k", "rwkv_gn_g", "rwkv_gn_b")], axis=2))
    jj, ii = np.meshgrid(np.arange(64), np.arange(64), indexing="ij")
    m["mask64"] = np.ascontiguousarray(np.stack([jj < ii, jj <= ii, jj > ii, jj >= ii], axis=1).astype(np.float32))
    bo = np.zeros((128, 128), np.float32); bo[:64, :64] = 1; bo[64:, 64:] = 1
    m["bones"] = bo
    m["w_branch"] = f("w_branch"); m["w_out"] = f("w_out"); m["ffn_w_gu"] = f("ffn_w_gu"); m["ffn_w_down"] = f("ffn_w_down")
    fm8 = lambda a: a.reshape(nl, 8, 128).transpose(2, 0, 1)
    m["lnp"] = np.ascontiguousarray(np.stack([fm8(f("ln1_g")), fm8(f("ln1_b")), fm8(f("ln2_g")), fm8(f("ln2_b"))], axis=2))
    return m


_NA_IDX = None


def na_bias_layout(rpb):
    global _NA_IDX
    if _NA_IDX is None:
        ridx = np.zeros((5, 128, 896), np.int64); cidx = np.zeros((5, 128, 896), np.int64); valid = np.zeros((5, 128, 896), bool)
        zero = np.zeros((5, 128, 896), bool)
        for pi, r in enumerate((0, 2, 4, 60, 62)):
            kb = min(max(r - 4, 0), 54)
            for qi in range(128):
                qr, c = r + qi // 64, qi % 64
                row0 = min(max(qr - 4, 0), 56); col0 = min(max(c - 8, 0), 48)
                for j in range(10):
                    kr = kb + j
                    if not (row0 <= kr < row0 + 8):
                        continue
                    for kc in range(col0, col0 + 16):
                        ridx[pi, qi, j * 64 + kc] = kr - qr + 7; cidx[pi, qi, j * 64 + kc] = kc - c + 15; valid[pi, qi, j * 64 + kc] = True
            zero[pi, :, 640:] = True
        _NA_IDX = (ridx, cidx, valid, zero)
    ridx, cidx, valid, zero = _NA_IDX
    g = rpb[:, :, ridx, cidx]
    g = np.where(valid[None, None], g, np.float32(-30000.0))
    g = np.where(zero[None, None], np.float32(0.0), g)
    return np.ascontiguousarray(g.transpose(0, 2, 3, 1, 4)).astype(np.float32)


_CACHE = {}


def _host_params(inputs, l0, nl):
    key = (id(inputs.get("w_in")), l0, nl)
    if key not in _CACHE:
        m = host_inputs(inputs, 0, l0, nl, xT=np.zeros((1,), np.float32))
        m.pop("xT_in"); m.pop("ccT")
        _CACHE[key] = m
    return _CACHE[key]


def kernel(**inputs):
    P, _ = build(n_layers=DEPTH, mode="full")
    nc = P.finalize()
    par = dict(host_inputs(inputs, 0))
    par.pop("xin"); par.pop("ccT")
    in_maps = []
    for core in range(8):
        m = dict(par)
        hb = host_inputs_x(inputs, core // 2)
        m.update(hb)
        in_maps.append(m)
    res = run_bass_kernel_spmd(nc, in_maps, core_ids=list(range(8)))
    return np.stack([res.results[2 * b]["out"] for b in range(4)], axis=0).astype(np.float32)


def host_inputs_x(inputs, b):
    x = np.asarray(inputs["x"], np.float32); ctx = np.asarray(inputs["ctx"], np.float32)
    return {"xin": np.ascontiguousarray(np.concatenate([ctx[b], x[b]], axis=0)),
            "ccT": np.ascontiguousarray(np.stack([np.asarray(inputs["c"], np.float32)[b], np.asarray(inputs["c_ctx"], np.float32)], axis=1))}
```

```python
import numpy as np
import concourse.bass as bass
import concourse.mybir as mybir
from concourse.bass_utils import run_bass_kernel_spmd

F32 = mybir.dt.float32
BF16 = mybir.dt.bfloat16
F32R = mybir.dt.float32r
ALU = mybir.AluOpType
AF = mybir.ActivationFunctionType
AX = mybir.AxisListType

D = 1024
DEPTH = 4
LCTX = 256
SEQ = 4096
T = LCTX + SEQ
NT = T // 128
GROUPS = [(0, 256)] + [(256 + 512 * i, 512) for i in range(8)]
D_IN = 7552
D_FF = 2816
ALPHA = (2 * DEPTH) ** 0.25


class Prog:
    ENGS = ("tensor", "vector", "scalar", "gpsimd", "sync")

    def __init__(self):
        self.nc = bass.Bass("TRN2", target_bir_lowering=False)
        self.ops = []
        self.n_dma_sems = 32
        arena = self.nc.alloc_sbuf_tensor("arena", [128, 212000], mybir.dt.uint8)
        self.arena_base = self.nc.lookup_mloc(arena).addr
        self.arena_size = 212000
        self.sp = 0
        self.nalloc = 0

    def dram(self, name, shape, dtype, kind="Internal"):
        return self.nc.dram_tensor(name, list(shape), dtype, kind=kind).ap()

    def sbuf(self, name, shape, dtype=F32):
        esz = {F32: 4, BF16: 2, F32R: 4}[dtype]
        nbytes = int(np.prod(shape[1:])) * esz
        off = (self.sp + 63) // 64 * 64
        assert off + nbytes <= self.arena_size, "SBUF arena overflow at %s: %d + %d" % (name, off, nbytes)
        self.sp = off + nbytes
        self.nalloc += 1
        return self.nc.alloc_sbuf_tensor_at("%s_%d" % (name, self.nalloc), list(shape), dtype, offset=self.arena_base + off)

    def mark(self):
        return self.sp

    def release(self, m):
        self.sp = m

    def barrier(self, new_epoch=False):
        self.ops.append(("barrier", new_epoch, (), (), False))

    def psum(self, name, shape, dtype=F32):
        return self.nc.alloc_psum_tensor(name, list(shape), dtype)

    def op(self, eng, fn, reads=(), writes=(), dma=False):
        self.ops.append((eng, fn, tuple(reads), tuple(writes), dma))

    def dma(self, eng, out, in_, reads=(), writes=(), slow=False):
        if slow:
            self.op(eng, lambda e: e.dma_start(out=out, in_=in_, allow_slow_non_contiguous=True), reads, writes, dma=True)
        else:
            self.op(eng, lambda e: e.dma_start(out=out, in_=in_), reads, writes, dma=True)

    def finalize(self):
        nc = self.nc
        ops = self.ops
        last_w = {}
        readers = {}
        deps = []
        force_sig = set()
        last_on = {}
        for i, (eng, fn, rd, wr, isdma) in enumerate(ops):
            if eng == "barrier":
                force_sig.update(last_on.values())
                last_w = {}
                readers = {}
                deps.append(set())
                continue
            if not isdma:
                last_on[eng] = i
            d = set()
            for k in rd:
                if k in last_w:
                    d.add(last_w[k])
            for k in wr:
                if k in last_w:
                    d.add(last_w[k])
                d.update(readers.get(k, {}).values())
            for k in rd:
                readers.setdefault(k, {})[(eng, i) if isdma else eng] = i
            for k in wr:
                last_w[k] = i
                readers[k] = {}
            d.discard(i)
            if eng == "tensor" and not isdma:
                d = {j for j in d if not (ops[j][0] == "tensor" and not ops[j][4])}
            deps.append(d)
        has_dep = [False] * len(ops)
        for d in deps:
            for j in d:
                has_dep[j] = True
        for j in force_sig:
            has_dep[j] = True
        sems = {e: nc.alloc_semaphore("s_" + e) for e in self.ENGS}
        dsems = [nc.alloc_semaphore("d%d" % j) for j in range(self.n_dma_sems)]
        cnt = {e: 0 for e in self.ENGS}
        sig = [None] * len(ops)
        waited = {e: {} for e in self.ENGS}
        ndma = 0
        final = {}
        dlast = {}
        for i, (e, fn, rd, wr, isdma) in enumerate(ops):
            if e == "barrier":
                for e1 in self.ENGS:
                    eng1 = getattr(nc, e1)
                    for e2 in self.ENGS:
                        if cnt[e2] > waited[e1].get(sems[e2].num, 0):
                            waited[e1][sems[e2].num] = cnt[e2]
                            eng1.wait_ge(sems[e2], cnt[e2])
                    for jn, (sd, vd) in dlast.items():
                        if vd > waited[e1].get(jn, 0):
                            waited[e1][jn] = vd
                            eng1.wait_ge(sd, vd)
                if fn:
                    nep = getattr(self, "_nep", 0) + 1
                    self._nep = nep
                    sems = {e_: nc.alloc_semaphore("s%d_%s" % (nep, e_)) for e_ in self.ENGS}
                    cnt = {e_: 0 for e_ in self.ENGS}
                continue
            eng = getattr(nc, e)
            need = {}
            for j in deps[i]:
                s, v = sig[j]
                if need.get(s.num, (None, 0))[1] < v:
                    need[s.num] = (s, v)
            if isdma:
                js = ndma % self.n_dma_sems
                v = 16 * (ndma // self.n_dma_sems + 1)
                if v > 16 and need.get(dsems[js].num, (None, 0))[1] < v - 16:
                    need[dsems[js].num] = (dsems[js], v - 16)
            for sn, (s, val) in need.items():
                if waited[e].get(sn, 0) >= val:
                    continue
                waited[e][sn] = val
                eng.wait_ge(s, val)
            ins = fn(eng)
            if isdma:
                ins.then_inc(dsems[js], 16)
                sig[i] = (dsems[js], v)
                dlast[dsems[js].num] = (dsems[js], v)
                ndma += 1
                if any(str(k).startswith("OUT") for k in wr):
                    if final.get(dsems[js].num, (None, 0))[1] < v:
                        final[dsems[js].num] = (dsems[js], v)
            elif has_dep[i]:
                cnt[e] += 1
                ins.then_inc(sems[e], 1)
                sig[i] = (sems[e], cnt[e])
            else:
                sig[i] = (sems[e], cnt[e])
        for sn, (s, v) in final.items():
            nc.sync.wait_ge(s, v)
        self.counts = dict(cnt, ndma=ndma, nops=len(ops))
        return nc


def build(n_layers=DEPTH, stop=None, dbg=(), mode="full"):
    NL = n_layers
    P = Prog()
    nc = P.nc
    if mode == "full":
        xin = P.dram("xin", [T, D], F32, "ExternalInput")
        out_d = P.dram("out", [SEQ, D], F32, "ExternalOutput")
    else:
        xT_in = P.dram("xT_in", [D, T], F32, "ExternalInput")
        xT_out = P.dram("xT_out", [D, T], F32, "ExternalOutput")
    ccT = P.dram("ccT", [D, 2], F32, "ExternalInput")
    ident_d = P.dram("ident", [128, 128], F32, "ExternalInput")
    w_ada = P.dram("w_ada", [NL, D, 6 * D], F32, "ExternalInput")
    b_adaT = P.dram("b_adaT", [NL, 128, 48], F32, "ExternalInput")
    w_in = P.dram("w_in", [NL, D, D_IN], F32, "ExternalInput")
    dbg_t = {}

    def dbg_out(name, shape, dtype=F32):
        dbg_t[name] = P.dram("dbg_" + name, shape, dtype, "ExternalOutput")
        return dbg_t[name]

    xT = P.dram("xT", [D, T], F32)
    qT = P.dram("qT", [512, T], BF16)
    kT = P.dram("kT", [512, T], BF16)
    vtok = P.dram("vtok", [T, 512], BF16)
    prw = P.dram("prw", [1920, T], F32)
    sguU = P.dram("sguU", [512, T], F32)
    sguV = P.dram("sguV", [T, 512], F32)

    ident = P.sbuf("ident_s", [128, 128])
    PS = [P.psum("ps%d" % i, [128, 512]) for i in range(6)]
    PSB = [P.psum("psb%d" % i, [128, 1024], BF16) for i in range(2)]
    mod = P.sbuf("mod", [128, NL, 48, 2])
    mod1 = P.sbuf("mod1", [128, NL, 48, 2])
    P.dma("sync", ident[:], ident_d, writes=["ident"])

    cc = P.sbuf("cc", [128, 8, 2])
    scc = P.sbuf("scc", [128, 8, 2])
    badaT = P.sbuf("badaT", [128, NL, 48])
    P.dma("sync", cc[:], ccT.rearrange("(k p) c -> p k c", p=128), writes=["cc"])
    P.dma("sync", badaT[:], b_adaT.rearrange("l p j -> p l j"), writes=["badaT"])
    P.op("scalar", lambda e: e.activation(out=scc[:], in_=cc[:], func=AF.Silu), reads=["cc"], writes=["scc"])
    identb = P.sbuf("identb", [128, 128], BF16)
    P.op("vector", lambda e: e.tensor_copy(out=identb[:], in_=ident[:]), reads=["ident"], writes=["identb"])
    lnp = P.sbuf("lnp", [128, NL, 4, 8])
    ones = P.sbuf("ones", [128, 128])
    eps5 = P.sbuf("eps5", [128, 1])
    P.op("vector", lambda e: e.memset(eps5[:], 1e-5), writes=["eps"])
    m0 = P.mark()
    wblk0 = [P.sbuf("wblk0%d" % i, [128, 8, 512]) for i in range(2)]
    nb = 0
    for l in range(n_layers):
        for cb in range(12):
            s = nb % 2
            P.dma("sync", wblk0[s][:], w_ada[l, :, cb * 512:(cb + 1) * 512].rearrange("(k p) m -> p k m", p=128),
                  writes=[("wblk0", s)])
            for mi in range(4):
                j = cb * 4 + mi
                for k in range(8):
                    P.op("tensor", lambda e, s=s, mi=mi, k=k, j=j: e.matmul(
                        PS[0][:, 2 * j:2 * j + 2], lhsT=wblk0[s][:, k, mi * 128:(mi + 1) * 128], rhs=scc[:, k, :],
                        start=(k == 0), stop=(k == 7)), reads=[("wblk0", s), "scc"], writes=[("ps", 0)])
            nb += 1
        for c in range(2):
            P.op("vector", lambda e, l=l, c=c: e.tensor_tensor(
                out=mod[:, l, :, c], in0=PS[0][:, 0:96].rearrange("p (j c) -> p j c", c=2)[:, :, c], in1=badaT[:, l, :], op=ALU.add),
                reads=[("ps", 0), "badaT"], writes=["mod"])
        P.op("vector", lambda e, l=l: e.tensor_scalar_add(out=mod1[:, l], in0=mod[:, l], scalar1=1.0), reads=["mod"], writes=["mod1"])

    xtile = [P.sbuf("xtile%d" % i, [128, D]) for i in range(2)]
    xTst = [P.sbuf("xTst%d" % i, [128, 8, 128]) for i in range(2)]
    if mode != "full":
        for gi, (t0, n) in enumerate(GROUPS):
            P.dma("sync", xT[:, t0:t0 + n], xT_in[:, t0:t0 + n], writes=[("xT", gi)])
    for t in range(NT if mode == "full" else 0):
        s = t % 2
        P.dma("sync", xtile[s][:], xin[t * 128:(t + 1) * 128, :], writes=[("xtile", s)])
        for half in range(2):
            pb = 1 + half
            for kk in range(4):
                k = half * 4 + kk
                P.op("tensor", lambda e, s=s, k=k, kk=kk, pb=pb: e.transpose(
                    out=PS[pb][:, kk * 128:(kk + 1) * 128], in_=xtile[s][:, k * 128:(k + 1) * 128], identity=ident[:]),
                    reads=[("xtile", s), "ident"], writes=[("ps", pb)])
            P.op("scalar" if half == 0 else "vector", (lambda e, s=s, half=half, pb=pb: e.copy(
                out=xTst[s][:, half * 4:(half + 1) * 4, :], in_=PS[pb][:].rearrange("p (k t) -> p k t", k=4)))
                if half == 0 else (lambda e, s=s, half=half, pb=pb: e.tensor_copy(
                    out=xTst[s][:, half * 4:(half + 1) * 4, :], in_=PS[pb][:].rearrange("p (k t) -> p k t", k=4))),
                reads=[("ps", pb)], writes=[("xTst", s, half)])
        P.dma("gpsimd", xT.rearrange("(k p) t -> p k t", p=128)[:, :, t * 128:(t + 1) * 128], xTst[s][:],
              reads=[("xTst", s, 0), ("xTst", s, 1)], writes=[("xT", t // 4)])

    if "mod" in dbg:
        d = dbg_out("mod", [128, NL * 48 * 2])
        P.dma("sync", d, mod[:].rearrange("p l j c -> p (l j c)"), reads=["mod"], writes=["OUT_dbgmod"])
    if "xT" in dbg:
        d = dbg_out("xT", [D, T])
        P.dma("sync", d, xT, reads=[("xT", g) for g in range(9)], writes=["OUT_dbgxT"])
    if stop == "s0":
        return P, dbg_t
    P.barrier()
    P.release(m0)


    sgu_lng = P.dram("sgu_lng", [NL, 128, 512], F32, "ExternalInput")
    sgu_lnb = P.dram("sgu_lnb", [NL, 128, 512], F32, "ExternalInput")
    sgu_wT = P.dram("sgu_wT", [NL, 128, 8, 128], F32, "ExternalInput")
    sgu_bB = P.dram("sgu_bB", [NL, 64, 8, 128], F32, "ExternalInput")
    na_bias = P.dram("na_bias", [NL, 5, 128, 8, 896], F32, "ExternalInput")
    osgu_d = P.dram("osgu", [512, T], BF16)
    ona_d = P.dram("ona", [512, T], BF16)
    def gelu(src, dst, ta, tb, npart, rk, wk, pfx):
        P.op("gpsimd", lambda e: e.tensor_tensor(out=ta, in0=src, in1=src, op=ALU.mult), reads=rk, writes=[(pfx, "ta")])
        P.op("gpsimd", lambda e: e.tensor_scalar(out=ta, in0=ta, scalar1=0.044715, scalar2=1.0, op0=ALU.mult, op1=ALU.add),
             reads=[(pfx, "ta")], writes=[(pfx, "ta")])
        P.op("gpsimd", lambda e: e.tensor_tensor(out=ta, in0=ta, in1=src, op=ALU.mult), reads=[(pfx, "ta")] + list(rk), writes=[(pfx, "ta")])
        P.op("scalar", lambda e: e.activation(out=tb, in_=ta, func=AF.Sigmoid, scale=1.5957691216057308),
             reads=[(pfx, "ta")], writes=[(pfx, "tb")])
        P.op("vector", lambda e: e.tensor_tensor(out=dst, in0=tb, in1=src, op=ALU.mult), reads=[(pfx, "tb")] + list(rk), writes=wk)

    def stage_sgu(l):
        sg = {}
        if True:
            sg["lng"] = P.sbuf("sg_lng", [128, 512]); sg["lnb"] = P.sbuf("sg_lnb", [128, 512])
            sg["wT"] = P.sbuf("sg_wT", [128, 8, 128]); sg["bB"] = P.sbuf("sg_bB", [64, 8, 128])
            for nm, shp, dt in (("sv", [128, 512], F32), ("gv", [128, 512], F32), ("vn", [128, 512], F32), ("tva", [128, 512], F32),
                                ("tvb", [128, 512], F32), ("su", [64, 8, 128], F32), ("gu", [64, 8, 128], F32), ("tua", [64, 8, 128], F32),
                                ("tub", [64, 8, 128], F32), ("st6", [128, 6], F32), ("mv", [128, 2], F32), ("rstd", [128, 1], F32),
                                ("tmp", [64, 8, 128], F32), ("osg", [64, 8, 128], BF16)):
                sg[nm] = [P.sbuf("sg_%s%d" % (nm, i), shp, dt) for i in range(2)]
        P.dma("sync", sg["lng"][:], sgu_lng[l], writes=["sg_par"])
        P.dma("sync", sg["lnb"][:], sgu_lnb[l], writes=["sg_par"])
        P.dma("sync", sg["wT"][:], sgu_wT[l], writes=["sg_par"])
        P.dma("sync", sg["bB"][:], sgu_bB[l], writes=["sg_par"])
        sguU_v = sguU.rearrange("(g c) t -> c g t", c=64)
        osgu_v = osgu_d.rearrange("(g c) t -> c g t", c=64)
        for t in range(NT):
            s = t % 2
            gi = 0 if t < 2 else 1 + (t - 2) // 4
            sv, gv, vn, su, gu, tmp, osg = (sg[k][s] for k in ("sv", "gv", "vn", "su", "gu", "tmp", "osg"))
            st6, mv, rstd = sg["st6"][s], sg["mv"][s], sg["rstd"][s]
            P.dma("sync", sv[:], sguV[t * 128:(t + 1) * 128, :], reads=[("sguV", "tm", t)], writes=[("sv", s)])
            P.dma("sync", su[:], sguU_v[:, :, t * 128:(t + 1) * 128], reads=[("sguU", "fm", gi)], writes=[("su", s)])
            gelu(sv[:], gv[:], sg["tva"][s][:], sg["tvb"][s][:], 128, [("sv", s)], [("gv", s)], ("gv", s))
            P.op("vector", lambda e, st6=st6, gv=gv: e.bn_stats(out=st6[:], in_=gv[:]), reads=[("gv", s)], writes=[("st6", s)])
            P.op("vector", lambda e, st6=st6, mv=mv: e.bn_aggr(out=mv[:], in_=st6[:]), reads=[("st6", s)], writes=[("mv", s)])
            P.op("scalar", lambda e, mv=mv, rstd=rstd: e.activation(out=rstd[:], in_=mv[:, 1:2], func=AF.Sqrt, bias=eps5[:, 0:1], scale=1.0),
                 reads=[("mv", s), "eps"], writes=[("rstd", s)])
            P.op("vector", lambda e, rstd=rstd: e.reciprocal(out=rstd[:], in_=rstd[:]), reads=[("rstd", s)], writes=[("rstd", s)])
            P.op("vector", lambda e, vn=vn, gv=gv, mv=mv, rstd=rstd: e.tensor_scalar(
                out=vn[:], in0=gv[:], scalar1=mv[:, 0:1], scalar2=rstd[:, 0:1], op0=ALU.subtract, op1=ALU.mult),
                reads=[("gv", s), ("mv", s), ("rstd", s)], writes=[("vn", s)])
            P.op("gpsimd", lambda e, vn=vn: e.tensor_tensor(out=vn[:], in0=vn[:], in1=sg["lng"][:], op=ALU.mult),
                 reads=[("vn", s), "sg_par"], writes=[("vn", s)])
            P.op("gpsimd", lambda e, vn=vn: e.tensor_tensor(out=vn[:], in0=vn[:], in1=sg["lnb"][:], op=ALU.add),
                 reads=[("vn", s), "sg_par"], writes=[("vn", s)])
            gelu(su[:], gu[:], sg["tua"][s][:], sg["tub"][s][:], 64, [("su", s)], [("gu", s)], ("gu", s))
            for half in range(2):
                pb = 2 * s + half
                for gg in range(4):
                    g = half * 4 + gg
                    P.op("tensor", lambda e, vn=vn, g=g, gg=gg, pb=pb: e.matmul(
                        PS[pb][0:64, gg * 128:(gg + 1) * 128], lhsT=vn[:, g * 64:(g + 1) * 64], rhs=sg["wT"][:, g, :],
                        start=True, stop=True), reads=[("vn", s), "sg_par"], writes=[("ps", pb)])
                P.op("vector", lambda e, tmp=tmp, pb=pb, half=half: e.tensor_tensor(
                    out=tmp[:, half * 4:(half + 1) * 4, :], in0=PS[pb][0:64, :].rearrange("p (g t) -> p g t", g=4),
                    in1=sg["bB"][:, half * 4:(half + 1) * 4, :], op=ALU.add), reads=[("ps", pb), "sg_par"], writes=[("sgtmp", s, half)])
                P.op("gpsimd", lambda e, tmp=tmp, gu=gu, osg=osg, half=half: e.tensor_tensor(
                    out=osg[:, half * 4:(half + 1) * 4, :], in0=tmp[:, half * 4:(half + 1) * 4, :], in1=gu[:, half * 4:(half + 1) * 4, :],
                    op=ALU.mult), reads=[("sgtmp", s, half), ("gu", s)], writes=[("osg", s, half)])
            P.dma("gpsimd", osgu_v[:, :, t * 128:(t + 1) * 128], osg[:], reads=[("osg", s, 0), ("osg", s, 1)], writes=[("osgu", gi)])
        if "osgu" in dbg and l == 0:
            d = dbg_out("osgu", [512, T], BF16)
            P.dma("sync", d, osgu_d, reads=[("osgu", g) for g in range(9)], writes=["OUT_dbgosgu"])

    def stage_na(l):
        na = {}
        if True:
            na["q"] = P.sbuf("na_q", [128, 4, T], BF16); na["k"] = P.sbuf("na_k", [128, 4, T], BF16)
            na["v"] = P.sbuf("na_v", [128, NT, 512], BF16)
            na["bI"] = P.sbuf("na_bI", [128, 8, 896]); na["bE"] = P.sbuf("na_bE", [128, 8, 896])
            for nm, shp, dt in (("s", [128, 896], F32), ("p", [128, 896], BF16), ("pT", [128, 896], BF16), ("nmx", [128, 1], F32)):
                na[nm] = [P.sbuf("na_%s%d" % (nm, i), shp, dt) for i in range(2)]
            for nm, shp, dt in (("rs", [128, 8], F32), ("rinv", [128, 8], F32), ("o", [128, 512], BF16), ("oT", [128, 4, 128], BF16)):
                na[nm] = [P.sbuf("na_%s%d" % (nm, i), shp, dt) for i in range(2)]
        qT_v = qT.rearrange("(k p) t -> p k t", p=128); kT_v = kT.rearrange("(k p) t -> p k t", p=128)
        vt_v = vtok.rearrange("(t p) f -> p t f", p=128)
        for gi, (t0, n) in enumerate(GROUPS):
            P.dma("sync", na["q"][:, :, t0:t0 + n], qT_v[:, :, t0:t0 + n], reads=[("qT", "fm", gi)], writes=[("na_q", gi)])
            P.dma("sync", na["k"][:, :, t0:t0 + n], kT_v[:, :, t0:t0 + n], reads=[("kT", "fm", gi)], writes=[("na_k", gi)])
        for t in range(NT):
            P.dma("sync", na["v"][:, t, :], vt_v[:, t, :], reads=[("vtok", "tm", t)], writes=[("na_v", t)])
        P.dma("sync", na["bI"][:], na_bias[l, 2], writes=["na_bI"])
        ona_v = ona_d.rearrange("(k p) t -> p k t", p=128)
        grp_of_tok = lambda tok: 0 if tok < 256 else 1 + (tok - 256) // 512
        hcnt = 0
        for t in range(NT):
            isctx = t < 2
            so = t % 2
            if isctx:
                nk = 256; pat = None; vtiles = [0, 1]
                kgroups = [0]
            else:
                qt = t - 2
                kb = min(max(2 * qt - 4, 0), 54)
                ktok0 = 256 + kb * 64
                nk = 896
                pat = {0: 0, 1: 1, 30: 3, 31: 4}.get(qt, 2)
                vtiles = [2 + kb // 2 + j for j in range(5)] + [0, 1]
                kgroups = sorted({0, grp_of_tok(ktok0), grp_of_tok(ktok0 + 639)})
                if pat != 2:
                    P.dma("sync", na["bE"][:], na_bias[l, pat], writes=["na_bE"])
            bias = None if isctx else (na["bI"] if pat == 2 else na["bE"])
            bkey = "na_bI" if pat == 2 else "na_bE"
            psO = PS[4 + so]
            qg = grp_of_tok(t * 128)
            for h in range(8):
                ch, p0 = h // 2, (h % 2) * 64
                x = hcnt % 2
                hcnt += 1
                psA, psBk = PS[x], PS[2 + x]
                s_t, p_t, pT_t, nmx = na["s"][x], na["p"][x], na["pT"][x], na["nmx"][x]
                lhsT = na["q"][p0:p0 + 64, ch, t * 128:(t + 1) * 128]
                krd = [("na_q", qg)] + [("na_k", g) for g in kgroups]
                if isctx:
                    P.op("tensor", lambda e, lhsT=lhsT, psA=psA, ch=ch, p0=p0: e.matmul(
                        psA[:, 0:256], lhsT=lhsT, rhs=na["k"][p0:p0 + 64, ch, 0:256], start=True, stop=True),
                        reads=krd, writes=[("ps", x)])
                    P.op("vector", lambda e, s_t=s_t, psA=psA: e.tensor_scalar(out=s_t[:, 0:256], in0=psA[:, 0:256], scalar1=0.125,
                                                                               scalar2=None, op0=ALU.mult),
                         reads=[("ps", x)], writes=[("na_s", x)])
                else:
                    P.op("tensor", lambda e, lhsT=lhsT, psA=psA, ch=ch, p0=p0, ktok0=ktok0: e.matmul(
                        psA[:, 0:512], lhsT=lhsT, rhs=na["k"][p0:p0 + 64, ch, ktok0:ktok0 + 512], start=True, stop=True),
                        reads=krd, writes=[("ps", x)])
                    P.op("tensor", lambda e, lhsT=lhsT, psBk=psBk, ch=ch, p0=p0, ktok0=ktok0: e.matmul(
                        psBk[:, 0:128], lhsT=lhsT, rhs=na["k"][p0:p0 + 64, ch, ktok0 + 512:ktok0 + 640], start=True, stop=True),
                        reads=krd, writes=[("ps", 2 + x)])
                    P.op("tensor", lambda e, lhsT=lhsT, psBk=psBk, ch=ch, p0=p0: e.matmul(
                        psBk[:, 128:384], lhsT=lhsT, rhs=na["k"][p0:p0 + 64, ch, 0:256], start=True, stop=True),
                        reads=krd, writes=[("ps", 2 + x)])
                    P.op("vector", lambda e, s_t=s_t, psA=psA, bias=bias, h=h: e.scalar_tensor_tensor(
                        out=s_t[:, 0:512], in0=psA[:, 0:512], scalar=0.125, in1=bias[:, h, 0:512], op0=ALU.mult, op1=ALU.add),
                        reads=[("ps", x), bkey], writes=[("na_s", x)])
                    P.op("vector", lambda e, s_t=s_t, psBk=psBk, bias=bias, h=h: e.scalar_tensor_tensor(
                        out=s_t[:, 512:896], in0=psBk[:, 0:384], scalar=0.125, in1=bias[:, h, 512:896], op0=ALU.mult, op1=ALU.add),
                        reads=[("ps", 2 + x), bkey], writes=[("na_s", x)])
                P.op("vector", lambda e, s_t=s_t, nmx=nmx, nk=nk: e.tensor_reduce(out=nmx[:, 0:1], in_=s_t[:, 0:nk], axis=AX.X, op=ALU.max,
                                                                                 negate=True),
                     reads=[("na_s", x)], writes=[("na_nmx", x)])
                P.op("scalar", lambda e, s_t=s_t, p_t=p_t, nmx=nmx, nk=nk, h=h, so=so: e.activation(
                    out=p_t[:, 0:nk], in_=s_t[:, 0:nk], func=AF.Exp, bias=nmx[:, 0:1], scale=1.0, accum_out=na["rs"][so][:, h:h + 1]),
                    reads=[("na_s", x), ("na_nmx", x)], writes=[("na_p", x), ("na_rs", so)])
                for j in range(nk // 128):
                    P.op("tensor", lambda e, p_t=p_t, j=j, x=x: e.transpose(out=PSB[x][:, j * 128:(j + 1) * 128],
                                                                          in_=p_t[:, j * 128:(j + 1) * 128], identity=identb[:]),
                         reads=[("na_p", x), "identb"], writes=[("psb", x)])
                if h % 2 == 0:
                    P.op("scalar", lambda e, pT_t=pT_t, x=x, nk=nk: e.copy(out=pT_t[:, 0:nk], in_=PSB[x][:, 0:nk]),
                         reads=[("psb", x)], writes=[("na_pT", x)])
                else:
                    P.op("vector", lambda e, pT_t=pT_t, x=x, nk=nk: e.tensor_copy(out=pT_t[:, 0:nk], in_=PSB[x][:, 0:nk]),
                         reads=[("psb", x)], writes=[("na_pT", x)])
                for j, vt in enumerate(vtiles):
                    P.op("tensor", lambda e, pT_t=pT_t, j=j, vt=vt, h=h, psO=psO, last=(j == len(vtiles) - 1): e.matmul(
                        psO[:, h * 64:(h + 1) * 64], lhsT=pT_t[:, j * 128:(j + 1) * 128], rhs=na["v"][:, vt, h * 64:(h + 1) * 64],
                        start=(j == 0), stop=last), reads=[("na_pT", x), ("na_v", vt)], writes=[("ps", 4 + so)])
            rs, rinv, o_t, oT = na["rs"][so], na["rinv"][so], na["o"][so], na["oT"][so]
            P.op("vector", lambda e, rs=rs, rinv=rinv: e.reciprocal(out=rinv[:], in_=rs[:]), reads=[("na_rs", so)], writes=[("na_rinv", so)])
            P.op("vector", lambda e, o_t=o_t, psO=psO, rinv=rinv: e.tensor_tensor(
                out=o_t[:].rearrange("p (h d) -> p h d", h=8), in0=psO[:].rearrange("p (h d) -> p h d", h=8),
                in1=rinv[:].unsqueeze(2).to_broadcast([128, 8, 64]), op=ALU.mult),
                reads=[("ps", 4 + so), ("na_rinv", so)], writes=[("na_o", so)])
            xx = hcnt % 2
            for c4 in range(4):
                P.op("tensor", lambda e, o_t=o_t, c4=c4, xx=xx: e.transpose(out=PSB[xx][:, c4 * 128:(c4 + 1) * 128],
                                                                          in_=o_t[:, c4 * 128:(c4 + 1) * 128], identity=identb[:]),
                     reads=[("na_o", so), "identb"], writes=[("psb", xx)])
            P.op("scalar", lambda e, oT=oT, xx=xx: e.copy(out=oT[:], in_=PSB[xx][:, 0:512].rearrange("p (c t) -> p c t", c=4)),
                 reads=[("psb", xx)], writes=[("na_oT", so)])
            P.dma("gpsimd", ona_v[:, :, t * 128:(t + 1) * 128], oT[:], reads=[("na_oT", so)], writes=[("ona", qg)])
        if "ona" in dbg and l == 0:
            d = dbg_out("ona", [512, T], BF16)
            P.dma("sync", d, ona_d, reads=[("ona", g) for g in range(9)], writes=["OUT_dbgona"])


    w_branch = P.dram("w_branch", [NL, 3, 512, D], F32, "ExternalInput")
    w_out = P.dram("w_out", [NL, D, D], F32, "ExternalInput")
    w_gu = P.dram("ffn_w_gu", [NL, D, 2 * D_FF], F32, "ExternalInput")
    w_down = P.dram("ffn_w_down", [NL, D_FF, D], F32, "ExternalInput")
    lnp_d = P.dram("lnp", [128, NL, 4, 8], F32, "ExternalInput")
    orw_d = P.dram("orw", [512, T], BF16)
    P.dma("sync", lnp[:], lnp_d, writes=["lnp"])
    P.op("vector", lambda e: e.memset(ones[:], 1.0), writes=["ones"])

    def load_w_bf16(dst_view, src_view, stg, npart, nfree, tag):
        a, b = dst_view.shape[1], dst_view.shape[2]
        rows = max(1, 4096 // b)
        i = 0
        for a0 in range(0, a, rows):
            a1 = min(a, a0 + rows)
            st = stg[i % 2]
            P.dma("sync", st[0:npart, 0:(a1 - a0) * b].rearrange("p (a b) -> p a b", b=b), src_view[:, a0:a1, :], writes=[("stg", i % 2)])
            P.op("gpsimd" if i % 2 == 0 else "vector", lambda e, st=st, a0=a0, a1=a1: e.tensor_copy(
                out=dst_view[:, a0:a1, :], in_=st[0:npart, 0:(a1 - a0) * b].rearrange("p (a b) -> p a b", b=b)),
                reads=[("stg", i % 2)], writes=[tag])
            i += 1

    def layer_norm_T(r, n, l, gi_, bi_, out_view, tagr, tagout, sq, stat):
        for k in range(8):
            P.op("tensor", lambda e, k=k: e.matmul(PS[0][:, :n], lhsT=ones[:], rhs=r[:, k, :n], start=(k == 0), stop=(k == 7)),
                 reads=[tagr, "ones"], writes=[("ps", 0)])
        for k in range(8):
            sqk = sq[k % 2]
            P.op("scalar", lambda e, k=k, sqk=sqk: e.activation(out=sqk[:, :n], in_=r[:, k, :n], func=AF.Square),
                 reads=[tagr], writes=[("lnsq", k % 2)])
            P.op("tensor", lambda e, k=k, sqk=sqk: e.matmul(PS[1][:, :n], lhsT=ones[:], rhs=sqk[:, :n], start=(k == 0), stop=(k == 7)),
                 reads=[("lnsq", k % 2), "ones"], writes=[("ps", 1)])
        mean, var, rstd = stat
        P.op("vector", lambda e: e.tensor_scalar(out=mean[:, :n], in0=PS[0][:, :n], scalar1=1.0 / 1024, scalar2=None, op0=ALU.mult),
             reads=[("ps", 0)], writes=["ln_mean"])
        P.op("vector", lambda e: e.tensor_tensor(out=var[:, :n], in0=mean[:, :n], in1=mean[:, :n], op=ALU.mult),
             reads=["ln_mean"], writes=["ln_var"])
        P.op("vector", lambda e: e.scalar_tensor_tensor(out=var[:, :n], in0=PS[1][:, :n], scalar=1.0 / 1024, in1=var[:, :n],
                                                        op0=ALU.mult, op1=ALU.subtract), reads=[("ps", 1), "ln_var"], writes=["ln_var"])
        P.op("scalar", lambda e: e.activation(out=rstd[:, :n], in_=var[:, :n], func=AF.Sqrt, bias=eps5[:, 0:1], scale=1.0),
             reads=["ln_var", "eps"], writes=["ln_rstd"])
        P.op("vector", lambda e: e.reciprocal(out=rstd[:, :n], in_=rstd[:, :n]), reads=["ln_rstd"], writes=["ln_rstd"])
        for k in range(8):
            eng = "vector" if k % 2 == 0 else "gpsimd"
            P.op(eng, lambda e, k=k: e.tensor_tensor(out=r[:, k, :n], in0=r[:, k, :n], in1=mean[:, :n], op=ALU.subtract),
                 reads=[tagr, "ln_mean", "ln_rstd"], writes=[(tagr, "c", k)])
            P.op(eng, lambda e, k=k: e.tensor_tensor(out=r[:, k, :n], in0=r[:, k, :n], in1=rstd[:, :n], op=ALU.mult),
                 reads=[tagr, (tagr, "c", k), "ln_rstd"], writes=[(tagr, "c", k)])
            P.op(eng, lambda e, k=k: e.tensor_scalar(out=out_view[:, k, :n], in0=r[:, k, :n], scalar1=lnp[:, l, gi_, k:k + 1],
                                                     scalar2=lnp[:, l, bi_, k:k + 1], op0=ALU.mult, op1=ALU.add),
                 reads=[tagr, (tagr, "c", k), "lnp"], writes=[(tagout, k)])

    def stage_merge(l):
        wg = P.sbuf("m_wg", [128, 8, 3072], BF16)
        wbn = P.sbuf("m_wbn", [128, 4, 1024], BF16); wbr = P.sbuf("m_wbr", [128, 4, 1024], BF16)
        wbs = P.sbuf("m_wbs", [64, 8, 1024], BF16); wo = P.sbuf("m_wo", [128, 8, 1024], BF16)
        stg = [P.sbuf("m_stg%d" % i, [128, 4096]) for i in range(2)]
        xr = P.sbuf("m_xr", [128, 8, 512]); hh = P.sbuf("m_h", [128, 8, 512], BF16)
        o_n = P.sbuf("m_on", [128, 4, 512], BF16); o_r = P.sbuf("m_or", [128, 4, 512], BF16); o_s = P.sbuf("m_os", [64, 8, 512], BF16)
        sig = [P.sbuf("m_sig%d" % i, [128, 512]) for i in range(3)]
        ypre = P.sbuf("m_ypre", [128, 8, 512], BF16); yacc = P.sbuf("m_yacc", [128, 512]); ytmp = P.sbuf("m_ytmp", [128, 512])
        sq = [P.sbuf("m_sq%d" % i, [128, 512]) for i in range(2)]
        stat = [P.sbuf("m_st%d" % i, [128, 512]) for i in range(3)]
        load_w_bf16(wg[:], w_in[l, :, 0:3072].rearrange("(k p) m -> p k m", p=128), stg, 128, 0, "m_wg")
        load_w_bf16(wbn[:], w_branch[l, 0].rearrange("(k p) m -> p k m", p=128), stg, 128, 0, "m_wbn")
        load_w_bf16(wbr[:], w_branch[l, 1].rearrange("(k p) m -> p k m", p=128), stg, 128, 0, "m_wbr")
        load_w_bf16(wbs[:], w_branch[l, 2].rearrange("(g c) m -> c g m", c=64), stg, 64, 0, "m_wbs")
        load_w_bf16(wo[:], w_out[l].rearrange("(k p) m -> p k m", p=128), stg, 128, 0, "m_wo")
        ona_v = ona_d.rearrange("(k p) t -> p k t", p=128); orw_v = orw_d.rearrange("(k p) t -> p k t", p=128)
        osgu_v = osgu_d.rearrange("(g c) t -> c g t", c=64)
        pr = 0
        for gi, (t0, n) in enumerate(GROUPS):
            c = 1 if gi == 0 else 0
            P.dma("sync", xr[:, :, :n], xT_v[:, :, t0:t0 + n], reads=[("xT", gi)], writes=["m_xr"])
            P.dma("sync", o_n[:, :, :n], ona_v[:, :, t0:t0 + n], reads=[("ona", gi)], writes=["m_on"])
            P.dma("sync", o_r[:, :, :n], orw_v[:, :, t0:t0 + n], reads=[("orw", gi)], writes=["m_or"])
            P.dma("sync", o_s[:, :, :n], osgu_v[:, :, t0:t0 + n], reads=[("osgu", gi)], writes=["m_os"])
            for k in range(8):
                P.op("vector" if k % 2 == 0 else "gpsimd", lambda e, k=k, c=c: e.tensor_scalar(
                    out=hh[:, k, :n], in0=xr[:, k, :n], scalar1=mod1[:, l, 8 + k, c:c + 1], scalar2=mod[:, l, k, c:c + 1],
                    op0=ALU.mult, op1=ALU.add), reads=["m_xr", "mod", "mod1"], writes=["m_h"])
            for m in range(8):
                for i in range(3):
                    pg, pb = PS[2 * (pr % 3)], PS[2 * (pr % 3) + 1]
                    kg, kb_ = ("ps", 2 * (pr % 3)), ("ps", 2 * (pr % 3) + 1)
                    pr += 1
                    mc = i * 8 + m
                    for k in range(8):
                        P.op("tensor", lambda e, k=k, mc=mc, pg=pg: e.matmul(pg[:, :n], lhsT=wg[:, k, mc * 128:(mc + 1) * 128], rhs=hh[:, k, :n],
                                                                             start=(k == 0), stop=(k == 7)), reads=["m_wg", "m_h"], writes=[kg])
                    P.op("scalar", lambda e, i=i, pg=pg: e.activation(out=sig[i][:, :n], in_=pg[:, :n], func=AF.Sigmoid),
                         reads=[kg], writes=[("m_sig", i)])
                    if i < 2:
                        wb_, ob_, ok_ = (wbn, o_n, "m_on") if i == 0 else (wbr, o_r, "m_or")
                        for k in range(4):
                            P.op("tensor", lambda e, k=k, m=m, pb=pb, wb_=wb_, ob_=ob_: e.matmul(
                                pb[:, :n], lhsT=wb_[:, k, m * 128:(m + 1) * 128], rhs=ob_[:, k, :n], start=(k == 0), stop=(k == 3)),
                                reads=["m_wbn", "m_wbr", ok_], writes=[kb_])
                    else:
                        for g in range(8):
                            P.op("tensor", lambda e, g=g, m=m, pb=pb: e.matmul(
                                pb[:, :n], lhsT=wbs[:, g, m * 128:(m + 1) * 128], rhs=o_s[:, g, :n], start=(g == 0), stop=(g == 7)),
                                reads=["m_wbs", "m_os"], writes=[kb_])
                    if i == 0:
                        P.op("vector", lambda e, pb=pb: e.tensor_tensor(out=yacc[:, :n], in0=pb[:, :n], in1=sig[0][:, :n], op=ALU.mult),
                             reads=[kb_, ("m_sig", 0)], writes=["m_yacc"])
                    else:
                        P.op("vector", lambda e, pb=pb, i=i: e.tensor_tensor(out=ytmp[:, :n], in0=pb[:, :n], in1=sig[i][:, :n], op=ALU.mult),
                             reads=[kb_, ("m_sig", i)], writes=["m_ytmp"])
                        if i == 1:
                            P.op("gpsimd", lambda e: e.tensor_tensor(out=yacc[:, :n], in0=yacc[:, :n], in1=ytmp[:, :n], op=ALU.add),
                                 reads=["m_yacc", "m_ytmp"], writes=["m_yacc"])
                        else:
                            P.op("gpsimd", lambda e, m=m: e.tensor_tensor(out=ypre[:, m, :n], in0=yacc[:, :n], in1=ytmp[:, :n], op=ALU.add),
                                 reads=["m_yacc", "m_ytmp"], writes=["m_ypre"])
            for m in range(8):
                pg = PS[2 + m % 4]; kg = ("ps", 2 + m % 4)
                for k in range(8):
                    P.op("tensor", lambda e, k=k, m=m, pg=pg: e.matmul(pg[:, :n], lhsT=wo[:, k, m * 128:(m + 1) * 128], rhs=ypre[:, k, :n],
                                                                         start=(k == 0), stop=(k == 7)), reads=["m_wo", "m_ypre"], writes=[kg])
                P.op("scalar", lambda e, m=m, pg=pg, c=c: e.activation(out=ytmp[:, :n], in_=pg[:, :n], func=AF.Copy,
                                                                      scale=mod[:, l, 16 + m, c:c + 1]), reads=[kg, "mod"], writes=["m_ytmp"])
                P.op("vector", lambda e, m=m: e.scalar_tensor_tensor(out=xr[:, m, :n], in0=xr[:, m, :n], scalar=ALPHA, in1=ytmp[:, :n],
                                                                    op0=ALU.mult, op1=ALU.add), reads=["m_xr", "m_ytmp"], writes=["m_xr"])
            layer_norm_T(xr, n, l, 0, 1, xr, "m_xr", "m_x1", sq, stat)
            P.dma("gpsimd", xT_v[:, :, t0:t0 + n], xr[:, :, :n], reads=["m_xr"] + [("m_x1", k) for k in range(8)], writes=[("xT", gi)])
        if "x1" in dbg and l == 0:
            d = dbg_out("x1", [D, T])
            P.dma("sync", d, xT, reads=[("xT", g) for g in range(9)], writes=["OUT_dbgx1"])

    FG = [(t0, 256) for t0 in range(0, T, 256)]

    def stage_ffn(l):
        wgu = P.sbuf("f_wgu", [128, 8, 2 * D_FF], BF16)
        wd = P.sbuf("f_wd", [128, 22, 1024], BF16)
        stg = [P.sbuf("f_stg%d" % i, [128, 2048]) for i in range(2)]
        xr = P.sbuf("f_xr", [128, 8, 256]); ff = P.sbuf("f_f", [128, 8, 256], BF16)
        act = P.sbuf("f_a", [128, 22, 256], BF16)
        sgt = [P.sbuf("f_sg%d" % i, [128, 256]) for i in range(2)]
        ytmp = P.sbuf("f_ytmp", [128, 256])
        sq = [P.sbuf("f_sq%d" % i, [128, 256]) for i in range(2)]
        stat = [P.sbuf("f_st%d" % i, [128, 256]) for i in range(3)]

        def load2(dst_view, src_view, tag):
            a, b = dst_view.shape[1], dst_view.shape[2]
            i = 0
            for a0 in range(a):
                for b0 in range(0, b, 2048):
                    b1 = min(b, b0 + 2048)
                    st = stg[i % 2]
                    P.dma("sync", st[:, 0:b1 - b0], src_view[:, a0, b0:b1], writes=[("fstg", i % 2)])
                    P.op("gpsimd" if i % 2 == 0 else "vector", lambda e, st=st, a0=a0, b0=b0, b1=b1: e.tensor_copy(
                        out=dst_view[:, a0, b0:b1], in_=st[:, 0:b1 - b0]), reads=[("fstg", i % 2)], writes=[tag])
                    i += 1
        load2(wgu[:], w_gu[l].rearrange("(k p) m -> p k m", p=128), "f_wgu")
        load2(wd[:], w_down[l].rearrange("(k p) m -> p k m", p=128), "f_wd")
        for fi, (t0, n) in enumerate(FG):
            gi = 0 if t0 < 256 else 1 + (t0 - 256) // 512
            c = 1 if fi == 0 else 0
            P.dma("sync", xr[:, :, :n], xT_v[:, :, t0:t0 + n], reads=[("xT", gi)], writes=["f_xr"])
            for k in range(8):
                P.op("vector" if k % 2 == 0 else "gpsimd", lambda e, k=k, c=c: e.tensor_scalar(
                    out=ff[:, k, :n], in0=xr[:, k, :n], scalar1=mod1[:, l, 32 + k, c:c + 1], scalar2=mod[:, l, 24 + k, c:c + 1],
                    op0=ALU.mult, op1=ALU.add), reads=["f_xr", "mod", "mod1"], writes=["f_f"])
            for j in range(22):
                x2 = j % 2
                pg, pu = PS[2 + 2 * x2], PS[3 + 2 * x2]
                kg, ku = ("ps", 2 + 2 * x2), ("ps", 3 + 2 * x2)
                for k in range(8):
                    P.op("tensor", lambda e, k=k, j=j, pg=pg: e.matmul(pg[:, :n], lhsT=wgu[:, k, j * 128:(j + 1) * 128], rhs=ff[:, k, :n],
                                                                         start=(k == 0), stop=(k == 7)), reads=["f_wgu", "f_f"], writes=[kg])
                for k in range(8):
                    P.op("tensor", lambda e, k=k, j=j, pu=pu: e.matmul(pu[:, :n], lhsT=wgu[:, k, D_FF + j * 128:D_FF + (j + 1) * 128],
                                                                         rhs=ff[:, k, :n], start=(k == 0), stop=(k == 7)),
                         reads=["f_wgu", "f_f"], writes=[ku])
                P.op("scalar", lambda e, pg=pg, x2=x2: e.activation(out=sgt[x2][:, :n], in_=pg[:, :n], func=AF.Silu),
                     reads=[kg], writes=[("f_sg", x2)])
                P.op("vector", lambda e, pu=pu, x2=x2, j=j: e.tensor_tensor(out=act[:, j, :n], in0=pu[:, :n], in1=sgt[x2][:, :n], op=ALU.mult),
                     reads=[ku, ("f_sg", x2)], writes=["f_a"])
            for m in range(8):
                pg = PS[2 + m % 4]; kg = ("ps", 2 + m % 4)
                for j in range(22):
                    P.op("tensor", lambda e, j=j, m=m, pg=pg: e.matmul(pg[:, :n], lhsT=wd[:, j, m * 128:(m + 1) * 128], rhs=act[:, j, :n],
                                                                         start=(j == 0), stop=(j == 21)), reads=["f_wd", "f_a"], writes=[kg])
                P.op("scalar", lambda e, m=m, pg=pg, c=c: e.activation(out=ytmp[:, :n], in_=pg[:, :n], func=AF.Copy,
                                                                      scale=mod[:, l, 40 + m, c:c + 1]), reads=[kg, "mod"], writes=["f_ytmp"])
                P.op("vector", lambda e, m=m: e.scalar_tensor_tensor(out=xr[:, m, :n], in0=xr[:, m, :n], scalar=ALPHA, in1=ytmp[:, :n],
                                                                    op0=ALU.mult, op1=ALU.add), reads=["f_xr", "f_ytmp"], writes=["f_xr"])
            layer_norm_T(xr, n, l, 2, 3, xr, "f_xr", "f_x2", sq, stat)
            P.dma("gpsimd", xT_v[:, :, t0:t0 + n], xr[:, :, :n], reads=["f_xr"] + [("f_x2", k) for k in range(8)], writes=[("xT", gi)])
        if "x2" in dbg and l == 0:
            d = dbg_out("x2", [D, T])
            P.dma("sync", d, xT, reads=[("xT", g) for g in range(9)], writes=["OUT_dbgx2"])

    def stage_final():
        xg_ = [P.sbuf("o_xg%d" % i, [128, 8, 128]) for i in range(2)]
        ot = [P.sbuf("o_t%d" % i, [128, 1024]) for i in range(2)]
        for t in range(2, NT):
            s_ = t % 2
            gi = 1 + (t - 2) // 4
            P.dma("sync", xg_[s_][:], xT_v[:, :, t * 128:(t + 1) * 128], reads=[("xT", gi)], writes=[("o_xg", s_)])
            for half in range(2):
                pb = 2 + 2 * s_ + half
                for kk in range(4):
                    k = half * 4 + kk
                    P.op("tensor", lambda e, s_=s_, k=k, kk=kk, pb=pb: e.transpose(
                        out=PS[pb][:, kk * 128:(kk + 1) * 128], in_=xg_[s_][:, k, :], identity=ident[:]),
                        reads=[("o_xg", s_), "ident"], writes=[("ps", pb)])
                if half == 0:
                    P.op("scalar", lambda e, s_=s_, pb=pb: e.copy(out=ot[s_][:, 0:512], in_=PS[pb][:]), reads=[("ps", pb)], writes=[("o_t", s_, 0)])
                else:
                    P.op("vector", lambda e, s_=s_, pb=pb: e.tensor_copy(out=ot[s_][:, 512:1024], in_=PS[pb][:]), reads=[("ps", pb)],
                         writes=[("o_t", s_, 1)])
            P.dma("gpsimd", out_d[(t - 2) * 128:(t - 1) * 128, :], ot[s_][:], reads=[("o_t", s_, 0), ("o_t", s_, 1)], writes=["OUT_%d" % t])

    rw_mu_d = P.dram("rw_mu", [128, NL, 2, 15], F32, "ExternalInput")
    rw_w0a0_d = P.dram("rw_w0a0", [128, NL, 2, 2, 4], F32, "ExternalInput")
    rw_w2_d = P.dram("rw_w2", [NL, 128, 512], F32, "ExternalInput")
    rw_a2_d = P.dram("rw_a2", [NL, 128, 512], F32, "ExternalInput")
    rw_g2_d = P.dram("rw_g2", [NL, 128, 512], F32, "ExternalInput")
    rw_vec_d = P.dram("rw_vec", [128, NL, 5, 4], F32, "ExternalInput")
    mask64_d = P.dram("mask64", [64, 4, 64], F32, "ExternalInput")
    bones_d = P.dram("bones", [128, 128], F32, "ExternalInput")
    g_d = P.dram("rw_g", [512, T], F32)
    bv_d = P.dram("rw_bv", [512, T], F32)
    NCH = T // 64
    summ_d = P.dram("rw_summ", [2, NCH, 64, 2056], F32)
    etot_d = P.dram("rw_etot", [2, NCH, 512], F32)
    y_d = P.dram("rw_y", [2, T, 512], F32)
    CDEC = float(np.exp(-0.5))
    psctr = [0]

    def psn():
        i = psctr[0] % 6
        psctr[0] += 1
        return PS[i], ("ps", i)

    def stage_rwkv(l):
        import os
        RW_NG = int(os.environ.get("RW_NGROUPS", "17")); RW_CH = int(os.environ.get("RW_CHUNKS", "1")); RW_PH = int(os.environ.get("RW_PHASES", "3"))
        RW_RND = int(os.environ.get("RW_ROUNDS", "6")); RW_ST = int(os.environ.get("RW_STEPS", "99"))
        NG = 256
        prw_v = prw.rearrange("(k p) t -> p k t", p=128)
        g_v = g_d.rearrange("(k p) t -> p k t", p=128)
        bv_v = bv_d.rearrange("(k p) t -> p k t", p=128)
        grp512 = lambda tok: 0 if tok < 256 else 1 + (tok - 256) // 512
        mu = P.sbuf("rw_mu", [128, 2, 15]); c0 = P.sbuf("rw_c0", [128, 15])
        w0a0 = P.sbuf("rw_w0a0", [128, 2, 2, 4]); w2s = P.sbuf("rw_w2", [128, 512]); a2s = P.sbuf("rw_a2", [128, 512])
        g2s = P.sbuf("rw_g2", [128, 512]); vec = P.sbuf("rw_vec", [128, 5, 4]); omka = P.sbuf("rw_omka", [128, 4])
        mask = P.sbuf("rw_mask", [64, 4, 64]); bones = P.sbuf("rw_bones", [128, 128]); eps12 = P.sbuf("rw_eps12", [128, 1])
        eps_gn = P.sbuf("rw_epsgn", [128, 1]); rmask = P.sbuf("rw_rmask", [128, 16, 64])
        P.dma("sync", mu[:], rw_mu_d[:, l], writes=["rw_par"]); P.dma("sync", w0a0[:], rw_w0a0_d[:, l], writes=["rw_par"])
        P.dma("sync", w2s[:], rw_w2_d[l], writes=["rw_par"]); P.dma("sync", a2s[:], rw_a2_d[l], writes=["rw_par"])
        P.dma("sync", g2s[:], rw_g2_d[l], writes=["rw_par"]); P.dma("sync", vec[:], rw_vec_d[:, l], writes=["rw_par"])
        P.dma("sync", mask[:], mask64_d, writes=["rw_par"]); P.dma("sync", bones[:], bones_d, writes=["rw_par"])
        P.op("vector", lambda e: e.tensor_tensor(out=c0[:], in0=mu[:, 0, :], in1=mu[:, 1, :], op=ALU.add), reads=["rw_par"], writes=["rw_c0"])
        P.op("vector", lambda e: e.tensor_scalar(out=c0[:], in0=c0[:], scalar1=-1.0, scalar2=1.0, op0=ALU.mult, op1=ALU.add),
             reads=["rw_c0"], writes=["rw_c0"])
        P.op("vector", lambda e: e.tensor_scalar(out=omka[:], in0=vec[:, 1, :], scalar1=-1.0, scalar2=1.0, op0=ALU.mult, op1=ALU.add),
             reads=["rw_par"], writes=["rw_omka"])
        P.op("vector", lambda e: e.memset(eps12[:], 1e-12), writes=["rw_eps"])
        P.op("vector", lambda e: e.memset(eps_gn[:], 64e-5), writes=["rw_eps"])
        P.op("vector", lambda e: e.memset(rmask[:], 1.0), writes=["rw_rmask"])
        P.op("vector", lambda e: e.memset(rmask[:, :, 0:1], 0.0), reads=["rw_rmask"], writes=["rw_rmask"])
        m1 = P.mark()
        pin = P.sbuf("rw_pin", [128, 15, NG + 2]); psh = P.sbuf("rw_psh", [128, 15, NG])
        sgw = [P.sbuf("rw_sgw%d" % d, [128, 4, NG]) for d in range(2)]
        aa = [P.sbuf("rw_a%d" % d, [128, 4, NG]) for d in range(2)]
        kd = [P.sbuf("rw_kd%d" % d, [128, 4, NG]) for d in range(2)]
        kk = P.sbuf("rw_kk", [128, 4, NG]); tA = P.sbuf("rw_tA", [128, 4, NG]); tB = P.sbuf("rw_tB", [128, 4, NG])
        Lp = P.sbuf("rw_Lp", [128, 4, NG]); Li = P.sbuf("rw_Li", [128, 4, NG]); Le = P.sbuf("rw_Le", [128, 4, NG])
        rt = P.sbuf("rw_rt", [128, 4, NG], F32R); at = P.sbuf("rw_at", [128, 4, NG], F32R); kt = P.sbuf("rw_kt", [128, 4, NG], F32R)
        bt = P.sbuf("rw_bt", [128, 4, NG], F32R); Kh = P.sbuf("rw_Kh", [128, 4, NG]); Bh = P.sbuf("rw_Bh", [128, 4, NG])
        etot = P.sbuf("rw_etot", [128, 4, 4], F32R); gst = P.sbuf("rw_gst", [128, 4, NG])
        mk = {nm: [P.sbuf("rw_%s%s" % (nm, eo), [128, 4, NG], F32R) for eo in "EO"] for nm in ("at", "kt", "bt")}
        zsrc = P.sbuf("rw_zsrc", [128, 1024])
        P.op("vector", lambda e: e.memset(zsrc[:], 0.0), writes=["rw_zsrc"])
        Vt = [P.sbuf("rw_Vt%d" % c, [128, 512], F32R) for c in range(4)]
        KhT = P.sbuf("rw_KhT", [128, 512], F32R); BhT = P.sbuf("rw_BhT", [128, 512], F32R)
        Mka = P.sbuf("rw_Mka", [128, 8, 64], F32R); Mkr = P.sbuf("rw_Mkr", [128, 8, 64], F32R); Mbr = P.sbuf("rw_Mbr", [128, 8, 64], F32R)
        Nn = [P.sbuf("rw_N%d" % i, [128, 8, 64], F32R) for i in range(2)]; NTr = [P.sbuf("rw_NT%d" % i, [128, 8, 64], F32R) for i in range(2)]
        Z = P.sbuf("rw_Z", [128, 8, 128], F32R); Zn = P.sbuf("rw_Zn", [128, 8, 128], F32R)
        for tl, key in ([(Vt[c], ("rw_Vt", c)) for c in range(4)] + [(KhT, "rw_KhT"), (BhT, "rw_BhT"), (Mka, "rw_Mka"), (Mkr, "rw_Mkr"),
                        (Mbr, "rw_Mbr"), (Nn[0], ("rw_N", 0)), (Nn[1], ("rw_N", 1)), (NTr[0], ("rw_NT", 0)), (NTr[1], ("rw_NT", 1)),
                        (Z, "rw_Z"), (Zn, "rw_Zn")]):
            nfree = int(np.prod(tl.shape[1:]))
            src_ = zsrc[64:128, 0:nfree] if len(tl.shape) == 2 else zsrc[64:128, 0:nfree].rearrange("p (a b) -> p a b", a=tl.shape[1])
            P.op("vector", lambda e, tl=tl, src_=src_: e.tensor_copy(out=tl[64:128], in_=src_), reads=["rw_zsrc"],
                 writes=[key] + (["rw_Z2"] if key == "rw_Z" else []))
        identr = P.sbuf("rw_identr", [128, 128], F32R)
        P.op("vector", lambda e: e.tensor_copy(out=identr[:], in_=ident[:]), reads=["ident"], writes=["rw_identr"])
        summ = [P.sbuf("rw_summ%d" % i, [64, 2056]) for i in range(2)]
        r_ = psh[:, 0:4, :]; k_ = psh[:, 4:8, :]; v_ = psh[:, 8:12, :]
        TA = [("rw_tA", fc) for fc in range(4)]; TB = [("rw_tB", fc) for fc in range(4)]
        nsum = 0
        for gx in range(min(T // NG, RW_NG)):
            t0 = gx * NG
            isctx = gx == 0
            rdk = sorted({grp512(max(t0 - 1, 0)), grp512(t0), grp512(min(t0 + NG, T - 1))})
            rdk = [("prw", "fm", g) for g in rdk]
            hasL = not (isctx or gx == 1)
            hasR = not (isctx or gx == T // NG - 1)
            lo = t0 - 1 if hasL else t0
            hi = t0 + NG + 1 if hasR else t0 + NG
            P.dma("sync", pin[:, :, lo - (t0 - 1):hi - (t0 - 1)], prw_v[:, :, lo:hi], reads=rdk, writes=["rw_pin"])
            if not hasL:
                P.op("gpsimd", lambda e: e.memset(pin[:, :, 0:1], 0.0), reads=["rw_pin"], writes=["rw_pinL"])
            if not hasR:
                P.op("gpsimd", lambda e: e.memset(pin[:, :, NG + 1:NG + 2], 0.0), reads=["rw_pin"], writes=["rw_pinR"])
            pk = ["rw_pin", "rw_pinL", "rw_pinR"]
            for ch in range(15):
                P.op("gpsimd", lambda e, ch=ch: e.tensor_scalar(out=psh[:, ch, :], in0=pin[:, ch, 1:NG + 1], scalar1=c0[:, ch:ch + 1],
                                                               scalar2=None, op0=ALU.mult), reads=pk + ["rw_c0"], writes=[("rw_psh", ch)])
                P.op("vector", lambda e, ch=ch: e.scalar_tensor_tensor(out=psh[:, ch, :], in0=pin[:, ch, 0:NG], scalar=mu[:, 0, ch:ch + 1],
                                                                      in1=psh[:, ch, :], op0=ALU.mult, op1=ALU.add),
                     reads=pk + ["rw_par", ("rw_psh", ch)], writes=[("rw_psh", ch)])
                P.op("vector", lambda e, ch=ch: e.scalar_tensor_tensor(out=psh[:, ch, :], in0=pin[:, ch, 2:NG + 2], scalar=mu[:, 1, ch:ch + 1],
                                                                      in1=psh[:, ch, :], op0=ALU.mult, op1=ALU.add),
                     reads=pk + ["rw_par", ("rw_psh", ch)], writes=[("rw_psh", ch)])
            RK = [("rw_psh", c) for c in range(0, 4)]; KK = [("rw_psh", c) for c in range(4, 8)]; VK = [("rw_psh", c) for c in range(8, 12)]
            P.op("scalar", lambda e: e.activation(out=psh[:, 12, :], in_=psh[:, 12, :], func=AF.Tanh), reads=[("rw_psh", 12)], writes=[("rw_psh", 12)])
            P.op("scalar", lambda e: e.activation(out=psh[:, 14, :], in_=psh[:, 14, :], func=AF.Sigmoid), reads=[("rw_psh", 14)],
                 writes=[("rw_psh", 14)])
            for which, (wsrc, srcch, dst) in enumerate(((w2s, 12, sgw), (a2s, 13, aa))):
                for d in range(2):
                    for fc in range(4):
                        ps, pk_ = psn()
                        P.op("tensor", lambda e, d=d, fc=fc, ps=ps, wsrc=wsrc, srcch=srcch: e.matmul(
                            ps[:, 0:NG], lhsT=wsrc[d * 64:(d + 1) * 64, fc * 128:(fc + 1) * 128], rhs=psh[d * 64:(d + 1) * 64, srcch, :],
                            start=True, stop=True), reads=["rw_par", ("rw_psh", srcch)], writes=[pk_])
                        P.op("scalar", lambda e, d=d, fc=fc, ps=ps, dst=dst, which=which: e.activation(
                            out=dst[d][:, fc, :], in_=ps[:, 0:NG], func=AF.Sigmoid, bias=w0a0[:, which, d, fc:fc + 1], scale=1.0),
                            reads=[pk_, "rw_par"], writes=[("rw_sa", which, d)])
            for fc in range(4):
                ps, pk_ = psn()
                P.op("tensor", lambda e, fc=fc, ps=ps: e.matmul(ps[:, 0:NG], lhsT=g2s[:, fc * 128:(fc + 1) * 128], rhs=psh[:, 14, :],
                                                              start=True, stop=True), reads=["rw_par", ("rw_psh", 14)], writes=[pk_])
                P.op("scalar", lambda e, fc=fc, ps=ps: e.copy(out=gst[:, fc, :], in_=ps[:, 0:NG]), reads=[pk_], writes=["rw_gst"])
            P.dma("gpsimd", g_v[:, :, t0:t0 + NG], gst[:], reads=["rw_gst"], writes=[("rw_g", gx // 2)])
            for fc in range(4):
                P.op("gpsimd", lambda e, fc=fc: e.tensor_scalar(out=kk[:, fc, :], in0=psh[:, 4 + fc, :], scalar1=vec[:, 0, fc:fc + 1], scalar2=None,
                                                               op0=ALU.mult), reads=KK + ["rw_par"], writes=[("rw_kk", fc)])
                P.op("scalar", lambda e, fc=fc: e.activation(out=tA[:, fc, :], in_=kk[:, fc, :], func=AF.Square), reads=[("rw_kk", fc)],
                     writes=[("rw_tA", fc)])
                ps, pk_ = psn()
                P.op("tensor", lambda e, fc=fc, ps=ps: e.matmul(ps[:, 0:NG], lhsT=bones[:], rhs=tA[:, fc, :], start=True, stop=True),
                     reads=["rw_par", ("rw_tA", fc)], writes=[pk_])
                P.op("scalar", lambda e, fc=fc, ps=ps: e.activation(out=tB[:, fc, :], in_=ps[:, 0:NG], func=AF.Sqrt, bias=eps12[:, 0:1], scale=1.0),
                     reads=[pk_, "rw_eps"], writes=[("rw_tB", fc)])
                P.op("vector", lambda e, fc=fc: e.reciprocal(out=tB[:, fc, :], in_=tB[:, fc, :]), reads=[("rw_tB", fc)], writes=[("rw_tB", fc)])
                P.op("vector", lambda e, fc=fc: e.tensor_tensor(out=kk[:, fc, :], in0=kk[:, fc, :], in1=tB[:, fc, :], op=ALU.mult),
                     reads=[("rw_kk", fc), ("rw_tB", fc)], writes=[("rw_kk", fc)])
            KKN = [("rw_kk", fc) for fc in range(4)]
            for d in range(2):
                for fc in range(4):
                    P.op("gpsimd", lambda e, d=d, fc=fc: e.tensor_scalar(out=kd[d][:, fc, :], in0=aa[d][:, fc, :], scalar1=vec[:, 1, fc:fc + 1],
                                                                        scalar2=omka[:, fc:fc + 1], op0=ALU.mult, op1=ALU.add),
                         reads=[("rw_sa", 1, d), "rw_par", "rw_omka"], writes=[("rw_kd", d)])
                P.op("gpsimd", lambda e, d=d: e.tensor_tensor(out=kd[d][:], in0=kd[d][:], in1=k_, op=ALU.mult),
                     reads=[("rw_kd", d)] + KK, writes=[("rw_kd", d)])
                P.op("vector", lambda e, d=d: e.tensor_tensor(out=aa[d][:], in0=aa[d][:], in1=kk[:], op=ALU.mult),
                     reads=[("rw_sa", 1, d), ("rw_kd", d)] + KKN, writes=[("rw_sa", 1, d)])
            P.op("gpsimd", lambda e: e.tensor_tensor(out=tA[:], in0=kd[0][:], in1=kd[1][:], op=ALU.add),
                 reads=[("rw_kd", 0), ("rw_kd", 1)], writes=TA)
            P.op("gpsimd", lambda e: e.tensor_tensor(out=tA[:], in0=tA[:], in1=r_, op=ALU.mult), reads=TA + RK, writes=TA)
            for fc in range(4):
                P.op("gpsimd", lambda e, fc=fc: e.tensor_scalar(out=tA[:, fc, :], in0=tA[:, fc, :], scalar1=vec[:, 2, fc:fc + 1], scalar2=None,
                                                               op0=ALU.mult), reads=[("rw_tA", fc), "rw_par"], writes=[("rw_tA", fc)])
                ps, pk_ = psn()
                P.op("tensor", lambda e, fc=fc, ps=ps: e.matmul(ps[:, 0:NG], lhsT=bones[:], rhs=tA[:, fc, :], start=True, stop=True),
                     reads=["rw_par", ("rw_tA", fc)], writes=[pk_])
                P.op("vector", lambda e, fc=fc, ps=ps: e.tensor_tensor(out=gst[:, fc, :], in0=ps[:, 0:NG], in1=psh[:, 8 + fc, :], op=ALU.mult),
                     reads=[pk_, "rw_gst"] + VK, writes=["rw_gst"])
            P.dma("gpsimd", bv_v[:, :, t0:t0 + NG], gst[:], reads=["rw_gst"], writes=[("rw_bv", gx // 2)])
            for c in range(4):
                ps, pk_ = psn()
                for fc in range(4):
                    P.op("tensor", lambda e, c=c, fc=fc, ps=ps: e.transpose(out=ps[0:64, fc * 128:(fc + 1) * 128],
                                                                          in_=psh[:, 8 + fc, c * 64:(c + 1) * 64], identity=ident[:]),
                         reads=VK + ["ident"], writes=[pk_])
                P.op("scalar", lambda e, c=c, ps=ps: e.copy(out=Vt[c][0:64, :], in_=ps[0:64, :]), reads=[pk_], writes=[("rw_Vt", c)])
            for d in range(2):
                P.op("vector", lambda e, d=d: e.tensor_tensor_scan(out=Lp[:].rearrange("p a b -> p (a b)"),
                                                                  data0=rmask[:].rearrange("p a b -> p (a b)"),
                                                                  data1=sgw[d][:].rearrange("p a b -> p (a b)"), initial=0.0,
                                                                  op0=ALU.mult, op1=ALU.add),
                     reads=[("rw_sa", 0, d), "rw_rmask"], writes=["rw_Lp"])
                Lp3 = Lp[:].rearrange("p a (c t) -> p (a c) t", t=64)
                Li3 = Li[:].rearrange("p a (c t) -> p (a c) t", t=64); Le3 = Le[:].rearrange("p a (c t) -> p (a c) t", t=64)
                tot_b = Lp3[:, :, 63:64].to_broadcast([128, 16, 64])
                if d == 0:
                    P.op("gpsimd", lambda e: e.tensor_copy(out=Li[:], in_=Lp[:]), reads=["rw_Lp"], writes=["rw_Li"])
                    P.op("vector", lambda e, d=d: e.tensor_tensor(out=Le[:], in0=Lp[:], in1=sgw[d][:], op=ALU.subtract),
                         reads=["rw_Lp", ("rw_sa", 0, d)], writes=["rw_Le"])
                else:
                    P.op("vector", lambda e, d=d: e.tensor_tensor(out=Le[:], in0=Lp[:], in1=sgw[d][:], op=ALU.subtract),
                         reads=["rw_Lp", ("rw_sa", 0, d)], writes=["rw_Le"])
                    P.op("vector", lambda e: e.tensor_tensor(out=Li3, in0=tot_b, in1=Le3, op=ALU.subtract), reads=["rw_Lp", "rw_Le"],
                         writes=["rw_Li"])
                    P.op("vector", lambda e: e.tensor_tensor(out=Le3, in0=tot_b, in1=Lp3, op=ALU.subtract), reads=["rw_Lp", "rw_Li"],
                         writes=["rw_Le"])
                P.op("scalar", lambda e: e.activation(out=etot[:].rearrange("p a c -> p (a c)"), in_=Lp3[:, :, 63], func=AF.Exp, scale=-CDEC),
                     reads=["rw_Lp"], writes=["rw_etot"])
                P.op("scalar", lambda e: e.activation(out=tA[:], in_=Li[:], func=AF.Exp, scale=-CDEC), reads=["rw_Li"], writes=TA)
                P.op("vector", lambda e: e.tensor_tensor(out=rt[:], in0=r_, in1=tA[:], op=ALU.mult), reads=RK + TA, writes=["rw_rt"])
                P.op("scalar", lambda e: e.activation(out=tB[:], in_=Le[:], func=AF.Exp, scale=-CDEC), reads=["rw_Le"], writes=TB)
                P.op("vector", lambda e: e.tensor_tensor(out=at[:], in0=kk[:], in1=tB[:], op=ALU.mult), reads=KKN + TB, writes=["rw_at"])
                P.op("scalar", lambda e: e.activation(out=tA[:], in_=Li[:], func=AF.Exp, scale=CDEC), reads=["rw_Li"], writes=TA)
                P.op("vector", lambda e, d=d: e.tensor_tensor(out=kt[:], in0=kd[d][:], in1=tA[:], op=ALU.mult), reads=[("rw_kd", d)] + TA, writes=["rw_kt"])
                P.op("vector", lambda e, d=d: e.tensor_tensor(out=bt[:], in0=aa[d][:], in1=tA[:], op=ALU.mult), reads=[("rw_sa", 1, d)] + TA, writes=["rw_bt"])
                et_b = etot[:].bitcast(F32).rearrange("p a c -> p (a c)").unsqueeze(2).to_broadcast([128, 16, 64])
                P.op("vector", lambda e: e.tensor_tensor(out=Kh[:].rearrange("p a (c t) -> p (a c) t", t=64),
                                                         in0=kt[:].bitcast(F32).rearrange("p a (c t) -> p (a c) t", t=64), in1=et_b, op=ALU.mult),
                     reads=["rw_kt", "rw_etot"], writes=["rw_Kh"])
                P.op("gpsimd", lambda e: e.tensor_tensor(out=Bh[:].rearrange("p a (c t) -> p (a c) t", t=64),
                                                         in0=bt[:].bitcast(F32).rearrange("p a (c t) -> p (a c) t", t=64), in1=et_b, op=ALU.mult),
                     reads=["rw_bt", "rw_etot"], writes=["rw_Bh"])
                for xi, (nm, X) in enumerate((("at", at), ("kt", kt), ("bt", bt))):
                    for eo in range(2):
                        if (xi + eo) % 2 == 0:
                            P.op("scalar", lambda e, nm=nm, X=X, eo=eo: e.activation(
                                out=mk[nm][eo][:], in_=X[:].bitcast(F32), func=AF.Copy, scale=bones[:, eo * 64:eo * 64 + 1]),
                                reads=["rw_" + nm, "rw_par"], writes=[("rw_mk", nm, eo)])
                        else:
                            P.op("vector", lambda e, nm=nm, X=X, eo=eo: e.tensor_scalar(
                                out=mk[nm][eo][:], in0=X[:].bitcast(F32), scalar1=bones[:, eo * 64:eo * 64 + 1], scalar2=None, op0=ALU.mult),
                                reads=["rw_" + nm, "rw_par"], writes=[("rw_mk", nm, eo)])
                cg0 = gx * 4
                mS = 0 if d == 0 else 2
                mST = 2 if d == 0 else 0
                mI = 1 if d == 0 else 3
                for c in range(4 if RW_CH else 0):
                    sl = slice(c * 64, (c + 1) * 64)
                    cg = cg0 + c
                    sm = summ[nsum % 2]; smk = ("rw_summ", nsum % 2)
                    nsum += 1
                    hd = lambda X, h, sl=sl: X[:, h // 2, sl]
                    hdL = lambda nm, h, sl=sl: mk[nm][h % 2][:, h // 2, sl]
                    for src, skey, dst, dkey in ((at[:].bitcast(F32), "rw_at", None, "rw_Z"), (Kh[:], "rw_Kh", KhT, "rw_KhT"),
                                                 (Bh[:], "rw_Bh", BhT, "rw_BhT")):
                        ps, pk_ = psn()
                        for fc in range(4):
                            P.op("tensor", lambda e, fc=fc, ps=ps, src=src, sl=sl: e.transpose(out=ps[0:64, fc * 128:(fc + 1) * 128],
                                                                                            in_=src[:, fc, sl], identity=ident[:]),
                                 reads=[skey, "ident"], writes=[pk_])
                        if dst is None:
                            P.op("scalar", lambda e, ps=ps: e.copy(out=Z[0:64, :, 0:64], in_=ps[0:64, :].rearrange("p (h k) -> p h k", h=8)),
                                 reads=[pk_], writes=["rw_Z"])
                        else:
                            P.op("scalar", lambda e, ps=ps, dst=dst: e.copy(out=dst[0:64, :], in_=ps[0:64, :]), reads=[pk_], writes=[dkey])
                    if RW_ST <= 1:
                        continue
                    specs = (("bt", at, "MKbt", "rw_at", Nn[0], ("rw_N", 0), mS), ("at", bt, "MKat", "rw_bt", NTr[0], ("rw_NT", 0), mST),
                             ("kt", at, "MKkt", "rw_at", Mka, "rw_Mka", mS), ("kt", rt, "MKkt", "rw_rt", Mkr, "rw_Mkr", mI),
                             ("bt", rt, "MKbt", "rw_rt", Mbr, "rw_Mbr", mI))
                    for si, (L_, R_, lk, rk_, dst, dkey, mi) in enumerate(specs):
                        ps, pk_ = psn()
                        for h in range(8):
                            P.op("tensor", lambda e, h=h, ps=ps, la=hdL(L_, h), ra=hd(R_, h): e.matmul(ps[0:64, h * 64:(h + 1) * 64], lhsT=la, rhs=ra,
                                                                                                     start=True, stop=True), reads=[("rw_mk", lk[2:], 0), ("rw_mk", lk[2:], 1), rk_], writes=[pk_])
                        P.op("vector" if si % 2 == 0 else "gpsimd" if False else "vector", lambda e, ps=ps, dst=dst, mi=mi: e.tensor_tensor(
                            out=dst[0:64], in0=ps[0:64, :].rearrange("p (h i) -> p h i", h=8), in1=mask[:, mi:mi + 1, :].to_broadcast([64, 8, 64]),
                            op=ALU.mult), reads=[pk_, "rw_par"], writes=[dkey])
                    if RW_ST <= 2:
                        continue
                    ps, pk_ = psn()
                    for h in range(8):
                        P.op("tensor", lambda e, h=h, ps=ps, c=c: e.matmul(ps[0:64, h * 64:(h + 1) * 64], lhsT=Mka[:, h, :],
                                                                          rhs=Vt[c][:, h * 64:(h + 1) * 64], start=True, stop=True),
                             reads=["rw_Mka", ("rw_Vt", c)], writes=[pk_])
                    P.op("scalar", lambda e, ps=ps: e.copy(out=Z[0:64, :, 64:128], in_=ps[0:64, :].rearrange("p (h k) -> p h k", h=8)),
                         reads=[pk_], writes=["rw_Z2"])
                    if RW_ST <= 3:
                        continue
                    cur = 0
                    for rnd in range(RW_RND):
                        N_, NT_ = Nn[cur], NTr[cur]
                        nk, ntk = ("rw_N", cur), ("rw_NT", cur)
                        pz = [psn(), psn()]
                        for h in range(8):
                            ps, pk_ = pz[h // 4]
                            P.op("tensor", lambda e, h=h, ps=ps, N_=N_: e.matmul(ps[0:64, (h % 4) * 128:(h % 4 + 1) * 128], lhsT=N_[:, h, :],
                                                                                rhs=Z[:, h, :], start=True, stop=True),
                                 reads=[nk, "rw_Z", "rw_Z2"], writes=[pk_])
                        for half in range(2):
                            ps, pk_ = pz[half]
                            P.op("vector", lambda e, half=half, ps=ps, rnd=rnd: e.tensor_tensor(
                                out=Z[0:64, half * 4:(half + 1) * 4, :], in0=Z[0:64, half * 4:(half + 1) * 4, :].bitcast(F32),
                                in1=ps[0:64, :].rearrange("p (h k) -> p h k", h=4), op=(ALU.subtract if rnd == 0 else ALU.add)),
                                reads=[pk_, "rw_Z", "rw_Z2"], writes=["rw_Z", "rw_Z2"])
                        if rnd < 5:
                            nxt = 1 - cur
                            ps, pk_ = psn()
                            for h in range(8):
                                P.op("tensor", lambda e, h=h, ps=ps, N_=N_, NT_=NT_: e.matmul(ps[0:64, h * 64:(h + 1) * 64], lhsT=NT_[:, h, :],
                                                                                             rhs=N_[:, h, :], start=True, stop=True),
                                     reads=[nk, ntk], writes=[pk_])
                            P.op("scalar", lambda e, ps=ps, nxt=nxt: e.copy(out=Nn[nxt][0:64], in_=ps[0:64, :].rearrange("p (h k) -> p h k", h=8)),
                                 reads=[pk_], writes=[("rw_N", nxt)])
                            if rnd < 4:
                                ps, pk_ = psn()
                                for h in range(8):
                                    P.op("tensor", lambda e, h=h, ps=ps, N_=N_, NT_=NT_: e.matmul(ps[0:64, h * 64:(h + 1) * 64], lhsT=N_[:, h, :],
                                                                                                 rhs=NT_[:, h, :], start=True, stop=True),
                                         reads=[nk, ntk], writes=[pk_])
                                P.op("gpsimd" if False else "scalar", lambda e, ps=ps, nxt=nxt: e.copy(
                                    out=NTr[nxt][0:64], in_=ps[0:64, :].rearrange("p (h k) -> p h k", h=8)), reads=[pk_], writes=[("rw_NT", nxt)])
                            cur = nxt
                    P.op("scalar", lambda e: e.activation(out=Zn[0:64], in_=Z[0:64].bitcast(F32), func=AF.Copy, scale=-1.0),
                         reads=["rw_Z", "rw_Z2"], writes=["rw_Zn"])
                    if RW_ST <= 4:
                        continue
                    ps, pk_ = psn()
                    for h in range(8):
                        P.op("tensor", lambda e, h=h, ps=ps: e.matmul(ps[0:64, h * 64:(h + 1) * 64], lhsT=Zn[:, h, 0:64],
                                                                     rhs=BhT[:, h * 64:(h + 1) * 64], start=True, stop=True),
                             reads=["rw_Zn", "rw_BhT"], writes=[pk_])
                    P.op("scalar", lambda e, ps=ps, sm=sm: e.copy(out=sm[:, 0:512], in_=ps[0:64, :]), reads=[pk_], writes=[(smk, 0)])
                    if RW_ST <= 5:
                        continue
                    ps, pk_ = psn()
                    for h in range(8):
                        P.op("tensor", lambda e, h=h, ps=ps, c=c: e.matmul(ps[0:64, h * 64:(h + 1) * 64], lhsT=KhT[:, h * 64:(h + 1) * 64],
                                                                          rhs=Vt[c][:, h * 64:(h + 1) * 64], start=True, stop=False),
                             reads=["rw_KhT", ("rw_Vt", c)], writes=[pk_])
                        P.op("tensor", lambda e, h=h, ps=ps: e.matmul(ps[0:64, h * 64:(h + 1) * 64], lhsT=BhT[:, h * 64:(h + 1) * 64],
                                                                     rhs=Zn[:, h, 64:128], start=False, stop=True),
                             reads=["rw_BhT", "rw_Zn"], writes=[pk_])
                    P.op("vector", lambda e, ps=ps, sm=sm: e.tensor_copy(out=sm[:, 512:1024], in_=ps[0:64, :]), reads=[pk_], writes=[(smk, 1)])
                    if RW_ST <= 6:
                        continue
                    ps, pk_ = psn()
                    for h in range(8):
                        p0 = (h % 2) * 64
                        P.op("tensor", lambda e, h=h, ps=ps, p0=p0, ra=hd(rt, h): e.matmul(ps[0:64, h * 64:(h + 1) * 64],
                                                                                          lhsT=identr[:, p0:p0 + 64], rhs=ra,
                                                                                          start=True, stop=False),
                             reads=["rw_identr", "rw_rt"], writes=[pk_])
                        P.op("tensor", lambda e, h=h, ps=ps: e.matmul(ps[0:64, h * 64:(h + 1) * 64], lhsT=Zn[:, h, 0:64], rhs=Mbr[:, h, :],
                                                                     start=False, stop=True), reads=["rw_Zn", "rw_Mbr"], writes=[pk_])
                    P.op("scalar", lambda e, ps=ps, sm=sm: e.copy(out=sm[:, 1024:1536], in_=ps[0:64, :]), reads=[pk_], writes=[(smk, 2)])
                    if RW_ST <= 7:
                        continue
                    ps, pk_ = psn()
                    for h in range(8):
                        P.op("tensor", lambda e, h=h, ps=ps, c=c: e.matmul(ps[0:64, h * 64:(h + 1) * 64], lhsT=Mkr[:, h, :],
                                                                          rhs=Vt[c][:, h * 64:(h + 1) * 64], start=True, stop=False),
                             reads=["rw_Mkr", ("rw_Vt", c)], writes=[pk_])
                        P.op("tensor", lambda e, h=h, ps=ps: e.matmul(ps[0:64, h * 64:(h + 1) * 64], lhsT=Mbr[:, h, :], rhs=Zn[:, h, 64:128],
                                                                     start=False, stop=True), reads=["rw_Mbr", "rw_Zn"], writes=[pk_])
                    P.op("vector", lambda e, ps=ps, sm=sm: e.tensor_copy(out=sm[:, 1536:2048], in_=ps[0:64, :]), reads=[pk_], writes=[(smk, 3)])
                    if RW_ST <= 8:
                        continue
                    ps, pk_ = psn()
                    for h in range(8):
                        p0 = (h % 2) * 64
                        P.op("tensor", lambda e, h=h, ps=ps, p0=p0, c=c: e.matmul(ps[0:64, h:h + 1], lhsT=ident[:, p0:p0 + 64],
                                                                                 rhs=etot[:].bitcast(F32)[:, h // 2, c:c + 1], start=True, stop=True),
                             reads=["ident", "rw_etot"], writes=[pk_])
                    P.op("vector", lambda e, ps=ps, sm=sm: e.tensor_copy(out=sm[:, 2048:2056], in_=ps[0:64, 0:8]), reads=[pk_], writes=[(smk, 4)])
                    P.dma("gpsimd", summ_d[d, cg], sm[:], reads=[(smk, i) for i in range(5)], writes=[("rw_summd", d, cg)])
        if "rwprep" in dbg and l == 0:
            for nm, src in (("g", g_d), ("bv", bv_d)):
                d_ = dbg_out("rw_" + nm, [512, T])
                P.dma("sync", d_, src, reads=[("rw_" + nm, i) for i in range(9)], writes=["OUT_dbgrw" + nm])
        P.barrier(); P.release(m1)
        if RW_PH < 2:
            return
        ST = [P.sbuf("rw_ST%d" % i, [64, 8, 64]) for i in range(2)]
        sm2 = [P.sbuf("rw_sm2_%d" % i, [64, 2056]) for i in range(3)]
        yt = [P.sbuf("rw_yt%d" % i, [64, 512]) for i in range(2)]
        stt = P.sbuf("rw_stt", [64, 8, 64])
        step = 0
        for d in range(2):
            order = list(range(NCH)) if d == 0 else [3, 2, 1, 0] + list(range(NCH - 1, 3, -1))
            cur = 0
            P.op("vector", lambda e: e.memset(ST[0][:], 0.0), reads=[("rw_ST", 0)], writes=[("rw_ST", 0)])
            for cg in order:
                b3 = step % 3
                b2 = step % 2
                step += 1
                P.dma("sync", sm2[b3][:], summ_d[d, cg], reads=[], writes=[("rw_sm2", b3)])
                S_, Sn_ = ST[cur], ST[1 - cur]
                psy, pyk = psn(); pss, psk = psn()
                for h in range(8):
                    P.op("tensor", lambda e, h=h, psy=psy, S_=S_, b3=b3: e.matmul(psy[0:64, h * 64:(h + 1) * 64],
                                                                               lhsT=sm2[b3][:, 1024 + h * 64:1024 + (h + 1) * 64], rhs=S_[:, h, :],
                                                                               start=True, stop=True),
                         reads=[("rw_sm2", b3), ("rw_ST", cur)], writes=[pyk])
                for h in range(8):
                    P.op("tensor", lambda e, h=h, pss=pss, S_=S_, b3=b3: e.matmul(pss[0:64, h * 64:(h + 1) * 64],
                                                                               lhsT=sm2[b3][:, h * 64:(h + 1) * 64], rhs=S_[:, h, :],
                                                                               start=True, stop=True),
                         reads=[("rw_sm2", b3), ("rw_ST", cur)], writes=[psk])
                P.op("gpsimd", lambda e, S_=S_, b3=b3: e.tensor_tensor(out=stt[:], in0=S_[:], in1=sm2[b3][:, 2048:2056].unsqueeze(2).to_broadcast([64, 8, 64]),
                                                                      op=ALU.mult), reads=[("rw_ST", cur), ("rw_sm2", b3)], writes=["rw_stt"])
                P.op("gpsimd", lambda e, b3=b3: e.tensor_tensor(out=stt[:], in0=stt[:], in1=sm2[b3][:, 512:1024].rearrange("p (h k) -> p h k", h=8),
                                                               op=ALU.add), reads=["rw_stt", ("rw_sm2", b3)], writes=["rw_stt"])
                P.op("vector", lambda e, pss=pss, Sn_=Sn_: e.tensor_tensor(out=Sn_[:], in0=pss[0:64, :].rearrange("p (h k) -> p h k", h=8),
                                                                          in1=stt[:], op=ALU.add), reads=[psk, "rw_stt"], writes=[("rw_ST", 1 - cur)])
                P.op("vector", lambda e, psy=psy, b2=b2, b3=b3: e.tensor_tensor(out=yt[b2][:], in0=psy[0:64, :], in1=sm2[b3][:, 1536:2048], op=ALU.add),
                     reads=[pyk, ("rw_sm2", b3)], writes=[("rw_yt", b2)])
                P.dma("gpsimd", y_d[d, cg * 64:(cg + 1) * 64, :], yt[b2][:], reads=[("rw_yt", b2)], writes=[("rw_yd", d, cg // 2)])
                cur = 1 - cur
        if "rwy" in dbg and l == 0:
            d_ = dbg_out("rw_y", [2, T, 512])
            P.dma("sync", d_, y_d, reads=[("rw_yd", d, t) for d in range(2) for t in range(NT)], writes=["OUT_dbgrwy"])
        P.barrier(); P.release(m1)
        if RW_PH < 3:
            return
        y0 = [P.sbuf("rw_y0_%d" % i, [128, 512]) for i in range(2)]; y1 = [P.sbuf("rw_y1_%d" % i, [128, 512]) for i in range(2)]
        sqt = [P.sbuf("rw_sq%d" % i, [128, 512]) for i in range(2)]
        st8 = [P.sbuf("rw_st8_%d" % i, [128, 4, 8]) for i in range(2)]
        bvt = [P.sbuf("rw_bvt%d" % i, [128, 4, 128]) for i in range(2)]; gt = [P.sbuf("rw_gt%d" % i, [128, 4, 128]) for i in range(2)]
        of = [P.sbuf("rw_of%d" % i, [128, 4, 128]) for i in range(2)]; ob = [P.sbuf("rw_ob%d" % i, [128, 4, 128], BF16) for i in range(2)]
        orw_v = orw_d.rearrange("(k p) t -> p k t", p=128)
        for t in range(NT):
            s_ = t % 2
            gi = 0 if t < 2 else 1 + (t - 2) // 4
            P.dma("sync", y0[s_][:], y_d[0, t * 128:(t + 1) * 128, :], reads=[("rw_yd", 0, t)], writes=[("rw_y0", s_)])
            P.dma("sync", y1[s_][:], y_d[1, t * 128:(t + 1) * 128, :], reads=[("rw_yd", 1, t)], writes=[("rw_y1", s_)])
            P.dma("sync", bvt[s_][:], bv_v[:, :, t * 128:(t + 1) * 128], reads=[("rw_bv", i) for i in range(9)], writes=[("rw_bvt", s_)])
            P.dma("sync", gt[s_][:], g_v[:, :, t * 128:(t + 1) * 128], reads=[("rw_g", i) for i in range(9)], writes=[("rw_gt", s_)])
            ys = y0[s_]; st = st8[s_]
            P.op("gpsimd", lambda e, s_=s_: e.tensor_tensor(out=y0[s_][:], in0=y0[s_][:], in1=y1[s_][:], op=ALU.add),
                 reads=[("rw_y0", s_), ("rw_y1", s_)], writes=[("rw_y0", s_)])
            y3 = ys[:].rearrange("p (h n) -> p h n", h=8)
            P.op("vector", lambda e, st=st, y3=y3: e.tensor_reduce(out=st[:, 0, :], in_=y3, axis=AX.X, op=ALU.add), reads=[("rw_y0", s_)],
                 writes=[("rw_st8", s_)])
            P.op("scalar", lambda e, s_=s_, ys=ys: e.activation(out=sqt[s_][:], in_=ys[:], func=AF.Square), reads=[("rw_y0", s_)],
                 writes=[("rw_sq", s_)])
            P.op("vector", lambda e, st=st, s_=s_: e.tensor_reduce(out=st[:, 1, :], in_=sqt[s_][:].rearrange("p (h n) -> p h n", h=8), axis=AX.X,
                                                                  op=ALU.add), reads=[("rw_sq", s_), ("rw_st8", s_)], writes=[("rw_st8", s_)])
            P.op("vector", lambda e, st=st: e.tensor_scalar(out=st[:, 0, :], in0=st[:, 0, :], scalar1=1.0 / 64, scalar2=None, op0=ALU.mult),
                 reads=[("rw_st8", s_)], writes=[("rw_st8", s_)])
            P.op("vector", lambda e, st=st: e.tensor_tensor(out=st[:, 2, :], in0=st[:, 0, :], in1=st[:, 0, :], op=ALU.mult),
                 reads=[("rw_st8", s_)], writes=[("rw_st8", s_)])
            P.op("vector", lambda e, st=st: e.scalar_tensor_tensor(out=st[:, 2, :], in0=st[:, 1, :], scalar=1.0 / 64, in1=st[:, 2, :],
                                                                  op0=ALU.mult, op1=ALU.subtract), reads=[("rw_st8", s_)], writes=[("rw_st8", s_)])
            P.op("scalar", lambda e, st=st: e.activation(out=st[:, 3, :], in_=st[:, 2, :], func=AF.Sqrt, bias=eps_gn[:, 0:1], scale=1.0),
                 reads=[("rw_st8", s_), "rw_eps"], writes=[("rw_st8", s_)])
            P.op("vector", lambda e, st=st: e.reciprocal(out=st[:, 3, :], in_=st[:, 3, :]), reads=[("rw_st8", s_)], writes=[("rw_st8", s_)])
            P.op("vector", lambda e, st=st, y3=y3: e.tensor_tensor(out=y3, in0=y3, in1=st[:, 0, :].unsqueeze(2).to_broadcast([128, 8, 64]),
                                                                  op=ALU.subtract), reads=[("rw_y0", s_), ("rw_st8", s_), ("rw_sq", s_)],
                 writes=[("rw_y0", s_)])
            P.op("vector", lambda e, st=st, y3=y3: e.tensor_tensor(out=y3, in0=y3, in1=st[:, 3, :].unsqueeze(2).to_broadcast([128, 8, 64]),
                                                                  op=ALU.mult), reads=[("rw_y0", s_), ("rw_st8", s_)], writes=[("rw_y0", s_)])
            ps, pk_ = psn()
            for fc in range(4):
                P.op("tensor", lambda e, fc=fc, ps=ps, ys=ys: e.transpose(out=ps[:, fc * 128:(fc + 1) * 128], in_=ys[:, fc * 128:(fc + 1) * 128],
                                                                        identity=ident[:]), reads=[("rw_y0", s_), "ident"], writes=[pk_])
            for fc in range(4):
                P.op("vector", lambda e, fc=fc, ps=ps, s_=s_: e.tensor_scalar(out=of[s_][:, fc, :], in0=ps[:, fc * 128:(fc + 1) * 128],
                                                                            scalar1=vec[:, 3, fc:fc + 1], scalar2=vec[:, 4, fc:fc + 1],
                                                                            op0=ALU.mult, op1=ALU.add),
                     reads=[pk_, "rw_par"], writes=[("rw_of", s_, fc)])
            P.op("gpsimd", lambda e, s_=s_: e.tensor_tensor(out=of[s_][:], in0=of[s_][:], in1=bvt[s_][:], op=ALU.add),
                 reads=[("rw_of", s_, fc) for fc in range(4)] + [("rw_bvt", s_)], writes=[("rw_of2", s_)])
            P.op("gpsimd", lambda e, s_=s_: e.tensor_tensor(out=ob[s_][:], in0=of[s_][:], in1=gt[s_][:], op=ALU.mult),
                 reads=[("rw_of2", s_), ("rw_gt", s_)], writes=[("rw_ob", s_)])
            P.dma("gpsimd", orw_v[:, :, t * 128:(t + 1) * 128], ob[s_][:], reads=[("rw_ob", s_)], writes=[("orw", gi)])
        if "orw" in dbg and l == 0:
            d_ = dbg_out("orw", [512, T], BF16)
            P.dma("sync", d_, orw_d, reads=[("orw", g) for g in range(9)], writes=["OUT_dbgorw"])

    xT_v = xT.rearrange("(k p) t -> p k t", p=128)
    def stage_inproj(l):
        hT = P.sbuf("hT", [128, 8, T], BF16)
        xg = [P.sbuf("xg%d" % i, [128, 8, 512]) for i in range(2)]
        wblk = [P.sbuf("wblk%d" % i, [128, 8, 512]) for i in range(2)]
        wbf = [P.sbuf("wbf%d" % i, [128, 8, 512], BF16) for i in range(2)]
        ost = [P.sbuf("ost%d" % i, [128, 512]) for i in range(4)]
        ostb = [P.sbuf("ostb%d" % i, [128, 512], BF16) for i in range(4)]
        for gi, (t0, n) in enumerate(GROUPS):
            s = gi % 2
            c = 1 if gi == 0 else 0
            P.dma("sync", xg[s][:, :, :n], xT_v[:, :, t0:t0 + n], reads=[("xT", gi)], writes=[("xg", s)])
            for k in range(8):
                eng = "vector" if k % 2 == 0 else "gpsimd"
                P.op(eng, lambda e, s=s, k=k, c=c, l=l, t0=t0, n=n: e.tensor_scalar(
                    out=hT[:, k, t0:t0 + n], in0=xg[s][:, k, :n], scalar1=mod1[:, l, 8 + k, c:c + 1],
                    scalar2=mod[:, l, k, c:c + 1], op0=ALU.mult, op1=ALU.add),
                    reads=[("xg", s), "mod", "mod1"], writes=[("hT", gi)])
        if "hT" in dbg and l == 0:
            d = dbg_out("hT", [128, 8 * T], BF16)
            P.dma("sync", d, hT[:].rearrange("p k t -> p (k t)"), reads=[("hT", g) for g in range(9)], writes=["OUT_dbghT"])
        blocks = [(3072, 512, "fm_bf", qT, 0), (3584, 512, "fm_bf", kT, 0), (4096, 512, "tm_bf", vtok, 0),
                  (4608, 512, "fm", prw, 0), (5120, 512, "fm", prw, 512), (5632, 512, "fm", prw, 1024),
                  (6144, 384, "fm", prw, 1536), (6528, 512, "fm", sguU, 0), (7040, 512, "tm", sguV, 0)]
        nev = 0
        for bi, (c0, ncol, kind, dst, r0) in enumerate(blocks):
            s = bi % 2
            P.dma("sync", wblk[s][:, :, :ncol], w_in[l, :, c0:c0 + ncol].rearrange("(k p) m -> p k m", p=128),
                  writes=[("wblk", s)])
            for k in range(8):
                eng = "gpsimd" if k % 2 == 0 else "vector"
                P.op(eng, lambda e, s=s, k=k, ncol=ncol: e.tensor_copy(out=wbf[s][:, k, :ncol], in_=wblk[s][:, k, :ncol]),
                     reads=[("wblk", s)], writes=[("wbf", s)])
            if kind.startswith("fm"):
                for gi, (t0, n) in enumerate(GROUPS):
                    for mi in range(ncol // 128):
                        pb = 2 + nev % 4
                        for k in range(8):
                            P.op("tensor", lambda e, s=s, k=k, mi=mi, t0=t0, n=n, pb=pb: e.matmul(
                                PS[pb][:, :n], lhsT=wbf[s][:, k, mi * 128:(mi + 1) * 128], rhs=hT[:, k, t0:t0 + n],
                                start=(k == 0), stop=(k == 7)), reads=[("wbf", s), ("hT", gi)], writes=[("ps", pb)])
                        so = nev % 4
                        o_t = ostb[so] if kind == "fm_bf" else ost[so]
                        okey = ("ostb", so) if kind == "fm_bf" else ("ost", so)
                        if nev % 2 == 0:
                            P.op("scalar", lambda e, o_t=o_t, pb=pb, n=n: e.copy(out=o_t[:, :n], in_=PS[pb][:, :n]),
                                 reads=[("ps", pb)], writes=[okey])
                        else:
                            P.op("vector", lambda e, o_t=o_t, pb=pb, n=n: e.tensor_copy(out=o_t[:, :n], in_=PS[pb][:, :n]),
                                 reads=[("ps", pb)], writes=[okey])
                        rr = r0 + mi * 128
                        P.dma("gpsimd", dst[rr:rr + 128, t0:t0 + n], o_t[:, :n], reads=[okey], writes=[(dst.name, "fm", gi)])
                        nev += 1
            else:
                for t in range(NT):
                    pb = 2 + nev % 4
                    gi = 0 if t < 2 else 1 + (t - 2) // 4
                    for k in range(8):
                        P.op("tensor", lambda e, s=s, k=k, t=t, pb=pb: e.matmul(
                            PS[pb][:, :], lhsT=hT[:, k, t * 128:(t + 1) * 128], rhs=wbf[s][:, k, :],
                            start=(k == 0), stop=(k == 7)), reads=[("wbf", s), ("hT", gi)], writes=[("ps", pb)])
                    so = nev % 4
                    o_t = ostb[so] if kind == "tm_bf" else ost[so]
                    okey = ("ostb", so) if kind == "tm_bf" else ("ost", so)
                    if nev % 2 == 0:
                        P.op("scalar", lambda e, o_t=o_t, pb=pb: e.copy(out=o_t[:], in_=PS[pb][:]), reads=[("ps", pb)], writes=[okey])
                    else:
                        P.op("vector", lambda e, o_t=o_t, pb=pb: e.tensor_copy(out=o_t[:], in_=PS[pb][:]), reads=[("ps", pb)], writes=[okey])
                    P.dma("gpsimd", dst[t * 128:(t + 1) * 128, :], o_t[:], reads=[okey], writes=[(dst.name, "tm", t)])
                    nev += 1
        if "p" in dbg and l == 0:
            for nm, src, shp, dt in (("qT", qT, [512, T], BF16), ("kT", kT, [512, T], BF16), ("vtok", vtok, [T, 512], BF16),
                                     ("prw", prw, [1920, T], F32), ("sguU", sguU, [512, T], F32), ("sguV", sguV, [T, 512], F32)):
                d = dbg_out(nm, shp, dt)
                rk = [(src.name, "fm", g) for g in range(9)] + [(src.name, "tm", t) for t in range(NT)]
                P.dma("sync", d, src, reads=rk, writes=["OUT_dbg" + nm])
    for l in range(n_layers):
        stage_inproj(l)
        if stop == "s1":
            return P, dbg_t
        P.barrier(); P.release(m0)
        stage_sgu(l)
        if stop == "s2":
            return P, dbg_t
        P.barrier(); P.release(m0)
        stage_na(l)
        if stop == "s3":
            return P, dbg_t
        P.barrier(); P.release(m0)
        stage_rwkv(l)
        if stop == "s4":
            return P, dbg_t
        P.barrier(); P.release(m0)
        stage_merge(l)
        if stop == "s5":
            return P, dbg_t
        P.barrier(); P.release(m0)
        stage_ffn(l)
        if stop == "s6":
            return P, dbg_t
        P.barrier(new_epoch=True); P.release(m0)
    if mode == "full":
        stage_final()
    else:
        for gi, (t0, n) in enumerate(GROUPS):
            P.dma("sync", xT_out[:, t0:t0 + n], xT[:, t0:t0 + n], reads=[("xT", gi)], writes=["OUT_x%d" % gi])
    return P, dbg_t


def host_inputs(inputs, b, l0=0, nl=DEPTH, xT=None):
    m = {}
    if xT is None:
        x = np.asarray(inputs["x"], np.float32)
        ctx = np.asarray(inputs["ctx"], np.float32)
        m["xin"] = np.ascontiguousarray(np.concatenate([ctx[b], x[b]], axis=0))
    else:
        m["xT_in"] = xT
    m["ccT"] = np.ascontiguousarray(np.stack([np.asarray(inputs["c"], np.float32)[b], np.asarray(inputs["c_ctx"], np.float32)], axis=1))
    m["ident"] = np.eye(128, dtype=np.float32)
    f = lambda k: np.asarray(inputs[k], np.float32)[l0:l0 + nl]
    m["w_ada"] = f("w_ada")
    m["b_adaT"] = np.ascontiguousarray(f("b_ada").reshape(nl, 48, 128).transpose(0, 2, 1))
    m["w_in"] = f("w_in")
    m["sgu_lng"] = np.ascontiguousarray(np.broadcast_to(f("sgu_ln_g")[:, None, :], (nl, 128, 512)))
    m["sgu_lnb"] = np.ascontiguousarray(np.broadcast_to(f("sgu_ln_b")[:, None, :], (nl, 128, 512)))
    m["sgu_wT"] = np.ascontiguousarray(f("sgu_w").transpose(0, 3, 1, 2))
    m["sgu_bB"] = np.ascontiguousarray(np.broadcast_to(f("sgu_b")[:, None, :, :], (nl, 64, 8, 128)))
    m["na_bias"] = na_bias_layout(f("na_rpb"))
    fmN = lambda a, n: a.reshape(a.shape[0], n, 128).transpose(2, 0, 1)
    m["rw_mu"] = np.ascontiguousarray(np.stack([fmN(f("rwkv_mu_prev"), 15), fmN(f("rwkv_mu_next"), 15)], axis=2))
    w0 = f("rwkv_w0").reshape(nl, 2, 4, 128).transpose(3, 0, 1, 2); a0 = f("rwkv_a0").reshape(nl, 2, 4, 128).transpose(3, 0, 1, 2)
    m["rw_w0a0"] = np.ascontiguousarray(np.stack([w0, a0], axis=2))
    m["rw_w2"] = np.ascontiguousarray(f("rwkv_w2").reshape(nl, 128, 512)); m["rw_a2"] = np.ascontiguousarray(f("rwkv_a2").reshape(nl, 128, 512))
    m["rw_g2"] = f("rwkv_g2")
    m["rw_vec"] = np.ascontiguousarray(np.stack([fmN(f(k).reshape(nl, 512), 4) for k in
                                                 ("rwkv_k_k", "rwkv_k_a", "rwkv_r_k", "rwkv_gn_g", "rwkv_gn_b")], axis=2))
    jj, ii = np.meshgrid(np.arange(64), np.arange(64), indexing="ij")
    m["mask64"] = np.ascontiguousarray(np.stack([jj < ii, jj <= ii, jj > ii, jj >= ii], axis=1).astype(np.float32))
    bo = np.zeros((128, 128), np.float32); bo[:64, :64] = 1; bo[64:, 64:] = 1
    m["bones"] = bo
    m["w_branch"] = f("w_branch"); m["w_out"] = f("w_out"); m["ffn_w_gu"] = f("ffn_w_gu"); m["ffn_w_down"] = f("ffn_w_down")
    fm8 = lambda a: a.reshape(nl, 8, 128).transpose(2, 0, 1)
    m["lnp"] = np.ascontiguousarray(np.stack([fm8(f("ln1_g")), fm8(f("ln1_b")), fm8(f("ln2_g")), fm8(f("ln2_b"))], axis=2))
    return m


_NA_IDX = None


def na_bias_layout(rpb):
    global _NA_IDX
    if _NA_IDX is None:
        ridx = np.zeros((5, 128, 896), np.int64); cidx = np.zeros((5, 128, 896), np.int64); valid = np.zeros((5, 128, 896), bool)
        zero = np.zeros((5, 128, 896), bool)
        for pi, r in enumerate((0, 2, 4, 60, 62)):
            kb = min(max(r - 4, 0), 54)
            for qi in range(128):
                qr, c = r + qi // 64, qi % 64
                row0 = min(max(qr - 4, 0), 56); col0 = min(max(c - 8, 0), 48)
                for j in range(10):
                    kr = kb + j
                    if not (row0 <= kr < row0 + 8):
                        continue
                    for kc in range(col0, col0 + 16):
                        ridx[pi, qi, j * 64 + kc] = kr - qr + 7; cidx[pi, qi, j * 64 + kc] = kc - c + 15; valid[pi, qi, j * 64 + kc] = True
            zero[pi, :, 640:] = True
        _NA_IDX = (ridx, cidx, valid, zero)
    ridx, cidx, valid, zero = _NA_IDX
    g = rpb[:, :, ridx, cidx]
    g = np.where(valid[None, None], g, np.float32(-30000.0))
    g = np.where(zero[None, None], np.float32(0.0), g)
    return np.ascontiguousarray(g.transpose(0, 2, 3, 1, 4)).astype(np.float32)


_CACHE = {}


def _host_params(inputs, l0, nl):
    key = (id(inputs.get("w_in")), l0, nl)
    if key not in _CACHE:
        m = host_inputs(inputs, 0, l0, nl, xT=np.zeros((1,), np.float32))
        m.pop("xT_in"); m.pop("ccT")
        _CACHE[key] = m
    return _CACHE[key]


def kernel(**inputs):
    P, _ = build(n_layers=DEPTH, mode="full")
    nc = P.finalize()
    par = dict(host_inputs(inputs, 0))
    par.pop("xin"); par.pop("ccT")
    in_maps = []
    for core in range(8):
        m = dict(par)
        hb = host_inputs_x(inputs, core // 2)
        m.update(hb)
        in_maps.append(m)
    res = run_bass_kernel_spmd(nc, in_maps, core_ids=list(range(8)))
    return np.stack([res.results[2 * b]["out"] for b in range(4)], axis=0).astype(np.float32)


def host_inputs_x(inputs, b):
    x = np.asarray(inputs["x"], np.float32); ctx = np.asarray(inputs["ctx"], np.float32)
    return {"xin": np.ascontiguousarray(np.concatenate([ctx[b], x[b]], axis=0)),
            "ccT": np.ascontiguousarray(np.stack([np.asarray(inputs["c"], np.float32)[b], np.asarray(inputs["c_ctx"], np.float32)], axis=1))}
```

```python
import numpy as np
import concourse.bass as bass
import concourse.mybir as mybir
from concourse.bass_utils import run_bass_kernel_spmd

F32 = mybir.dt.float32
BF16 = mybir.dt.bfloat16
F32R = mybir.dt.float32r
ALU = mybir.AluOpType
AF = mybir.ActivationFunctionType
AX = mybir.AxisListType

D = 1024
DEPTH = 4
LCTX = 256
SEQ = 4096
T = LCTX + SEQ
NT = T // 128
GROUPS = [(0, 256)] + [(256 + 512 * i, 512) for i in range(8)]
D_IN = 7552
D_FF = 2816
ALPHA = (2 * DEPTH) ** 0.25


class Prog:
    ENGS = ("tensor", "vector", "scalar", "gpsimd", "sync")

    def __init__(self):
        self.nc = bass.Bass("TRN2", target_bir_lowering=False)
        self.ops = []
        self.n_dma_sems = 32
        arena = self.nc.alloc_sbuf_tensor("arena", [128, 212000], mybir.dt.uint8)
        self.arena_base = self.nc.lookup_mloc(arena).addr
        self.arena_size = 212000
        self.sp = 0
        self.nalloc = 0

    def dram(self, name, shape, dtype, kind="Internal"):
        return self.nc.dram_tensor(name, list(shape), dtype, kind=kind).ap()

    def sbuf(self, name, shape, dtype=F32):
        esz = {F32: 4, BF16: 2, F32R: 4}[dtype]
        nbytes = int(np.prod(shape[1:])) * esz
        off = (self.sp + 63) // 64 * 64
        assert off + nbytes <= self.arena_size, "SBUF arena overflow at %s: %d + %d" % (name, off, nbytes)
        self.sp = off + nbytes
        self.nalloc += 1
        return self.nc.alloc_sbuf_tensor_at("%s_%d" % (name, self.nalloc), list(shape), dtype, offset=self.arena_base + off)

    def mark(self):
        return self.sp

    def release(self, m):
        self.sp = m

    def barrier(self, new_epoch=False):
        self.ops.append(("barrier", new_epoch, (), (), False))

    def psum(self, name, shape, dtype=F32):
        return self.nc.alloc_psum_tensor(name, list(shape), dtype)

    def op(self, eng, fn, reads=(), writes=(), dma=False):
        self.ops.append((eng, fn, tuple(reads), tuple(writes), dma))

    def dma(self, eng, out, in_, reads=(), writes=(), slow=False):
        if slow:
            self.op(eng, lambda e: e.dma_start(out=out, in_=in_, allow_slow_non_contiguous=True), reads, writes, dma=True)
        else:
            self.op(eng, lambda e: e.dma_start(out=out, in_=in_), reads, writes, dma=True)

    def finalize(self):
        nc = self.nc
        ops = self.ops
        last_w = {}
        readers = {}
        deps = []
        force_sig = set()
        last_on = {}
        for i, (eng, fn, rd, wr, isdma) in enumerate(ops):
            if eng == "barrier":
                force_sig.update(last_on.values())
                last_w = {}
                readers = {}
                deps.append(set())
                continue
            if not isdma:
                last_on[eng] = i
            d = set()
            for k in rd:
                if k in last_w:
                    d.add(last_w[k])
            for k in wr:
                if k in last_w:
                    d.add(last_w[k])
                d.update(readers.get(k, {}).values())
            for k in rd:
                readers.setdefault(k, {})[(eng, i) if isdma else eng] = i
            for k in wr:
                last_w[k] = i
                readers[k] = {}
            d.discard(i)
            if eng == "tensor" and not isdma:
                d = {j for j in d if not (ops[j][0] == "tensor" and not ops[j][4])}
            deps.append(d)
        has_dep = [False] * len(ops)
        for d in deps:
            for j in d:
                has_dep[j] = True
        for j in force_sig:
            has_dep[j] = True
        sems = {e: nc.alloc_semaphore("s_" + e) for e in self.ENGS}
        dsems = [nc.alloc_semaphore("d%d" % j) for j in range(self.n_dma_sems)]
        cnt = {e: 0 for e in self.ENGS}
        sig = [None] * len(ops)
        waited = {e: {} for e in self.ENGS}
        ndma = 0
        final = {}
        dlast = {}
        for i, (e, fn, rd, wr, isdma) in enumerate(ops):
            if e == "barrier":
                for e1 in self.ENGS:
                    eng1 = getattr(nc, e1)
                    for e2 in self.ENGS:
                        if cnt[e2] > waited[e1].get(sems[e2].num, 0):
                            waited[e1][sems[e2].num] = cnt[e2]
                            eng1.wait_ge(sems[e2], cnt[e2])
                    for jn, (sd, vd) in dlast.items():
                        if vd > waited[e1].get(jn, 0):
                            waited[e1][jn] = vd
                            eng1.wait_ge(sd, vd)
                if fn:
                    nep = getattr(self, "_nep", 0) + 1
                    self._nep = nep
                    sems = {e_: nc.alloc_semaphore("s%d_%s" % (nep, e_)) for e_ in self.ENGS}
                    cnt = {e_: 0 for e_ in self.ENGS}
                continue
            eng = getattr(nc, e)
            need = {}
            for j in deps[i]:
                s, v = sig[j]
                if need.get(s.num, (None, 0))[1] < v:
                    need[s.num] = (s, v)
            if isdma:
                js = ndma % self.n_dma_sems
                v = 16 * (ndma // self.n_dma_sems + 1)
                if v > 16 and need.get(dsems[js].num, (None, 0))[1] < v - 16:
                    need[dsems[js].num] = (dsems[js], v - 16)
            for sn, (s, val) in need.items():
                if waited[e].get(sn, 0) >= val:
                    continue
                waited[e][sn] = val
                eng.wait_ge(s, val)
            ins = fn(eng)
            if isdma:
                ins.then_inc(dsems[js], 16)
                sig[i] = (dsems[js], v)
                dlast[dsems[js].num] = (dsems[js], v)
                ndma += 1
                if any(str(k).startswith("OUT") for k in wr):
                    if final.get(dsems[js].num, (None, 0))[1] < v:
                        final[dsems[js].num] = (dsems[js], v)
            elif has_dep[i]:
                cnt[e] += 1
                ins.then_inc(sems[e], 1)
                sig[i] = (sems[e], cnt[e])
            else:
                sig[i] = (sems[e], cnt[e])
        for sn, (s, v) in final.items():
            nc.sync.wait_ge(s, v)
        self.counts = dict(cnt, ndma=ndma, nops=len(ops))
        return nc


def build(n_layers=DEPTH, stop=None, dbg=(), mode="full"):
    NL = n_layers
    P = Prog()
    nc = P.nc
    if mode == "full":
        xin = P.dram("xin", [T, D], F32, "ExternalInput")
        out_d = P.dram("out", [SEQ, D], F32, "ExternalOutput")
    else:
        xT_in = P.dram("xT_in", [D, T], F32, "ExternalInput")
        xT_out = P.dram("xT_out", [D, T], F32, "ExternalOutput")
    ccT = P.dram("ccT", [D, 2], F32, "ExternalInput")
    ident_d = P.dram("ident", [128, 128], F32, "ExternalInput")
    w_ada = P.dram("w_ada", [NL, D, 6 * D], F32, "ExternalInput")
    b_adaT = P.dram("b_adaT", [NL, 128, 48], F32, "ExternalInput")
    w_in = P.dram("w_in", [NL, D, D_IN], F32, "ExternalInput")
    dbg_t = {}

    def dbg_out(name, shape, dtype=F32):
        dbg_t[name] = P.dram("dbg_" + name, shape, dtype, "ExternalOutput")
        return dbg_t[name]

    xT = P.dram("xT", [D, T], F32)
    qT = P.dram("qT", [512, T], BF16)
    kT = P.dram("kT", [512, T], BF16)
    vtok = P.dram("vtok", [T, 512], BF16)
    prw = P.dram("prw", [1920, T], F32)
    sguU = P.dram("sguU", [512, T], F32)
    sguV = P.dram("sguV", [T, 512], F32)

    ident = P.sbuf("ident_s", [128, 128])
    PS = [P.psum("ps%d" % i, [128, 512]) for i in range(6)]
    PSB = [P.psum("psb%d" % i, [128, 1024], BF16) for i in range(2)]
    mod = P.sbuf("mod", [128, NL, 48, 2])
    mod1 = P.sbuf("mod1", [128, NL, 48, 2])
    P.dma("sync", ident[:], ident_d, writes=["ident"])

    cc = P.sbuf("cc", [128, 8, 2])
    scc = P.sbuf("scc", [128, 8, 2])
    badaT = P.sbuf("badaT", [128, NL, 48])
    P.dma("sync", cc[:], ccT.rearrange("(k p) c -> p k c", p=128), writes=["cc"])
    P.dma("sync", badaT[:], b_adaT.rearrange("l p j -> p l j"), writes=["badaT"])
    P.op("scalar", lambda e: e.activation(out=scc[:], in_=cc[:], func=AF.Silu), reads=["cc"], writes=["scc"])
    identb = P.sbuf("identb", [128, 128], BF16)
    P.op("vector", lambda e: e.tensor_copy(out=identb[:], in_=ident[:]), reads=["ident"], writes=["identb"])
    lnp = P.sbuf("lnp", [128, NL, 4, 8])
    ones = P.sbuf("ones", [128, 128])
    eps5 = P.sbuf("eps5", [128, 1])
    P.op("vector", lambda e: e.memset(eps5[:], 1e-5), writes=["eps"])
    m0 = P.mark()
    wblk0 = [P.sbuf("wblk0%d" % i, [128, 8, 512]) for i in range(2)]
    nb = 0
    for l in range(n_layers):
        for cb in range(12):
            s = nb % 2
            P.dma("sync", wblk0[s][:], w_ada[l, :, cb * 512:(cb + 1) * 512].rearrange("(k p) m -> p k m", p=128),
                  writes=[("wblk0", s)])
            for mi in range(4):
                j = cb * 4 + mi
                for k in range(8):
                    P.op("tensor", lambda e, s=s, mi=mi, k=k, j=j: e.matmul(
                        PS[0][:, 2 * j:2 * j + 2], lhsT=wblk0[s][:, k, mi * 128:(mi + 1) * 128], rhs=scc[:, k, :],
                        start=(k == 0), stop=(k == 7)), reads=[("wblk0", s), "scc"], writes=[("ps", 0)])
            nb += 1
        for c in range(2):
            P.op("vector", lambda e, l=l, c=c: e.tensor_tensor(
                out=mod[:, l, :, c], in0=PS[0][:, 0:96].rearrange("p (j c) -> p j c", c=2)[:, :, c], in1=badaT[:, l, :], op=ALU.add),
                reads=[("ps", 0), "badaT"], writes=["mod"])
        P.op("vector", lambda e, l=l: e.tensor_scalar_add(out=mod1[:, l], in0=mod[:, l], scalar1=1.0), reads=["mod"], writes=["mod1"])

    xtile = [P.sbuf("xtile%d" % i, [128, D]) for i in range(2)]
    xTst = [P.sbuf("xTst%d" % i, [128, 8, 128]) for i in range(2)]
    if mode != "full":
        for gi, (t0, n) in enumerate(GROUPS):
            P.dma("sync", xT[:, t0:t0 + n], xT_in[:, t0:t0 + n], writes=[("xT", gi)])
    for t in range(NT if mode == "full" else 0):
        s = t % 2
        P.dma("sync", xtile[s][:], xin[t * 128:(t + 1) * 128, :], writes=[("xtile", s)])
        for half in range(2):
            pb = 1 + half
            for kk in range(4):
                k = half * 4 + kk
                P.op("tensor", lambda e, s=s, k=k, kk=kk, pb=pb: e.transpose(
                    out=PS[pb][:, kk * 128:(kk + 1) * 128], in_=xtile[s][:, k * 128:(k + 1) * 128], identity=ident[:]),
                    reads=[("xtile", s), "ident"], writes=[("ps", pb)])
            P.op("scalar" if half == 0 else "vector", (lambda e, s=s, half=half, pb=pb: e.copy(
                out=xTst[s][:, half * 4:(half + 1) * 4, :], in_=PS[pb][:].rearrange("p (k t) -> p k t", k=4)))
                if half == 0 else (lambda e, s=s, half=half, pb=pb: e.tensor_copy(
                    out=xTst[s][:, half * 4:(half + 1) * 4, :], in_=PS[pb][:].rearrange("p (k t) -> p k t", k=4))),
                reads=[("ps", pb)], writes=[("xTst", s, half)])
        P.dma("gpsimd", xT.rearrange("(k p) t -> p k t", p=128)[:, :, t * 128:(t + 1) * 128], xTst[s][:],
              reads=[("xTst", s, 0), ("xTst", s, 1)], writes=[("xT", t // 4)])

    if "mod" in dbg:
        d = dbg_out("mod", [128, NL * 48 * 2])
        P.dma("sync", d, mod[:].rearrange("p l j c -> p (l j c)"), reads=["mod"], writes=["OUT_dbgmod"])
    if "xT" in dbg:
        d = dbg_out("xT", [D, T])
        P.dma("sync", d, xT, reads=[("xT", g) for g in range(9)], writes=["OUT_dbgxT"])
    if stop == "s0":
        return P, dbg_t
    P.barrier()
    P.release(m0)


    sgu_lng = P.dram("sgu_lng", [NL, 128, 512], F32, "ExternalInput")
    sgu_lnb = P.dram("sgu_lnb", [NL, 128, 512], F32, "ExternalInput")
    sgu_wT = P.dram("sgu_wT", [NL, 128, 8, 128], F32, "ExternalInput")
    sgu_bB = P.dram("sgu_bB", [NL, 64, 8, 128], F32, "ExternalInput")
    na_bias = P.dram("na_bias", [NL, 5, 128, 8, 896], F32, "ExternalInput")
    osgu_d = P.dram("osgu", [512, T], BF16)
    ona_d = P.dram("ona", [512, T], BF16)
    def gelu(src, dst, ta, tb, npart, rk, wk, pfx):
        P.op("gpsimd", lambda e: e.tensor_tensor(out=ta, in0=src, in1=src, op=ALU.mult), reads=rk, writes=[(pfx, "ta")])
        P.op("gpsimd", lambda e: e.tensor_scalar(out=ta, in0=ta, scalar1=0.044715, scalar2=1.0, op0=ALU.mult, op1=ALU.add),
             reads=[(pfx, "ta")], writes=[(pfx, "ta")])
        P.op("gpsimd", lambda e: e.tensor_tensor(out=ta, in0=ta, in1=src, op=ALU.mult), reads=[(pfx, "ta")] + list(rk), writes=[(pfx, "ta")])
        P.op("scalar", lambda e: e.activation(out=tb, in_=ta, func=AF.Sigmoid, scale=1.5957691216057308),
             reads=[(pfx, "ta")], writes=[(pfx, "tb")])
        P.op("vector", lambda e: e.tensor_tensor(out=dst, in0=tb, in1=src, op=ALU.mult), reads=[(pfx, "tb")] + list(rk), writes=wk)

    def stage_sgu(l):
        sg = {}
        if True:
            sg["lng"] = P.sbuf("sg_lng", [128, 512]); sg["lnb"] = P.sbuf("sg_lnb", [128, 512])
            sg["wT"] = P.sbuf("sg_wT", [128, 8, 128]); sg["bB"] = P.sbuf("sg_bB", [64, 8, 128])
            for nm, shp, dt in (("sv", [128, 512], F32), ("gv", [128, 512], F32), ("vn", [128, 512], F32), ("tva", [128, 512], F32),
                                ("tvb", [128, 512], F32), ("su", [64, 8, 128], F32), ("gu", [64, 8, 128], F32), ("tua", [64, 8, 128], F32),
                                ("tub", [64, 8, 128], F32), ("st6", [128, 6], F32), ("mv", [128, 2], F32), ("rstd", [128, 1], F32),
                                ("tmp", [64, 8, 128], F32), ("osg", [64, 8, 128], BF16)):
                sg[nm] = [P.sbuf("sg_%s%d" % (nm, i), shp, dt) for i in range(2)]
        P.dma("sync", sg["lng"][:], sgu_lng[l], writes=["sg_par"])
        P.dma("sync", sg["lnb"][:], sgu_lnb[l], writes=["sg_par"])
        P.dma("sync", sg["wT"][:], sgu_wT[l], writes=["sg_par"])
        P.dma("sync", sg["bB"][:], sgu_bB[l], writes=["sg_par"])
        sguU_v = sguU.rearrange("(g c) t -> c g t", c=64)
        osgu_v = osgu_d.rearrange("(g c) t -> c g t", c=64)
        for t in range(NT):
            s = t % 2
            gi = 0 if t < 2 else 1 + (t - 2) // 4
            sv, gv, vn, su, gu, tmp, osg = (sg[k][s] for k in ("sv", "gv", "vn", "su", "gu", "tmp", "osg"))
            st6, mv, rstd = sg["st6"][s], sg["mv"][s], sg["rstd"][s]
            P.dma("sync", sv[:], sguV[t * 128:(t + 1) * 128, :], reads=[("sguV", "tm", t)], writes=[("sv", s)])
            P.dma("sync", su[:], sguU_v[:, :, t * 128:(t + 1) * 128], reads=[("sguU", "fm", gi)], writes=[("su", s)])
            gelu(sv[:], gv[:], sg["tva"][s][:], sg["tvb"][s][:], 128, [("sv", s)], [("gv", s)], ("gv", s))
            P.op("vector", lambda e, st6=st6, gv=gv: e.bn_stats(out=st6[:], in_=gv[:]), reads=[("gv", s)], writes=[("st6", s)])
            P.op("vector", lambda e, st6=st6, mv=mv: e.bn_aggr(out=mv[:], in_=st6[:]), reads=[("st6", s)], writes=[("mv", s)])
            P.op("scalar", lambda e, mv=mv, rstd=rstd: e.activation(out=rstd[:], in_=mv[:, 1:2], func=AF.Sqrt, bias=eps5[:, 0:1], scale=1.0),
                 reads=[("mv", s), "eps"], writes=[("rstd", s)])
            P.op("vector", lambda e, rstd=rstd: e.reciprocal(out=rstd[:], in_=rstd[:]), reads=[("rstd", s)], writes=[("rstd", s)])
            P.op("vector", lambda e, vn=vn, gv=gv, mv=mv, rstd=rstd: e.tensor_scalar(
                out=vn[:], in0=gv[:], scalar1=mv[:, 0:1], scalar2=rstd[:, 0:1], op0=ALU.subtract, op1=ALU.mult),
                reads=[("gv", s), ("mv", s), ("rstd", s)], writes=[("vn", s)])
            P.op("gpsimd", lambda e, vn=vn: e.tensor_tensor(out=vn[:], in0=vn[:], in1=sg["lng"][:], op=ALU.mult),
                 reads=[("vn", s), "sg_par"], writes=[("vn", s)])
            P.op("gpsimd", lambda e, vn=vn: e.tensor_tensor(out=vn[:], in0=vn[:], in1=sg["lnb"][:], op=ALU.add),
                 reads=[("vn", s), "sg_par"], writes=[("vn", s)])
            gelu(su[:], gu[:], sg["tua"][s][:], sg["tub"][s][:], 64, [("su", s)], [("gu", s)], ("gu", s))
            for half in range(2):
                pb = 2 * s + half
                for gg in range(4):
                    g = half * 4 + gg
                    P.op("tensor", lambda e, vn=vn, g=g, gg=gg, pb=pb: e.matmul(
                        PS[pb][0:64, gg * 128:(gg + 1) * 128], lhsT=vn[:, g * 64:(g + 1) * 64], rhs=sg["wT"][:, g, :],
                        start=True, stop=True), reads=[("vn", s), "sg_par"], writes=[("ps", pb)])
                P.op("vector", lambda e, tmp=tmp, pb=pb, half=half: e.tensor_tensor(
                    out=tmp[:, half * 4:(half + 1) * 4, :], in0=PS[pb][0:64, :].rearrange("p (g t) -> p g t", g=4),
                    in1=sg["bB"][:, half * 4:(half + 1) * 4, :], op=ALU.add), reads=[("ps", pb), "sg_par"], writes=[("sgtmp", s, half)])
                P.op("gpsimd", lambda e, tmp=tmp, gu=gu, osg=osg, half=half: e.tensor_tensor(
                    out=osg[:, half * 4:(half + 1) * 4, :], in0=tmp[:, half * 4:(half + 1) * 4, :], in1=gu[:, half * 4:(half + 1) * 4, :],
                    op=ALU.mult), reads=[("sgtmp", s, half), ("gu", s)], writes=[("osg", s, half)])
            P.dma("gpsimd", osgu_v[:, :, t * 128:(t + 1) * 128], osg[:], reads=[("osg", s, 0), ("osg", s, 1)], writes=[("osgu", gi)])
        if "osgu" in dbg and l == 0:
            d = dbg_out("osgu", [512, T], BF16)
            P.dma("sync", d, osgu_d, reads=[("osgu", g) for g in range(9)], writes=["OUT_dbgosgu"])

    def stage_na(l):
        na = {}
        if True:
            na["q"] = P.sbuf("na_q", [128, 4, T], BF16); na["k"] = P.sbuf("na_k", [128, 4, T], BF16)
            na["v"] = P.sbuf("na_v", [128, NT, 512], BF16)
            na["bI"] = P.sbuf("na_bI", [128, 8, 896]); na["bE"] = P.sbuf("na_bE", [128, 8, 896])
            for nm, shp, dt in (("s", [128, 896], F32), ("p", [128, 896], BF16), ("pT", [128, 896], BF16), ("nmx", [128, 1], F32)):
                na[nm] = [P.sbuf("na_%s%d" % (nm, i), shp, dt) for i in range(2)]
            for nm, shp, dt in (("rs", [128, 8], F32), ("rinv", [128, 8], F32), ("o", [128, 512], BF16), ("oT", [128, 4, 128], BF16)):
                na[nm] = [P.sbuf("na_%s%d" % (nm, i), shp, dt) for i in range(2)]
        qT_v = qT.rearrange("(k p) t -> p k t", p=128); kT_v = kT.rearrange("(k p) t -> p k t", p=128)
        vt_v = vtok.rearrange("(t p) f -> p t f", p=128)
        for gi, (t0, n) in enumerate(GROUPS):
            P.dma("sync", na["q"][:, :, t0:t0 + n], qT_v[:, :, t0:t0 + n], reads=[("qT", "fm", gi)], writes=[("na_q", gi)])
            P.dma("sync", na["k"][:, :, t0:t0 + n], kT_v[:, :, t0:t0 + n], reads=[("kT", "fm", gi)], writes=[("na_k", gi)])
        for t in range(NT):
            P.dma("sync", na["v"][:, t, :], vt_v[:, t, :], reads=[("vtok", "tm", t)], writes=[("na_v", t)])
        P.dma("sync", na["bI"][:], na_bias[l, 2], writes=["na_bI"])
        ona_v = ona_d.rearrange("(k p) t -> p k t", p=128)
        grp_of_tok = lambda tok: 0 if tok < 256 else 1 + (tok - 256) // 512
        hcnt = 0
        for t in range(NT):
            isctx = t < 2
            so = t % 2
            if isctx:
                nk = 256; pat = None; vtiles = [0, 1]
                kgroups = [0]
            else:
                qt = t - 2
                kb = min(max(2 * qt - 4, 0), 54)
                ktok0 = 256 + kb * 64
                nk = 896
                pat = {0: 0, 1: 1, 30: 3, 31: 4}.get(qt, 2)
                vtiles = [2 + kb // 2 + j for j in range(5)] + [0, 1]
                kgroups = sorted({0, grp_of_tok(ktok0), grp_of_tok(ktok0 + 639)})
                if pat != 2:
                    P.dma("sync", na["bE"][:], na_bias[l, pat], writes=["na_bE"])
            bias = None if isctx else (na["bI"] if pat == 2 else na["bE"])
            bkey = "na_bI" if pat == 2 else "na_bE"
            psO = PS[4 + so]
            qg = grp_of_tok(t * 128)
            for h in range(8):
                ch, p0 = h // 2, (h % 2) * 64
                x = hcnt % 2
                hcnt += 1
                psA, psBk = PS[x], PS[2 + x]
                s_t, p_t, pT_t, nmx = na["s"][x], na["p"][x], na["pT"][x], na["nmx"][x]
                lhsT = na["q"][p0:p0 + 64, ch, t * 128:(t + 1) * 128]
                krd = [("na_q", qg)] + [("na_k", g) for g in kgroups]
                if isctx:
                    P.op("tensor", lambda e, lhsT=lhsT, psA=psA, ch=ch, p0=p0: e.matmul(
                        psA[:, 0:256], lhsT=lhsT, rhs=na["k"][p0:p0 + 64, ch, 0:256], start=True, stop=True),
                        reads=krd, writes=[("ps", x)])
                    P.op("vector", lambda e, s_t=s_t, psA=psA: e.tensor_scalar(out=s_t[:, 0:256], in0=psA[:, 0:256], scalar1=0.125,
                                                                               scalar2=None, op0=ALU.mult),
                         reads=[("ps", x)], writes=[("na_s", x)])
                else:
                    P.op("tensor", lambda e, lhsT=lhsT, psA=psA, ch=ch, p0=p0, ktok0=ktok0: e.matmul(
                        psA[:, 0:512], lhsT=lhsT, rhs=na["k"][p0:p0 + 64, ch, ktok0:ktok0 + 512], start=True, stop=True),
                        reads=krd, writes=[("ps", x)])
                    P.op("tensor", lambda e, lhsT=lhsT, psBk=psBk, ch=ch, p0=p0, ktok0=ktok0: e.matmul(
                        psBk[:, 0:128], lhsT=lhsT, rhs=na["k"][p0:p0 + 64, ch, ktok0 + 512:ktok0 + 640], start=True, stop=True),
                        reads=krd, writes=[("ps", 2 + x)])
                    P.op("tensor", lambda e, lhsT=lhsT, psBk=psBk, ch=ch, p0=p0: e.matmul(
                        psBk[:, 128:384], lhsT=lhsT, rhs=na["k"][p0:p0 + 64, ch, 0:256], start=True, stop=True),
                        reads=krd, writes=[("ps", 2 + x)])
                    P.op("vector", lambda e, s_t=s_t, psA=psA, bias=bias, h=h: e.scalar_tensor_tensor(
                        out=s_t[:, 0:512], in0=psA[:, 0:512], scalar=0.125, in1=bias[:, h, 0:512], op0=ALU.mult, op1=ALU.add),
                        reads=[("ps", x), bkey], writes=[("na_s", x)])
                    P.op("vector", lambda e, s_t=s_t, psBk=psBk, bias=bias, h=h: e.scalar_tensor_tensor(
                        out=s_t[:, 512:896], in0=psBk[:, 0:384], scalar=0.125, in1=bias[:, h, 512:896], op0=ALU.mult, op1=ALU.add),
                        reads=[("ps", 2 + x), bkey], writes=[("na_s", x)])
                P.op("vector", lambda e, s_t=s_t, nmx=nmx, nk=nk: e.tensor_reduce(out=nmx[:, 0:1], in_=s_t[:, 0:nk], axis=AX.X, op=ALU.max,
                                                                                 negate=True),
                     reads=[("na_s", x)], writes=[("na_nmx", x)])
                P.op("scalar", lambda e, s_t=s_t, p_t=p_t, nmx=nmx, nk=nk, h=h, so=so: e.activation(
                    out=p_t[:, 0:nk], in_=s_t[:, 0:nk], func=AF.Exp, bias=nmx[:, 0:1], scale=1.0, accum_out=na["rs"][so][:, h:h + 1]),
                    reads=[("na_s", x), ("na_nmx", x)], writes=[("na_p", x), ("na_rs", so)])
                for j in range(nk // 128):
                    P.op("tensor", lambda e, p_t=p_t, j=j, x=x: e.transpose(out=PSB[x][:, j * 128:(j + 1) * 128],
                                                                          in_=p_t[:, j * 128:(j + 1) * 128], identity=identb[:]),
                         reads=[("na_p", x), "identb"], writes=[("psb", x)])
                if h % 2 == 0:
                    P.op("scalar", lambda e, pT_t=pT_t, x=x, nk=nk: e.copy(out=pT_t[:, 0:nk], in_=PSB[x][:, 0:nk]),
                         reads=[("psb", x)], writes=[("na_pT", x)])
                else:
                    P.op("vector", lambda e, pT_t=pT_t, x=x, nk=nk: e.tensor_copy(out=pT_t[:, 0:nk], in_=PSB[x][:, 0:nk]),
                         reads=[("psb", x)], writes=[("na_pT", x)])
                for j, vt in enumerate(vtiles):
                    P.op("tensor", lambda e, pT_t=pT_t, j=j, vt=vt, h=h, psO=psO, last=(j == len(vtiles) - 1): e.matmul(
                        psO[:, h * 64:(h + 1) * 64], lhsT=pT_t[:, j * 128:(j + 1) * 128], rhs=na["v"][:, vt, h * 64:(h + 1) * 64],
                        start=(j == 0), stop=last), reads=[("na_pT", x), ("na_v", vt)], writes=[("ps", 4 + so)])
            rs, rinv, o_t, oT = na["rs"][so], na["rinv"][so], na["o"][so], na["oT"][so]
            P.op("vector", lambda e, rs=rs, rinv=rinv: e.reciprocal(out=rinv[:], in_=rs[:]), reads=[("na_rs", so)], writes=[("na_rinv", so)])
            P.op("vector", lambda e, o_t=o_t, psO=psO, rinv=rinv: e.tensor_tensor(
                out=o_t[:].rearrange("p (h d) -> p h d", h=8), in0=psO[:].rearrange("p (h d) -> p h d", h=8),
                in1=rinv[:].unsqueeze(2).to_broadcast([128, 8, 64]), op=ALU.mult),
                reads=[("ps", 4 + so), ("na_rinv", so)], writes=[("na_o", so)])
            xx = hcnt % 2
            for c4 in range(4):
                P.op("tensor", lambda e, o_t=o_t, c4=c4, xx=xx: e.transpose(out=PSB[xx][:, c4 * 128:(c4 + 1) * 128],
                                                                          in_=o_t[:, c4 * 128:(c4 + 1) * 128], identity=identb[:]),
                     reads=[("na_o", so), "identb"], writes=[("psb", xx)])
            P.op("scalar", lambda e, oT=oT, xx=xx: e.copy(out=oT[:], in_=PSB[xx][:, 0:512].rearrange("p (c t) -> p c t", c=4)),
                 reads=[("psb", xx)], writes=[("na_oT", so)])
            P.dma("gpsimd", ona_v[:, :, t * 128:(t + 1) * 128], oT[:], reads=[("na_oT", so)], writes=[("ona", qg)])
        if "ona" in dbg and l == 0:
            d = dbg_out("ona", [512, T], BF16)
            P.dma("sync", d, ona_d, reads=[("ona", g) for g in range(9)], writes=["OUT_dbgona"])


    w_branch = P.dram("w_branch", [NL, 3, 512, D], F32, "ExternalInput")
    w_out = P.dram("w_out", [NL, D, D], F32, "ExternalInput")
    w_gu = P.dram("ffn_w_gu", [NL, D, 2 * D_FF], F32, "ExternalInput")
    w_down = P.dram("ffn_w_down", [NL, D_FF, D], F32, "ExternalInput")
    lnp_d = P.dram("lnp", [128, NL, 4, 8], F32, "ExternalInput")
    orw_d = P.dram("orw", [512, T], BF16)
    P.dma("sync", lnp[:], lnp_d, writes=["lnp"])
    P.op("vector", lambda e: e.memset(ones[:], 1.0), writes=["ones"])

    def load_w_bf16(dst_view, src_view, stg, npart, nfree, tag):
        a, b = dst_view.shape[1], dst_view.shape[2]
        rows = max(1, 4096 // b)
        i = 0
        for a0 in range(0, a, rows):
            a1 = min(a, a0 + rows)
            st = stg[i % 2]
            P.dma("sync", st[0:npart, 0:(a1 - a0) * b].rearrange("p (a b) -> p a b", b=b), src_view[:, a0:a1, :], writes=[("stg", i % 2)])
            P.op("gpsimd" if i % 2 == 0 else "vector", lambda e, st=st, a0=a0, a1=a1: e.tensor_copy(
                out=dst_view[:, a0:a1, :], in_=st[0:npart, 0:(a1 - a0) * b].rearrange("p (a b) -> p a b", b=b)),
                reads=[("stg", i % 2)], writes=[tag])
            i += 1

    def layer_norm_T(r, n, l, gi_, bi_, out_view, tagr, tagout, sq, stat):
        for k in range(8):
            P.op("tensor", lambda e, k=k: e.matmul(PS[0][:, :n], lhsT=ones[:], rhs=r[:, k, :n], start=(k == 0), stop=(k == 7)),
                 reads=[tagr, "ones"], writes=[("ps", 0)])
        for k in range(8):
            sqk = sq[k % 2]
            P.op("scalar", lambda e, k=k, sqk=sqk: e.activation(out=sqk[:, :n], in_=r[:, k, :n], func=AF.Square),
                 reads=[tagr], writes=[("lnsq", k % 2)])
            P.op("tensor", lambda e, k=k, sqk=sqk: e.matmul(PS[1][:, :n], lhsT=ones[:], rhs=sqk[:, :n], start=(k == 0), stop=(k == 7)),
                 reads=[("lnsq", k % 2), "ones"], writes=[("ps", 1)])
        mean, var, rstd = stat
        P.op("vector", lambda e: e.tensor_scalar(out=mean[:, :n], in0=PS[0][:, :n], scalar1=1.0 / 1024, scalar2=None, op0=ALU.mult),
             reads=[("ps", 0)], writes=["ln_mean"])
        P.op("vector", lambda e: e.tensor_tensor(out=var[:, :n], in0=mean[:, :n], in1=mean[:, :n], op=ALU.mult),
             reads=["ln_mean"], writes=["ln_var"])
        P.op("vector", lambda e: e.scalar_tensor_tensor(out=var[:, :n], in0=PS[1][:, :n], scalar=1.0 / 1024, in1=var[:, :n],
                                                        op0=ALU.mult, op1=ALU.subtract), reads=[("ps", 1), "ln_var"], writes=["ln_var"])
        P.op("scalar", lambda e: e.activation(out=rstd[:, :n], in_=var[:, :n], func=AF.Sqrt, bias=eps5[:, 0:1], scale=1.0),
             reads=["ln_var", "eps"], writes=["ln_rstd"])
        P.op("vector", lambda e: e.reciprocal(out=rstd[:, :n], in_=rstd[:, :n]), reads=["ln_rstd"], writes=["ln_rstd"])
        for k in range(8):
            eng = "vector" if k % 2 == 0 else "gpsimd"
            P.op(eng, lambda e, k=k: e.tensor_tensor(out=r[:, k, :n], in0=r[:, k, :n], in1=mean[:, :n], op=ALU.subtract),
                 reads=[tagr, "ln_mean", "ln_rstd"], writes=[(tagr, "c", k)])
            P.op(eng, lambda e, k=k: e.tensor_tensor(out=r[:, k, :n], in0=r[:, k, :n], in1=rstd[:, :n], op=ALU.mult),
                 reads=[tagr, (tagr, "c", k), "ln_rstd"], writes=[(tagr, "c", k)])
            P.op(eng, lambda e, k=k: e.tensor_scalar(out=out_view[:, k, :n], in0=r[:, k, :n], scalar1=lnp[:, l, gi_, k:k + 1],
                                                     scalar2=lnp[:, l, bi_, k:k + 1], op0=ALU.mult, op1=ALU.add),
                 reads=[tagr, (tagr, "c", k), "lnp"], writes=[(tagout, k)])

    def stage_merge(l):
        wg = P.sbuf("m_wg", [128, 8, 3072], BF16)
        wbn = P.sbuf("m_wbn", [128, 4, 1024], BF16); wbr = P.sbuf("m_wbr", [128, 4, 1024], BF16)
        wbs = P.sbuf("m_wbs", [64, 8, 1024], BF16); wo = P.sbuf("m_wo", [128, 8, 1024], BF16)
        stg = [P.sbuf("m_stg%d" % i, [128, 4096]) for i in range(2)]
        xr = P.sbuf("m_xr", [128, 8, 512]); hh = P.sbuf("m_h", [128, 8, 512], BF16)
        o_n = P.sbuf("m_on", [128, 4, 512], BF16); o_r = P.sbuf("m_or", [128, 4, 512], BF16); o_s = P.sbuf("m_os", [64, 8, 512], BF16)
        sig = [P.sbuf("m_sig%d" % i, [128, 512]) for i in range(3)]
        ypre = P.sbuf("m_ypre", [128, 8, 512], BF16); yacc = P.sbuf("m_yacc", [128, 512]); ytmp = P.sbuf("m_ytmp", [128, 512])
        sq = [P.sbuf("m_sq%d" % i, [128, 512]) for i in range(2)]
        stat = [P.sbuf("m_st%d" % i, [128, 512]) for i in range(3)]
        load_w_bf16(wg[:], w_in[l, :, 0:3072].rearrange("(k p) m -> p k m", p=128), stg, 128, 0, "m_wg")
        load_w_bf16(wbn[:], w_branch[l, 0].rearrange("(k p) m -> p k m", p=128), stg, 128, 0, "m_wbn")
        load_w_bf16(wbr[:], w_branch[l, 1].rearrange("(k p) m -> p k m", p=128), stg, 128, 0, "m_wbr")
        load_w_bf16(wbs[:], w_branch[l, 2].rearrange("(g c) m -> c g m", c=64), stg, 64, 0, "m_wbs")
        load_w_bf16(wo[:], w_out[l].rearrange("(k p) m -> p k m", p=128), stg, 128, 0, "m_wo")
        ona_v = ona_d.rearrange("(k p) t -> p k t", p=128); orw_v = orw_d.rearrange("(k p) t -> p k t", p=128)
        osgu_v = osgu_d.rearrange("(g c) t -> c g t", c=64)
        pr = 0
        for gi, (t0, n) in enumerate(GROUPS):
            c = 1 if gi == 0 else 0
            P.dma("sync", xr[:, :, :n], xT_v[:, :, t0:t0 + n], reads=[("xT", gi)], writes=["m_xr"])
            P.dma("sync", o_n[:, :, :n], ona_v[:, :, t0:t0 + n], reads=[("ona", gi)], writes=["m_on"])
            P.dma("sync", o_r[:, :, :n], orw_v[:, :, t0:t0 + n], reads=[("orw", gi)], writes=["m_or"])
            P.dma("sync", o_s[:, :, :n], osgu_v[:, :, t0:t0 + n], reads=[("osgu", gi)], writes=["m_os"])
            for k in range(8):
                P.op("vector" if k % 2 == 0 else "gpsimd", lambda e, k=k, c=c: e.tensor_scalar(
                    out=hh[:, k, :n], in0=xr[:, k, :n], scalar1=mod1[:, l, 8 + k, c:c + 1], scalar2=mod[:, l, k, c:c + 1],
                    op0=ALU.mult, op1=ALU.add), reads=["m_xr", "mod", "mod1"], writes=["m_h"])
            for m in range(8):
                for i in range(3):
                    pg, pb = PS[2 * (pr % 3)], PS[2 * (pr % 3) + 1]
                    kg, kb_ = ("ps", 2 * (pr % 3)), ("ps", 2 * (pr % 3) + 1)
                    pr += 1
                    mc = i * 8 + m
                    for k in range(8):
                        P.op("tensor", lambda e, k=k, mc=mc, pg=pg: e.matmul(pg[:, :n], lhsT=wg[:, k, mc * 128:(mc + 1) * 128], rhs=hh[:, k, :n],
                                                                             start=(k == 0), stop=(k == 7)), reads=["m_wg", "m_h"], writes=[kg])
                    P.op("scalar", lambda e, i=i, pg=pg: e.activation(out=sig[i][:, :n], in_=pg[:, :n], func=AF.Sigmoid),
                         reads=[kg], writes=[("m_sig", i)])
                    if i < 2:
                        wb_, ob_, ok_ = (wbn, o_n, "m_on") if i == 0 else (wbr, o_r, "m_or")
                        for k in range(4):
                            P.op("tensor", lambda e, k=k, m=m, pb=pb, wb_=wb_, ob_=ob_: e.matmul(
                                pb[:, :n], lhsT=wb_[:, k, m * 128:(m + 1) * 128], rhs=ob_[:, k, :n], start=(k == 0), stop=(k == 3)),
                                reads=["m_wbn", "m_wbr", ok_], writes=[kb_])
                    else:
                        for g in range(8):
                            P.op("tensor", lambda e, g=g, m=m, pb=pb: e.matmul(
                                pb[:, :n], lhsT=wbs[:, g, m * 128:(m + 1) * 128], rhs=o_s[:, g, :n], start=(g == 0), stop=(g == 7)),
                                reads=["m_wbs", "m_os"], writes=[kb_])
                    if i == 0:
                        P.op("vector", lambda e, pb=pb: e.tensor_tensor(out=yacc[:, :n], in0=pb[:, :n], in1=sig[0][:, :n], op=ALU.mult),
                             reads=[kb_, ("m_sig", 0)], writes=["m_yacc"])
                    else:
                        P.op("vector", lambda e, pb=pb, i=i: e.tensor_tensor(out=ytmp[:, :n], in0=pb[:, :n], in1=sig[i][:, :n], op=ALU.mult),
                             reads=[kb_, ("m_sig", i)], writes=["m_ytmp"])
                        if i == 1:
                            P.op("gpsimd", lambda e: e.tensor_tensor(out=yacc[:, :n], in0=yacc[:, :n], in1=ytmp[:, :n], op=ALU.add),
                                 reads=["m_yacc", "m_ytmp"], writes=["m_yacc"])
                        else:
                            P.op("gpsimd", lambda e, m=m: e.tensor_tensor(out=ypre[:, m, :n], in0=yacc[:, :n], in1=ytmp[:, :n], op=ALU.add),
                                 reads=["m_yacc", "m_ytmp"], writes=["m_ypre"])
            for m in range(8):
                pg = PS[2 + m % 4]; kg = ("ps", 2 + m % 4)
                for k in range(8):
                    P.op("tensor", lambda e, k=k, m=m, pg=pg: e.matmul(pg[:, :n], lhsT=wo[:, k, m * 128:(m + 1) * 128], rhs=ypre[:, k, :n],
                                                                         start=(k == 0), stop=(k == 7)), reads=["m_wo", "m_ypre"], writes=[kg])
                P.op("scalar", lambda e, m=m, pg=pg, c=c: e.activation(out=ytmp[:, :n], in_=pg[:, :n], func=AF.Copy,
                                                                      scale=mod[:, l, 16 + m, c:c + 1]), reads=[kg, "mod"], writes=["m_ytmp"])
                P.op("vector", lambda e, m=m: e.scalar_tensor_tensor(out=xr[:, m, :n], in0=xr[:, m, :n], scalar=ALPHA, in1=ytmp[:, :n],
                                                                    op0=ALU.mult, op1=ALU.add), reads=["m_xr", "m_ytmp"], writes=["m_xr"])
            layer_norm_T(xr, n, l, 0, 1, xr, "m_xr", "m_x1", sq, stat)
            P.dma("gpsimd", xT_v[:, :, t0:t0 + n], xr[:, :, :n], reads=["m_xr"] + [("m_x1", k) for k in range(8)], writes=[("xT", gi)])
        if "x1" in dbg and l == 0:
            d = dbg_out("x1", [D, T])
            P.dma("sync", d, xT, reads=[("xT", g) for g in range(9)], writes=["OUT_dbgx1"])

    FG = [(t0, 256) for t0 in range(0, T, 256)]

    def stage_ffn(l):
        wgu = P.sbuf("f_wgu", [128, 8, 2 * D_FF], BF16)
        wd = P.sbuf("f_wd", [128, 22, 1024], BF16)
        stg = [P.sbuf("f_stg%d" % i, [128, 2048]) for i in range(2)]
        xr = P.sbuf("f_xr", [128, 8, 256]); ff = P.sbuf("f_f", [128, 8, 256], BF16)
        act = P.sbuf("f_a", [128, 22, 256], BF16)
        sgt = [P.sbuf("f_sg%d" % i, [128, 256]) for i in range(2)]
        ytmp = P.sbuf("f_ytmp", [128, 256])
        sq = [P.sbuf("f_sq%d" % i, [128, 256]) for i in range(2)]
        stat = [P.sbuf("f_st%d" % i, [128, 256]) for i in range(3)]

        def load2(dst_view, src_view, tag):
            a, b = dst_view.shape[1], dst_view.shape[2]
            i = 0
            for a0 in range(a):
                for b0 in range(0, b, 2048):
                    b1 = min(b, b0 + 2048)
                    st = stg[i % 2]
                    P.dma("sync", st[:, 0:b1 - b0], src_view[:, a0, b0:b1], writes=[("fstg", i % 2)])
                    P.op("gpsimd" if i % 2 == 0 else "vector", lambda e, st=st, a0=a0, b0=b0, b1=b1: e.tensor_copy(
                        out=dst_view[:, a0, b0:b1], in_=st[:, 0:b1 - b0]), reads=[("fstg", i % 2)], writes=[tag])
                    i += 1
        load2(wgu[:], w_gu[l].rearrange("(k p) m -> p k m", p=128), "f_wgu")
        load2(wd[:], w_down[l].rearrange("(k p) m -> p k m", p=128), "f_wd")
        for fi, (t0, n) in enumerate(FG):
            gi = 0 if t0 < 256 else 1 + (t0 - 256) // 512
            c = 1 if fi == 0 else 0
            P.dma("sync", xr[:, :, :n], xT_v[:, :, t0:t0 + n], reads=[("xT", gi)], writes=["f_xr"])
            for k in range(8):
                P.op("vector" if k % 2 == 0 else "gpsimd", lambda e, k=k, c=c: e.tensor_scalar(
                    out=ff[:, k, :n], in0=xr[:, k, :n], scalar1=mod1[:, l, 32 + k, c:c + 1], scalar2=mod[:, l, 24 + k, c:c + 1],
                    op0=ALU.mult, op1=ALU.add), reads=["f_xr", "mod", "mod1"], writes=["f_f"])
            for j in range(22):
                x2 = j % 2
                pg, pu = PS[2 + 2 * x2], PS[3 + 2 * x2]
                kg, ku = ("ps", 2 + 2 * x2), ("ps", 3 + 2 * x2)
                for k in range(8):
                    P.op("tensor", lambda e, k=k, j=j, pg=pg: e.matmul(pg[:, :n], lhsT=wgu[:, k, j * 128:(j + 1) * 128], rhs=ff[:, k, :n],
                                                                         start=(k == 0), stop=(k == 7)), reads=["f_wgu", "f_f"], writes=[kg])
                for k in range(8):
                    P.op("tensor", lambda e, k=k, j=j, pu=pu: e.matmul(pu[:, :n], lhsT=wgu[:, k, D_FF + j * 128:D_FF + (j + 1) * 128],
                                                                         rhs=ff[:, k, :n], start=(k == 0), stop=(k == 7)),
                         reads=["f_wgu", "f_f"], writes=[ku])
                P.op("scalar", lambda e, pg=pg, x2=x2: e.activation(out=sgt[x2][:, :n], in_=pg[:, :n], func=AF.Silu),
                     reads=[kg], writes=[("f_sg", x2)])
                P.op("vector", lambda e, pu=pu, x2=x2, j=j: e.tensor_tensor(out=act[:, j, :n], in0=pu[:, :n], in1=sgt[x2][:, :n], op=ALU.mult),
                     reads=[ku, ("f_sg", x2)], writes=["f_a"])
            for m in range(8):
                pg = PS[2 + m % 4]; kg = ("ps", 2 + m % 4)
                for j in range(22):
                    P.op("tensor", lambda e, j=j, m=m, pg=pg: e.matmul(pg[:, :n], lhsT=wd[:, j, m * 128:(m + 1) * 128], rhs=act[:, j, :n],
                                                                         start=(j == 0), stop=(j == 21)), reads=["f_wd", "f_a"], writes=[kg])
                P.op("scalar", lambda e, m=m, pg=pg, c=c: e.activation(out=ytmp[:, :n], in_=pg[:, :n], func=AF.Copy,
                                                                      scale=mod[:, l, 40 + m, c:c + 1]), reads=[kg, "mod"], writes=["f_ytmp"])
                P.op("vector", lambda e, m=m: e.scalar_tensor_tensor(out=xr[:, m, :n], in0=xr[:, m, :n], scalar=ALPHA, in1=ytmp[:, :n],
                                                                    op0=ALU.mult, op1=ALU.add), reads=["f_xr", "f_ytmp"], writes=["f_xr"])
            layer_norm_T(xr, n, l, 2, 3, xr, "f_xr", "f_x2", sq, stat)
            P.dma("gpsimd", xT_v[:, :, t0:t0 + n], xr[:, :, :n], reads=["f_xr"] + [("f_x2", k) for k in range(8)], writes=[("xT", gi)])
        if "x2" in dbg and l == 0:
            d = dbg_out("x2", [D, T])
            P.dma("sync", d, xT, reads=[("xT", g) for g in range(9)], writes=["OUT_dbgx2"])

    def stage_final():
        xg_ = [P.sbuf("o_xg%d" % i, [128, 8, 128]) for i in range(2)]
        ot = [P.sbuf("o_t%d" % i, [128, 1024]) for i in range(2)]
        for t in range(2, NT):
            s_ = t % 2
            gi = 1 + (t - 2) // 4
            P.dma("sync", xg_[s_][:], xT_v[:, :, t * 128:(t + 1) * 128], reads=[("xT", gi)], writes=[("o_xg", s_)])
            for half in range(2):
                pb = 2 + 2 * s_ + half
                for kk in range(4):
                    k = half * 4 + kk
                    P.op("tensor", lambda e, s_=s_, k=k, kk=kk, pb=pb: e.transpose(
                        out=PS[pb][:, kk * 128:(kk + 1) * 128], in_=xg_[s_][:, k, :], identity=ident[:]),
                        reads=[("o_xg", s_), "ident"], writes=[("ps", pb)])
                if half == 0:
                    P.op("scalar", lambda e, s_=s_, pb=pb: e.copy(out=ot[s_][:, 0:512], in_=PS[pb][:]), reads=[("ps", pb)], writes=[("o_t", s_, 0)])
                else:
                    P.op("vector", lambda e, s_=s_, pb=pb: e.tensor_copy(out=ot[s_][:, 512:1024], in_=PS[pb][:]), reads=[("ps", pb)],
                         writes=[("o_t", s_, 1)])
            P.dma("gpsimd", out_d[(t - 2) * 128:(t - 1) * 128, :], ot[s_][:], reads=[("o_t", s_, 0), ("o_t", s_, 1)], writes=["OUT_%d" % t])

    rw_mu_d = P.dram("rw_mu", [128, NL, 2, 15], F32, "ExternalInput")
    rw_w0a0_d = P.dram("rw_w0a0", [128, NL, 2, 2, 4], F32, "ExternalInput")
    rw_w2_d = P.dram("rw_w2", [NL, 128, 512], F32, "ExternalInput")
    rw_a2_d = P.dram("rw_a2", [NL, 128, 512], F32, "ExternalInput")
    rw_g2_d = P.dram("rw_g2", [NL, 128, 512], F32, "ExternalInput")
    rw_vec_d = P.dram("rw_vec", [128, NL, 5, 4], F32, "ExternalInput")
    mask64_d = P.dram("mask64", [64, 4, 64], F32, "ExternalInput")
    bones_d = P.dram("bones", [128, 128], F32, "ExternalInput")
    g_d = P.dram("rw_g", [512, T], F32)
    bv_d = P.dram("rw_bv", [512, T], F32)
    NCH = T // 64
    summ_d = P.dram("rw_summ", [2, NCH, 64, 2056], F32)
    etot_d = P.dram("rw_etot", [2, NCH, 512], F32)
    y_d = P.dram("rw_y", [2, T, 512], F32)
    CDEC = float(np.exp(-0.5))
    psctr = [0]

    def psn():
        i = psctr[0] % 6
        psctr[0] += 1
        return PS[i], ("ps", i)

    def stage_rwkv(l):
        import os
        RW_NG = int(os.environ.get("RW_NGROUPS", "99")); RW_CH = int(os.environ.get("RW_CHUNKS", "1")); RW_PH = int(os.environ.get("RW_PHASES", "3"))
        RW_RND = int(os.environ.get("RW_ROUNDS", "6")); RW_ST = int(os.environ.get("RW_STEPS", "99"))
        NG = 128
        NC = NG // 64
        prw_v = prw.rearrange("(k p) t -> p k t", p=128)
        g_v = g_d.rearrange("(k p) t -> p k t", p=128)
        bv_v = bv_d.rearrange("(k p) t -> p k t", p=128)
        grp512 = lambda tok: 0 if tok < 256 else 1 + (tok - 256) // 512
        mu = P.sbuf("rw_mu", [128, 2, 15]); c0 = P.sbuf("rw_c0", [128, 15])
        w0a0 = P.sbuf("rw_w0a0", [128, 2, 2, 4]); w2s = P.sbuf("rw_w2", [128, 512]); a2s = P.sbuf("rw_a2", [128, 512])
        g2s = P.sbuf("rw_g2", [128, 512]); vec = P.sbuf("rw_vec", [128, 5, 4]); omka = P.sbuf("rw_omka", [128, 4])
        mask = P.sbuf("rw_mask", [64, 4, 64]); bones = P.sbuf("rw_bones", [128, 128]); eps12 = P.sbuf("rw_eps12", [128, 1])
        eps_gn = P.sbuf("rw_epsgn", [128, 1]); rmask = P.sbuf("rw_rmask", [128, 4 * NC, 64])
        P.dma("sync", mu[:], rw_mu_d[:, l], writes=["rw_par"]); P.dma("sync", w0a0[:], rw_w0a0_d[:, l], writes=["rw_par"])
        P.dma("sync", w2s[:], rw_w2_d[l], writes=["rw_par"]); P.dma("sync", a2s[:], rw_a2_d[l], writes=["rw_par"])
        P.dma("sync", g2s[:], rw_g2_d[l], writes=["rw_par"]); P.dma("sync", vec[:], rw_vec_d[:, l], writes=["rw_par"])
        P.dma("sync", mask[:], mask64_d, writes=["rw_par"]); P.dma("sync", bones[:], bones_d, writes=["rw_par"])
        P.op("vector", lambda e: e.tensor_tensor(out=c0[:], in0=mu[:, 0, :], in1=mu[:, 1, :], op=ALU.add), reads=["rw_par"], writes=["rw_c0"])
        P.op("vector", lambda e: e.tensor_scalar(out=c0[:], in0=c0[:], scalar1=-1.0, scalar2=1.0, op0=ALU.mult, op1=ALU.add),
             reads=["rw_c0"], writes=["rw_c0"])
        P.op("vector", lambda e: e.tensor_scalar(out=omka[:], in0=vec[:, 1, :], scalar1=-1.0, scalar2=1.0, op0=ALU.mult, op1=ALU.add),
             reads=["rw_par"], writes=["rw_omka"])
        P.op("vector", lambda e: e.memset(eps12[:], 1e-12), writes=["rw_eps"])
        P.op("vector", lambda e: e.memset(eps_gn[:], 64e-5), writes=["rw_eps"])
        P.op("vector", lambda e: e.memset(rmask[:], 1.0), writes=["rw_rmask"])
        P.op("vector", lambda e: e.memset(rmask[:, :, 0:1], 0.0), reads=["rw_rmask"], writes=["rw_rmask"])
        m1 = P.mark()
        pin = P.sbuf("rw_pin", [128, 15, NG + 2]); psh = P.sbuf("rw_psh", [128, 15, NG])
        sgw = [P.sbuf("rw_sgw%d" % d, [128, 4, NG]) for d in range(2)]
        aa = [P.sbuf("rw_a%d" % d, [128, 4, NG]) for d in range(2)]
        kd = [P.sbuf("rw_kd%d" % d, [128, 4, NG]) for d in range(2)]
        kk = P.sbuf("rw_kk", [128, 4, NG]); tA = P.sbuf("rw_tA", [128, 4, NG]); tB = P.sbuf("rw_tB", [128, 4, NG])
        Lp = P.sbuf("rw_Lp", [128, 4, NG]); Li = P.sbuf("rw_Li", [128, 4, NG]); Le = P.sbuf("rw_Le", [128, 4, NG])
        rt = P.sbuf("rw_rt", [128, 4, NG], F32R); at = P.sbuf("rw_at", [128, 4, NG], F32R); kt = P.sbuf("rw_kt", [128, 4, NG], F32R)
        bt = P.sbuf("rw_bt", [128, 4, NG], F32R); Kh = P.sbuf("rw_Kh", [128, 4, NG]); Bh = P.sbuf("rw_Bh", [128, 4, NG])
        etot = P.sbuf("rw_etot", [128, 4, NC], F32R); gst = P.sbuf("rw_gst", [128, 4, NG])
        mk = {nm: [P.sbuf("rw_%s%s" % (nm, eo), [128, 4, NG], F32R) for eo in "EO"] for nm in ("at", "kt", "bt")}
        zsrc = P.sbuf("rw_zsrc", [128, 1024])
        P.op("vector", lambda e: e.memset(zsrc[:], 0.0), writes=["rw_zsrc"])
        Vt = [P.sbuf("rw_Vt%d" % c, [128, 512], F32R) for c in range(NC)]
        UT = []
        zl = [(Vt[c], ("rw_Vt", c)) for c in range(NC)]
        for u_ in range(2):
            ut = dict(KhT=P.sbuf("rw_KhT%d" % u_, [128, 512], F32R), BhT=P.sbuf("rw_BhT%d" % u_, [128, 512], F32R),
                      Mka=P.sbuf("rw_Mka%d" % u_, [128, 8, 64], F32R), Mkr=P.sbuf("rw_Mkr%d" % u_, [128, 8, 64], F32R),
                      Mbr=P.sbuf("rw_Mbr%d" % u_, [128, 8, 64], F32R),
                      Nn=[P.sbuf("rw_N%d_%d" % (u_, i), [128, 8, 64], F32R) for i in range(2)],
                      NTr=[P.sbuf("rw_NT%d_%d" % (u_, i), [128, 8, 64], F32R) for i in range(2)],
                      Z=P.sbuf("rw_Z%d" % u_, [128, 8, 128], F32R), Zn=P.sbuf("rw_Zn%d" % u_, [128, 8, 128], F32R),
                      summ=P.sbuf("rw_summ%d" % u_, [64, 2056]))
            UT.append(ut)
            zl += [(ut["KhT"], ("rw_KhT", u_)), (ut["BhT"], ("rw_BhT", u_)), (ut["Mka"], ("rw_Mka", u_)), (ut["Mkr"], ("rw_Mkr", u_)),
                   (ut["Mbr"], ("rw_Mbr", u_)), (ut["Nn"][0], ("rw_N", u_, 0)), (ut["Nn"][1], ("rw_N", u_, 1)), (ut["NTr"][0], ("rw_NT", u_, 0)),
                   (ut["NTr"][1], ("rw_NT", u_, 1)), (ut["Z"], ("rw_Z", u_)), (ut["Zn"], ("rw_Zn", u_))]
        for tl, key in zl:
            nfree = int(np.prod(tl.shape[1:]))
            src_ = zsrc[64:128, 0:nfree] if len(tl.shape) == 2 else zsrc[64:128, 0:nfree].rearrange("p (a b) -> p a b", a=tl.shape[1])
            P.op("vector", lambda e, tl=tl, src_=src_: e.tensor_copy(out=tl[64:128], in_=src_), reads=["rw_zsrc"],
                 writes=[key] + ([("rw_Z2", key[1])] if key[0] == "rw_Z" else []))
        identr = P.sbuf("rw_identr", [128, 128], F32R)
        P.op("vector", lambda e: e.tensor_copy(out=identr[:], in_=ident[:]), reads=["ident"], writes=["rw_identr"])
        r_ = psh[:, 0:4, :]; k_ = psh[:, 4:8, :]; v_ = psh[:, 8:12, :]
        TA = [("rw_tA", fc) for fc in range(4)]; TB = [("rw_tB", fc) for fc in range(4)]
        for gx in range(min(T // NG, RW_NG)):
            t0 = gx * NG
            isctx = t0 < 256
            rdk = sorted({grp512(max(t0 - 1, 0)), grp512(t0), grp512(min(t0 + NG, T - 1))})
            rdk = [("prw", "fm", g) for g in rdk]
            hasL = not (t0 == 0 or t0 == 256)
            hasR = not (t0 + NG == 256 or t0 + NG == T)
            lo = t0 - 1 if hasL else t0
            hi = t0 + NG + 1 if hasR else t0 + NG
            P.dma("sync", pin[:, :, lo - (t0 - 1):hi - (t0 - 1)], prw_v[:, :, lo:hi], reads=rdk, writes=["rw_pin"])
            if not hasL:
                P.op("gpsimd", lambda e: e.memset(pin[:, :, 0:1], 0.0), reads=["rw_pin"], writes=["rw_pinL"])
            if not hasR:
                P.op("gpsimd", lambda e: e.memset(pin[:, :, NG + 1:NG + 2], 0.0), reads=["rw_pin"], writes=["rw_pinR"])
            pk = ["rw_pin", "rw_pinL", "rw_pinR"]
            for ch in range(15):
                P.op("gpsimd", lambda e, ch=ch: e.tensor_scalar(out=psh[:, ch, :], in0=pin[:, ch, 1:NG + 1], scalar1=c0[:, ch:ch + 1],
                                                               scalar2=None, op0=ALU.mult), reads=pk + ["rw_c0"], writes=[("rw_psh", ch)])
                P.op("vector", lambda e, ch=ch: e.scalar_tensor_tensor(out=psh[:, ch, :], in0=pin[:, ch, 0:NG], scalar=mu[:, 0, ch:ch + 1],
                                                                      in1=psh[:, ch, :], op0=ALU.mult, op1=ALU.add),
                     reads=pk + ["rw_par", ("rw_psh", ch)], writes=[("rw_psh", ch)])
                P.op("vector", lambda e, ch=ch: e.scalar_tensor_tensor(out=psh[:, ch, :], in0=pin[:, ch, 2:NG + 2], scalar=mu[:, 1, ch:ch + 1],
                                                                      in1=psh[:, ch, :], op0=ALU.mult, op1=ALU.add),
                     reads=pk + ["rw_par", ("rw_psh", ch)], writes=[("rw_psh", ch)])
            RK = [("rw_psh", c) for c in range(0, 4)]; KK = [("rw_psh", c) for c in range(4, 8)]; VK = [("rw_psh", c) for c in range(8, 12)]
            P.op("scalar", lambda e: e.activation(out=psh[:, 12, :], in_=psh[:, 12, :], func=AF.Tanh), reads=[("rw_psh", 12)], writes=[("rw_psh", 12)])
            P.op("scalar", lambda e: e.activation(out=psh[:, 14, :], in_=psh[:, 14, :], func=AF.Sigmoid), reads=[("rw_psh", 14)],
                 writes=[("rw_psh", 14)])
            for which, (wsrc, srcch, dst) in enumerate(((w2s, 12, sgw), (a2s, 13, aa))):
                for d in range(2):
                    for fc in range(4):
                        ps, pk_ = psn()
                        P.op("tensor", lambda e, d=d, fc=fc, ps=ps, wsrc=wsrc, srcch=srcch: e.matmul(
                            ps[:, 0:NG], lhsT=wsrc[d * 64:(d + 1) * 64, fc * 128:(fc + 1) * 128], rhs=psh[d * 64:(d + 1) * 64, srcch, :],
                            start=True, stop=True), reads=["rw_par", ("rw_psh", srcch)], writes=[pk_])
                        P.op("scalar", lambda e, d=d, fc=fc, ps=ps, dst=dst, which=which: e.activation(
                            out=dst[d][:, fc, :], in_=ps[:, 0:NG], func=AF.Sigmoid, bias=w0a0[:, which, d, fc:fc + 1], scale=1.0),
                            reads=[pk_, "rw_par"], writes=[("rw_sa", which, d)])
            for fc in range(4):
                ps, pk_ = psn()
                P.op("tensor", lambda e, fc=fc, ps=ps: e.matmul(ps[:, 0:NG], lhsT=g2s[:, fc * 128:(fc + 1) * 128], rhs=psh[:, 14, :],
                                                              start=True, stop=True), reads=["rw_par", ("rw_psh", 14)], writes=[pk_])
                P.op("scalar", lambda e, fc=fc, ps=ps: e.copy(out=gst[:, fc, :], in_=ps[:, 0:NG]), reads=[pk_], writes=["rw_gst"])
            P.dma("gpsimd", g_v[:, :, t0:t0 + NG], gst[:], reads=["rw_gst"], writes=[("rw_g", grp512(t0))])
            for fc in range(4):
                P.op("gpsimd", lambda e, fc=fc: e.tensor_scalar(out=kk[:, fc, :], in0=psh[:, 4 + fc, :], scalar1=vec[:, 0, fc:fc + 1], scalar2=None,
                                                               op0=ALU.mult), reads=KK + ["rw_par"], writes=[("rw_kk", fc)])
                P.op("scalar", lambda e, fc=fc: e.activation(out=tA[:, fc, :], in_=kk[:, fc, :], func=AF.Square), reads=[("rw_kk", fc)],
                     writes=[("rw_tA", fc)])
                ps, pk_ = psn()
                P.op("tensor", lambda e, fc=fc, ps=ps: e.matmul(ps[:, 0:NG], lhsT=bones[:], rhs=tA[:, fc, :], start=True, stop=True),
                     reads=["rw_par", ("rw_tA", fc)], writes=[pk_])
                P.op("scalar", lambda e, fc=fc, ps=ps: e.activation(out=tB[:, fc, :], in_=ps[:, 0:NG], func=AF.Sqrt, bias=eps12[:, 0:1], scale=1.0),
                     reads=[pk_, "rw_eps"], writes=[("rw_tB", fc)])
                P.op("vector", lambda e, fc=fc: e.reciprocal(out=tB[:, fc, :], in_=tB[:, fc, :]), reads=[("rw_tB", fc)], writes=[("rw_tB", fc)])
                P.op("vector", lambda e, fc=fc: e.tensor_tensor(out=kk[:, fc, :], in0=kk[:, fc, :], in1=tB[:, fc, :], op=ALU.mult),
                     reads=[("rw_kk", fc), ("rw_tB", fc)], writes=[("rw_kk", fc)])
            KKN = [("rw_kk", fc) for fc in range(4)]
            for d in range(2):
                for fc in range(4):
                    P.op("gpsimd", lambda e, d=d, fc=fc: e.tensor_scalar(out=kd[d][:, fc, :], in0=aa[d][:, fc, :], scalar1=vec[:, 1, fc:fc + 1],
                                                                        scalar2=omka[:, fc:fc + 1], op0=ALU.mult, op1=ALU.add),
                         reads=[("rw_sa", 1, d), "rw_par", "rw_omka"], writes=[("rw_kd", d)])
                P.op("gpsimd", lambda e, d=d: e.tensor_tensor(out=kd[d][:], in0=kd[d][:], in1=k_, op=ALU.mult),
                     reads=[("rw_kd", d)] + KK, writes=[("rw_kd", d)])
                P.op("vector", lambda e, d=d: e.tensor_tensor(out=aa[d][:], in0=aa[d][:], in1=kk[:], op=ALU.mult),
                     reads=[("rw_sa", 1, d), ("rw_kd", d)] + KKN, writes=[("rw_sa", 1, d)])
            P.op("gpsimd", lambda e: e.tensor_tensor(out=tA[:], in0=kd[0][:], in1=kd[1][:], op=ALU.add),
                 reads=[("rw_kd", 0), ("rw_kd", 1)], writes=TA)
            P.op("gpsimd", lambda e: e.tensor_tensor(out=tA[:], in0=tA[:], in1=r_, op=ALU.mult), reads=TA + RK, writes=TA)
            for fc in range(4):
                P.op("gpsimd", lambda e, fc=fc: e.tensor_scalar(out=tA[:, fc, :], in0=tA[:, fc, :], scalar1=vec[:, 2, fc:fc + 1], scalar2=None,
                                                               op0=ALU.mult), reads=[("rw_tA", fc), "rw_par"], writes=[("rw_tA", fc)])
                ps, pk_ = psn()
                P.op("tensor", lambda e, fc=fc, ps=ps: e.matmul(ps[:, 0:NG], lhsT=bones[:], rhs=tA[:, fc, :], start=True, stop=True),
                     reads=["rw_par", ("rw_tA", fc)], writes=[pk_])
                P.op("vector", lambda e, fc=fc, ps=ps: e.tensor_tensor(out=gst[:, fc, :], in0=ps[:, 0:NG], in1=psh[:, 8 + fc, :], op=ALU.mult),
                     reads=[pk_, "rw_gst"] + VK, writes=["rw_gst"])
            P.dma("gpsimd", bv_v[:, :, t0:t0 + NG], gst[:], reads=["rw_gst"], writes=[("rw_bv", grp512(t0))])
            for c in range(NC):
                ps, pk_ = psn()
                for fc in range(4):
                    P.op("tensor", lambda e, c=c, fc=fc, ps=ps: e.transpose(out=ps[0:64, fc * 128:(fc + 1) * 128],
                                                                          in_=psh[:, 8 + fc, c * 64:(c + 1) * 64], identity=ident[:]),
                         reads=VK + ["ident"], writes=[pk_])
                P.op("scalar", lambda e, c=c, ps=ps: e.copy(out=Vt[c][0:64, :], in_=ps[0:64, :]), reads=[pk_], writes=[("rw_Vt", c)])
            for d in range(2):
                P.op("vector", lambda e, d=d: e.tensor_tensor_scan(out=Lp[:].rearrange("p a b -> p (a b)"),
                                                                  data0=rmask[:].rearrange("p a b -> p (a b)"),
                                                                  data1=sgw[d][:].rearrange("p a b -> p (a b)"), initial=0.0,
                                                                  op0=ALU.mult, op1=ALU.add),
                     reads=[("rw_sa", 0, d), "rw_rmask"], writes=["rw_Lp"])
                Lp3 = Lp[:].rearrange("p a (c t) -> p (a c) t", t=64)
                Li3 = Li[:].rearrange("p a (c t) -> p (a c) t", t=64); Le3 = Le[:].rearrange("p a (c t) -> p (a c) t", t=64)
                tot_b = Lp3[:, :, 63:64].to_broadcast([128, 4 * NC, 64])
                if d == 0:
                    P.op("gpsimd", lambda e: e.tensor_copy(out=Li[:], in_=Lp[:]), reads=["rw_Lp"], writes=["rw_Li"])
                    P.op("vector", lambda e, d=d: e.tensor_tensor(out=Le[:], in0=Lp[:], in1=sgw[d][:], op=ALU.subtract),
                         reads=["rw_Lp", ("rw_sa", 0, d)], writes=["rw_Le"])
                else:
                    P.op("vector", lambda e, d=d: e.tensor_tensor(out=Le[:], in0=Lp[:], in1=sgw[d][:], op=ALU.subtract),
                         reads=["rw_Lp", ("rw_sa", 0, d)], writes=["rw_Le"])
                    P.op("vector", lambda e: e.tensor_tensor(out=Li3, in0=tot_b, in1=Le3, op=ALU.subtract), reads=["rw_Lp", "rw_Le"],
                         writes=["rw_Li"])
                    P.op("vector", lambda e: e.tensor_tensor(out=Le3, in0=tot_b, in1=Lp3, op=ALU.subtract), reads=["rw_Lp", "rw_Li"],
                         writes=["rw_Le"])
                P.op("scalar", lambda e: e.activation(out=etot[:].rearrange("p a c -> p (a c)"), in_=Lp3[:, :, 63], func=AF.Exp, scale=-CDEC),
                     reads=["rw_Lp"], writes=["rw_etot"])
                P.op("scalar", lambda e: e.activation(out=tA[:], in_=Li[:], func=AF.Exp, scale=-CDEC), reads=["rw_Li"], writes=TA)
                P.op("vector", lambda e: e.tensor_tensor(out=rt[:], in0=r_, in1=tA[:], op=ALU.mult), reads=RK + TA, writes=["rw_rt"])
                P.op("scalar", lambda e: e.activation(out=tB[:], in_=Le[:], func=AF.Exp, scale=-CDEC), reads=["rw_Le"], writes=TB)
                P.op("vector", lambda e: e.tensor_tensor(out=at[:], in0=kk[:], in1=tB[:], op=ALU.mult), reads=KKN + TB, writes=["rw_at"])
                P.op("scalar", lambda e: e.activation(out=tA[:], in_=Li[:], func=AF.Exp, scale=CDEC), reads=["rw_Li"], writes=TA)
                P.op("vector", lambda e, d=d: e.tensor_tensor(out=kt[:], in0=kd[d][:], in1=tA[:], op=ALU.mult), reads=[("rw_kd", d)] + TA, writes=["rw_kt"])
                P.op("vector", lambda e, d=d: e.tensor_tensor(out=bt[:], in0=aa[d][:], in1=tA[:], op=ALU.mult), reads=[("rw_sa", 1, d)] + TA, writes=["rw_bt"])
                et_b = etot[:].bitcast(F32).rearrange("p a c -> p (a c)").unsqueeze(2).to_broadcast([128, 4 * NC, 64])
                P.op("vector", lambda e: e.tensor_tensor(out=Kh[:].rearrange("p a (c t) -> p (a c) t", t=64),
                                                         in0=kt[:].bitcast(F32).rearrange("p a (c t) -> p (a c) t", t=64), in1=et_b, op=ALU.mult),
                     reads=["rw_kt", "rw_etot"], writes=["rw_Kh"])
                P.op("gpsimd", lambda e: e.tensor_tensor(out=Bh[:].rearrange("p a (c t) -> p (a c) t", t=64),
                                                         in0=bt[:].bitcast(F32).rearrange("p a (c t) -> p (a c) t", t=64), in1=et_b, op=ALU.mult),
                     reads=["rw_bt", "rw_etot"], writes=["rw_Bh"])
                for xi, (nm, X) in enumerate((("at", at), ("kt", kt), ("bt", bt))):
                    for eo in range(2):
                        if (xi + eo) % 2 == 0:
                            P.op("scalar", lambda e, nm=nm, X=X, eo=eo: e.activation(
                                out=mk[nm][eo][:], in_=X[:].bitcast(F32), func=AF.Copy, scale=bones[:, eo * 64:eo * 64 + 1]),
                                reads=["rw_" + nm, "rw_par"], writes=[("rw_mk", nm, eo)])
                        else:
                            P.op("vector", lambda e, nm=nm, X=X, eo=eo: e.tensor_scalar(
                                out=mk[nm][eo][:], in0=X[:].bitcast(F32), scalar1=bones[:, eo * 64:eo * 64 + 1], scalar2=None, op0=ALU.mult),
                                reads=["rw_" + nm, "rw_par"], writes=[("rw_mk", nm, eo)])
                cg0 = gx * NC
                mS = 0 if d == 0 else 2
                mST = 2 if d == 0 else 0
                mI = 1 if d == 0 else 3
                def chunk_gen(c, cg, u):
                    sl = slice(c * 64, (c + 1) * 64)
                    KhT, BhT, Mka, Mkr, Mbr, Nn, NTr, Z, Zn = (UT[u][k_] for k_ in ("KhT", "BhT", "Mka", "Mkr", "Mbr", "Nn", "NTr", "Z", "Zn"))
                    sm = UT[u]["summ"]; smk = ("rw_summ", u)
                    yield
                    hd = lambda X, h, sl=sl: X[:, h // 2, sl]
                    hdL = lambda nm, h, sl=sl: mk[nm][h % 2][:, h // 2, sl]
                    for src, skey, dst, dkey in ((at[:].bitcast(F32), "rw_at", None, ("rw_Z", u)), (Kh[:], "rw_Kh", KhT, ("rw_KhT", u)),
                                                 (Bh[:], "rw_Bh", BhT, ("rw_BhT", u))):
                        ps, pk_ = psn()
                        for fc in range(4):
                            P.op("tensor", lambda e, fc=fc, ps=ps, src=src, sl=sl: e.transpose(out=ps[0:64, fc * 128:(fc + 1) * 128],
                                                                                            in_=src[:, fc, sl], identity=ident[:]),
                                 reads=[skey, "ident"], writes=[pk_])
                        if dst is None:
                            P.op("scalar", lambda e, ps=ps: e.copy(out=Z[0:64, :, 0:64], in_=ps[0:64, :].rearrange("p (h k) -> p h k", h=8)),
                                 reads=[pk_], writes=[("rw_Z", u)])
                        else:
                            P.op("scalar", lambda e, ps=ps, dst=dst: e.copy(out=dst[0:64, :], in_=ps[0:64, :]), reads=[pk_], writes=[dkey])
                    if RW_ST <= 1:
                        return
                    yield
                    specs = (("bt", at, "MKbt", "rw_at", Nn[0], ("rw_N", u, 0), mS), ("at", bt, "MKat", "rw_bt", NTr[0], ("rw_NT", u, 0), mST),
                             ("kt", at, "MKkt", "rw_at", Mka, ("rw_Mka", u), mS), ("kt", rt, "MKkt", "rw_rt", Mkr, ("rw_Mkr", u), mI),
                             ("bt", rt, "MKbt", "rw_rt", Mbr, ("rw_Mbr", u), mI))
                    for si, (L_, R_, lk, rk_, dst, dkey, mi) in enumerate(specs):
                        ps, pk_ = psn()
                        for h in range(8):
                            P.op("tensor", lambda e, h=h, ps=ps, la=hdL(L_, h), ra=hd(R_, h): e.matmul(ps[0:64, h * 64:(h + 1) * 64], lhsT=la, rhs=ra,
                                                                                                     start=True, stop=True), reads=[("rw_mk", lk[2:], 0), ("rw_mk", lk[2:], 1), rk_], writes=[pk_])
                        P.op("vector" if si % 2 == 0 else "gpsimd" if False else "vector", lambda e, ps=ps, dst=dst, mi=mi: e.tensor_tensor(
                            out=dst[0:64], in0=ps[0:64, :].rearrange("p (h i) -> p h i", h=8), in1=mask[:, mi:mi + 1, :].to_broadcast([64, 8, 64]),
                            op=ALU.mult), reads=[pk_, "rw_par"], writes=[dkey])
                    if RW_ST <= 2:
                        return
                    yield
                    ps, pk_ = psn()
                    for h in range(8):
                        P.op("tensor", lambda e, h=h, ps=ps, c=c: e.matmul(ps[0:64, h * 64:(h + 1) * 64], lhsT=Mka[:, h, :],
                                                                          rhs=Vt[c][:, h * 64:(h + 1) * 64], start=True, stop=True),
                             reads=[("rw_Mka", u), ("rw_Vt", c)], writes=[pk_])
                    P.op("scalar", lambda e, ps=ps: e.copy(out=Z[0:64, :, 64:128], in_=ps[0:64, :].rearrange("p (h k) -> p h k", h=8)),
                         reads=[pk_], writes=[("rw_Z2", u)])
                    if RW_ST <= 3:
                        return
                    yield
                    cur = 0
                    for rnd in range(RW_RND):
                        N_, NT_ = Nn[cur], NTr[cur]
                        nk, ntk = ("rw_N", u, cur), ("rw_NT", u, cur)
                        pz = [psn(), psn()]
                        for h in range(8):
                            ps, pk_ = pz[h // 4]
                            P.op("tensor", lambda e, h=h, ps=ps, N_=N_: e.matmul(ps[0:64, (h % 4) * 128:(h % 4 + 1) * 128], lhsT=N_[:, h, :],
                                                                                rhs=Z[:, h, :], start=True, stop=True),
                                 reads=[nk, ("rw_Z", u), ("rw_Z2", u)], writes=[pk_])
                        for half in range(2):
                            ps, pk_ = pz[half]
                            P.op("vector", lambda e, half=half, ps=ps, rnd=rnd: e.tensor_tensor(
                                out=Z[0:64, half * 4:(half + 1) * 4, :], in0=Z[0:64, half * 4:(half + 1) * 4, :].bitcast(F32),
                                in1=ps[0:64, :].rearrange("p (h k) -> p h k", h=4), op=(ALU.subtract if rnd == 0 else ALU.add)),
                                reads=[pk_, ("rw_Z", u), ("rw_Z2", u)], writes=[("rw_Z", u), ("rw_Z2", u)])
                        yield
                        if rnd < 5:
                            nxt = 1 - cur
                            ps, pk_ = psn()
                            for h in range(8):
                                P.op("tensor", lambda e, h=h, ps=ps, N_=N_, NT_=NT_: e.matmul(ps[0:64, h * 64:(h + 1) * 64], lhsT=NT_[:, h, :],
                                                                                             rhs=N_[:, h, :], start=True, stop=True),
                                     reads=[nk, ntk], writes=[pk_])
                            P.op("scalar", lambda e, ps=ps, nxt=nxt: e.copy(out=Nn[nxt][0:64], in_=ps[0:64, :].rearrange("p (h k) -> p h k", h=8)),
                                 reads=[pk_], writes=[("rw_N", u, nxt)])
                            if rnd < 4:
                                ps, pk_ = psn()
                                for h in range(8):
                                    P.op("tensor", lambda e, h=h, ps=ps, N_=N_, NT_=NT_: e.matmul(ps[0:64, h * 64:(h + 1) * 64], lhsT=N_[:, h, :],
                                                                                                 rhs=NT_[:, h, :], start=True, stop=True),
                                         reads=[nk, ntk], writes=[pk_])
                                P.op("gpsimd" if False else "scalar", lambda e, ps=ps, nxt=nxt: e.copy(
                                    out=NTr[nxt][0:64], in_=ps[0:64, :].rearrange("p (h k) -> p h k", h=8)), reads=[pk_], writes=[("rw_NT", u, nxt)])
                            cur = nxt
                        yield
                    P.op("scalar", lambda e: e.activation(out=Zn[0:64], in_=Z[0:64].bitcast(F32), func=AF.Copy, scale=-1.0),
                         reads=[("rw_Z", u), ("rw_Z2", u)], writes=[("rw_Zn", u)])
                    if RW_ST <= 4:
                        return
                    yield
                    ps, pk_ = psn()
                    for h in range(8):
                        P.op("tensor", lambda e, h=h, ps=ps: e.matmul(ps[0:64, h * 64:(h + 1) * 64], lhsT=Zn[:, h, 0:64],
                                                                     rhs=BhT[:, h * 64:(h + 1) * 64], start=True, stop=True),
                             reads=[("rw_Zn", u), ("rw_BhT", u)], writes=[pk_])
                    P.op("scalar", lambda e, ps=ps, sm=sm: e.copy(out=sm[:, 0:512], in_=ps[0:64, :]), reads=[pk_], writes=[(smk, 0)])
                    if RW_ST <= 5:
                        return
                    yield
                    ps, pk_ = psn()
                    for h in range(8):
                        P.op("tensor", lambda e, h=h, ps=ps, c=c: e.matmul(ps[0:64, h * 64:(h + 1) * 64], lhsT=KhT[:, h * 64:(h + 1) * 64],
                                                                          rhs=Vt[c][:, h * 64:(h + 1) * 64], start=True, stop=False),
                             reads=[("rw_KhT", u), ("rw_Vt", c)], writes=[pk_])
                        P.op("tensor", lambda e, h=h, ps=ps: e.matmul(ps[0:64, h * 64:(h + 1) * 64], lhsT=BhT[:, h * 64:(h + 1) * 64],
                                                                     rhs=Zn[:, h, 64:128], start=False, stop=True),
                             reads=[("rw_BhT", u), ("rw_Zn", u)], writes=[pk_])
                    P.op("vector", lambda e, ps=ps, sm=sm: e.tensor_copy(out=sm[:, 512:1024], in_=ps[0:64, :]), reads=[pk_], writes=[(smk, 1)])
                    if RW_ST <= 6:
                        return
                    yield
                    ps, pk_ = psn()
                    for h in range(8):
                        p0 = (h % 2) * 64
                        P.op("tensor", lambda e, h=h, ps=ps, p0=p0, ra=hd(rt, h): e.matmul(ps[0:64, h * 64:(h + 1) * 64],
                                                                                          lhsT=identr[:, p0:p0 + 64], rhs=ra,
                                                                                          start=True, stop=False),
                             reads=["rw_identr", "rw_rt"], writes=[pk_])
                        P.op("tensor", lambda e, h=h, ps=ps: e.matmul(ps[0:64, h * 64:(h + 1) * 64], lhsT=Zn[:, h, 0:64], rhs=Mbr[:, h, :],
                                                                     start=False, stop=True), reads=[("rw_Zn", u), ("rw_Mbr", u)], writes=[pk_])
                    P.op("scalar", lambda e, ps=ps, sm=sm: e.copy(out=sm[:, 1024:1536], in_=ps[0:64, :]), reads=[pk_], writes=[(smk, 2)])
                    if RW_ST <= 7:
                        return
                    yield
                    ps, pk_ = psn()
                    for h in range(8):
                        P.op("tensor", lambda e, h=h, ps=ps, c=c: e.matmul(ps[0:64, h * 64:(h + 1) * 64], lhsT=Mkr[:, h, :],
                                                                          rhs=Vt[c][:, h * 64:(h + 1) * 64], start=True, stop=False),
                             reads=[("rw_Mkr", u), ("rw_Vt", c)], writes=[pk_])
                        P.op("tensor", lambda e, h=h, ps=ps: e.matmul(ps[0:64, h * 64:(h + 1) * 64], lhsT=Mbr[:, h, :], rhs=Zn[:, h, 64:128],
                                                                     start=False, stop=True), reads=[("rw_Mbr", u), ("rw_Zn", u)], writes=[pk_])
                    P.op("vector", lambda e, ps=ps, sm=sm: e.tensor_copy(out=sm[:, 1536:2048], in_=ps[0:64, :]), reads=[pk_], writes=[(smk, 3)])
                    if RW_ST <= 8:
                        return
                    yield
                    ps, pk_ = psn()
                    for h in range(8):
                        p0 = (h % 2) * 64
                        P.op("tensor", lambda e, h=h, ps=ps, p0=p0, c=c: e.matmul(ps[0:64, h:h + 1], lhsT=ident[:, p0:p0 + 64],
                                                                                 rhs=etot[:].bitcast(F32)[:, h // 2, c:c + 1], start=True, stop=True),
                             reads=["ident", "rw_etot"], writes=[pk_])
                    P.op("vector", lambda e, ps=ps, sm=sm: e.tensor_copy(out=sm[:, 2048:2056], in_=ps[0:64, 0:8]), reads=[pk_], writes=[(smk, 4)])
                    P.dma("gpsimd", summ_d[d, cg], sm[:], reads=[(smk, i) for i in range(5)], writes=[("rw_summd", d, cg)])
                gens = [chunk_gen(c, cg0 + c, c % 2) for c in range(NC if RW_CH else 0)]
                while gens:
                    for g_ in list(gens):
                        try:
                            next(g_)
                        except StopIteration:
                            gens.remove(g_)
        if "rwprep" in dbg and l == 0:
            for nm, src in (("g", g_d), ("bv", bv_d)):
                d_ = dbg_out("rw_" + nm, [512, T])
                P.dma("sync", d_, src, reads=[("rw_" + nm, i) for i in range(9)], writes=["OUT_dbgrw" + nm])
        P.barrier(); P.release(m1)
        if RW_PH < 2:
            return
        ST = [P.sbuf("rw_ST%d" % i, [64, 8, 64]) for i in range(2)]
        sm2 = [P.sbuf("rw_sm2_%d" % i, [64, 2056]) for i in range(3)]
        yt = [P.sbuf("rw_yt%d" % i, [64, 512]) for i in range(2)]
        stt = P.sbuf("rw_stt", [64, 8, 64])
        step = 0
        for d in range(2):
            order = list(range(NCH)) if d == 0 else [3, 2, 1, 0] + list(range(NCH - 1, 3, -1))
            cur = 0
            P.op("vector", lambda e: e.memset(ST[0][:], 0.0), reads=[("rw_ST", 0)], writes=[("rw_ST", 0)])
            for cg in order:
                b3 = step % 3
                b2 = step % 2
                step += 1
                P.dma("sync", sm2[b3][:], summ_d[d, cg], reads=[], writes=[("rw_sm2", b3)])
                S_, Sn_ = ST[cur], ST[1 - cur]
                psy, pyk = psn(); pss, psk = psn()
                for h in range(8):
                    P.op("tensor", lambda e, h=h, psy=psy, S_=S_, b3=b3: e.matmul(psy[0:64, h * 64:(h + 1) * 64],
                                                                               lhsT=sm2[b3][:, 1024 + h * 64:1024 + (h + 1) * 64], rhs=S_[:, h, :],
                                                                               start=True, stop=True),
                         reads=[("rw_sm2", b3), ("rw_ST", cur)], writes=[pyk])
                for h in range(8):
                    P.op("tensor", lambda e, h=h, pss=pss, S_=S_, b3=b3: e.matmul(pss[0:64, h * 64:(h + 1) * 64],
                                                                               lhsT=sm2[b3][:, h * 64:(h + 1) * 64], rhs=S_[:, h, :],
                                                                               start=True, stop=True),
                         reads=[("rw_sm2", b3), ("rw_ST", cur)], writes=[psk])
                P.op("gpsimd", lambda e, S_=S_, b3=b3: e.tensor_tensor(out=stt[:], in0=S_[:], in1=sm2[b3][:, 2048:2056].unsqueeze(2).to_broadcast([64, 8, 64]),
                                                                      op=ALU.mult), reads=[("rw_ST", cur), ("rw_sm2", b3)], writes=["rw_stt"])
                P.op("gpsimd", lambda e, b3=b3: e.tensor_tensor(out=stt[:], in0=stt[:], in1=sm2[b3][:, 512:1024].rearrange("p (h k) -> p h k", h=8),
                                                               op=ALU.add), reads=["rw_stt", ("rw_sm2", b3)], writes=["rw_stt"])
                P.op("vector", lambda e, pss=pss, Sn_=Sn_: e.tensor_tensor(out=Sn_[:], in0=pss[0:64, :].rearrange("p (h k) -> p h k", h=8),
                                                                          in1=stt[:], op=ALU.add), reads=[psk, "rw_stt"], writes=[("rw_ST", 1 - cur)])
                P.op("vector", lambda e, psy=psy, b2=b2, b3=b3: e.tensor_tensor(out=yt[b2][:], in0=psy[0:64, :], in1=sm2[b3][:, 1536:2048], op=ALU.add),
                     reads=[pyk, ("rw_sm2", b3)], writes=[("rw_yt", b2)])
                P.dma("gpsimd", y_d[d, cg * 64:(cg + 1) * 64, :], yt[b2][:], reads=[("rw_yt", b2)], writes=[("rw_yd", d, cg // 2)])
                cur = 1 - cur
        if "rwy" in dbg and l == 0:
            d_ = dbg_out("rw_y", [2, T, 512])
            P.dma("sync", d_, y_d, reads=[("rw_yd", d, t) for d in range(2) for t in range(NT)], writes=["OUT_dbgrwy"])
        P.barrier(); P.release(m1)
        if RW_PH < 3:
            return
        y0 = [P.sbuf("rw_y0_%d" % i, [128, 512]) for i in range(2)]; y1 = [P.sbuf("rw_y1_%d" % i, [128, 512]) for i in range(2)]
        sqt = [P.sbuf("rw_sq%d" % i, [128, 512]) for i in range(2)]
        st8 = [P.sbuf("rw_st8_%d" % i, [128, 4, 8]) for i in range(2)]
        bvt = [P.sbuf("rw_bvt%d" % i, [128, 4, 128]) for i in range(2)]; gt = [P.sbuf("rw_gt%d" % i, [128, 4, 128]) for i in range(2)]
        of = [P.sbuf("rw_of%d" % i, [128, 4, 128]) for i in range(2)]; ob = [P.sbuf("rw_ob%d" % i, [128, 4, 128], BF16) for i in range(2)]
        orw_v = orw_d.rearrange("(k p) t -> p k t", p=128)
        for t in range(NT):
            s_ = t % 2
            gi = 0 if t < 2 else 1 + (t - 2) // 4
            P.dma("sync", y0[s_][:], y_d[0, t * 128:(t + 1) * 128, :], reads=[("rw_yd", 0, t)], writes=[("rw_y0", s_)])
            P.dma("sync", y1[s_][:], y_d[1, t * 128:(t + 1) * 128, :], reads=[("rw_yd", 1, t)], writes=[("rw_y1", s_)])
            P.dma("sync", bvt[s_][:], bv_v[:, :, t * 128:(t + 1) * 128], reads=[("rw_bv", i) for i in range(9)], writes=[("rw_bvt", s_)])
            P.dma("sync", gt[s_][:], g_v[:, :, t * 128:(t + 1) * 128], reads=[("rw_g", i) for i in range(9)], writes=[("rw_gt", s_)])
            ys = y0[s_]; st = st8[s_]
            P.op("gpsimd", lambda e, s_=s_: e.tensor_tensor(out=y0[s_][:], in0=y0[s_][:], in1=y1[s_][:], op=ALU.add),
                 reads=[("rw_y0", s_), ("rw_y1", s_)], writes=[("rw_y0", s_)])
            y3 = ys[:].rearrange("p (h n) -> p h n", h=8)
            P.op("vector", lambda e, st=st, y3=y3: e.tensor_reduce(out=st[:, 0, :], in_=y3, axis=AX.X, op=ALU.add), reads=[("rw_y0", s_)],
                 writes=[("rw_st8", s_)])
            P.op("scalar", lambda e, s_=s_, ys=ys: e.activation(out=sqt[s_][:], in_=ys[:], func=AF.Square), reads=[("rw_y0", s_)],
                 writes=[("rw_sq", s_)])
            P.op("vector", lambda e, st=st, s_=s_: e.tensor_reduce(out=st[:, 1, :], in_=sqt[s_][:].rearrange("p (h n) -> p h n", h=8), axis=AX.X,
                                                                  op=ALU.add), reads=[("rw_sq", s_), ("rw_st8", s_)], writes=[("rw_st8", s_)])
            P.op("vector", lambda e, st=st: e.tensor_scalar(out=st[:, 0, :], in0=st[:, 0, :], scalar1=1.0 / 64, scalar2=None, op0=ALU.mult),
                 reads=[("rw_st8", s_)], writes=[("rw_st8", s_)])
            P.op("vector", lambda e, st=st: e.tensor_tensor(out=st[:, 2, :], in0=st[:, 0, :], in1=st[:, 0, :], op=ALU.mult),
                 reads=[("rw_st8", s_)], writes=[("rw_st8", s_)])
            P.op("vector", lambda e, st=st: e.scalar_tensor_tensor(out=st[:, 2, :], in0=st[:, 1, :], scalar=1.0 / 64, in1=st[:, 2, :],
                                                                  op0=ALU.mult, op1=ALU.subtract), reads=[("rw_st8", s_)], writes=[("rw_st8", s_)])
            P.op("scalar", lambda e, st=st: e.activation(out=st[:, 3, :], in_=st[:, 2, :], func=AF.Sqrt, bias=eps_gn[:, 0:1], scale=1.0),
                 reads=[("rw_st8", s_), "rw_eps"], writes=[("rw_st8", s_)])
            P.op("vector", lambda e, st=st: e.reciprocal(out=st[:, 3, :], in_=st[:, 3, :]), reads=[("rw_st8", s_)], writes=[("rw_st8", s_)])
            P.op("vector", lambda e, st=st, y3=y3: e.tensor_tensor(out=y3, in0=y3, in1=st[:, 0, :].unsqueeze(2).to_broadcast([128, 8, 64]),
                                                                  op=ALU.subtract), reads=[("rw_y0", s_), ("rw_st8", s_), ("rw_sq", s_)],
                 writes=[("rw_y0", s_)])
            P.op("vector", lambda e, st=st, y3=y3: e.tensor_tensor(out=y3, in0=y3, in1=st[:, 3, :].unsqueeze(2).to_broadcast([128, 8, 64]),
                                                                  op=ALU.mult), reads=[("rw_y0", s_), ("rw_st8", s_)], writes=[("rw_y0", s_)])
            ps, pk_ = psn()
            for fc in range(4):
                P.op("tensor", lambda e, fc=fc, ps=ps, ys=ys: e.transpose(out=ps[:, fc * 128:(fc + 1) * 128], in_=ys[:, fc * 128:(fc + 1) * 128],
                                                                        identity=ident[:]), reads=[("rw_y0", s_), "ident"], writes=[pk_])
            for fc in range(4):
                P.op("vector", lambda e, fc=fc, ps=ps, s_=s_: e.tensor_scalar(out=of[s_][:, fc, :], in0=ps[:, fc * 128:(fc + 1) * 128],
                                                                            scalar1=vec[:, 3, fc:fc + 1], scalar2=vec[:, 4, fc:fc + 1],
                                                                            op0=ALU.mult, op1=ALU.add),
                     reads=[pk_, "rw_par"], writes=[("rw_of", s_, fc)])
            P.op("gpsimd", lambda e, s_=s_: e.tensor_tensor(out=of[s_][:], in0=of[s_][:], in1=bvt[s_][:], op=ALU.add),
                 reads=[("rw_of", s_, fc) for fc in range(4)] + [("rw_bvt", s_)], writes=[("rw_of2", s_)])
            P.op("gpsimd", lambda e, s_=s_: e.tensor_tensor(out=ob[s_][:], in0=of[s_][:], in1=gt[s_][:], op=ALU.mult),
                 reads=[("rw_of2", s_), ("rw_gt", s_)], writes=[("rw_ob", s_)])
            P.dma("gpsimd", orw_v[:, :, t * 128:(t + 1) * 128], ob[s_][:], reads=[("rw_ob", s_)], writes=[("orw", gi)])
        if "orw" in dbg and l == 0:
            d_ = dbg_out("orw", [512, T], BF16)
            P.dma("sync", d_, orw_d, reads=[("orw", g) for g in range(9)], writes=["OUT_dbgorw"])

    xT_v = xT.rearrange("(k p) t -> p k t", p=128)
    def stage_inproj(l):
        hT = P.sbuf("hT", [128, 8, T], BF16)
        xg = [P.sbuf("xg%d" % i, [128, 8, 512]) for i in range(2)]
        wblk = [P.sbuf("wblk%d" % i, [128, 8, 512]) for i in range(2)]
        wbf = [P.sbuf("wbf%d" % i, [128, 8, 512], BF16) for i in range(2)]
        ost = [P.sbuf("ost%d" % i, [128, 512]) for i in range(4)]
        ostb = [P.sbuf("ostb%d" % i, [128, 512], BF16) for i in range(4)]
        for gi, (t0, n) in enumerate(GROUPS):
            s = gi % 2
            c = 1 if gi == 0 else 0
            P.dma("sync", xg[s][:, :, :n], xT_v[:, :, t0:t0 + n], reads=[("xT", gi)], writes=[("xg", s)])
            for k in range(8):
                eng = "vector" if k % 2 == 0 else "gpsimd"
                P.op(eng, lambda e, s=s, k=k, c=c, l=l, t0=t0, n=n: e.tensor_scalar(
                    out=hT[:, k, t0:t0 + n], in0=xg[s][:, k, :n], scalar1=mod1[:, l, 8 + k, c:c + 1],
                    scalar2=mod[:, l, k, c:c + 1], op0=ALU.mult, op1=ALU.add),
                    reads=[("xg", s), "mod", "mod1"], writes=[("hT", gi)])
        if "hT" in dbg and l == 0:
            d = dbg_out("hT", [128, 8 * T], BF16)
            P.dma("sync", d, hT[:].rearrange("p k t -> p (k t)"), reads=[("hT", g) for g in range(9)], writes=["OUT_dbghT"])
        blocks = [(3072, 512, "fm_bf", qT, 0), (3584, 512, "fm_bf", kT, 0), (4096, 512, "tm_bf", vtok, 0),
                  (4608, 512, "fm", prw, 0), (5120, 512, "fm", prw, 512), (5632, 512, "fm", prw, 1024),
                  (6144, 384, "fm", prw, 1536), (6528, 512, "fm", sguU, 0), (7040, 512, "tm", sguV, 0)]
        nev = 0
        for bi, (c0, ncol, kind, dst, r0) in enumerate(blocks):
            s = bi % 2
            P.dma("sync", wblk[s][:, :, :ncol], w_in[l, :, c0:c0 + ncol].rearrange("(k p) m -> p k m", p=128),
                  writes=[("wblk", s)])
            for k in range(8):
                eng = "gpsimd" if k % 2 == 0 else "vector"
                P.op(eng, lambda e, s=s, k=k, ncol=ncol: e.tensor_copy(out=wbf[s][:, k, :ncol], in_=wblk[s][:, k, :ncol]),
                     reads=[("wblk", s)], writes=[("wbf", s)])
            if kind.startswith("fm"):
                for gi, (t0, n) in enumerate(GROUPS):
                    for mi in range(ncol // 128):
                        pb = 2 + nev % 4
                        for k in range(8):
                            P.op("tensor", lambda e, s=s, k=k, mi=mi, t0=t0, n=n, pb=pb: e.matmul(
                                PS[pb][:, :n], lhsT=wbf[s][:, k, mi * 128:(mi + 1) * 128], rhs=hT[:, k, t0:t0 + n],
                                start=(k == 0), stop=(k == 7)), reads=[("wbf", s), ("hT", gi)], writes=[("ps", pb)])
                        so = nev % 4
                        o_t = ostb[so] if kind == "fm_bf" else ost[so]
                        okey = ("ostb", so) if kind == "fm_bf" else ("ost", so)
                        if nev % 2 == 0:
                            P.op("scalar", lambda e, o_t=o_t, pb=pb, n=n: e.copy(out=o_t[:, :n], in_=PS[pb][:, :n]),
                                 reads=[("ps", pb)], writes=[okey])
                        else:
                            P.op("vector", lambda e, o_t=o_t, pb=pb, n=n: e.tensor_copy(out=o_t[:, :n], in_=PS[pb][:, :n]),
                                 reads=[("ps", pb)], writes=[okey])
                        rr = r0 + mi * 128
                        P.dma("gpsimd", dst[rr:rr + 128, t0:t0 + n], o_t[:, :n], reads=[okey], writes=[(dst.name, "fm", gi)])
                        nev += 1
            else:
                for t in range(NT):
                    pb = 2 + nev % 4
                    gi = 0 if t < 2 else 1 + (t - 2) // 4
                    for k in range(8):
                        P.op("tensor", lambda e, s=s, k=k, t=t, pb=pb: e.matmul(
                            PS[pb][:, :], lhsT=hT[:, k, t * 128:(t + 1) * 128], rhs=wbf[s][:, k, :],
                            start=(k == 0), stop=(k == 7)), reads=[("wbf", s), ("hT", gi)], writes=[("ps", pb)])
                    so = nev % 4
                    o_t = ostb[so] if kind == "tm_bf" else ost[so]
                    okey = ("ostb", so) if kind == "tm_bf" else ("ost", so)
                    if nev % 2 == 0:
                        P.op("scalar", lambda e, o_t=o_t, pb=pb: e.copy(out=o_t[:], in_=PS[pb][:]), reads=[("ps", pb)], writes=[okey])
                    else:
                        P.op("vector", lambda e, o_t=o_t, pb=pb: e.tensor_copy(out=o_t[:], in_=PS[pb][:]), reads=[("ps", pb)], writes=[okey])
                    P.dma("gpsimd", dst[t * 128:(t + 1) * 128, :], o_t[:], reads=[okey], writes=[(dst.name, "tm", t)])
                    nev += 1
        if "p" in dbg and l == 0:
            for nm, src, shp, dt in (("qT", qT, [512, T], BF16), ("kT", kT, [512, T], BF16), ("vtok", vtok, [T, 512], BF16),
                                     ("prw", prw, [1920, T], F32), ("sguU", sguU, [512, T], F32), ("sguV", sguV, [T, 512], F32)):
                d = dbg_out(nm, shp, dt)
                rk = [(src.name, "fm", g) for g in range(9)] + [(src.name, "tm", t) for t in range(NT)]
                P.dma("sync", d, src, reads=rk, writes=["OUT_dbg" + nm])
    for l in range(n_layers):
        stage_inproj(l)
        if stop == "s1":
            return P, dbg_t
        P.barrier(); P.release(m0)
        stage_sgu(l)
        if stop == "s2":
            return P, dbg_t
        P.barrier(); P.release(m0)
        stage_na(l)
        if stop == "s3":
            return P, dbg_t
        P.barrier(); P.release(m0)
        stage_rwkv(l)
        if stop == "s4":
            return P, dbg_t
        P.barrier(); P.release(m0)
        stage_merge(l)
        if stop == "s5":
            return P, dbg_t
        P.barrier(); P.release(m0)
        stage_ffn(l)
        if stop == "s6":
            return P, dbg_t
        P.barrier(new_epoch=True); P.release(m0)
    if mode == "full":
        stage_final()
    else:
        for gi, (t0, n) in enumerate(GROUPS):
            P.dma("sync", xT_out[:, t0:t0 + n], xT[:, t0:t0 + n], reads=[("xT", gi)], writes=["OUT_x%d" % gi])
    return P, dbg_t


def host_inputs(inputs, b, l0=0, nl=DEPTH, xT=None):
    m = {}
    if xT is None:
        x = np.asarray(inputs["x"], np.float32)
        ctx = np.asarray(inputs["ctx"], np.float32)
        m["xin"] = np.ascontiguousarray(np.concatenate([ctx[b], x[b]], axis=0))
    else:
        m["xT_in"] = xT
    m["ccT"] = np.ascontiguousarray(np.stack([np.asarray(inputs["c"], np.float32)[b], np.asarray(inputs["c_ctx"], np.float32)], axis=1))
    m["ident"] = np.eye(128, dtype=np.float32)
    f = lambda k: np.asarray(inputs[k], np.float32)[l0:l0 + nl]
    m["w_ada"] = f("w_ada")
    m["b_adaT"] = np.ascontiguousarray(f("b_ada").reshape(nl, 48, 128).transpose(0, 2, 1))
    m["w_in"] = f("w_in")
    m["sgu_lng"] = np.ascontiguousarray(np.broadcast_to(f("sgu_ln_g")[:, None, :], (nl, 128, 512)))
    m["sgu_lnb"] = np.ascontiguousarray(np.broadcast_to(f("sgu_ln_b")[:, None, :], (nl, 128, 512)))
    m["sgu_wT"] = np.ascontiguousarray(f("sgu_w").transpose(0, 3, 1, 2))
    m["sgu_bB"] = np.ascontiguousarray(np.broadcast_to(f("sgu_b")[:, None, :, :], (nl, 64, 8, 128)))
    m["na_bias"] = na_bias_layout(f("na_rpb"))
    fmN = lambda a, n: a.reshape(a.shape[0], n, 128).transpose(2, 0, 1)
    m["rw_mu"] = np.ascontiguousarray(np.stack([fmN(f("rwkv_mu_prev"), 15), fmN(f("rwkv_mu_next"), 15)], axis=2))
    w0 = f("rwkv_w0").reshape(nl, 2, 4, 128).transpose(3, 0, 1, 2); a0 = f("rwkv_a0").reshape(nl, 2, 4, 128).transpose(3, 0, 1, 2)
    m["rw_w0a0"] = np.ascontiguousarray(np.stack([w0, a0], axis=2))
    m["rw_w2"] = np.ascontiguousarray(f("rwkv_w2").reshape(nl, 128, 512)); m["rw_a2"] = np.ascontiguousarray(f("rwkv_a2").reshape(nl, 128, 512))
    m["rw_g2"] = f("rwkv_g2")
    m["rw_vec"] = np.ascontiguousarray(np.stack([fmN(f(k).reshape(nl, 512), 4) for k in
                                                 ("rwkv_k_k", "rwkv_k_a", "rwkv_r_k", "rwkv_gn_g", "rwkv_gn_b")], axis=2))
    jj, ii = np.meshgrid(np.arange(64), np.arange(64), indexing="ij")
    m["mask64"] = np.ascontiguousarray(np.stack([jj < ii, jj <= ii, jj > ii, jj >= ii], axis=1).astype(np.float32))
    bo = np.zeros((128, 128), np.float32); bo[:64, :64] = 1; bo[64:, 64:] = 1
    m["bones"] = bo
    m["w_branch"] = f("w_branch"); m["w_out"] = f("w_out"); m["ffn_w_gu"] = f("ffn_w_gu"); m["ffn_w_down"] = f("ffn_w_down")
    fm8 = lambda a: a.reshape(nl, 8, 128).transpose(2, 0, 1)
    m["lnp"] = np.ascontiguousarray(np.stack([fm8(f("ln1_g")), fm8(f("ln1_b")), fm8(f("ln2_g")), fm8(f("ln2_b"))], axis=2))
    return m


_NA_IDX = None


def na_bias_layout(rpb):
    global _NA_IDX
    if _NA_IDX is None:
        ridx = np.zeros((5, 128, 896), np.int64); cidx = np.zeros((5, 128, 896), np.int64); valid = np.zeros((5, 128, 896), bool)
        zero = np.zeros((5, 128, 896), bool)
        for pi, r in enumerate((0, 2, 4, 60, 62)):
            kb = min(max(r - 4, 0), 54)
            for qi in range(128):
                qr, c = r + qi // 64, qi % 64
                row0 = min(max(qr - 4, 0), 56); col0 = min(max(c - 8, 0), 48)
                for j in range(10):
                    kr = kb + j
                    if not (row0 <= kr < row0 + 8):
                        continue
                    for kc in range(col0, col0 + 16):
                        ridx[pi, qi, j * 64 + kc] = kr - qr + 7; cidx[pi, qi, j * 64 + kc] = kc - c + 15; valid[pi, qi, j * 64 + kc] = True
            zero[pi, :, 640:] = True
        _NA_IDX = (ridx, cidx, valid, zero)
    ridx, cidx, valid, zero = _NA_IDX
    g = rpb[:, :, ridx, cidx]
    g = np.where(valid[None, None], g, np.float32(-30000.0))
    g = np.where(zero[None, None], np.float32(0.0), g)
    return np.ascontiguousarray(g.transpose(0, 2, 3, 1, 4)).astype(np.float32)


_CACHE = {}


def _host_params(inputs, l0, nl):
    key = (id(inputs.get("w_in")), l0, nl)
    if key not in _CACHE:
        m = host_inputs(inputs, 0, l0, nl, xT=np.zeros((1,), np.float32))
        m.pop("xT_in"); m.pop("ccT")
        _CACHE[key] = m
    return _CACHE[key]


def kernel(**inputs):
    P, _ = build(n_layers=DEPTH, mode="full")
    nc = P.finalize()
    par = dict(host_inputs(inputs, 0))
    par.pop("xin"); par.pop("ccT")
    in_maps = []
    for core in range(8):
        m = dict(par)
        hb = host_inputs_x(inputs, core // 2)
        m.update(hb)
        in_maps.append(m)
    res = run_bass_kernel_spmd(nc, in_maps, core_ids=list(range(8)))
    return np.stack([res.results[2 * b]["out"] for b in range(4)], axis=0).astype(np.float32)


def host_inputs_x(inputs, b):
    x = np.asarray(inputs["x"], np.float32); ctx = np.asarray(inputs["ctx"], np.float32)
    return {"xin": np.ascontiguousarray(np.concatenate([ctx[b], x[b]], axis=0)),
            "ccT": np.ascontiguousarray(np.stack([np.asarray(inputs["c"], np.float32)[b], np.asarray(inputs["c_ctx"], np.float32)], axis=1))}
```

```python
import numpy as np
import concourse.bass as bass
import concourse.mybir as mybir
from concourse.bass_utils import run_bass_kernel_spmd

F32 = mybir.dt.float32
BF16 = mybir.dt.bfloat16
F32R = mybir.dt.float32r
ALU = mybir.AluOpType
AF = mybir.ActivationFunctionType
AX = mybir.AxisListType

D = 1024
DEPTH = 4
LCTX = 256
SEQ = 4096
T = LCTX + SEQ
NT = T // 128
GROUPS = [(0, 256)] + [(256 + 512 * i, 512) for i in range(8)]
D_IN = 7552
D_FF = 2816
ALPHA = (2 * DEPTH) ** 0.25


class Prog:
    ENGS = ("tensor", "vector", "scalar", "gpsimd", "sync")

    def __init__(self):
        self.nc = bass.Bass("TRN2", target_bir_lowering=False)
        self.ops = []
        self.n_dma_sems = 32
        arena = self.nc.alloc_sbuf_tensor("arena", [128, 212000], mybir.dt.uint8)
        self.arena_base = self.nc.lookup_mloc(arena).addr
        self.arena_size = 212000
        self.sp = 0
        self.nalloc = 0

    def dram(self, name, shape, dtype, kind="Internal"):
        return self.nc.dram_tensor(name, list(shape), dtype, kind=kind).ap()

    def sbuf(self, name, shape, dtype=F32):
        esz = {F32: 4, BF16: 2, F32R: 4}[dtype]
        nbytes = int(np.prod(shape[1:])) * esz
        off = (self.sp + 63) // 64 * 64
        assert off + nbytes <= self.arena_size, "SBUF arena overflow at %s: %d + %d" % (name, off, nbytes)
        self.sp = off + nbytes
        self.nalloc += 1
        return self.nc.alloc_sbuf_tensor_at("%s_%d" % (name, self.nalloc), list(shape), dtype, offset=self.arena_base + off)

    def mark(self):
        return self.sp

    def release(self, m):
        self.sp = m

    def barrier(self, new_epoch=False):
        self.ops.append(("barrier", new_epoch, (), (), False))

    def psum(self, name, shape, dtype=F32):
        return self.nc.alloc_psum_tensor(name, list(shape), dtype)

    def op(self, eng, fn, reads=(), writes=(), dma=False):
        self.ops.append((eng, fn, tuple(reads), tuple(writes), dma))

    def dma(self, eng, out, in_, reads=(), writes=(), slow=False):
        if slow:
            self.op(eng, lambda e: e.dma_start(out=out, in_=in_, allow_slow_non_contiguous=True), reads, writes, dma=True)
        else:
            self.op(eng, lambda e: e.dma_start(out=out, in_=in_), reads, writes, dma=True)

    def finalize(self):
        nc = self.nc
        ops = self.ops
        last_w = {}
        readers = {}
        deps = []
        force_sig = set()
        last_on = {}
        for i, (eng, fn, rd, wr, isdma) in enumerate(ops):
            if eng == "barrier":
                force_sig.update(last_on.values())
                last_w = {}
                readers = {}
                deps.append(set())
                continue
            if not isdma:
                last_on[eng] = i
            d = set()
            for k in rd:
                if k in last_w:
                    d.add(last_w[k])
            for k in wr:
                if k in last_w:
                    d.add(last_w[k])
                d.update(readers.get(k, {}).values())
            for k in rd:
                readers.setdefault(k, {})[(eng, i) if isdma else eng] = i
            for k in wr:
                last_w[k] = i
                readers[k] = {}
            d.discard(i)
            if eng == "tensor" and not isdma:
                d = {j for j in d if not (ops[j][0] == "tensor" and not ops[j][4])}
            deps.append(d)
        has_dep = [False] * len(ops)
        for d in deps:
            for j in d:
                has_dep[j] = True
        for j in force_sig:
            has_dep[j] = True
        sems = {e: nc.alloc_semaphore("s_" + e) for e in self.ENGS}
        dsems = [nc.alloc_semaphore("d%d" % j) for j in range(self.n_dma_sems)]
        cnt = {e: 0 for e in self.ENGS}
        sig = [None] * len(ops)
        waited = {e: {} for e in self.ENGS}
        ndma = 0
        final = {}
        dlast = {}
        for i, (e, fn, rd, wr, isdma) in enumerate(ops):
            if e == "barrier":
                for e1 in self.ENGS:
                    eng1 = getattr(nc, e1)
                    for e2 in self.ENGS:
                        if cnt[e2] > waited[e1].get(sems[e2].num, 0):
                            waited[e1][sems[e2].num] = cnt[e2]
                            eng1.wait_ge(sems[e2], cnt[e2])
                    for jn, (sd, vd) in dlast.items():
                        if vd > waited[e1].get(jn, 0):
                            waited[e1][jn] = vd
                            eng1.wait_ge(sd, vd)
                if fn:
                    nep = getattr(self, "_nep", 0) + 1
                    self._nep = nep
                    sems = {e_: nc.alloc_semaphore("s%d_%s" % (nep, e_)) for e_ in self.ENGS}
                    cnt = {e_: 0 for e_ in self.ENGS}
                continue
            eng = getattr(nc, e)
            need = {}
            for j in deps[i]:
                s, v = sig[j]
                if need.get(s.num, (None, 0))[1] < v:
                    need[s.num] = (s, v)
            if isdma:
                js = ndma % self.n_dma_sems
                v = 16 * (ndma // self.n_dma_sems + 1)
                if v > 16 and need.get(dsems[js].num, (None, 0))[1] < v - 16:
                    need[dsems[js].num] = (dsems[js], v - 16)
            for sn, (s, val) in need.items():
                if waited[e].get(sn, 0) >= val:
                    continue
                waited[e][sn] = val
                eng.wait_ge(s, val)
            ins = fn(eng)
            if isdma:
                ins.then_inc(dsems[js], 16)
                sig[i] = (dsems[js], v)
                dlast[dsems[js].num] = (dsems[js], v)
                ndma += 1
                if any(str(k).startswith("OUT") for k in wr):
                    if final.get(dsems[js].num, (None, 0))[1] < v:
                        final[dsems[js].num] = (dsems[js], v)
            elif has_dep[i]:
                cnt[e] += 1
                ins.then_inc(sems[e], 1)
                sig[i] = (sems[e], cnt[e])
            else:
                sig[i] = (sems[e], cnt[e])
        for sn, (s, v) in final.items():
            nc.sync.wait_ge(s, v)
        self.counts = dict(cnt, ndma=ndma, nops=len(ops))
        return nc


def build(n_layers=DEPTH, stop=None, dbg=(), mode="full"):
    NL = n_layers
    P = Prog()
    nc = P.nc
    if mode == "full":
        xin = P.dram("xin", [T, D], F32, "ExternalInput")
        out_d = P.dram("out", [SEQ, D], F32, "ExternalOutput")
    else:
        xT_in = P.dram("xT_in", [D, T], F32, "ExternalInput")
        xT_out = P.dram("xT_out", [D, T], F32, "ExternalOutput")
    ccT = P.dram("ccT", [D, 2], F32, "ExternalInput")
    ident_d = P.dram("ident", [128, 128], F32, "ExternalInput")
    w_ada = P.dram("w_ada", [NL, D, 6 * D], F32, "ExternalInput")
    b_adaT = P.dram("b_adaT", [NL, 128, 48], F32, "ExternalInput")
    w_in = P.dram("w_in", [NL, D, D_IN], F32, "ExternalInput")
    dbg_t = {}

    def dbg_out(name, shape, dtype=F32):
        dbg_t[name] = P.dram("dbg_" + name, shape, dtype, "ExternalOutput")
        return dbg_t[name]

    xT = P.dram("xT", [D, T], F32)
    qT = P.dram("qT", [512, T], BF16)
    kT = P.dram("kT", [512, T], BF16)
    vtok = P.dram("vtok", [T, 512], BF16)
    prw = P.dram("prw", [1920, T], F32)
    sguU = P.dram("sguU", [512, T], F32)
    sguV = P.dram("sguV", [T, 512], F32)

    ident = P.sbuf("ident_s", [128, 128])
    PS = [P.psum("ps%d" % i, [128, 512]) for i in range(6)]
    PSB = [P.psum("psb%d" % i, [128, 1024], BF16) for i in range(2)]
    mod = P.sbuf("mod", [128, NL, 48, 2])
    mod1 = P.sbuf("mod1", [128, NL, 48, 2])
    P.dma("sync", ident[:], ident_d, writes=["ident"])

    cc = P.sbuf("cc", [128, 8, 2])
    scc = P.sbuf("scc", [128, 8, 2])
    badaT = P.sbuf("badaT", [128, NL, 48])
    P.dma("sync", cc[:], ccT.rearrange("(k p) c -> p k c", p=128), writes=["cc"])
    P.dma("sync", badaT[:], b_adaT.rearrange("l p j -> p l j"), writes=["badaT"])
    P.op("scalar", lambda e: e.activation(out=scc[:], in_=cc[:], func=AF.Silu), reads=["cc"], writes=["scc"])
    identb = P.sbuf("identb", [128, 128], BF16)
    P.op("vector", lambda e: e.tensor_copy(out=identb[:], in_=ident[:]), reads=["ident"], writes=["identb"])
    lnp = P.sbuf("lnp", [128, NL, 4, 8])
    ones = P.sbuf("ones", [128, 128])
    eps5 = P.sbuf("eps5", [128, 1])
    P.op("vector", lambda e: e.memset(eps5[:], 1e-5), writes=["eps"])
    m0 = P.mark()
    wblk0 = [P.sbuf("wblk0%d" % i, [128, 8, 512]) for i in range(2)]
    nb = 0
    for l in range(n_layers):
        for cb in range(12):
            s = nb % 2
            P.dma("sync", wblk0[s][:], w_ada[l, :, cb * 512:(cb + 1) * 512].rearrange("(k p) m -> p k m", p=128),
                  writes=[("wblk0", s)])
            for mi in range(4):
                j = cb * 4 + mi
                for k in range(8):
                    P.op("tensor", lambda e, s=s, mi=mi, k=k, j=j: e.matmul(
                        PS[0][:, 2 * j:2 * j + 2], lhsT=wblk0[s][:, k, mi * 128:(mi + 1) * 128], rhs=scc[:, k, :],
                        start=(k == 0), stop=(k == 7)), reads=[("wblk0", s), "scc"], writes=[("ps", 0)])
            nb += 1
        for c in range(2):
            P.op("vector", lambda e, l=l, c=c: e.tensor_tensor(
                out=mod[:, l, :, c], in0=PS[0][:, 0:96].rearrange("p (j c) -> p j c", c=2)[:, :, c], in1=badaT[:, l, :], op=ALU.add),
                reads=[("ps", 0), "badaT"], writes=["mod"])
        P.op("vector", lambda e, l=l: e.tensor_scalar_add(out=mod1[:, l], in0=mod[:, l], scalar1=1.0), reads=["mod"], writes=["mod1"])

    xtile = [P.sbuf("xtile%d" % i, [128, D]) for i in range(2)]
    xTst = [P.sbuf("xTst%d" % i, [128, 8, 128]) for i in range(2)]
    if mode != "full":
        for gi, (t0, n) in enumerate(GROUPS):
            P.dma("sync", xT[:, t0:t0 + n], xT_in[:, t0:t0 + n], writes=[("xT", gi)])
    for t in range(NT if mode == "full" else 0):
        s = t % 2
        P.dma("sync", xtile[s][:], xin[t * 128:(t + 1) * 128, :], writes=[("xtile", s)])
        for half in range(2):
            pb = 1 + half
            for kk in range(4):
                k = half * 4 + kk
                P.op("tensor", lambda e, s=s, k=k, kk=kk, pb=pb: e.transpose(
                    out=PS[pb][:, kk * 128:(kk + 1) * 128], in_=xtile[s][:, k * 128:(k + 1) * 128], identity=ident[:]),
                    reads=[("xtile", s), "ident"], writes=[("ps", pb)])
            P.op("scalar" if half == 0 else "vector", (lambda e, s=s, half=half, pb=pb: e.copy(
                out=xTst[s][:, half * 4:(half + 1) * 4, :], in_=PS[pb][:].rearrange("p (k t) -> p k t", k=4)))
                if half == 0 else (lambda e, s=s, half=half, pb=pb: e.tensor_copy(
                    out=xTst[s][:, half * 4:(half + 1) * 4, :], in_=PS[pb][:].rearrange("p (k t) -> p k t", k=4))),
                reads=[("ps", pb)], writes=[("xTst", s, half)])
        P.dma("gpsimd", xT.rearrange("(k p) t -> p k t", p=128)[:, :, t * 128:(t + 1) * 128], xTst[s][:],
              reads=[("xTst", s, 0), ("xTst", s, 1)], writes=[("xT", t // 4)])

    if "mod" in dbg:
        d = dbg_out("mod", [128, NL * 48 * 2])
        P.dma("sync", d, mod[:].rearrange("p l j c -> p (l j c)"), reads=["mod"], writes=["OUT_dbgmod"])
    if "xT" in dbg:
        d = dbg_out("xT", [D, T])
        P.dma("sync", d, xT, reads=[("xT", g) for g in range(9)], writes=["OUT_dbgxT"])
    if stop == "s0":
        return P, dbg_t
    P.barrier()
    P.release(m0)


    sgu_lng = P.dram("sgu_lng", [NL, 128, 512], F32, "ExternalInput")
    sgu_lnb = P.dram("sgu_lnb", [NL, 128, 512], F32, "ExternalInput")
    sgu_wT = P.dram("sgu_wT", [NL, 128, 8, 128], F32, "ExternalInput")
    sgu_bB = P.dram("sgu_bB", [NL, 64, 8, 128], F32, "ExternalInput")
    na_bias = P.dram("na_bias", [NL, 5, 128, 8, 896], F32, "ExternalInput")
    osgu_d = P.dram("osgu", [512, T], BF16)
    ona_d = P.dram("ona", [512, T], BF16)
    def gelu(src, dst, ta, tb, npart, rk, wk, pfx):
        P.op("gpsimd", lambda e: e.tensor_tensor(out=ta, in0=src, in1=src, op=ALU.mult), reads=rk, writes=[(pfx, "ta")])
        P.op("gpsimd", lambda e: e.tensor_scalar(out=ta, in0=ta, scalar1=0.044715, scalar2=1.0, op0=ALU.mult, op1=ALU.add),
             reads=[(pfx, "ta")], writes=[(pfx, "ta")])
        P.op("gpsimd", lambda e: e.tensor_tensor(out=ta, in0=ta, in1=src, op=ALU.mult), reads=[(pfx, "ta")] + list(rk), writes=[(pfx, "ta")])
        P.op("scalar", lambda e: e.activation(out=tb, in_=ta, func=AF.Sigmoid, scale=1.5957691216057308),
             reads=[(pfx, "ta")], writes=[(pfx, "tb")])
        P.op("vector", lambda e: e.tensor_tensor(out=dst, in0=tb, in1=src, op=ALU.mult), reads=[(pfx, "tb")] + list(rk), writes=wk)

    def stage_sgu(l):
        sg = {}
        if True:
            sg["lng"] = P.sbuf("sg_lng", [128, 512]); sg["lnb"] = P.sbuf("sg_lnb", [128, 512])
            sg["wT"] = P.sbuf("sg_wT", [128, 8, 128]); sg["bB"] = P.sbuf("sg_bB", [64, 8, 128])
            for nm, shp, dt in (("sv", [128, 512], F32), ("gv", [128, 512], F32), ("vn", [128, 512], F32), ("tva", [128, 512], F32),
                                ("tvb", [128, 512], F32), ("su", [64, 8, 128], F32), ("gu", [64, 8, 128], F32), ("tua", [64, 8, 128], F32),
                                ("tub", [64, 8, 128], F32), ("st6", [128, 6], F32), ("mv", [128, 2], F32), ("rstd", [128, 1], F32),
                                ("tmp", [64, 8, 128], F32), ("osg", [64, 8, 128], BF16)):
                sg[nm] = [P.sbuf("sg_%s%d" % (nm, i), shp, dt) for i in range(2)]
        P.dma("sync", sg["lng"][:], sgu_lng[l], writes=["sg_par"])
        P.dma("sync", sg["lnb"][:], sgu_lnb[l], writes=["sg_par"])
        P.dma("sync", sg["wT"][:], sgu_wT[l], writes=["sg_par"])
        P.dma("sync", sg["bB"][:], sgu_bB[l], writes=["sg_par"])
        sguU_v = sguU.rearrange("(g c) t -> c g t", c=64)
        osgu_v = osgu_d.rearrange("(g c) t -> c g t", c=64)
        for t in range(NT):
            s = t % 2
            gi = 0 if t < 2 else 1 + (t - 2) // 4
            sv, gv, vn, su, gu, tmp, osg = (sg[k][s] for k in ("sv", "gv", "vn", "su", "gu", "tmp", "osg"))
            st6, mv, rstd = sg["st6"][s], sg["mv"][s], sg["rstd"][s]
            P.dma("sync", sv[:], sguV[t * 128:(t + 1) * 128, :], reads=[("sguV", "tm", t)], writes=[("sv", s)])
            P.dma("sync", su[:], sguU_v[:, :, t * 128:(t + 1) * 128], reads=[("sguU", "fm", gi)], writes=[("su", s)])
            gelu(sv[:], gv[:], sg["tva"][s][:], sg["tvb"][s][:], 128, [("sv", s)], [("gv", s)], ("gv", s))
            P.op("vector", lambda e, st6=st6, gv=gv: e.bn_stats(out=st6[:], in_=gv[:]), reads=[("gv", s)], writes=[("st6", s)])
            P.op("vector", lambda e, st6=st6, mv=mv: e.bn_aggr(out=mv[:], in_=st6[:]), reads=[("st6", s)], writes=[("mv", s)])
            P.op("scalar", lambda e, mv=mv, rstd=rstd: e.activation(out=rstd[:], in_=mv[:, 1:2], func=AF.Sqrt, bias=eps5[:, 0:1], scale=1.0),
                 reads=[("mv", s), "eps"], writes=[("rstd", s)])
            P.op("vector", lambda e, rstd=rstd: e.reciprocal(out=rstd[:], in_=rstd[:]), reads=[("rstd", s)], writes=[("rstd", s)])
            P.op("vector", lambda e, vn=vn, gv=gv, mv=mv, rstd=rstd: e.tensor_scalar(
                out=vn[:], in0=gv[:], scalar1=mv[:, 0:1], scalar2=rstd[:, 0:1], op0=ALU.subtract, op1=ALU.mult),
                reads=[("gv", s), ("mv", s), ("rstd", s)], writes=[("vn", s)])
            P.op("gpsimd", lambda e, vn=vn: e.tensor_tensor(out=vn[:], in0=vn[:], in1=sg["lng"][:], op=ALU.mult),
                 reads=[("vn", s), "sg_par"], writes=[("vn", s)])
            P.op("gpsimd", lambda e, vn=vn: e.tensor_tensor(out=vn[:], in0=vn[:], in1=sg["lnb"][:], op=ALU.add),
                 reads=[("vn", s), "sg_par"], writes=[("vn", s)])
            gelu(su[:], gu[:], sg["tua"][s][:], sg["tub"][s][:], 64, [("su", s)], [("gu", s)], ("gu", s))
            for half in range(2):
                pb = 2 * s + half
                for gg in range(4):
                    g = half * 4 + gg
                    P.op("tensor", lambda e, vn=vn, g=g, gg=gg, pb=pb: e.matmul(
                        PS[pb][0:64, gg * 128:(gg + 1) * 128], lhsT=vn[:, g * 64:(g + 1) * 64], rhs=sg["wT"][:, g, :],
                        start=True, stop=True), reads=[("vn", s), "sg_par"], writes=[("ps", pb)])
                P.op("vector", lambda e, tmp=tmp, pb=pb, half=half: e.tensor_tensor(
                    out=tmp[:, half * 4:(half + 1) * 4, :], in0=PS[pb][0:64, :].rearrange("p (g t) -> p g t", g=4),
                    in1=sg["bB"][:, half * 4:(half + 1) * 4, :], op=ALU.add), reads=[("ps", pb), "sg_par"], writes=[("sgtmp", s, half)])
                P.op("gpsimd", lambda e, tmp=tmp, gu=gu, osg=osg, half=half: e.tensor_tensor(
                    out=osg[:, half * 4:(half + 1) * 4, :], in0=tmp[:, half * 4:(half + 1) * 4, :], in1=gu[:, half * 4:(half + 1) * 4, :],
                    op=ALU.mult), reads=[("sgtmp", s, half), ("gu", s)], writes=[("osg", s, half)])
            P.dma("gpsimd", osgu_v[:, :, t * 128:(t + 1) * 128], osg[:], reads=[("osg", s, 0), ("osg", s, 1)], writes=[("osgu", gi)])
        if "osgu" in dbg and l == 0:
            d = dbg_out("osgu", [512, T], BF16)
            P.dma("sync", d, osgu_d, reads=[("osgu", g) for g in range(9)], writes=["OUT_dbgosgu"])

    def stage_na(l):
        na = {}
        if True:
            na["q"] = P.sbuf("na_q", [128, 4, T], BF16); na["k"] = P.sbuf("na_k", [128, 4, T], BF16)
            na["v"] = P.sbuf("na_v", [128, NT, 512], BF16)
            na["bI"] = P.sbuf("na_bI", [128, 8, 896]); na["bE"] = P.sbuf("na_bE", [128, 8, 896])
            for nm, shp, dt in (("s", [128, 896], F32), ("p", [128, 896], BF16), ("pT", [128, 896], BF16), ("nmx", [128, 1], F32)):
                na[nm] = [P.sbuf("na_%s%d" % (nm, i), shp, dt) for i in range(2)]
            for nm, shp, dt in (("rs", [128, 8], F32), ("rinv", [128, 8], F32), ("o", [128, 512], BF16), ("oT", [128, 4, 128], BF16)):
                na[nm] = [P.sbuf("na_%s%d" % (nm, i), shp, dt) for i in range(2)]
        qT_v = qT.rearrange("(k p) t -> p k t", p=128); kT_v = kT.rearrange("(k p) t -> p k t", p=128)
        vt_v = vtok.rearrange("(t p) f -> p t f", p=128)
        for gi, (t0, n) in enumerate(GROUPS):
            P.dma("sync", na["q"][:, :, t0:t0 + n], qT_v[:, :, t0:t0 + n], reads=[("qT", "fm", gi)], writes=[("na_q", gi)])
            P.dma("sync", na["k"][:, :, t0:t0 + n], kT_v[:, :, t0:t0 + n], reads=[("kT", "fm", gi)], writes=[("na_k", gi)])
        for t in range(NT):
            P.dma("sync", na["v"][:, t, :], vt_v[:, t, :], reads=[("vtok", "tm", t)], writes=[("na_v", t)])
        P.dma("sync", na["bI"][:], na_bias[l, 2], writes=["na_bI"])
        ona_v = ona_d.rearrange("(k p) t -> p k t", p=128)
        grp_of_tok = lambda tok: 0 if tok < 256 else 1 + (tok - 256) // 512
        hcnt = 0
        for t in range(NT):
            isctx = t < 2
            so = t % 2
            if isctx:
                nk = 256; pat = None; vtiles = [0, 1]
                kgroups = [0]
            else:
                qt = t - 2
                kb = min(max(2 * qt - 4, 0), 54)
                ktok0 = 256 + kb * 64
                nk = 896
                pat = {0: 0, 1: 1, 30: 3, 31: 4}.get(qt, 2)
                vtiles = [2 + kb // 2 + j for j in range(5)] + [0, 1]
                kgroups = sorted({0, grp_of_tok(ktok0), grp_of_tok(ktok0 + 639)})
                if pat != 2:
                    P.dma("sync", na["bE"][:], na_bias[l, pat], writes=["na_bE"])
            bias = None if isctx else (na["bI"] if pat == 2 else na["bE"])
            bkey = "na_bI" if pat == 2 else "na_bE"
            psO = PS[4 + so]
            qg = grp_of_tok(t * 128)
            def head_gen(h, x):
                ch, p0 = h // 2, (h % 2) * 64
                yield
                psA, psBk = PS[x], PS[2 + x]
                s_t, p_t, pT_t, nmx = na["s"][x], na["p"][x], na["pT"][x], na["nmx"][x]
                lhsT = na["q"][p0:p0 + 64, ch, t * 128:(t + 1) * 128]
                krd = [("na_q", qg)] + [("na_k", g) for g in kgroups]
                if isctx:
                    P.op("tensor", lambda e, lhsT=lhsT, psA=psA, ch=ch, p0=p0: e.matmul(
                        psA[:, 0:256], lhsT=lhsT, rhs=na["k"][p0:p0 + 64, ch, 0:256], start=True, stop=True),
                        reads=krd, writes=[("ps", x)])
                    yield
                    P.op("vector", lambda e, s_t=s_t, psA=psA: e.tensor_scalar(out=s_t[:, 0:256], in0=psA[:, 0:256], scalar1=0.125,
                                                                               scalar2=None, op0=ALU.mult),
                         reads=[("ps", x)], writes=[("na_s", x)])
                else:
                    P.op("tensor", lambda e, lhsT=lhsT, psA=psA, ch=ch, p0=p0, ktok0=ktok0: e.matmul(
                        psA[:, 0:512], lhsT=lhsT, rhs=na["k"][p0:p0 + 64, ch, ktok0:ktok0 + 512], start=True, stop=True),
                        reads=krd, writes=[("ps", x)])
                    P.op("tensor", lambda e, lhsT=lhsT, psBk=psBk, ch=ch, p0=p0, ktok0=ktok0: e.matmul(
                        psBk[:, 0:128], lhsT=lhsT, rhs=na["k"][p0:p0 + 64, ch, ktok0 + 512:ktok0 + 640], start=True, stop=True),
                        reads=krd, writes=[("ps", 2 + x)])
                    P.op("tensor", lambda e, lhsT=lhsT, psBk=psBk, ch=ch, p0=p0: e.matmul(
                        psBk[:, 128:384], lhsT=lhsT, rhs=na["k"][p0:p0 + 64, ch, 0:256], start=True, stop=True),
                        reads=krd, writes=[("ps", 2 + x)])
                    yield
                    P.op("vector", lambda e, s_t=s_t, psA=psA, bias=bias, h=h: e.scalar_tensor_tensor(
                        out=s_t[:, 0:512], in0=psA[:, 0:512], scalar=0.125, in1=bias[:, h, 0:512], op0=ALU.mult, op1=ALU.add),
                        reads=[("ps", x), bkey], writes=[("na_s", x)])
                    P.op("vector", lambda e, s_t=s_t, psBk=psBk, bias=bias, h=h: e.scalar_tensor_tensor(
                        out=s_t[:, 512:896], in0=psBk[:, 0:384], scalar=0.125, in1=bias[:, h, 512:896], op0=ALU.mult, op1=ALU.add),
                        reads=[("ps", 2 + x), bkey], writes=[("na_s", x)])
                yield
                P.op("vector", lambda e, s_t=s_t, nmx=nmx, nk=nk: e.tensor_reduce(out=nmx[:, 0:1], in_=s_t[:, 0:nk], axis=AX.X, op=ALU.max,
                                                                                 negate=True),
                     reads=[("na_s", x)], writes=[("na_nmx", x)])
                P.op("scalar", lambda e, s_t=s_t, p_t=p_t, nmx=nmx, nk=nk, h=h, so=so: e.activation(
                    out=p_t[:, 0:nk], in_=s_t[:, 0:nk], func=AF.Exp, bias=nmx[:, 0:1], scale=1.0, accum_out=na["rs"][so][:, h:h + 1]),
                    reads=[("na_s", x), ("na_nmx", x)], writes=[("na_p", x), ("na_rs", so)])
                yield
                for j in range(nk // 128):
                    P.op("tensor", lambda e, p_t=p_t, j=j, x=x: e.transpose(out=PSB[x][:, j * 128:(j + 1) * 128],
                                                                          in_=p_t[:, j * 128:(j + 1) * 128], identity=identb[:]),
                         reads=[("na_p", x), "identb"], writes=[("psb", x)])
                yield
                if h % 2 == 0:
                    P.op("scalar", lambda e, pT_t=pT_t, x=x, nk=nk: e.copy(out=pT_t[:, 0:nk], in_=PSB[x][:, 0:nk]),
                         reads=[("psb", x)], writes=[("na_pT", x)])
                else:
                    P.op("vector", lambda e, pT_t=pT_t, x=x, nk=nk: e.tensor_copy(out=pT_t[:, 0:nk], in_=PSB[x][:, 0:nk]),
                         reads=[("psb", x)], writes=[("na_pT", x)])
                yield
                for j, vt in enumerate(vtiles):
                    P.op("tensor", lambda e, pT_t=pT_t, j=j, vt=vt, h=h, psO=psO, last=(j == len(vtiles) - 1): e.matmul(
                        psO[:, h * 64:(h + 1) * 64], lhsT=pT_t[:, j * 128:(j + 1) * 128], rhs=na["v"][:, vt, h * 64:(h + 1) * 64],
                        start=(j == 0), stop=last), reads=[("na_pT", x), ("na_v", vt)], writes=[("ps", 4 + so)])
            for hp in range(0, 8, 2):
                gens = [head_gen(hp, hcnt % 2), head_gen(hp + 1, (hcnt + 1) % 2)]
                hcnt += 2
                while gens:
                    for g_ in list(gens):
                        try:
                            next(g_)
                        except StopIteration:
                            gens.remove(g_)
            rs, rinv, o_t, oT = na["rs"][so], na["rinv"][so], na["o"][so], na["oT"][so]
            P.op("vector", lambda e, rs=rs, rinv=rinv: e.reciprocal(out=rinv[:], in_=rs[:]), reads=[("na_rs", so)], writes=[("na_rinv", so)])
            P.op("vector", lambda e, o_t=o_t, psO=psO, rinv=rinv: e.tensor_tensor(
                out=o_t[:].rearrange("p (h d) -> p h d", h=8), in0=psO[:].rearrange("p (h d) -> p h d", h=8),
                in1=rinv[:].unsqueeze(2).to_broadcast([128, 8, 64]), op=ALU.mult),
                reads=[("ps", 4 + so), ("na_rinv", so)], writes=[("na_o", so)])
            xx = hcnt % 2
            for c4 in range(4):
                P.op("tensor", lambda e, o_t=o_t, c4=c4, xx=xx: e.transpose(out=PSB[xx][:, c4 * 128:(c4 + 1) * 128],
                                                                          in_=o_t[:, c4 * 128:(c4 + 1) * 128], identity=identb[:]),
                     reads=[("na_o", so), "identb"], writes=[("psb", xx)])
            P.op("scalar", lambda e, oT=oT, xx=xx: e.copy(out=oT[:], in_=PSB[xx][:, 0:512].rearrange("p (c t) -> p c t", c=4)),
                 reads=[("psb", xx)], writes=[("na_oT", so)])
            P.dma("gpsimd", ona_v[:, :, t * 128:(t + 1) * 128], oT[:], reads=[("na_oT", so)], writes=[("ona", qg)])
        if "ona" in dbg and l == 0:
            d = dbg_out("ona", [512, T], BF16)
            P.dma("sync", d, ona_d, reads=[("ona", g) for g in range(9)], writes=["OUT_dbgona"])


    w_branch = P.dram("w_branch", [NL, 3, 512, D], F32, "ExternalInput")
    w_out = P.dram("w_out", [NL, D, D], F32, "ExternalInput")
    w_gu = P.dram("ffn_w_gu", [NL, D, 2 * D_FF], F32, "ExternalInput")
    w_down = P.dram("ffn_w_down", [NL, D_FF, D], F32, "ExternalInput")
    lnp_d = P.dram("lnp", [128, NL, 4, 8], F32, "ExternalInput")
    orw_d = P.dram("orw", [512, T], BF16)
    P.dma("sync", lnp[:], lnp_d, writes=["lnp"])
    P.op("vector", lambda e: e.memset(ones[:], 1.0), writes=["ones"])

    def load_w_bf16(dst_view, src_view, stg, npart, nfree, tag):
        a, b = dst_view.shape[1], dst_view.shape[2]
        rows = max(1, 4096 // b)
        i = 0
        for a0 in range(0, a, rows):
            a1 = min(a, a0 + rows)
            st = stg[i % 2]
            P.dma("sync", st[0:npart, 0:(a1 - a0) * b].rearrange("p (a b) -> p a b", b=b), src_view[:, a0:a1, :], writes=[("stg", i % 2)])
            P.op("gpsimd" if i % 2 == 0 else "vector", lambda e, st=st, a0=a0, a1=a1: e.tensor_copy(
                out=dst_view[:, a0:a1, :], in_=st[0:npart, 0:(a1 - a0) * b].rearrange("p (a b) -> p a b", b=b)),
                reads=[("stg", i % 2)], writes=[tag])
            i += 1

    def layer_norm_T(r, n, l, gi_, bi_, out_view, tagr, tagout, sq, stat):
        for k in range(8):
            P.op("tensor", lambda e, k=k: e.matmul(PS[0][:, :n], lhsT=ones[:], rhs=r[:, k, :n], start=(k == 0), stop=(k == 7)),
                 reads=[tagr, "ones"], writes=[("ps", 0)])
        for k in range(8):
            sqk = sq[k % 2]
            P.op("scalar", lambda e, k=k, sqk=sqk: e.activation(out=sqk[:, :n], in_=r[:, k, :n], func=AF.Square),
                 reads=[tagr], writes=[("lnsq", k % 2)])
            P.op("tensor", lambda e, k=k, sqk=sqk: e.matmul(PS[1][:, :n], lhsT=ones[:], rhs=sqk[:, :n], start=(k == 0), stop=(k == 7)),
                 reads=[("lnsq", k % 2), "ones"], writes=[("ps", 1)])
        mean, var, rstd = stat
        P.op("vector", lambda e: e.tensor_scalar(out=mean[:, :n], in0=PS[0][:, :n], scalar1=1.0 / 1024, scalar2=None, op0=ALU.mult),
             reads=[("ps", 0)], writes=["ln_mean"])
        P.op("vector", lambda e: e.tensor_tensor(out=var[:, :n], in0=mean[:, :n], in1=mean[:, :n], op=ALU.mult),
             reads=["ln_mean"], writes=["ln_var"])
        P.op("vector", lambda e: e.scalar_tensor_tensor(out=var[:, :n], in0=PS[1][:, :n], scalar=1.0 / 1024, in1=var[:, :n],
                                                        op0=ALU.mult, op1=ALU.subtract), reads=[("ps", 1), "ln_var"], writes=["ln_var"])
        P.op("scalar", lambda e: e.activation(out=rstd[:, :n], in_=var[:, :n], func=AF.Sqrt, bias=eps5[:, 0:1], scale=1.0),
             reads=["ln_var", "eps"], writes=["ln_rstd"])
        P.op("vector", lambda e: e.reciprocal(out=rstd[:, :n], in_=rstd[:, :n]), reads=["ln_rstd"], writes=["ln_rstd"])
        for k in range(8):
            eng = "vector" if k % 2 == 0 else "gpsimd"
            P.op(eng, lambda e, k=k: e.tensor_tensor(out=r[:, k, :n], in0=r[:, k, :n], in1=mean[:, :n], op=ALU.subtract),
                 reads=[tagr, "ln_mean", "ln_rstd"], writes=[(tagr, "c", k)])
            P.op(eng, lambda e, k=k: e.tensor_tensor(out=r[:, k, :n], in0=r[:, k, :n], in1=rstd[:, :n], op=ALU.mult),
                 reads=[tagr, (tagr, "c", k), "ln_rstd"], writes=[(tagr, "c", k)])
            P.op(eng, lambda e, k=k: e.tensor_scalar(out=out_view[:, k, :n], in0=r[:, k, :n], scalar1=lnp[:, l, gi_, k:k + 1],
                                                     scalar2=lnp[:, l, bi_, k:k + 1], op0=ALU.mult, op1=ALU.add),
                 reads=[tagr, (tagr, "c", k), "lnp"], writes=[(tagout, k)])

    def stage_merge(l):
        wg = P.sbuf("m_wg", [128, 8, 3072], BF16)
        wbn = P.sbuf("m_wbn", [128, 4, 1024], BF16); wbr = P.sbuf("m_wbr", [128, 4, 1024], BF16)
        wbs = P.sbuf("m_wbs", [64, 8, 1024], BF16); wo = P.sbuf("m_wo", [128, 8, 1024], BF16)
        stg = [P.sbuf("m_stg%d" % i, [128, 4096]) for i in range(2)]
        xr = P.sbuf("m_xr", [128, 8, 512]); hh = P.sbuf("m_h", [128, 8, 512], BF16)
        o_n = P.sbuf("m_on", [128, 4, 512], BF16); o_r = P.sbuf("m_or", [128, 4, 512], BF16); o_s = P.sbuf("m_os", [64, 8, 512], BF16)
        sig = [P.sbuf("m_sig%d" % i, [128, 512]) for i in range(3)]
        ypre = P.sbuf("m_ypre", [128, 8, 512], BF16); yacc = P.sbuf("m_yacc", [128, 512]); ytmp = P.sbuf("m_ytmp", [128, 512])
        sq = [P.sbuf("m_sq%d" % i, [128, 512]) for i in range(2)]
        stat = [P.sbuf("m_st%d" % i, [128, 512]) for i in range(3)]
        load_w_bf16(wg[:], w_in[l, :, 0:3072].rearrange("(k p) m -> p k m", p=128), stg, 128, 0, "m_wg")
        load_w_bf16(wbn[:], w_branch[l, 0].rearrange("(k p) m -> p k m", p=128), stg, 128, 0, "m_wbn")
        load_w_bf16(wbr[:], w_branch[l, 1].rearrange("(k p) m -> p k m", p=128), stg, 128, 0, "m_wbr")
        load_w_bf16(wbs[:], w_branch[l, 2].rearrange("(g c) m -> c g m", c=64), stg, 64, 0, "m_wbs")
        load_w_bf16(wo[:], w_out[l].rearrange("(k p) m -> p k m", p=128), stg, 128, 0, "m_wo")
        ona_v = ona_d.rearrange("(k p) t -> p k t", p=128); orw_v = orw_d.rearrange("(k p) t -> p k t", p=128)
        osgu_v = osgu_d.rearrange("(g c) t -> c g t", c=64)
        pr = 0
        for gi, (t0, n) in enumerate(GROUPS):
            c = 1 if gi == 0 else 0
            P.dma("sync", xr[:, :, :n], xT_v[:, :, t0:t0 + n], reads=[("xT", gi)], writes=["m_xr"])
            P.dma("sync", o_n[:, :, :n], ona_v[:, :, t0:t0 + n], reads=[("ona", gi)], writes=["m_on"])
            P.dma("sync", o_r[:, :, :n], orw_v[:, :, t0:t0 + n], reads=[("orw", gi)], writes=["m_or"])
            P.dma("sync", o_s[:, :, :n], osgu_v[:, :, t0:t0 + n], reads=[("osgu", gi)], writes=["m_os"])
            for k in range(8):
                P.op("vector" if k % 2 == 0 else "gpsimd", lambda e, k=k, c=c: e.tensor_scalar(
                    out=hh[:, k, :n], in0=xr[:, k, :n], scalar1=mod1[:, l, 8 + k, c:c + 1], scalar2=mod[:, l, k, c:c + 1],
                    op0=ALU.mult, op1=ALU.add), reads=["m_xr", "mod", "mod1"], writes=["m_h"])
            for m in range(8):
                for i in range(3):
                    pg, pb = PS[2 * (pr % 3)], PS[2 * (pr % 3) + 1]
                    kg, kb_ = ("ps", 2 * (pr % 3)), ("ps", 2 * (pr % 3) + 1)
                    pr += 1
                    mc = i * 8 + m
                    for k in range(8):
                        P.op("tensor", lambda e, k=k, mc=mc, pg=pg: e.matmul(pg[:, :n], lhsT=wg[:, k, mc * 128:(mc + 1) * 128], rhs=hh[:, k, :n],
                                                                             start=(k == 0), stop=(k == 7)), reads=["m_wg", "m_h"], writes=[kg])
                    P.op("scalar", lambda e, i=i, pg=pg: e.activation(out=sig[i][:, :n], in_=pg[:, :n], func=AF.Sigmoid),
                         reads=[kg], writes=[("m_sig", i)])
                    if i < 2:
                        wb_, ob_, ok_ = (wbn, o_n, "m_on") if i == 0 else (wbr, o_r, "m_or")
                        for k in range(4):
                            P.op("tensor", lambda e, k=k, m=m, pb=pb, wb_=wb_, ob_=ob_: e.matmul(
                                pb[:, :n], lhsT=wb_[:, k, m * 128:(m + 1) * 128], rhs=ob_[:, k, :n], start=(k == 0), stop=(k == 3)),
                                reads=["m_wbn", "m_wbr", ok_], writes=[kb_])
                    else:
                        for g in range(8):
                            P.op("tensor", lambda e, g=g, m=m, pb=pb: e.matmul(
                                pb[:, :n], lhsT=wbs[:, g, m * 128:(m + 1) * 128], rhs=o_s[:, g, :n], start=(g == 0), stop=(g == 7)),
                                reads=["m_wbs", "m_os"], writes=[kb_])
                    if i == 0:
                        P.op("vector", lambda e, pb=pb: e.tensor_tensor(out=yacc[:, :n], in0=pb[:, :n], in1=sig[0][:, :n], op=ALU.mult),
                             reads=[kb_, ("m_sig", 0)], writes=["m_yacc"])
                    else:
                        P.op("vector", lambda e, pb=pb, i=i: e.tensor_tensor(out=ytmp[:, :n], in0=pb[:, :n], in1=sig[i][:, :n], op=ALU.mult),
                             reads=[kb_, ("m_sig", i)], writes=["m_ytmp"])
                        if i == 1:
                            P.op("gpsimd", lambda e: e.tensor_tensor(out=yacc[:, :n], in0=yacc[:, :n], in1=ytmp[:, :n], op=ALU.add),
                                 reads=["m_yacc", "m_ytmp"], writes=["m_yacc"])
                        else:
                            P.op("gpsimd", lambda e, m=m: e.tensor_tensor(out=ypre[:, m, :n], in0=yacc[:, :n], in1=ytmp[:, :n], op=ALU.add),
                                 reads=["m_yacc", "m_ytmp"], writes=["m_ypre"])
            for m in range(8):
                pg = PS[2 + m % 4]; kg = ("ps", 2 + m % 4)
                for k in range(8):
                    P.op("tensor", lambda e, k=k, m=m, pg=pg: e.matmul(pg[:, :n], lhsT=wo[:, k, m * 128:(m + 1) * 128], rhs=ypre[:, k, :n],
                                                                         start=(k == 0), stop=(k == 7)), reads=["m_wo", "m_ypre"], writes=[kg])
                P.op("scalar", lambda e, m=m, pg=pg, c=c: e.activation(out=ytmp[:, :n], in_=pg[:, :n], func=AF.Copy,
                                                                      scale=mod[:, l, 16 + m, c:c + 1]), reads=[kg, "mod"], writes=["m_ytmp"])
                P.op("vector", lambda e, m=m: e.scalar_tensor_tensor(out=xr[:, m, :n], in0=xr[:, m, :n], scalar=ALPHA, in1=ytmp[:, :n],
                                                                    op0=ALU.mult, op1=ALU.add), reads=["m_xr", "m_ytmp"], writes=["m_xr"])
            layer_norm_T(xr, n, l, 0, 1, xr, "m_xr", "m_x1", sq, stat)
            P.dma("gpsimd", xT_v[:, :, t0:t0 + n], xr[:, :, :n], reads=["m_xr"] + [("m_x1", k) for k in range(8)], writes=[("xT", gi)])
        if "x1" in dbg and l == 0:
            d = dbg_out("x1", [D, T])
            P.dma("sync", d, xT, reads=[("xT", g) for g in range(9)], writes=["OUT_dbgx1"])

    FG = [(t0, 256) for t0 in range(0, T, 256)]

    def stage_ffn(l):
        wgu = P.sbuf("f_wgu", [128, 8, 2 * D_FF], BF16)
        wd = P.sbuf("f_wd", [128, 22, 1024], BF16)
        stg = [P.sbuf("f_stg%d" % i, [128, 2048]) for i in range(2)]
        xr = P.sbuf("f_xr", [128, 8, 256]); ff = P.sbuf("f_f", [128, 8, 256], BF16)
        act = P.sbuf("f_a", [128, 22, 256], BF16)
        sgt = [P.sbuf("f_sg%d" % i, [128, 256]) for i in range(2)]
        ytmp = P.sbuf("f_ytmp", [128, 256])
        sq = [P.sbuf("f_sq%d" % i, [128, 256]) for i in range(2)]
        stat = [P.sbuf("f_st%d" % i, [128, 256]) for i in range(3)]

        def load2(dst_view, src_view, tag):
            a, b = dst_view.shape[1], dst_view.shape[2]
            i = 0
            for a0 in range(a):
                for b0 in range(0, b, 2048):
                    b1 = min(b, b0 + 2048)
                    st = stg[i % 2]
                    P.dma("sync", st[:, 0:b1 - b0], src_view[:, a0, b0:b1], writes=[("fstg", i % 2)])
                    P.op("gpsimd" if i % 2 == 0 else "vector", lambda e, st=st, a0=a0, b0=b0, b1=b1: e.tensor_copy(
                        out=dst_view[:, a0, b0:b1], in_=st[:, 0:b1 - b0]), reads=[("fstg", i % 2)], writes=[tag])
                    i += 1
        load2(wgu[:], w_gu[l].rearrange("(k p) m -> p k m", p=128), "f_wgu")
        load2(wd[:], w_down[l].rearrange("(k p) m -> p k m", p=128), "f_wd")
        for fi, (t0, n) in enumerate(FG):
            gi = 0 if t0 < 256 else 1 + (t0 - 256) // 512
            c = 1 if fi == 0 else 0
            P.dma("sync", xr[:, :, :n], xT_v[:, :, t0:t0 + n], reads=[("xT", gi)], writes=["f_xr"])
            for k in range(8):
                P.op("vector" if k % 2 == 0 else "gpsimd", lambda e, k=k, c=c: e.tensor_scalar(
                    out=ff[:, k, :n], in0=xr[:, k, :n], scalar1=mod1[:, l, 32 + k, c:c + 1], scalar2=mod[:, l, 24 + k, c:c + 1],
                    op0=ALU.mult, op1=ALU.add), reads=["f_xr", "mod", "mod1"], writes=["f_f"])
            for j in range(22):
                x2 = j % 2
                pg, pu = PS[2 + 2 * x2], PS[3 + 2 * x2]
                kg, ku = ("ps", 2 + 2 * x2), ("ps", 3 + 2 * x2)
                for k in range(8):
                    P.op("tensor", lambda e, k=k, j=j, pg=pg: e.matmul(pg[:, :n], lhsT=wgu[:, k, j * 128:(j + 1) * 128], rhs=ff[:, k, :n],
                                                                         start=(k == 0), stop=(k == 7)), reads=["f_wgu", "f_f"], writes=[kg])
                for k in range(8):
                    P.op("tensor", lambda e, k=k, j=j, pu=pu: e.matmul(pu[:, :n], lhsT=wgu[:, k, D_FF + j * 128:D_FF + (j + 1) * 128],
                                                                         rhs=ff[:, k, :n], start=(k == 0), stop=(k == 7)),
                         reads=["f_wgu", "f_f"], writes=[ku])
                P.op("scalar", lambda e, pg=pg, x2=x2: e.activation(out=sgt[x2][:, :n], in_=pg[:, :n], func=AF.Silu),
                     reads=[kg], writes=[("f_sg", x2)])
                P.op("vector", lambda e, pu=pu, x2=x2, j=j: e.tensor_tensor(out=act[:, j, :n], in0=pu[:, :n], in1=sgt[x2][:, :n], op=ALU.mult),
                     reads=[ku, ("f_sg", x2)], writes=["f_a"])
            for m in range(8):
                pg = PS[2 + m % 4]; kg = ("ps", 2 + m % 4)
                for j in range(22):
                    P.op("tensor", lambda e, j=j, m=m, pg=pg: e.matmul(pg[:, :n], lhsT=wd[:, j, m * 128:(m + 1) * 128], rhs=act[:, j, :n],
                                                                         start=(j == 0), stop=(j == 21)), reads=["f_wd", "f_a"], writes=[kg])
                P.op("scalar", lambda e, m=m, pg=pg, c=c: e.activation(out=ytmp[:, :n], in_=pg[:, :n], func=AF.Copy,
                                                                      scale=mod[:, l, 40 + m, c:c + 1]), reads=[kg, "mod"], writes=["f_ytmp"])
                P.op("vector", lambda e, m=m: e.scalar_tensor_tensor(out=xr[:, m, :n], in0=xr[:, m, :n], scalar=ALPHA, in1=ytmp[:, :n],
                                                                    op0=ALU.mult, op1=ALU.add), reads=["f_xr", "f_ytmp"], writes=["f_xr"])
            layer_norm_T(xr, n, l, 2, 3, xr, "f_xr", "f_x2", sq, stat)
            P.dma("gpsimd", xT_v[:, :, t0:t0 + n], xr[:, :, :n], reads=["f_xr"] + [("f_x2", k) for k in range(8)], writes=[("xT", gi)])
        if "x2" in dbg and l == 0:
            d = dbg_out("x2", [D, T])
            P.dma("sync", d, xT, reads=[("xT", g) for g in range(9)], writes=["OUT_dbgx2"])

    def stage_final():
        xg_ = [P.sbuf("o_xg%d" % i, [128, 8, 128]) for i in range(2)]
        ot = [P.sbuf("o_t%d" % i, [128, 1024]) for i in range(2)]
        for t in range(2, NT):
            s_ = t % 2
            gi = 1 + (t - 2) // 4
            P.dma("sync", xg_[s_][:], xT_v[:, :, t * 128:(t + 1) * 128], reads=[("xT", gi)], writes=[("o_xg", s_)])
            for half in range(2):
                pb = 2 + 2 * s_ + half
                for kk in range(4):
                    k = half * 4 + kk
                    P.op("tensor", lambda e, s_=s_, k=k, kk=kk, pb=pb: e.transpose(
                        out=PS[pb][:, kk * 128:(kk + 1) * 128], in_=xg_[s_][:, k, :], identity=ident[:]),
                        reads=[("o_xg", s_), "ident"], writes=[("ps", pb)])
                if half == 0:
                    P.op("scalar", lambda e, s_=s_, pb=pb: e.copy(out=ot[s_][:, 0:512], in_=PS[pb][:]), reads=[("ps", pb)], writes=[("o_t", s_, 0)])
                else:
                    P.op("vector", lambda e, s_=s_, pb=pb: e.tensor_copy(out=ot[s_][:, 512:1024], in_=PS[pb][:]), reads=[("ps", pb)],
                         writes=[("o_t", s_, 1)])
            P.dma("gpsimd", out_d[(t - 2) * 128:(t - 1) * 128, :], ot[s_][:], reads=[("o_t", s_, 0), ("o_t", s_, 1)], writes=["OUT_%d" % t])

    rw_mu_d = P.dram("rw_mu", [128, NL, 2, 15], F32, "ExternalInput")
    rw_w0a0_d = P.dram("rw_w0a0", [128, NL, 2, 2, 4], F32, "ExternalInput")
    rw_w2_d = P.dram("rw_w2", [NL, 128, 512], F32, "ExternalInput")
    rw_a2_d = P.dram("rw_a2", [NL, 128, 512], F32, "ExternalInput")
    rw_g2_d = P.dram("rw_g2", [NL, 128, 512], F32, "ExternalInput")
    rw_vec_d = P.dram("rw_vec", [128, NL, 5, 4], F32, "ExternalInput")
    mask64_d = P.dram("mask64", [64, 4, 64], F32, "ExternalInput")
    bones_d = P.dram("bones", [128, 128], F32, "ExternalInput")
    g_d = P.dram("rw_g", [512, T], F32)
    bv_d = P.dram("rw_bv", [512, T], F32)
    NCH = T // 64
    summ_d = P.dram("rw_summ", [2, NCH, 64, 2056], F32)
    etot_d = P.dram("rw_etot", [2, NCH, 512], F32)
    y_d = P.dram("rw_y", [2, T, 512], F32)
    CDEC = float(np.exp(-0.5))
    psctr = [0]

    def psn():
        i = psctr[0] % 6
        psctr[0] += 1
        return PS[i], ("ps", i)

    def stage_rwkv(l):
        import os
        RW_NG = int(os.environ.get("RW_NGROUPS", "99")); RW_CH = int(os.environ.get("RW_CHUNKS", "1")); RW_PH = int(os.environ.get("RW_PHASES", "3"))
        RW_RND = int(os.environ.get("RW_ROUNDS", "6")); RW_ST = int(os.environ.get("RW_STEPS", "99"))
        NG = 128
        NC = NG // 64
        prw_v = prw.rearrange("(k p) t -> p k t", p=128)
        g_v = g_d.rearrange("(k p) t -> p k t", p=128)
        bv_v = bv_d.rearrange("(k p) t -> p k t", p=128)
        grp512 = lambda tok: 0 if tok < 256 else 1 + (tok - 256) // 512
        mu = P.sbuf("rw_mu", [128, 2, 15]); c0 = P.sbuf("rw_c0", [128, 15])
        w0a0 = P.sbuf("rw_w0a0", [128, 2, 2, 4]); w2s = P.sbuf("rw_w2", [128, 512]); a2s = P.sbuf("rw_a2", [128, 512])
        g2s = P.sbuf("rw_g2", [128, 512]); vec = P.sbuf("rw_vec", [128, 5, 4]); omka = P.sbuf("rw_omka", [128, 4])
        mask = P.sbuf("rw_mask", [64, 4, 64]); bones = P.sbuf("rw_bones", [128, 128]); eps12 = P.sbuf("rw_eps12", [128, 1])
        eps_gn = P.sbuf("rw_epsgn", [128, 1]); rmask = P.sbuf("rw_rmask", [128, 4 * NC, 64])
        P.dma("sync", mu[:], rw_mu_d[:, l], writes=["rw_par"]); P.dma("sync", w0a0[:], rw_w0a0_d[:, l], writes=["rw_par"])
        P.dma("sync", w2s[:], rw_w2_d[l], writes=["rw_par"]); P.dma("sync", a2s[:], rw_a2_d[l], writes=["rw_par"])
        P.dma("sync", g2s[:], rw_g2_d[l], writes=["rw_par"]); P.dma("sync", vec[:], rw_vec_d[:, l], writes=["rw_par"])
        P.dma("sync", mask[:], mask64_d, writes=["rw_par"]); P.dma("sync", bones[:], bones_d, writes=["rw_par"])
        P.op("vector", lambda e: e.tensor_tensor(out=c0[:], in0=mu[:, 0, :], in1=mu[:, 1, :], op=ALU.add), reads=["rw_par"], writes=["rw_c0"])
        P.op("vector", lambda e: e.tensor_scalar(out=c0[:], in0=c0[:], scalar1=-1.0, scalar2=1.0, op0=ALU.mult, op1=ALU.add),
             reads=["rw_c0"], writes=["rw_c0"])
        P.op("vector", lambda e: e.tensor_scalar(out=omka[:], in0=vec[:, 1, :], scalar1=-1.0, scalar2=1.0, op0=ALU.mult, op1=ALU.add),
             reads=["rw_par"], writes=["rw_omka"])
        P.op("vector", lambda e: e.memset(eps12[:], 1e-12), writes=["rw_eps"])
        P.op("vector", lambda e: e.memset(eps_gn[:], 64e-5), writes=["rw_eps"])
        P.op("vector", lambda e: e.memset(rmask[:], 1.0), writes=["rw_rmask"])
        P.op("vector", lambda e: e.memset(rmask[:, :, 0:1], 0.0), reads=["rw_rmask"], writes=["rw_rmask"])
        m1 = P.mark()
        pin = P.sbuf("rw_pin", [128, 15, NG + 2]); psh = P.sbuf("rw_psh", [128, 15, NG])
        sgw = [P.sbuf("rw_sgw%d" % d, [128, 4, NG]) for d in range(2)]
        aa = [P.sbuf("rw_a%d" % d, [128, 4, NG]) for d in range(2)]
        kd = [P.sbuf("rw_kd%d" % d, [128, 4, NG]) for d in range(2)]
        kk = P.sbuf("rw_kk", [128, 4, NG]); tA = P.sbuf("rw_tA", [128, 4, NG]); tB = P.sbuf("rw_tB", [128, 4, NG])
        Lp = P.sbuf("rw_Lp", [128, 4, NG]); Li = P.sbuf("rw_Li", [128, 4, NG]); Le = P.sbuf("rw_Le", [128, 4, NG])
        rt = P.sbuf("rw_rt", [128, 4, NG], F32R); at = P.sbuf("rw_at", [128, 4, NG], F32R); kt = P.sbuf("rw_kt", [128, 4, NG], F32R)
        bt = P.sbuf("rw_bt", [128, 4, NG], F32R); Kh = P.sbuf("rw_Kh", [128, 4, NG]); Bh = P.sbuf("rw_Bh", [128, 4, NG])
        etot = P.sbuf("rw_etot", [128, 4, NC], F32R); gst = P.sbuf("rw_gst", [128, 4, NG])
        mk = {nm: [P.sbuf("rw_%s%s" % (nm, eo), [128, 4, NG], F32R) for eo in "EO"] for nm in ("at", "kt", "bt")}
        zsrc = P.sbuf("rw_zsrc", [128, 1024])
        P.op("vector", lambda e: e.memset(zsrc[:], 0.0), writes=["rw_zsrc"])
        Vt = [P.sbuf("rw_Vt%d" % c, [128, 512], F32R) for c in range(NC)]
        UT = []
        zl = [(Vt[c], ("rw_Vt", c)) for c in range(NC)]
        for u_ in range(2):
            ut = dict(KhT=P.sbuf("rw_KhT%d" % u_, [128, 512], F32R), BhT=P.sbuf("rw_BhT%d" % u_, [128, 512], F32R),
                      Mka=P.sbuf("rw_Mka%d" % u_, [128, 8, 64], F32R), Mkr=P.sbuf("rw_Mkr%d" % u_, [128, 8, 64], F32R),
                      Mbr=P.sbuf("rw_Mbr%d" % u_, [128, 8, 64], F32R),
                      Nn=[P.sbuf("rw_N%d_%d" % (u_, i), [128, 8, 64], F32R) for i in range(2)],
                      NTr=[P.sbuf("rw_NT%d_%d" % (u_, i), [128, 8, 64], F32R) for i in range(2)],
                      Z=P.sbuf("rw_Z%d" % u_, [128, 8, 128], F32R), Zn=P.sbuf("rw_Zn%d" % u_, [128, 8, 128], F32R),
                      summ=P.sbuf("rw_summ%d" % u_, [64, 2056]))
            UT.append(ut)
            zl += [(ut["KhT"], ("rw_KhT", u_)), (ut["BhT"], ("rw_BhT", u_)), (ut["Mka"], ("rw_Mka", u_)), (ut["Mkr"], ("rw_Mkr", u_)),
                   (ut["Mbr"], ("rw_Mbr", u_)), (ut["Nn"][0], ("rw_N", u_, 0)), (ut["Nn"][1], ("rw_N", u_, 1)), (ut["NTr"][0], ("rw_NT", u_, 0)),
                   (ut["NTr"][1], ("rw_NT", u_, 1)), (ut["Z"], ("rw_Z", u_)), (ut["Zn"], ("rw_Zn", u_))]
        for tl, key in zl:
            nfree = int(np.prod(tl.shape[1:]))
            src_ = zsrc[64:128, 0:nfree] if len(tl.shape) == 2 else zsrc[64:128, 0:nfree].rearrange("p (a b) -> p a b", a=tl.shape[1])
            P.op("vector", lambda e, tl=tl, src_=src_: e.tensor_copy(out=tl[64:128], in_=src_), reads=["rw_zsrc"],
                 writes=[key] + ([("rw_Z2", key[1])] if key[0] == "rw_Z" else []))
        identr = P.sbuf("rw_identr", [128, 128], F32R)
        P.op("vector", lambda e: e.tensor_copy(out=identr[:], in_=ident[:]), reads=["ident"], writes=["rw_identr"])
        r_ = psh[:, 0:4, :]; k_ = psh[:, 4:8, :]; v_ = psh[:, 8:12, :]
        TA = [("rw_tA", fc) for fc in range(4)]; TB = [("rw_tB", fc) for fc in range(4)]
        for gx in range(min(T // NG, RW_NG)):
            t0 = gx * NG
            isctx = t0 < 256
            rdk = sorted({grp512(max(t0 - 1, 0)), grp512(t0), grp512(min(t0 + NG, T - 1))})
            rdk = [("prw", "fm", g) for g in rdk]
            hasL = not (t0 == 0 or t0 == 256)
            hasR = not (t0 + NG == 256 or t0 + NG == T)
            lo = t0 - 1 if hasL else t0
            hi = t0 + NG + 1 if hasR else t0 + NG
            P.dma("sync", pin[:, :, lo - (t0 - 1):hi - (t0 - 1)], prw_v[:, :, lo:hi], reads=rdk, writes=["rw_pin"])
            if not hasL:
                P.op("gpsimd", lambda e: e.memset(pin[:, :, 0:1], 0.0), reads=["rw_pin"], writes=["rw_pinL"])
            if not hasR:
                P.op("gpsimd", lambda e: e.memset(pin[:, :, NG + 1:NG + 2], 0.0), reads=["rw_pin"], writes=["rw_pinR"])
            pk = ["rw_pin", "rw_pinL", "rw_pinR"]
            for ch in range(15):
                P.op("gpsimd", lambda e, ch=ch: e.tensor_scalar(out=psh[:, ch, :], in0=pin[:, ch, 1:NG + 1], scalar1=c0[:, ch:ch + 1],
                                                               scalar2=None, op0=ALU.mult), reads=pk + ["rw_c0"], writes=[("rw_psh", ch)])
                P.op("vector", lambda e, ch=ch: e.scalar_tensor_tensor(out=psh[:, ch, :], in0=pin[:, ch, 0:NG], scalar=mu[:, 0, ch:ch + 1],
                                                                      in1=psh[:, ch, :], op0=ALU.mult, op1=ALU.add),
                     reads=pk + ["rw_par", ("rw_psh", ch)], writes=[("rw_psh", ch)])
                P.op("vector", lambda e, ch=ch: e.scalar_tensor_tensor(out=psh[:, ch, :], in0=pin[:, ch, 2:NG + 2], scalar=mu[:, 1, ch:ch + 1],
                                                                      in1=psh[:, ch, :], op0=ALU.mult, op1=ALU.add),
                     reads=pk + ["rw_par", ("rw_psh", ch)], writes=[("rw_psh", ch)])
            RK = [("rw_psh", c) for c in range(0, 4)]; KK = [("rw_psh", c) for c in range(4, 8)]; VK = [("rw_psh", c) for c in range(8, 12)]
            P.op("scalar", lambda e: e.activation(out=psh[:, 12, :], in_=psh[:, 12, :], func=AF.Tanh), reads=[("rw_psh", 12)], writes=[("rw_psh", 12)])
            P.op("scalar", lambda e: e.activation(out=psh[:, 14, :], in_=psh[:, 14, :], func=AF.Sigmoid), reads=[("rw_psh", 14)],
                 writes=[("rw_psh", 14)])
            for which, (wsrc, srcch, dst) in enumerate(((w2s, 12, sgw), (a2s, 13, aa))):
                for d in range(2):
                    for fc in range(4):
                        ps, pk_ = psn()
                        P.op("tensor", lambda e, d=d, fc=fc, ps=ps, wsrc=wsrc, srcch=srcch: e.matmul(
                            ps[:, 0:NG], lhsT=wsrc[d * 64:(d + 1) * 64, fc * 128:(fc + 1) * 128], rhs=psh[d * 64:(d + 1) * 64, srcch, :],
                            start=True, stop=True), reads=["rw_par", ("rw_psh", srcch)], writes=[pk_])
                        P.op("scalar", lambda e, d=d, fc=fc, ps=ps, dst=dst, which=which: e.activation(
                            out=dst[d][:, fc, :], in_=ps[:, 0:NG], func=AF.Sigmoid, bias=w0a0[:, which, d, fc:fc + 1], scale=1.0),
                            reads=[pk_, "rw_par"], writes=[("rw_sa", which, d)])
            for fc in range(4):
                ps, pk_ = psn()
                P.op("tensor", lambda e, fc=fc, ps=ps: e.matmul(ps[:, 0:NG], lhsT=g2s[:, fc * 128:(fc + 1) * 128], rhs=psh[:, 14, :],
                                                              start=True, stop=True), reads=["rw_par", ("rw_psh", 14)], writes=[pk_])
                P.op("scalar", lambda e, fc=fc, ps=ps: e.copy(out=gst[:, fc, :], in_=ps[:, 0:NG]), reads=[pk_], writes=["rw_gst"])
            P.dma("gpsimd", g_v[:, :, t0:t0 + NG], gst[:], reads=["rw_gst"], writes=[("rw_g", grp512(t0))])
            for fc in range(4):
                P.op("gpsimd", lambda e, fc=fc: e.tensor_scalar(out=kk[:, fc, :], in0=psh[:, 4 + fc, :], scalar1=vec[:, 0, fc:fc + 1], scalar2=None,
                                                               op0=ALU.mult), reads=KK + ["rw_par"], writes=[("rw_kk", fc)])
                P.op("scalar", lambda e, fc=fc: e.activation(out=tA[:, fc, :], in_=kk[:, fc, :], func=AF.Square), reads=[("rw_kk", fc)],
                     writes=[("rw_tA", fc)])
                ps, pk_ = psn()
                P.op("tensor", lambda e, fc=fc, ps=ps: e.matmul(ps[:, 0:NG], lhsT=bones[:], rhs=tA[:, fc, :], start=True, stop=True),
                     reads=["rw_par", ("rw_tA", fc)], writes=[pk_])
                P.op("scalar", lambda e, fc=fc, ps=ps: e.activation(out=tB[:, fc, :], in_=ps[:, 0:NG], func=AF.Sqrt, bias=eps12[:, 0:1], scale=1.0),
                     reads=[pk_, "rw_eps"], writes=[("rw_tB", fc)])
                P.op("vector", lambda e, fc=fc: e.reciprocal(out=tB[:, fc, :], in_=tB[:, fc, :]), reads=[("rw_tB", fc)], writes=[("rw_tB", fc)])
                P.op("vector", lambda e, fc=fc: e.tensor_tensor(out=kk[:, fc, :], in0=kk[:, fc, :], in1=tB[:, fc, :], op=ALU.mult),
                     reads=[("rw_kk", fc), ("rw_tB", fc)], writes=[("rw_kk", fc)])
            KKN = [("rw_kk", fc) for fc in range(4)]
            for d in range(2):
                for fc in range(4):
                    P.op("gpsimd", lambda e, d=d, fc=fc: e.tensor_scalar(out=kd[d][:, fc, :], in0=aa[d][:, fc, :], scalar1=vec[:, 1, fc:fc + 1],
                                                                        scalar2=omka[:, fc:fc + 1], op0=ALU.mult, op1=ALU.add),
                         reads=[("rw_sa", 1, d), "rw_par", "rw_omka"], writes=[("rw_kd", d)])
                P.op("gpsimd", lambda e, d=d: e.tensor_tensor(out=kd[d][:], in0=kd[d][:], in1=k_, op=ALU.mult),
                     reads=[("rw_kd", d)] + KK, writes=[("rw_kd", d)])
                P.op("vector", lambda e, d=d: e.tensor_tensor(out=aa[d][:], in0=aa[d][:], in1=kk[:], op=ALU.mult),
                     reads=[("rw_sa", 1, d), ("rw_kd", d)] + KKN, writes=[("rw_sa", 1, d)])
            P.op("gpsimd", lambda e: e.tensor_tensor(out=tA[:], in0=kd[0][:], in1=kd[1][:], op=ALU.add),
                 reads=[("rw_kd", 0), ("rw_kd", 1)], writes=TA)
            P.op("gpsimd", lambda e: e.tensor_tensor(out=tA[:], in0=tA[:], in1=r_, op=ALU.mult), reads=TA + RK, writes=TA)
            for fc in range(4):
                P.op("gpsimd", lambda e, fc=fc: e.tensor_scalar(out=tA[:, fc, :], in0=tA[:, fc, :], scalar1=vec[:, 2, fc:fc + 1], scalar2=None,
                                                               op0=ALU.mult), reads=[("rw_tA", fc), "rw_par"], writes=[("rw_tA", fc)])
                ps, pk_ = psn()
                P.op("tensor", lambda e, fc=fc, ps=ps: e.matmul(ps[:, 0:NG], lhsT=bones[:], rhs=tA[:, fc, :], start=True, stop=True),
                     reads=["rw_par", ("rw_tA", fc)], writes=[pk_])
                P.op("vector", lambda e, fc=fc, ps=ps: e.tensor_tensor(out=gst[:, fc, :], in0=ps[:, 0:NG], in1=psh[:, 8 + fc, :], op=ALU.mult),
                     reads=[pk_, "rw_gst"] + VK, writes=["rw_gst"])
            P.dma("gpsimd", bv_v[:, :, t0:t0 + NG], gst[:], reads=["rw_gst"], writes=[("rw_bv", grp512(t0))])
            for c in range(NC):
                ps, pk_ = psn()
                for fc in range(4):
                    P.op("tensor", lambda e, c=c, fc=fc, ps=ps: e.transpose(out=ps[0:64, fc * 128:(fc + 1) * 128],
                                                                          in_=psh[:, 8 + fc, c * 64:(c + 1) * 64], identity=ident[:]),
                         reads=VK + ["ident"], writes=[pk_])
                P.op("scalar", lambda e, c=c, ps=ps: e.copy(out=Vt[c][0:64, :], in_=ps[0:64, :]), reads=[pk_], writes=[("rw_Vt", c)])
            for d in range(2):
                P.op("vector", lambda e, d=d: e.tensor_tensor_scan(out=Lp[:].rearrange("p a b -> p (a b)"),
                                                                  data0=rmask[:].rearrange("p a b -> p (a b)"),
                                                                  data1=sgw[d][:].rearrange("p a b -> p (a b)"), initial=0.0,
                                                                  op0=ALU.mult, op1=ALU.add),
                     reads=[("rw_sa", 0, d), "rw_rmask"], writes=["rw_Lp"])
                Lp3 = Lp[:].rearrange("p a (c t) -> p (a c) t", t=64)
                Li3 = Li[:].rearrange("p a (c t) -> p (a c) t", t=64); Le3 = Le[:].rearrange("p a (c t) -> p (a c) t", t=64)
                tot_b = Lp3[:, :, 63:64].to_broadcast([128, 4 * NC, 64])
                if d == 0:
                    P.op("gpsimd", lambda e: e.tensor_copy(out=Li[:], in_=Lp[:]), reads=["rw_Lp"], writes=["rw_Li"])
                    P.op("vector", lambda e, d=d: e.tensor_tensor(out=Le[:], in0=Lp[:], in1=sgw[d][:], op=ALU.subtract),
                         reads=["rw_Lp", ("rw_sa", 0, d)], writes=["rw_Le"])
                else:
                    P.op("vector", lambda e, d=d: e.tensor_tensor(out=Le[:], in0=Lp[:], in1=sgw[d][:], op=ALU.subtract),
                         reads=["rw_Lp", ("rw_sa", 0, d)], writes=["rw_Le"])
                    P.op("vector", lambda e: e.tensor_tensor(out=Li3, in0=tot_b, in1=Le3, op=ALU.subtract), reads=["rw_Lp", "rw_Le"],
                         writes=["rw_Li"])
                    P.op("vector", lambda e: e.tensor_tensor(out=Le3, in0=tot_b, in1=Lp3, op=ALU.subtract), reads=["rw_Lp", "rw_Li"],
                         writes=["rw_Le"])
                P.op("scalar", lambda e: e.activation(out=etot[:].rearrange("p a c -> p (a c)"), in_=Lp3[:, :, 63], func=AF.Exp, scale=-CDEC),
                     reads=["rw_Lp"], writes=["rw_etot"])
                P.op("scalar", lambda e: e.activation(out=tA[:], in_=Li[:], func=AF.Exp, scale=-CDEC), reads=["rw_Li"], writes=TA)
                P.op("vector", lambda e: e.tensor_tensor(out=rt[:], in0=r_, in1=tA[:], op=ALU.mult), reads=RK + TA, writes=["rw_rt"])
                P.op("scalar", lambda e: e.activation(out=tB[:], in_=Le[:], func=AF.Exp, scale=-CDEC), reads=["rw_Le"], writes=TB)
                P.op("vector", lambda e: e.tensor_tensor(out=at[:], in0=kk[:], in1=tB[:], op=ALU.mult), reads=KKN + TB, writes=["rw_at"])
                P.op("scalar", lambda e: e.activation(out=tA[:], in_=Li[:], func=AF.Exp, scale=CDEC), reads=["rw_Li"], writes=TA)
                P.op("vector", lambda e, d=d: e.tensor_tensor(out=kt[:], in0=kd[d][:], in1=tA[:], op=ALU.mult), reads=[("rw_kd", d)] + TA, writes=["rw_kt"])
                P.op("vector", lambda e, d=d: e.tensor_tensor(out=bt[:], in0=aa[d][:], in1=tA[:], op=ALU.mult), reads=[("rw_sa", 1, d)] + TA, writes=["rw_bt"])
                et_b = etot[:].bitcast(F32).rearrange("p a c -> p (a c)").unsqueeze(2).to_broadcast([128, 4 * NC, 64])
                P.op("vector", lambda e: e.tensor_tensor(out=Kh[:].rearrange("p a (c t) -> p (a c) t", t=64),
                                                         in0=kt[:].bitcast(F32).rearrange("p a (c t) -> p (a c) t", t=64), in1=et_b, op=ALU.mult),
                     reads=["rw_kt", "rw_etot"], writes=["rw_Kh"])
                P.op("gpsimd", lambda e: e.tensor_tensor(out=Bh[:].rearrange("p a (c t) -> p (a c) t", t=64),
                                                         in0=bt[:].bitcast(F32).rearrange("p a (c t) -> p (a c) t", t=64), in1=et_b, op=ALU.mult),
                     reads=["rw_bt", "rw_etot"], writes=["rw_Bh"])
                for xi, (nm, X) in enumerate((("at", at), ("kt", kt), ("bt", bt))):
                    for eo in range(2):
                        if (xi + eo) % 2 == 0:
                            P.op("scalar", lambda e, nm=nm, X=X, eo=eo: e.activation(
                                out=mk[nm][eo][:], in_=X[:].bitcast(F32), func=AF.Copy, scale=bones[:, eo * 64:eo * 64 + 1]),
                                reads=["rw_" + nm, "rw_par"], writes=[("rw_mk", nm, eo)])
                        else:
                            P.op("vector", lambda e, nm=nm, X=X, eo=eo: e.tensor_scalar(
                                out=mk[nm][eo][:], in0=X[:].bitcast(F32), scalar1=bones[:, eo * 64:eo * 64 + 1], scalar2=None, op0=ALU.mult),
                                reads=["rw_" + nm, "rw_par"], writes=[("rw_mk", nm, eo)])
                cg0 = gx * NC
                mS = 0 if d == 0 else 2
                mST = 2 if d == 0 else 0
                mI = 1 if d == 0 else 3
                def chunk_gen(c, cg, u):
                    sl = slice(c * 64, (c + 1) * 64)
                    KhT, BhT, Mka, Mkr, Mbr, Nn, NTr, Z, Zn = (UT[u][k_] for k_ in ("KhT", "BhT", "Mka", "Mkr", "Mbr", "Nn", "NTr", "Z", "Zn"))
                    sm = UT[u]["summ"]; smk = ("rw_summ", u)
                    yield
                    hd = lambda X, h, sl=sl: X[:, h // 2, sl]
                    hdL = lambda nm, h, sl=sl: mk[nm][h % 2][:, h // 2, sl]
                    for src, skey, dst, dkey in ((at[:].bitcast(F32), "rw_at", None, ("rw_Z", u)), (Kh[:], "rw_Kh", KhT, ("rw_KhT", u)),
                                                 (Bh[:], "rw_Bh", BhT, ("rw_BhT", u))):
                        ps, pk_ = psn()
                        for fc in range(4):
                            P.op("tensor", lambda e, fc=fc, ps=ps, src=src, sl=sl: e.transpose(out=ps[0:64, fc * 128:(fc + 1) * 128],
                                                                                            in_=src[:, fc, sl], identity=ident[:]),
                                 reads=[skey, "ident"], writes=[pk_])
                        if dst is None:
                            P.op("scalar", lambda e, ps=ps: e.copy(out=Z[0:64, :, 0:64], in_=ps[0:64, :].rearrange("p (h k) -> p h k", h=8)),
                                 reads=[pk_], writes=[("rw_Z", u)])
                        else:
                            P.op("scalar", lambda e, ps=ps, dst=dst: e.copy(out=dst[0:64, :], in_=ps[0:64, :]), reads=[pk_], writes=[dkey])
                    if RW_ST <= 1:
                        return
                    yield
                    specs = (("bt", at, "MKbt", "rw_at", Nn[0], ("rw_N", u, 0), mS), ("at", bt, "MKat", "rw_bt", NTr[0], ("rw_NT", u, 0), mST),
                             ("kt", at, "MKkt", "rw_at", Mka, ("rw_Mka", u), mS), ("kt", rt, "MKkt", "rw_rt", Mkr, ("rw_Mkr", u), mI),
                             ("bt", rt, "MKbt", "rw_rt", Mbr, ("rw_Mbr", u), mI))
                    for si, (L_, R_, lk, rk_, dst, dkey, mi) in enumerate(specs):
                        ps, pk_ = psn()
                        for h in range(8):
                            P.op("tensor", lambda e, h=h, ps=ps, la=hdL(L_, h), ra=hd(R_, h): e.matmul(ps[0:64, h * 64:(h + 1) * 64], lhsT=la, rhs=ra,
                                                                                                     start=True, stop=True), reads=[("rw_mk", lk[2:], 0), ("rw_mk", lk[2:], 1), rk_], writes=[pk_])
                        P.op("vector" if si % 2 == 0 else "gpsimd" if False else "vector", lambda e, ps=ps, dst=dst, mi=mi: e.tensor_tensor(
                            out=dst[0:64], in0=ps[0:64, :].rearrange("p (h i) -> p h i", h=8), in1=mask[:, mi:mi + 1, :].to_broadcast([64, 8, 64]),
                            op=ALU.mult), reads=[pk_, "rw_par"], writes=[dkey])
                    if RW_ST <= 2:
                        return
                    yield
                    ps, pk_ = psn()
                    for h in range(8):
                        P.op("tensor", lambda e, h=h, ps=ps, c=c: e.matmul(ps[0:64, h * 64:(h + 1) * 64], lhsT=Mka[:, h, :],
                                                                          rhs=Vt[c][:, h * 64:(h + 1) * 64], start=True, stop=True),
                             reads=[("rw_Mka", u), ("rw_Vt", c)], writes=[pk_])
                    P.op("scalar", lambda e, ps=ps: e.copy(out=Z[0:64, :, 64:128], in_=ps[0:64, :].rearrange("p (h k) -> p h k", h=8)),
                         reads=[pk_], writes=[("rw_Z2", u)])
                    if RW_ST <= 3:
                        return
                    yield
                    cur = 0
                    for rnd in range(RW_RND):
                        N_, NT_ = Nn[cur], NTr[cur]
                        nk, ntk = ("rw_N", u, cur), ("rw_NT", u, cur)
                        pz = [psn(), psn()]
                        for h in range(8):
                            ps, pk_ = pz[h // 4]
                            P.op("tensor", lambda e, h=h, ps=ps, N_=N_: e.matmul(ps[0:64, (h % 4) * 128:(h % 4 + 1) * 128], lhsT=N_[:, h, :],
                                                                                rhs=Z[:, h, :], start=True, stop=True),
                                 reads=[nk, ("rw_Z", u), ("rw_Z2", u)], writes=[pk_])
                        for half in range(2):
                            ps, pk_ = pz[half]
                            P.op("vector", lambda e, half=half, ps=ps, rnd=rnd: e.tensor_tensor(
                                out=Z[0:64, half * 4:(half + 1) * 4, :], in0=Z[0:64, half * 4:(half + 1) * 4, :].bitcast(F32),
                                in1=ps[0:64, :].rearrange("p (h k) -> p h k", h=4), op=(ALU.subtract if rnd == 0 else ALU.add)),
                                reads=[pk_, ("rw_Z", u), ("rw_Z2", u)], writes=[("rw_Z", u), ("rw_Z2", u)])
                        yield
                        if rnd < 5:
                            nxt = 1 - cur
                            ps, pk_ = psn()
                            for h in range(8):
                                P.op("tensor", lambda e, h=h, ps=ps, N_=N_, NT_=NT_: e.matmul(ps[0:64, h * 64:(h + 1) * 64], lhsT=NT_[:, h, :],
                                                                                             rhs=N_[:, h, :], start=True, stop=True),
                                     reads=[nk, ntk], writes=[pk_])
                            P.op("scalar", lambda e, ps=ps, nxt=nxt: e.copy(out=Nn[nxt][0:64], in_=ps[0:64, :].rearrange("p (h k) -> p h k", h=8)),
                                 reads=[pk_], writes=[("rw_N", u, nxt)])
                            if rnd < 4:
                                ps, pk_ = psn()
                                for h in range(8):
                                    P.op("tensor", lambda e, h=h, ps=ps, N_=N_, NT_=NT_: e.matmul(ps[0:64, h * 64:(h + 1) * 64], lhsT=N_[:, h, :],
                                                                                                 rhs=NT_[:, h, :], start=True, stop=True),
                                         reads=[nk, ntk], writes=[pk_])
                                P.op("gpsimd" if False else "scalar", lambda e, ps=ps, nxt=nxt: e.copy(
                                    out=NTr[nxt][0:64], in_=ps[0:64, :].rearrange("p (h k) -> p h k", h=8)), reads=[pk_], writes=[("rw_NT", u, nxt)])
                            cur = nxt
                        yield
                    P.op("scalar", lambda e: e.activation(out=Zn[0:64], in_=Z[0:64].bitcast(F32), func=AF.Copy, scale=-1.0),
                         reads=[("rw_Z", u), ("rw_Z2", u)], writes=[("rw_Zn", u)])
                    if RW_ST <= 4:
                        return
                    yield
                    ps, pk_ = psn()
                    for h in range(8):
                        P.op("tensor", lambda e, h=h, ps=ps: e.matmul(ps[0:64, h * 64:(h + 1) * 64], lhsT=Zn[:, h, 0:64],
                                                                     rhs=BhT[:, h * 64:(h + 1) * 64], start=True, stop=True),
                             reads=[("rw_Zn", u), ("rw_BhT", u)], writes=[pk_])
                    P.op("scalar", lambda e, ps=ps, sm=sm: e.copy(out=sm[:, 0:512], in_=ps[0:64, :]), reads=[pk_], writes=[(smk, 0)])
                    if RW_ST <= 5:
                        return
                    yield
                    ps, pk_ = psn()
                    for h in range(8):
                        P.op("tensor", lambda e, h=h, ps=ps, c=c: e.matmul(ps[0:64, h * 64:(h + 1) * 64], lhsT=KhT[:, h * 64:(h + 1) * 64],
                                                                          rhs=Vt[c][:, h * 64:(h + 1) * 64], start=True, stop=False),
                             reads=[("rw_KhT", u), ("rw_Vt", c)], writes=[pk_])
                        P.op("tensor", lambda e, h=h, ps=ps: e.matmul(ps[0:64, h * 64:(h + 1) * 64], lhsT=BhT[:, h * 64:(h + 1) * 64],
                                                                     rhs=Zn[:, h, 64:128], start=False, stop=True),
                             reads=[("rw_BhT", u), ("rw_Zn", u)], writes=[pk_])
                    P.op("vector", lambda e, ps=ps, sm=sm: e.tensor_copy(out=sm[:, 512:1024], in_=ps[0:64, :]), reads=[pk_], writes=[(smk, 1)])
                    if RW_ST <= 6:
                        return
                    yield
                    ps, pk_ = psn()
                    for h in range(8):
                        p0 = (h % 2) * 64
                        P.op("tensor", lambda e, h=h, ps=ps, p0=p0, ra=hd(rt, h): e.matmul(ps[0:64, h * 64:(h + 1) * 64],
                                                                                          lhsT=identr[:, p0:p0 + 64], rhs=ra,
                                                                                          start=True, stop=False),
                             reads=["rw_identr", "rw_rt"], writes=[pk_])
                        P.op("tensor", lambda e, h=h, ps=ps: e.matmul(ps[0:64, h * 64:(h + 1) * 64], lhsT=Zn[:, h, 0:64], rhs=Mbr[:, h, :],
                                                                     start=False, stop=True), reads=[("rw_Zn", u), ("rw_Mbr", u)], writes=[pk_])
                    P.op("scalar", lambda e, ps=ps, sm=sm: e.copy(out=sm[:, 1024:1536], in_=ps[0:64, :]), reads=[pk_], writes=[(smk, 2)])
                    if RW_ST <= 7:
                        return
                    yield
                    ps, pk_ = psn()
                    for h in range(8):
                        P.op("tensor", lambda e, h=h, ps=ps, c=c: e.matmul(ps[0:64, h * 64:(h + 1) * 64], lhsT=Mkr[:, h, :],
                                                                          rhs=Vt[c][:, h * 64:(h + 1) * 64], start=True, stop=False),
                             reads=[("rw_Mkr", u), ("rw_Vt", c)], writes=[pk_])
                        P.op("tensor", lambda e, h=h, ps=ps: e.matmul(ps[0:64, h * 64:(h + 1) * 64], lhsT=Mbr[:, h, :], rhs=Zn[:, h, 64:128],
                                                                     start=False, stop=True), reads=[("rw_Mbr", u), ("rw_Zn", u)], writes=[pk_])
                    P.op("vector", lambda e, ps=ps, sm=sm: e.tensor_copy(out=sm[:, 1536:2048], in_=ps[0:64, :]), reads=[pk_], writes=[(smk, 3)])
                    if RW_ST <= 8:
                        return
                    yield
                    ps, pk_ = psn()
                    for h in range(8):
                        p0 = (h % 2) * 64
                        P.op("tensor", lambda e, h=h, ps=ps, p0=p0, c=c: e.matmul(ps[0:64, h:h + 1], lhsT=ident[:, p0:p0 + 64],
                                                                                 rhs=etot[:].bitcast(F32)[:, h // 2, c:c + 1], start=True, stop=True),
                             reads=["ident", "rw_etot"], writes=[pk_])
                    P.op("vector", lambda e, ps=ps, sm=sm: e.tensor_copy(out=sm[:, 2048:2056], in_=ps[0:64, 0:8]), reads=[pk_], writes=[(smk, 4)])
                    P.dma("gpsimd", summ_d[d, cg], sm[:], reads=[(smk, i) for i in range(5)], writes=[("rw_summd", d, cg)])
                gens = [chunk_gen(c, cg0 + c, c % 2) for c in range(NC if RW_CH else 0)]
                while gens:
                    for g_ in list(gens):
                        try:
                            next(g_)
                        except StopIteration:
                            gens.remove(g_)
        if "rwprep" in dbg and l == 0:
            for nm, src in (("g", g_d), ("bv", bv_d)):
                d_ = dbg_out("rw_" + nm, [512, T])
                P.dma("sync", d_, src, reads=[("rw_" + nm, i) for i in range(9)], writes=["OUT_dbgrw" + nm])
        P.barrier(); P.release(m1)
        if RW_PH < 2:
            return
        ST = [P.sbuf("rw_ST%d" % i, [64, 8, 64]) for i in range(2)]
        sm2 = [P.sbuf("rw_sm2_%d" % i, [64, 2056]) for i in range(3)]
        yt = [P.sbuf("rw_yt%d" % i, [64, 512]) for i in range(2)]
        stt = P.sbuf("rw_stt", [64, 8, 64])
        step = 0
        for d in range(2):
            order = list(range(NCH)) if d == 0 else [3, 2, 1, 0] + list(range(NCH - 1, 3, -1))
            cur = 0
            P.op("vector", lambda e: e.memset(ST[0][:], 0.0), reads=[("rw_ST", 0)], writes=[("rw_ST", 0)])
            for cg in order:
                b3 = step % 3
                b2 = step % 2
                step += 1
                P.dma("sync", sm2[b3][:], summ_d[d, cg], reads=[], writes=[("rw_sm2", b3)])
                S_, Sn_ = ST[cur], ST[1 - cur]
                psy, pyk = psn(); pss, psk = psn()
                for h in range(8):
                    P.op("tensor", lambda e, h=h, psy=psy, S_=S_, b3=b3: e.matmul(psy[0:64, h * 64:(h + 1) * 64],
                                                                               lhsT=sm2[b3][:, 1024 + h * 64:1024 + (h + 1) * 64], rhs=S_[:, h, :],
                                                                               start=True, stop=True),
                         reads=[("rw_sm2", b3), ("rw_ST", cur)], writes=[pyk])
                for h in range(8):
                    P.op("tensor", lambda e, h=h, pss=pss, S_=S_, b3=b3: e.matmul(pss[0:64, h * 64:(h + 1) * 64],
                                                                               lhsT=sm2[b3][:, h * 64:(h + 1) * 64], rhs=S_[:, h, :],
                                                                               start=True, stop=True),
                         reads=[("rw_sm2", b3), ("rw_ST", cur)], writes=[psk])
                P.op("gpsimd", lambda e, S_=S_, b3=b3: e.tensor_tensor(out=stt[:], in0=S_[:], in1=sm2[b3][:, 2048:2056].unsqueeze(2).to_broadcast([64, 8, 64]),
                                                                      op=ALU.mult), reads=[("rw_ST", cur), ("rw_sm2", b3)], writes=["rw_stt"])
                P.op("gpsimd", lambda e, b3=b3: e.tensor_tensor(out=stt[:], in0=stt[:], in1=sm2[b3][:, 512:1024].rearrange("p (h k) -> p h k", h=8),
                                                               op=ALU.add), reads=["rw_stt", ("rw_sm2", b3)], writes=["rw_stt"])
                P.op("vector", lambda e, pss=pss, Sn_=Sn_: e.tensor_tensor(out=Sn_[:], in0=pss[0:64, :].rearrange("p (h k) -> p h k", h=8),
                                                                          in1=stt[:], op=ALU.add), reads=[psk, "rw_stt"], writes=[("rw_ST", 1 - cur)])
                P.op("vector", lambda e, psy=psy, b2=b2, b3=b3: e.tensor_tensor(out=yt[b2][:], in0=psy[0:64, :], in1=sm2[b3][:, 1536:2048], op=ALU.add),
                     reads=[pyk, ("rw_sm2", b3)], writes=[("rw_yt", b2)])
                P.dma("gpsimd", y_d[d, cg * 64:(cg + 1) * 64, :], yt[b2][:], reads=[("rw_yt", b2)], writes=[("rw_yd", d, cg // 2)])
                cur = 1 - cur
        if "rwy" in dbg and l == 0:
            d_ = dbg_out("rw_y", [2, T, 512])
            P.dma("sync", d_, y_d, reads=[("rw_yd", d, t) for d in range(2) for t in range(NT)], writes=["OUT_dbgrwy"])
        P.barrier(); P.release(m1)
        if RW_PH < 3:
            return
        y0 = [P.sbuf("rw_y0_%d" % i, [128, 512]) for i in range(2)]; y1 = [P.sbuf("rw_y1_%d" % i, [128, 512]) for i in range(2)]
        sqt = [P.sbuf("rw_sq%d" % i, [128, 512]) for i in range(2)]
        st8 = [P.sbuf("rw_st8_%d" % i, [128, 4, 8]) for i in range(2)]
        bvt = [P.sbuf("rw_bvt%d" % i, [128, 4, 128]) for i in range(2)]; gt = [P.sbuf("rw_gt%d" % i, [128, 4, 128]) for i in range(2)]
        of = [P.sbuf("rw_of%d" % i, [128, 4, 128]) for i in range(2)]; ob = [P.sbuf("rw_ob%d" % i, [128, 4, 128], BF16) for i in range(2)]
        orw_v = orw_d.rearrange("(k p) t -> p k t", p=128)
        for t in range(NT):
            s_ = t % 2
            gi = 0 if t < 2 else 1 + (t - 2) // 4
            P.dma("sync", y0[s_][:], y_d[0, t * 128:(t + 1) * 128, :], reads=[("rw_yd", 0, t)], writes=[("rw_y0", s_)])
            P.dma("sync", y1[s_][:], y_d[1, t * 128:(t + 1) * 128, :], reads=[("rw_yd", 1, t)], writes=[("rw_y1", s_)])
            P.dma("sync", bvt[s_][:], bv_v[:, :, t * 128:(t + 1) * 128], reads=[("rw_bv", i) for i in range(9)], writes=[("rw_bvt", s_)])
            P.dma("sync", gt[s_][:], g_v[:, :, t * 128:(t + 1) * 128], reads=[("rw_g", i) for i in range(9)], writes=[("rw_gt", s_)])
            ys = y0[s_]; st = st8[s_]
            P.op("gpsimd", lambda e, s_=s_: e.tensor_tensor(out=y0[s_][:], in0=y0[s_][:], in1=y1[s_][:], op=ALU.add),
                 reads=[("rw_y0", s_), ("rw_y1", s_)], writes=[("rw_y0", s_)])
            y3 = ys[:].rearrange("p (h n) -> p h n", h=8)
            P.op("vector", lambda e, st=st, y3=y3: e.tensor_reduce(out=st[:, 0, :], in_=y3, axis=AX.X, op=ALU.add), reads=[("rw_y0", s_)],
                 writes=[("rw_st8", s_)])
            P.op("scalar", lambda e, s_=s_, ys=ys: e.activation(out=sqt[s_][:], in_=ys[:], func=AF.Square), reads=[("rw_y0", s_)],
                 writes=[("rw_sq", s_)])
            P.op("vector", lambda e, st=st, s_=s_: e.tensor_reduce(out=st[:, 1, :], in_=sqt[s_][:].rearrange("p (h n) -> p h n", h=8), axis=AX.X,
                                                                  op=ALU.add), reads=[("rw_sq", s_), ("rw_st8", s_)], writes=[("rw_st8", s_)])
            P.op("vector", lambda e, st=st: e.tensor_scalar(out=st[:, 0, :], in0=st[:, 0, :], scalar1=1.0 / 64, scalar2=None, op0=ALU.mult),
                 reads=[("rw_st8", s_)], writes=[("rw_st8", s_)])
            P.op("vector", lambda e, st=st: e.tensor_tensor(out=st[:, 2, :], in0=st[:, 0, :], in1=st[:, 0, :], op=ALU.mult),
                 reads=[("rw_st8", s_)], writes=[("rw_st8", s_)])
            P.op("vector", lambda e, st=st: e.scalar_tensor_tensor(out=st[:, 2, :], in0=st[:, 1, :], scalar=1.0 / 64, in1=st[:, 2, :],
                                                                  op0=ALU.mult, op1=ALU.subtract), reads=[("rw_st8", s_)], writes=[("rw_st8", s_)])
            P.op("scalar", lambda e, st=st: e.activation(out=st[:, 3, :], in_=st[:, 2, :], func=AF.Sqrt, bias=eps_gn[:, 0:1], scale=1.0),
                 reads=[("rw_st8", s_), "rw_eps"], writes=[("rw_st8", s_)])
            P.op("vector", lambda e, st=st: e.reciprocal(out=st[:, 3, :], in_=st[:, 3, :]), reads=[("rw_st8", s_)], writes=[("rw_st8", s_)])
            P.op("vector", lambda e, st=st, y3=y3: e.tensor_tensor(out=y3, in0=y3, in1=st[:, 0, :].unsqueeze(2).to_broadcast([128, 8, 64]),
                                                                  op=ALU.subtract), reads=[("rw_y0", s_), ("rw_st8", s_), ("rw_sq", s_)],
                 writes=[("rw_y0", s_)])
            P.op("vector", lambda e, st=st, y3=y3: e.tensor_tensor(out=y3, in0=y3, in1=st[:, 3, :].unsqueeze(2).to_broadcast([128, 8, 64]),
                                                                  op=ALU.mult), reads=[("rw_y0", s_), ("rw_st8", s_)], writes=[("rw_y0", s_)])
            ps, pk_ = psn()
            for fc in range(4):
                P.op("tensor", lambda e, fc=fc, ps=ps, ys=ys: e.transpose(out=ps[:, fc * 128:(fc + 1) * 128], in_=ys[:, fc * 128:(fc + 1) * 128],
                                                                        identity=ident[:]), reads=[("rw_y0", s_), "ident"], writes=[pk_])
            for fc in range(4):
                P.op("vector", lambda e, fc=fc, ps=ps, s_=s_: e.tensor_scalar(out=of[s_][:, fc, :], in0=ps[:, fc * 128:(fc + 1) * 128],
                                                                            scalar1=vec[:, 3, fc:fc + 1], scalar2=vec[:, 4, fc:fc + 1],
                                                                            op0=ALU.mult, op1=ALU.add),
                     reads=[pk_, "rw_par"], writes=[("rw_of", s_, fc)])
            P.op("gpsimd", lambda e, s_=s_: e.tensor_tensor(out=of[s_][:], in0=of[s_][:], in1=bvt[s_][:], op=ALU.add),
                 reads=[("rw_of", s_, fc) for fc in range(4)] + [("rw_bvt", s_)], writes=[("rw_of2", s_)])
            P.op("gpsimd", lambda e, s_=s_: e.tensor_tensor(out=ob[s_][:], in0=of[s_][:], in1=gt[s_][:], op=ALU.mult),
                 reads=[("rw_of2", s_), ("rw_gt", s_)], writes=[("rw_ob", s_)])
            P.dma("gpsimd", orw_v[:, :, t * 128:(t + 1) * 128], ob[s_][:], reads=[("rw_ob", s_)], writes=[("orw", gi)])
        if "orw" in dbg and l == 0:
            d_ = dbg_out("orw", [512, T], BF16)
            P.dma("sync", d_, orw_d, reads=[("orw", g) for g in range(9)], writes=["OUT_dbgorw"])

    xT_v = xT.rearrange("(k p) t -> p k t", p=128)
    def stage_inproj(l):
        hT = P.sbuf("hT", [128, 8, T], BF16)
        xg = [P.sbuf("xg%d" % i, [128, 8, 512]) for i in range(2)]
        wblk = [P.sbuf("wblk%d" % i, [128, 8, 512]) for i in range(2)]
        wbf = [P.sbuf("wbf%d" % i, [128, 8, 512], BF16) for i in range(2)]
        ost = [P.sbuf("ost%d" % i, [128, 512]) for i in range(4)]
        ostb = [P.sbuf("ostb%d" % i, [128, 512], BF16) for i in range(4)]
        for gi, (t0, n) in enumerate(GROUPS):
            s = gi % 2
            c = 1 if gi == 0 else 0
            P.dma("sync", xg[s][:, :, :n], xT_v[:, :, t0:t0 + n], reads=[("xT", gi)], writes=[("xg", s)])
            for k in range(8):
                eng = "vector" if k % 2 == 0 else "gpsimd"
                P.op(eng, lambda e, s=s, k=k, c=c, l=l, t0=t0, n=n: e.tensor_scalar(
                    out=hT[:, k, t0:t0 + n], in0=xg[s][:, k, :n], scalar1=mod1[:, l, 8 + k, c:c + 1],
                    scalar2=mod[:, l, k, c:c + 1], op0=ALU.mult, op1=ALU.add),
                    reads=[("xg", s), "mod", "mod1"], writes=[("hT", gi)])
        if "hT" in dbg and l == 0:
            d = dbg_out("hT", [128, 8 * T], BF16)
            P.dma("sync", d, hT[:].rearrange("p k t -> p (k t)"), reads=[("hT", g) for g in range(9)], writes=["OUT_dbghT"])
        blocks = [(3072, 512, "fm_bf", qT, 0), (3584, 512, "fm_bf", kT, 0), (4096, 512, "tm_bf", vtok, 0),
                  (4608, 512, "fm", prw, 0), (5120, 512, "fm", prw, 512), (5632, 512, "fm", prw, 1024),
                  (6144, 384, "fm", prw, 1536), (6528, 512, "fm", sguU, 0), (7040, 512, "tm", sguV, 0)]
        nev = 0
        for bi, (c0, ncol, kind, dst, r0) in enumerate(blocks):
            s = bi % 2
            P.dma("sync", wblk[s][:, :, :ncol], w_in[l, :, c0:c0 + ncol].rearrange("(k p) m -> p k m", p=128),
                  writes=[("wblk", s)])
            for k in range(8):
                eng = "gpsimd" if k % 2 == 0 else "vector"
                P.op(eng, lambda e, s=s, k=k, ncol=ncol: e.tensor_copy(out=wbf[s][:, k, :ncol], in_=wblk[s][:, k, :ncol]),
                     reads=[("wblk", s)], writes=[("wbf", s)])
            if kind.startswith("fm"):
                for gi, (t0, n) in enumerate(GROUPS):
                    for mi in range(ncol // 128):
                        pb = 2 + nev % 4
                        for k in range(8):
                            P.op("tensor", lambda e, s=s, k=k, mi=mi, t0=t0, n=n, pb=pb: e.matmul(
                                PS[pb][:, :n], lhsT=wbf[s][:, k, mi * 128:(mi + 1) * 128], rhs=hT[:, k, t0:t0 + n],
                                start=(k == 0), stop=(k == 7)), reads=[("wbf", s), ("hT", gi)], writes=[("ps", pb)])
                        so = nev % 4
                        o_t = ostb[so] if kind == "fm_bf" else ost[so]
                        okey = ("ostb", so) if kind == "fm_bf" else ("ost", so)
                        if nev % 2 == 0:
                            P.op("scalar", lambda e, o_t=o_t, pb=pb, n=n: e.copy(out=o_t[:, :n], in_=PS[pb][:, :n]),
                                 reads=[("ps", pb)], writes=[okey])
                        else:
                            P.op("vector", lambda e, o_t=o_t, pb=pb, n=n: e.tensor_copy(out=o_t[:, :n], in_=PS[pb][:, :n]),
                                 reads=[("ps", pb)], writes=[okey])
                        rr = r0 + mi * 128
                        P.dma("gpsimd", dst[rr:rr + 128, t0:t0 + n], o_t[:, :n], reads=[okey], writes=[(dst.name, "fm", gi)])
                        nev += 1
            else:
                for t in range(NT):
                    pb = 2 + nev % 4
                    gi = 0 if t < 2 else 1 + (t - 2) // 4
                    for k in range(8):
                        P.op("tensor", lambda e, s=s, k=k, t=t, pb=pb: e.matmul(
                            PS[pb][:, :], lhsT=hT[:, k, t * 128:(t + 1) * 128], rhs=wbf[s][:, k, :],
                            start=(k == 0), stop=(k == 7)), reads=[("wbf", s), ("hT", gi)], writes=[("ps", pb)])
                    so = nev % 4
                    o_t = ostb[so] if kind == "tm_bf" else ost[so]
                    okey = ("ostb", so) if kind == "tm_bf" else ("ost", so)
                    if nev % 2 == 0:
                        P.op("scalar", lambda e, o_t=o_t, pb=pb: e.copy(out=o_t[:], in_=PS[pb][:]), reads=[("ps", pb)], writes=[okey])
                    else:
                        P.op("vector", lambda e, o_t=o_t, pb=pb: e.tensor_copy(out=o_t[:], in_=PS[pb][:]), reads=[("ps", pb)], writes=[okey])
                    P.dma("gpsimd", dst[t * 128:(t + 1) * 128, :], o_t[:], reads=[okey], writes=[(dst.name, "tm", t)])
                    nev += 1
        if "p" in dbg and l == 0:
            for nm, src, shp, dt in (("qT", qT, [512, T], BF16), ("kT", kT, [512, T], BF16), ("vtok", vtok, [T, 512], BF16),
                                     ("prw", prw, [1920, T], F32), ("sguU", sguU, [512, T], F32), ("sguV", sguV, [T, 512], F32)):
                d = dbg_out(nm, shp, dt)
                rk = [(src.name, "fm", g) for g in range(9)] + [(src.name, "tm", t) for t in range(NT)]
                P.dma("sync", d, src, reads=rk, writes=["OUT_dbg" + nm])
    for l in range(n_layers):
        stage_inproj(l)
        if stop == "s1":
            return P, dbg_t
        P.barrier(); P.release(m0)
        stage_sgu(l)
        if stop == "s2":
            return P, dbg_t
        P.barrier(); P.release(m0)
        stage_na(l)
        if stop == "s3":
            return P, dbg_t
        P.barrier(); P.release(m0)
        stage_rwkv(l)
        if stop == "s4":
            return P, dbg_t
        P.barrier(); P.release(m0)
        stage_merge(l)
        if stop == "s5":
            return P, dbg_t
        P.barrier(); P.release(m0)
        stage_ffn(l)
        if stop == "s6":
            return P, dbg_t
        P.barrier(new_epoch=True); P.release(m0)
    if mode == "full":
        stage_final()
    else:
        for gi, (t0, n) in enumerate(GROUPS):
            P.dma("sync", xT_out[:, t0:t0 + n], xT[:, t0:t0 + n], reads=[("xT", gi)], writes=["OUT_x%d" % gi])
    return P, dbg_t


def host_inputs(inputs, b, l0=0, nl=DEPTH, xT=None):
    m = {}
    if xT is None:
        x = np.asarray(inputs["x"], np.float32)
        ctx = np.asarray(inputs["ctx"], np.float32)
        m["xin"] = np.ascontiguousarray(np.concatenate([ctx[b], x[b]], axis=0))
    else:
        m["xT_in"] = xT
    m["ccT"] = np.ascontiguousarray(np.stack([np.asarray(inputs["c"], np.float32)[b], np.asarray(inputs["c_ctx"], np.float32)], axis=1))
    m["ident"] = np.eye(128, dtype=np.float32)
    f = lambda k: np.asarray(inputs[k], np.float32)[l0:l0 + nl]
    m["w_ada"] = f("w_ada")
    m["b_adaT"] = np.ascontiguousarray(f("b_ada").reshape(nl, 48, 128).transpose(0, 2, 1))
    m["w_in"] = f("w_in")
    m["sgu_lng"] = np.ascontiguousarray(np.broadcast_to(f("sgu_ln_g")[:, None, :], (nl, 128, 512)))
    m["sgu_lnb"] = np.ascontiguousarray(np.broadcast_to(f("sgu_ln_b")[:, None, :], (nl, 128, 512)))
    m["sgu_wT"] = np.ascontiguousarray(f("sgu_w").transpose(0, 3, 1, 2))
    m["sgu_bB"] = np.ascontiguousarray(np.broadcast_to(f("sgu_b")[:, None, :, :], (nl, 64, 8, 128)))
    m["na_bias"] = na_bias_layout(f("na_rpb"))
    fmN = lambda a, n: a.reshape(a.shape[0], n, 128).transpose(2, 0, 1)
    m["rw_mu"] = np.ascontiguousarray(np.stack([fmN(f("rwkv_mu_prev"), 15), fmN(f("rwkv_mu_next"), 15)], axis=2))
    w0 = f("rwkv_w0").reshape(nl, 2, 4, 128).transpose(3, 0, 1, 2); a0 = f("rwkv_a0").reshape(nl, 2, 4, 128).transpose(3, 0, 1, 2)
    m["rw_w0a0"] = np.ascontiguousarray(np.stack([w0, a0], axis=2))
    m["rw_w2"] = np.ascontiguousarray(f("rwkv_w2").reshape(nl, 128, 512)); m["rw_a2"] = np.ascontiguousarray(f("rwkv_a2").reshape(nl, 128, 512))
    m["rw_g2"] = f("rwkv_g2")
    m["rw_vec"] = np.ascontiguousarray(np.stack([fmN(f(k).reshape(nl, 512), 4) for k in
                                                 ("rwkv_k_k", "rwkv_k_a", "rwkv_r_k", "rwkv_gn_g", "rwkv_gn_b")], axis=2))
    jj, ii = np.meshgrid(np.arange(64), np.arange(64), indexing="ij")
    m["mask64"] = np.ascontiguousarray(np.stack([jj < ii, jj <= ii, jj > ii, jj >= ii], axis=1).astype(np.float32))
    bo = np.zeros((128, 128), np.float32); bo[:64, :64] = 1; bo[64:, 64:] = 1
    m["bones"] = bo
    m["w_branch"] = f("w_branch"); m["w_out"] = f("w_out"); m["ffn_w_gu"] = f("ffn_w_gu"); m["ffn_w_down"] = f("ffn_w_down")
    fm8 = lambda a: a.reshape(nl, 8, 128).transpose(2, 0, 1)
    m["lnp"] = np.ascontiguousarray(np.stack([fm8(f("ln1_g")), fm8(f("ln1_b")), fm8(f("ln2_g")), fm8(f("ln2_b"))], axis=2))
    return m


_NA_IDX = None


def na_bias_layout(rpb):
    global _NA_IDX
    if _NA_IDX is None:
        ridx = np.zeros((5, 128, 896), np.int64); cidx = np.zeros((5, 128, 896), np.int64); valid = np.zeros((5, 128, 896), bool)
        zero = np.zeros((5, 128, 896), bool)
        for pi, r in enumerate((0, 2, 4, 60, 62)):
            kb = min(max(r - 4, 0), 54)
            for qi in range(128):
                qr, c = r + qi // 64, qi % 64
                row0 = min(max(qr - 4, 0), 56); col0 = min(max(c - 8, 0), 48)
                for j in range(10):
                    kr = kb + j
                    if not (row0 <= kr < row0 + 8):
                        continue
                    for kc in range(col0, col0 + 16):
                        ridx[pi, qi, j * 64 + kc] = kr - qr + 7; cidx[pi, qi, j * 64 + kc] = kc - c + 15; valid[pi, qi, j * 64 + kc] = True
            zero[pi, :, 640:] = True
        _NA_IDX = (ridx, cidx, valid, zero)
    ridx, cidx, valid, zero = _NA_IDX
    g = rpb[:, :, ridx, cidx]
    g = np.where(valid[None, None], g, np.float32(-30000.0))
    g = np.where(zero[None, None], np.float32(0.0), g)
    return np.ascontiguousarray(g.transpose(0, 2, 3, 1, 4)).astype(np.float32)


_CACHE = {}


def _host_params(inputs, l0, nl):
    key = (id(inputs.get("w_in")), l0, nl)
    if key not in _CACHE:
        m = host_inputs(inputs, 0, l0, nl, xT=np.zeros((1,), np.float32))
        m.pop("xT_in"); m.pop("ccT")
        _CACHE[key] = m
    return _CACHE[key]


def kernel(**inputs):
    P, _ = build(n_layers=DEPTH, mode="full")
    nc = P.finalize()
    par = dict(host_inputs(inputs, 0))
    par.pop("xin"); par.pop("ccT")
    in_maps = []
    for core in range(8):
        m = dict(par)
        hb = host_inputs_x(inputs, core // 2)
        m.update(hb)
        in_maps.append(m)
    res = run_bass_kernel_spmd(nc, in_maps, core_ids=list(range(8)))
    return np.stack([res.results[2 * b]["out"] for b in range(4)], axis=0).astype(np.float32)


def host_inputs_x(inputs, b):
    x = np.asarray(inputs["x"], np.float32); ctx = np.asarray(inputs["ctx"], np.float32)
    return {"xin": np.ascontiguousarray(np.concatenate([ctx[b], x[b]], axis=0)),
            "ccT": np.ascontiguousarray(np.stack([np.asarray(inputs["c"], np.float32)[b], np.asarray(inputs["c_ctx"], np.float32)], axis=1))}
```

```python
import numpy as np
import concourse.bass as bass
import concourse.mybir as mybir
from concourse.bass_utils import run_bass_kernel_spmd

F32 = mybir.dt.float32
BF16 = mybir.dt.bfloat16
F32R = mybir.dt.float32r
ALU = mybir.AluOpType
AF = mybir.ActivationFunctionType
AX = mybir.AxisListType

D = 1024
DEPTH = 4
LCTX = 256
SEQ = 4096
T = LCTX + SEQ
NT = T // 128
GROUPS = [(0, 256)] + [(256 + 512 * i, 512) for i in range(8)]
D_IN = 7552
D_FF = 2816
ALPHA = (2 * DEPTH) ** 0.25


class Prog:
    ENGS = ("tensor", "vector", "scalar", "gpsimd", "sync")

    def __init__(self):
        self.nc = bass.Bass("TRN2", target_bir_lowering=False)
        self.ops = []
        self.n_dma_sems = 32
        arena = self.nc.alloc_sbuf_tensor("arena", [128, 212000], mybir.dt.uint8)
        self.arena_base = self.nc.lookup_mloc(arena).addr
        self.arena_size = 212000
        self.sp = 0
        self.nalloc = 0

    def dram(self, name, shape, dtype, kind="Internal"):
        return self.nc.dram_tensor(name, list(shape), dtype, kind=kind).ap()

    def sbuf(self, name, shape, dtype=F32):
        esz = {F32: 4, BF16: 2, F32R: 4}[dtype]
        nbytes = int(np.prod(shape[1:])) * esz
        off = (self.sp + 63) // 64 * 64
        assert off + nbytes <= self.arena_size, "SBUF arena overflow at %s: %d + %d" % (name, off, nbytes)
        self.sp = off + nbytes
        self.nalloc += 1
        return self.nc.alloc_sbuf_tensor_at("%s_%d" % (name, self.nalloc), list(shape), dtype, offset=self.arena_base + off)

    def mark(self):
        return self.sp

    def release(self, m):
        self.sp = m

    def barrier(self, new_epoch=False):
        self.ops.append(("barrier", new_epoch, (), (), False))

    def psum(self, name, shape, dtype=F32):
        return self.nc.alloc_psum_tensor(name, list(shape), dtype)

    def op(self, eng, fn, reads=(), writes=(), dma=False):
        self.ops.append((eng, fn, tuple(reads), tuple(writes), dma))

    def dma(self, eng, out, in_, reads=(), writes=(), slow=False):
        if slow:
            self.op(eng, lambda e: e.dma_start(out=out, in_=in_, allow_slow_non_contiguous=True), reads, writes, dma=True)
        else:
            self.op(eng, lambda e: e.dma_start(out=out, in_=in_), reads, writes, dma=True)

    def finalize(self):
        nc = self.nc
        ops = self.ops
        last_w = {}
        readers = {}
        deps = []
        force_sig = set()
        last_on = {}
        for i, (eng, fn, rd, wr, isdma) in enumerate(ops):
            if eng == "barrier":
                force_sig.update(last_on.values())
                last_w = {}
                readers = {}
                deps.append(set())
                continue
            if not isdma:
                last_on[eng] = i
            d = set()
            for k in rd:
                if k in last_w:
                    d.add(last_w[k])
            for k in wr:
                if k in last_w:
                    d.add(last_w[k])
                d.update(readers.get(k, {}).values())
            for k in rd:
                readers.setdefault(k, {})[(eng, i) if isdma else eng] = i
            for k in wr:
                last_w[k] = i
                readers[k] = {}
            d.discard(i)
            if eng == "tensor" and not isdma:
                d = {j for j in d if not (ops[j][0] == "tensor" and not ops[j][4])}
            deps.append(d)
        has_dep = [False] * len(ops)
        for d in deps:
            for j in d:
                has_dep[j] = True
        for j in force_sig:
            has_dep[j] = True
        sems = {e: nc.alloc_semaphore("s_" + e) for e in self.ENGS}
        dsems = [nc.alloc_semaphore("d%d" % j) for j in range(self.n_dma_sems)]
        cnt = {e: 0 for e in self.ENGS}
        sig = [None] * len(ops)
        waited = {e: {} for e in self.ENGS}
        ndma = 0
        final = {}
        dlast = {}
        for i, (e, fn, rd, wr, isdma) in enumerate(ops):
            if e == "barrier":
                for e1 in self.ENGS:
                    eng1 = getattr(nc, e1)
                    for e2 in self.ENGS:
                        if cnt[e2] > waited[e1].get(sems[e2].num, 0):
                            waited[e1][sems[e2].num] = cnt[e2]
                            eng1.wait_ge(sems[e2], cnt[e2])
                    for jn, (sd, vd) in dlast.items():
                        if vd > waited[e1].get(jn, 0):
                            waited[e1][jn] = vd
                            eng1.wait_ge(sd, vd)
                if fn:
                    nep = getattr(self, "_nep", 0) + 1
                    self._nep = nep
                    sems = {e_: nc.alloc_semaphore("s%d_%s" % (nep, e_)) for e_ in self.ENGS}
                    cnt = {e_: 0 for e_ in self.ENGS}
                continue
            eng = getattr(nc, e)
            need = {}
            for j in deps[i]:
                s, v = sig[j]
                if need.get(s.num, (None, 0))[1] < v:
                    need[s.num] = (s, v)
            if isdma:
                js = ndma % self.n_dma_sems
                v = 16 * (ndma // self.n_dma_sems + 1)
                if v > 16 and need.get(dsems[js].num, (None, 0))[1] < v - 16:
                    need[dsems[js].num] = (dsems[js], v - 16)
            for sn, (s, val) in need.items():
                if waited[e].get(sn, 0) >= val:
                    continue
                waited[e][sn] = val
                eng.wait_ge(s, val)
            ins = fn(eng)
            if isdma:
                ins.then_inc(dsems[js], 16)
                sig[i] = (dsems[js], v)
                dlast[dsems[js].num] = (dsems[js], v)
                ndma += 1
                if any(str(k).startswith("OUT") for k in wr):
                    if final.get(dsems[js].num, (None, 0))[1] < v:
                        final[dsems[js].num] = (dsems[js], v)
            elif has_dep[i]:
                cnt[e] += 1
                ins.then_inc(sems[e], 1)
                sig[i] = (sems[e], cnt[e])
            else:
                sig[i] = (sems[e], cnt[e])
        for sn, (s, v) in final.items():
            nc.sync.wait_ge(s, v)
        self.counts = dict(cnt, ndma=ndma, nops=len(ops))
        return nc


def build(n_layers=DEPTH, stop=None, dbg=(), mode="full"):
    NL = n_layers
    P = Prog()
    nc = P.nc
    if mode == "full":
        xin = P.dram("xin", [T, D], F32, "ExternalInput")
        out_d = P.dram("out", [SEQ, D], F32, "ExternalOutput")
    else:
        xT_in = P.dram("xT_in", [D, T], F32, "ExternalInput")
        xT_out = P.dram("xT_out", [D, T], F32, "ExternalOutput")
    ccT = P.dram("ccT", [D, 2], F32, "ExternalInput")
    ident_d = P.dram("ident", [128, 128], F32, "ExternalInput")
    w_ada = P.dram("w_ada", [NL, D, 6 * D], F32, "ExternalInput")
    b_adaT = P.dram("b_adaT", [NL, 128, 48], F32, "ExternalInput")
    w_in = P.dram("w_in", [NL, D, D_IN], F32, "ExternalInput")
    dbg_t = {}

    def dbg_out(name, shape, dtype=F32):
        dbg_t[name] = P.dram("dbg_" + name, shape, dtype, "ExternalOutput")
        return dbg_t[name]

    xT = P.dram("xT", [D, T], F32)
    qT = P.dram("qT", [512, T], BF16)
    kT = P.dram("kT", [512, T], BF16)
    vtok = P.dram("vtok", [T, 512], BF16)
    prw = P.dram("prw", [1920, T], F32)
    sguU = P.dram("sguU", [512, T], F32)
    sguV = P.dram("sguV", [T, 512], F32)

    ident = P.sbuf("ident_s", [128, 128])
    PS = [P.psum("ps%d" % i, [128, 512]) for i in range(6)]
    PSB = [P.psum("psb%d" % i, [128, 1024], BF16) for i in range(2)]
    mod = P.sbuf("mod", [128, NL, 48, 2])
    mod1 = P.sbuf("mod1", [128, NL, 48, 2])
    P.dma("sync", ident[:], ident_d, writes=["ident"])

    cc = P.sbuf("cc", [128, 8, 2])
    scc = P.sbuf("scc", [128, 8, 2])
    badaT = P.sbuf("badaT", [128, NL, 48])
    P.dma("sync", cc[:], ccT.rearrange("(k p) c -> p k c", p=128), writes=["cc"])
    P.dma("sync", badaT[:], b_adaT.rearrange("l p j -> p l j"), writes=["badaT"])
    P.op("scalar", lambda e: e.activation(out=scc[:], in_=cc[:], func=AF.Silu), reads=["cc"], writes=["scc"])
    identb = P.sbuf("identb", [128, 128], BF16)
    P.op("vector", lambda e: e.tensor_copy(out=identb[:], in_=ident[:]), reads=["ident"], writes=["identb"])
    lnp = P.sbuf("lnp", [128, NL, 4, 8])
    ones = P.sbuf("ones", [128, 128])
    eps5 = P.sbuf("eps5", [128, 1])
    P.op("vector", lambda e: e.memset(eps5[:], 1e-5), writes=["eps"])
    m0 = P.mark()
    wblk0 = [P.sbuf("wblk0%d" % i, [128, 8, 512]) for i in range(2)]
    nb = 0
    for l in range(n_layers):
        for cb in range(12):
            s = nb % 2
            P.dma("sync", wblk0[s][:], w_ada[l, :, cb * 512:(cb + 1) * 512].rearrange("(k p) m -> p k m", p=128),
                  writes=[("wblk0", s)])
            for mi in range(4):
                j = cb * 4 + mi
                for k in range(8):
                    P.op("tensor", lambda e, s=s, mi=mi, k=k, j=j: e.matmul(
                        PS[0][:, 2 * j:2 * j + 2], lhsT=wblk0[s][:, k, mi * 128:(mi + 1) * 128], rhs=scc[:, k, :],
                        start=(k == 0), stop=(k == 7)), reads=[("wblk0", s), "scc"], writes=[("ps", 0)])
            nb += 1
        for c in range(2):
            P.op("vector", lambda e, l=l, c=c: e.tensor_tensor(
                out=mod[:, l, :, c], in0=PS[0][:, 0:96].rearrange("p (j c) -> p j c", c=2)[:, :, c], in1=badaT[:, l, :], op=ALU.add),
                reads=[("ps", 0), "badaT"], writes=["mod"])
        P.op("vector", lambda e, l=l: e.tensor_scalar_add(out=mod1[:, l], in0=mod[:, l], scalar1=1.0), reads=["mod"], writes=["mod1"])

    xtile = [P.sbuf("xtile%d" % i, [128, D]) for i in range(2)]
    xTst = [P.sbuf("xTst%d" % i, [128, 8, 128]) for i in range(2)]
    if mode != "full":
        for gi, (t0, n) in enumerate(GROUPS):
            P.dma("sync", xT[:, t0:t0 + n], xT_in[:, t0:t0 + n], writes=[("xT", gi)])
    for t in range(NT if mode == "full" else 0):
        s = t % 2
        P.dma("sync", xtile[s][:], xin[t * 128:(t + 1) * 128, :], writes=[("xtile", s)])
        for half in range(2):
            pb = 1 + half
            for kk in range(4):
                k = half * 4 + kk
                P.op("tensor", lambda e, s=s, k=k, kk=kk, pb=pb: e.transpose(
                    out=PS[pb][:, kk * 128:(kk + 1) * 128], in_=xtile[s][:, k * 128:(k + 1) * 128], identity=ident[:]),
                    reads=[("xtile", s), "ident"], writes=[("ps", pb)])
            P.op("scalar" if half == 0 else "vector", (lambda e, s=s, half=half, pb=pb: e.copy(
                out=xTst[s][:, half * 4:(half + 1) * 4, :], in_=PS[pb][:].rearrange("p (k t) -> p k t", k=4)))
                if half == 0 else (lambda e, s=s, half=half, pb=pb: e.tensor_copy(
                    out=xTst[s][:, half * 4:(half + 1) * 4, :], in_=PS[pb][:].rearrange("p (k t) -> p k t", k=4))),
                reads=[("ps", pb)], writes=[("xTst", s, half)])
        P.dma("gpsimd", xT.rearrange("(k p) t -> p k t", p=128)[:, :, t * 128:(t + 1) * 128], xTst[s][:],
              reads=[("xTst", s, 0), ("xTst", s, 1)], writes=[("xT", t // 4)])

    if "mod" in dbg:
        d = dbg_out("mod", [128, NL * 48 * 2])
        P.dma("sync", d, mod[:].rearrange("p l j c -> p (l j c)"), reads=["mod"], writes=["OUT_dbgmod"])
    if "xT" in dbg:
        d = dbg_out("xT", [D, T])
        P.dma("sync", d, xT, reads=[("xT", g) for g in range(9)], writes=["OUT_dbgxT"])
    if stop == "s0":
        return P, dbg_t
    P.barrier()
    P.release(m0)


    sgu_lng = P.dram("sgu_lng", [NL, 128, 512], F32, "ExternalInput")
    sgu_lnb = P.dram("sgu_lnb", [NL, 128, 512], F32, "ExternalInput")
    sgu_wT = P.dram("sgu_wT", [NL, 128, 8, 128], F32, "ExternalInput")
    sgu_bB = P.dram("sgu_bB", [NL, 64, 8, 128], F32, "ExternalInput")
    na_bias = P.dram("na_bias", [NL, 5, 128, 8, 896], F32, "ExternalInput")
    osgu_d = P.dram("osgu", [512, T], BF16)
    ona_d = P.dram("ona", [512, T], BF16)
    def gelu(src, dst, ta, tb, npart, rk, wk, pfx):
        P.op("gpsimd", lambda e: e.tensor_tensor(out=ta, in0=src, in1=src, op=ALU.mult), reads=rk, writes=[(pfx, "ta")])
        P.op("gpsimd", lambda e: e.tensor_scalar(out=ta, in0=ta, scalar1=0.044715, scalar2=1.0, op0=ALU.mult, op1=ALU.add),
             reads=[(pfx, "ta")], writes=[(pfx, "ta")])
        P.op("gpsimd", lambda e: e.tensor_tensor(out=ta, in0=ta, in1=src, op=ALU.mult), reads=[(pfx, "ta")] + list(rk), writes=[(pfx, "ta")])
        P.op("scalar", lambda e: e.activation(out=tb, in_=ta, func=AF.Sigmoid, scale=1.5957691216057308),
             reads=[(pfx, "ta")], writes=[(pfx, "tb")])
        P.op("vector", lambda e: e.tensor_tensor(out=dst, in0=tb, in1=src, op=ALU.mult), reads=[(pfx, "tb")] + list(rk), writes=wk)

    def stage_sgu(l):
        sg = {}
        if True:
            sg["lng"] = P.sbuf("sg_lng", [128, 512]); sg["lnb"] = P.sbuf("sg_lnb", [128, 512])
            sg["wT"] = P.sbuf("sg_wT", [128, 8, 128]); sg["bB"] = P.sbuf("sg_bB", [64, 8, 128])
            for nm, shp, dt in (("sv", [128, 512], F32), ("gv", [128, 512], F32), ("vn", [128, 512], F32), ("tva", [128, 512], F32),
                                ("tvb", [128, 512], F32), ("su", [64, 8, 128], F32), ("gu", [64, 8, 128], F32), ("tua", [64, 8, 128], F32),
                                ("tub", [64, 8, 128], F32), ("st6", [128, 6], F32), ("mv", [128, 2], F32), ("rstd", [128, 1], F32),
                                ("tmp", [64, 8, 128], F32), ("osg", [64, 8, 128], BF16)):
                sg[nm] = [P.sbuf("sg_%s%d" % (nm, i), shp, dt) for i in range(2)]
        P.dma("sync", sg["lng"][:], sgu_lng[l], writes=["sg_par"])
        P.dma("sync", sg["lnb"][:], sgu_lnb[l], writes=["sg_par"])
        P.dma("sync", sg["wT"][:], sgu_wT[l], writes=["sg_par"])
        P.dma("sync", sg["bB"][:], sgu_bB[l], writes=["sg_par"])
        sguU_v = sguU.rearrange("(g c) t -> c g t", c=64)
        osgu_v = osgu_d.rearrange("(g c) t -> c g t", c=64)
        def sgu_gen(t, s):
            yield
            gi = 0 if t < 2 else 1 + (t - 2) // 4
            sv, gv, vn, su, gu, tmp, osg = (sg[k][s] for k in ("sv", "gv", "vn", "su", "gu", "tmp", "osg"))
            st6, mv, rstd = sg["st6"][s], sg["mv"][s], sg["rstd"][s]
            P.dma("sync", sv[:], sguV[t * 128:(t + 1) * 128, :], reads=[("sguV", "tm", t)], writes=[("sv", s)])
            P.dma("sync", su[:], sguU_v[:, :, t * 128:(t + 1) * 128], reads=[("sguU", "fm", gi)], writes=[("su", s)])
            yield
            gelu(sv[:], gv[:], sg["tva"][s][:], sg["tvb"][s][:], 128, [("sv", s)], [("gv", s)], ("gv", s))
            yield
            P.op("vector", lambda e, st6=st6, gv=gv: e.bn_stats(out=st6[:], in_=gv[:]), reads=[("gv", s)], writes=[("st6", s)])
            P.op("vector", lambda e, st6=st6, mv=mv: e.bn_aggr(out=mv[:], in_=st6[:]), reads=[("st6", s)], writes=[("mv", s)])
            P.op("scalar", lambda e, mv=mv, rstd=rstd: e.activation(out=rstd[:], in_=mv[:, 1:2], func=AF.Sqrt, bias=eps5[:, 0:1], scale=1.0),
                 reads=[("mv", s), "eps"], writes=[("rstd", s)])
            P.op("vector", lambda e, rstd=rstd: e.reciprocal(out=rstd[:], in_=rstd[:]), reads=[("rstd", s)], writes=[("rstd", s)])
            P.op("vector", lambda e, vn=vn, gv=gv, mv=mv, rstd=rstd: e.tensor_scalar(
                out=vn[:], in0=gv[:], scalar1=mv[:, 0:1], scalar2=rstd[:, 0:1], op0=ALU.subtract, op1=ALU.mult),
                reads=[("gv", s), ("mv", s), ("rstd", s)], writes=[("vn", s)])
            P.op("gpsimd", lambda e, vn=vn: e.tensor_tensor(out=vn[:], in0=vn[:], in1=sg["lng"][:], op=ALU.mult),
                 reads=[("vn", s), "sg_par"], writes=[("vn", s)])
            P.op("gpsimd", lambda e, vn=vn: e.tensor_tensor(out=vn[:], in0=vn[:], in1=sg["lnb"][:], op=ALU.add),
                 reads=[("vn", s), "sg_par"], writes=[("vn", s)])
            yield
            gelu(su[:], gu[:], sg["tua"][s][:], sg["tub"][s][:], 64, [("su", s)], [("gu", s)], ("gu", s))
            yield
            for half in range(2):
                pb = 2 * s + half
                for gg in range(4):
                    g = half * 4 + gg
                    P.op("tensor", lambda e, vn=vn, g=g, gg=gg, pb=pb: e.matmul(
                        PS[pb][0:64, gg * 128:(gg + 1) * 128], lhsT=vn[:, g * 64:(g + 1) * 64], rhs=sg["wT"][:, g, :],
                        start=True, stop=True), reads=[("vn", s), "sg_par"], writes=[("ps", pb)])
                P.op("vector", lambda e, tmp=tmp, pb=pb, half=half: e.tensor_tensor(
                    out=tmp[:, half * 4:(half + 1) * 4, :], in0=PS[pb][0:64, :].rearrange("p (g t) -> p g t", g=4),
                    in1=sg["bB"][:, half * 4:(half + 1) * 4, :], op=ALU.add), reads=[("ps", pb), "sg_par"], writes=[("sgtmp", s, half)])
                P.op("gpsimd", lambda e, tmp=tmp, gu=gu, osg=osg, half=half: e.tensor_tensor(
                    out=osg[:, half * 4:(half + 1) * 4, :], in0=tmp[:, half * 4:(half + 1) * 4, :], in1=gu[:, half * 4:(half + 1) * 4, :],
                    op=ALU.mult), reads=[("sgtmp", s, half), ("gu", s)], writes=[("osg", s, half)])
            P.dma("gpsimd", osgu_v[:, :, t * 128:(t + 1) * 128], osg[:], reads=[("osg", s, 0), ("osg", s, 1)], writes=[("osgu", gi)])
        for tp in range(0, NT, 2):
            gens = [sgu_gen(tp, 0), sgu_gen(tp + 1, 1)]
            while gens:
                for g_ in list(gens):
                    try:
                        next(g_)
                    except StopIteration:
                        gens.remove(g_)
        if "osgu" in dbg and l == 0:
            d = dbg_out("osgu", [512, T], BF16)
            P.dma("sync", d, osgu_d, reads=[("osgu", g) for g in range(9)], writes=["OUT_dbgosgu"])

    def stage_na(l):
        na = {}
        if True:
            na["q"] = P.sbuf("na_q", [128, 4, T], BF16); na["k"] = P.sbuf("na_k", [128, 4, T], BF16)
            na["v"] = P.sbuf("na_v", [128, NT, 512], BF16)
            na["bI"] = P.sbuf("na_bI", [128, 8, 896]); na["bE"] = P.sbuf("na_bE", [128, 8, 896])
            for nm, shp, dt in (("s", [128, 896], F32), ("p", [128, 896], BF16), ("pT", [128, 896], BF16), ("nmx", [128, 1], F32)):
                na[nm] = [P.sbuf("na_%s%d" % (nm, i), shp, dt) for i in range(2)]
            for nm, shp, dt in (("rs", [128, 8], F32), ("rinv", [128, 8], F32), ("o", [128, 512], BF16), ("oT", [128, 4, 128], BF16)):
                na[nm] = [P.sbuf("na_%s%d" % (nm, i), shp, dt) for i in range(2)]
        qT_v = qT.rearrange("(k p) t -> p k t", p=128); kT_v = kT.rearrange("(k p) t -> p k t", p=128)
        vt_v = vtok.rearrange("(t p) f -> p t f", p=128)
        for gi, (t0, n) in enumerate(GROUPS):
            P.dma("sync", na["q"][:, :, t0:t0 + n], qT_v[:, :, t0:t0 + n], reads=[("qT", "fm", gi)], writes=[("na_q", gi)])
            P.dma("sync", na["k"][:, :, t0:t0 + n], kT_v[:, :, t0:t0 + n], reads=[("kT", "fm", gi)], writes=[("na_k", gi)])
        for t in range(NT):
            P.dma("sync", na["v"][:, t, :], vt_v[:, t, :], reads=[("vtok", "tm", t)], writes=[("na_v", t)])
        P.dma("sync", na["bI"][:], na_bias[l, 2], writes=["na_bI"])
        ona_v = ona_d.rearrange("(k p) t -> p k t", p=128)
        grp_of_tok = lambda tok: 0 if tok < 256 else 1 + (tok - 256) // 512
        hcnt = 0
        for t in range(NT):
            isctx = t < 2
            so = t % 2
            if isctx:
                nk = 256; pat = None; vtiles = [0, 1]
                kgroups = [0]
            else:
                qt = t - 2
                kb = min(max(2 * qt - 4, 0), 54)
                ktok0 = 256 + kb * 64
                nk = 896
                pat = {0: 0, 1: 1, 30: 3, 31: 4}.get(qt, 2)
                vtiles = [2 + kb // 2 + j for j in range(5)] + [0, 1]
                kgroups = sorted({0, grp_of_tok(ktok0), grp_of_tok(ktok0 + 639)})
                if pat != 2:
                    P.dma("sync", na["bE"][:], na_bias[l, pat], writes=["na_bE"])
            bias = None if isctx else (na["bI"] if pat == 2 else na["bE"])
            bkey = "na_bI" if pat == 2 else "na_bE"
            psO = PS[4 + so]
            qg = grp_of_tok(t * 128)
            def head_gen(h, x):
                ch, p0 = h // 2, (h % 2) * 64
                yield
                psA, psBk = PS[x], PS[2 + x]
                s_t, p_t, pT_t, nmx = na["s"][x], na["p"][x], na["pT"][x], na["nmx"][x]
                lhsT = na["q"][p0:p0 + 64, ch, t * 128:(t + 1) * 128]
                krd = [("na_q", qg)] + [("na_k", g) for g in kgroups]
                if isctx:
                    P.op("tensor", lambda e, lhsT=lhsT, psA=psA, ch=ch, p0=p0: e.matmul(
                        psA[:, 0:256], lhsT=lhsT, rhs=na["k"][p0:p0 + 64, ch, 0:256], start=True, stop=True),
                        reads=krd, writes=[("ps", x)])
                    yield
                    P.op("vector", lambda e, s_t=s_t, psA=psA: e.tensor_scalar(out=s_t[:, 0:256], in0=psA[:, 0:256], scalar1=0.125,
                                                                               scalar2=None, op0=ALU.mult),
                         reads=[("ps", x)], writes=[("na_s", x)])
                else:
                    P.op("tensor", lambda e, lhsT=lhsT, psA=psA, ch=ch, p0=p0, ktok0=ktok0: e.matmul(
                        psA[:, 0:512], lhsT=lhsT, rhs=na["k"][p0:p0 + 64, ch, ktok0:ktok0 + 512], start=True, stop=True),
                        reads=krd, writes=[("ps", x)])
                    P.op("tensor", lambda e, lhsT=lhsT, psBk=psBk, ch=ch, p0=p0, ktok0=ktok0: e.matmul(
                        psBk[:, 0:128], lhsT=lhsT, rhs=na["k"][p0:p0 + 64, ch, ktok0 + 512:ktok0 + 640], start=True, stop=True),
                        reads=krd, writes=[("ps", 2 + x)])
                    P.op("tensor", lambda e, lhsT=lhsT, psBk=psBk, ch=ch, p0=p0: e.matmul(
                        psBk[:, 128:384], lhsT=lhsT, rhs=na["k"][p0:p0 + 64, ch, 0:256], start=True, stop=True),
                        reads=krd, writes=[("ps", 2 + x)])
                    yield
                    P.op("vector", lambda e, s_t=s_t, psA=psA, bias=bias, h=h: e.scalar_tensor_tensor(
                        out=s_t[:, 0:512], in0=psA[:, 0:512], scalar=0.125, in1=bias[:, h, 0:512], op0=ALU.mult, op1=ALU.add),
                        reads=[("ps", x), bkey], writes=[("na_s", x)])
                    P.op("vector", lambda e, s_t=s_t, psBk=psBk, bias=bias, h=h: e.scalar_tensor_tensor(
                        out=s_t[:, 512:896], in0=psBk[:, 0:384], scalar=0.125, in1=bias[:, h, 512:896], op0=ALU.mult, op1=ALU.add),
                        reads=[("ps", 2 + x), bkey], writes=[("na_s", x)])
                yield
                P.op("vector", lambda e, s_t=s_t, nmx=nmx, nk=nk: e.tensor_reduce(out=nmx[:, 0:1], in_=s_t[:, 0:nk], axis=AX.X, op=ALU.max,
                                                                                 negate=True),
                     reads=[("na_s", x)], writes=[("na_nmx", x)])
                P.op("scalar", lambda e, s_t=s_t, p_t=p_t, nmx=nmx, nk=nk, h=h, so=so: e.activation(
                    out=p_t[:, 0:nk], in_=s_t[:, 0:nk], func=AF.Exp, bias=nmx[:, 0:1], scale=1.0, accum_out=na["rs"][so][:, h:h + 1]),
                    reads=[("na_s", x), ("na_nmx", x)], writes=[("na_p", x), ("na_rs", so)])
                yield
                for j in range(nk // 128):
                    P.op("tensor", lambda e, p_t=p_t, j=j, x=x: e.transpose(out=PSB[x][:, j * 128:(j + 1) * 128],
                                                                          in_=p_t[:, j * 128:(j + 1) * 128], identity=identb[:]),
                         reads=[("na_p", x), "identb"], writes=[("psb", x)])
                yield
                if h % 2 == 0:
                    P.op("scalar", lambda e, pT_t=pT_t, x=x, nk=nk: e.copy(out=pT_t[:, 0:nk], in_=PSB[x][:, 0:nk]),
                         reads=[("psb", x)], writes=[("na_pT", x)])
                else:
                    P.op("vector", lambda e, pT_t=pT_t, x=x, nk=nk: e.tensor_copy(out=pT_t[:, 0:nk], in_=PSB[x][:, 0:nk]),
                         reads=[("psb", x)], writes=[("na_pT", x)])
                yield
                for j, vt in enumerate(vtiles):
                    P.op("tensor", lambda e, pT_t=pT_t, j=j, vt=vt, h=h, psO=psO, last=(j == len(vtiles) - 1): e.matmul(
                        psO[:, h * 64:(h + 1) * 64], lhsT=pT_t[:, j * 128:(j + 1) * 128], rhs=na["v"][:, vt, h * 64:(h + 1) * 64],
                        start=(j == 0), stop=last), reads=[("na_pT", x), ("na_v", vt)], writes=[("ps", 4 + so)])
            for hp in range(0, 8, 2):
                gens = [head_gen(hp, hcnt % 2), head_gen(hp + 1, (hcnt + 1) % 2)]
                hcnt += 2
                while gens:
                    for g_ in list(gens):
                        try:
                            next(g_)
                        except StopIteration:
                            gens.remove(g_)
            rs, rinv, o_t, oT = na["rs"][so], na["rinv"][so], na["o"][so], na["oT"][so]
            P.op("vector", lambda e, rs=rs, rinv=rinv: e.reciprocal(out=rinv[:], in_=rs[:]), reads=[("na_rs", so)], writes=[("na_rinv", so)])
            P.op("vector", lambda e, o_t=o_t, psO=psO, rinv=rinv: e.tensor_tensor(
                out=o_t[:].rearrange("p (h d) -> p h d", h=8), in0=psO[:].rearrange("p (h d) -> p h d", h=8),
                in1=rinv[:].unsqueeze(2).to_broadcast([128, 8, 64]), op=ALU.mult),
                reads=[("ps", 4 + so), ("na_rinv", so)], writes=[("na_o", so)])
            xx = hcnt % 2
            for c4 in range(4):
                P.op("tensor", lambda e, o_t=o_t, c4=c4, xx=xx: e.transpose(out=PSB[xx][:, c4 * 128:(c4 + 1) * 128],
                                                                          in_=o_t[:, c4 * 128:(c4 + 1) * 128], identity=identb[:]),
                     reads=[("na_o", so), "identb"], writes=[("psb", xx)])
            P.op("scalar", lambda e, oT=oT, xx=xx: e.copy(out=oT[:], in_=PSB[xx][:, 0:512].rearrange("p (c t) -> p c t", c=4)),
                 reads=[("psb", xx)], writes=[("na_oT", so)])
            P.dma("gpsimd", ona_v[:, :, t * 128:(t + 1) * 128], oT[:], reads=[("na_oT", so)], writes=[("ona", qg)])
        if "ona" in dbg and l == 0:
            d = dbg_out("ona", [512, T], BF16)
            P.dma("sync", d, ona_d, reads=[("ona", g) for g in range(9)], writes=["OUT_dbgona"])


    w_branch = P.dram("w_branch", [NL, 3, 512, D], F32, "ExternalInput")
    w_out = P.dram("w_out", [NL, D, D], F32, "ExternalInput")
    w_gu = P.dram("ffn_w_gu", [NL, D, 2 * D_FF], F32, "ExternalInput")
    w_down = P.dram("ffn_w_down", [NL, D_FF, D], F32, "ExternalInput")
    lnp_d = P.dram("lnp", [128, NL, 4, 8], F32, "ExternalInput")
    orw_d = P.dram("orw", [512, T], BF16)
    P.dma("sync", lnp[:], lnp_d, writes=["lnp"])
    P.op("vector", lambda e: e.memset(ones[:], 1.0), writes=["ones"])

    def load_w_bf16(dst_view, src_view, stg, npart, nfree, tag):
        a, b = dst_view.shape[1], dst_view.shape[2]
        rows = max(1, 4096 // b)
        i = 0
        for a0 in range(0, a, rows):
            a1 = min(a, a0 + rows)
            st = stg[i % 2]
            P.dma("sync", st[0:npart, 0:(a1 - a0) * b].rearrange("p (a b) -> p a b", b=b), src_view[:, a0:a1, :], writes=[("stg", i % 2)])
            P.op("gpsimd" if i % 2 == 0 else "vector", lambda e, st=st, a0=a0, a1=a1: e.tensor_copy(
                out=dst_view[:, a0:a1, :], in_=st[0:npart, 0:(a1 - a0) * b].rearrange("p (a b) -> p a b", b=b)),
                reads=[("stg", i % 2)], writes=[tag])
            i += 1

    def layer_norm_T(r, n, l, gi_, bi_, out_view, tagr, tagout, sq, stat):
        for k in range(8):
            P.op("tensor", lambda e, k=k: e.matmul(PS[0][:, :n], lhsT=ones[:], rhs=r[:, k, :n], start=(k == 0), stop=(k == 7)),
                 reads=[tagr, "ones"], writes=[("ps", 0)])
        for k in range(8):
            sqk = sq[k % 2]
            P.op("scalar", lambda e, k=k, sqk=sqk: e.activation(out=sqk[:, :n], in_=r[:, k, :n], func=AF.Square),
                 reads=[tagr], writes=[("lnsq", k % 2)])
            P.op("tensor", lambda e, k=k, sqk=sqk: e.matmul(PS[1][:, :n], lhsT=ones[:], rhs=sqk[:, :n], start=(k == 0), stop=(k == 7)),
                 reads=[("lnsq", k % 2), "ones"], writes=[("ps", 1)])
        mean, var, rstd = stat
        P.op("vector", lambda e: e.tensor_scalar(out=mean[:, :n], in0=PS[0][:, :n], scalar1=1.0 / 1024, scalar2=None, op0=ALU.mult),
             reads=[("ps", 0)], writes=["ln_mean"])
        P.op("vector", lambda e: e.tensor_tensor(out=var[:, :n], in0=mean[:, :n], in1=mean[:, :n], op=ALU.mult),
             reads=["ln_mean"], writes=["ln_var"])
        P.op("vector", lambda e: e.scalar_tensor_tensor(out=var[:, :n], in0=PS[1][:, :n], scalar=1.0 / 1024, in1=var[:, :n],
                                                        op0=ALU.mult, op1=ALU.subtract), reads=[("ps", 1), "ln_var"], writes=["ln_var"])
        P.op("scalar", lambda e: e.activation(out=rstd[:, :n], in_=var[:, :n], func=AF.Sqrt, bias=eps5[:, 0:1], scale=1.0),
             reads=["ln_var", "eps"], writes=["ln_rstd"])
        P.op("vector", lambda e: e.reciprocal(out=rstd[:, :n], in_=rstd[:, :n]), reads=["ln_rstd"], writes=["ln_rstd"])
        for k in range(8):
            eng = "vector" if k % 2 == 0 else "gpsimd"
            P.op(eng, lambda e, k=k: e.tensor_tensor(out=r[:, k, :n], in0=r[:, k, :n], in1=mean[:, :n], op=ALU.subtract),
                 reads=[tagr, "ln_mean", "ln_rstd"], writes=[(tagr, "c", k)])
            P.op(eng, lambda e, k=k: e.tensor_tensor(out=r[:, k, :n], in0=r[:, k, :n], in1=rstd[:, :n], op=ALU.mult),
                 reads=[tagr, (tagr, "c", k), "ln_rstd"], writes=[(tagr, "c", k)])
            P.op(eng, lambda e, k=k: e.tensor_scalar(out=out_view[:, k, :n], in0=r[:, k, :n], scalar1=lnp[:, l, gi_, k:k + 1],
                                                     scalar2=lnp[:, l, bi_, k:k + 1], op0=ALU.mult, op1=ALU.add),
                 reads=[tagr, (tagr, "c", k), "lnp"], writes=[(tagout, k)])

    def stage_merge(l):
        wg = P.sbuf("m_wg", [128, 8, 3072], BF16)
        wbn = P.sbuf("m_wbn", [128, 4, 1024], BF16); wbr = P.sbuf("m_wbr", [128, 4, 1024], BF16)
        wbs = P.sbuf("m_wbs", [64, 8, 1024], BF16); wo = P.sbuf("m_wo", [128, 8, 1024], BF16)
        stg = [P.sbuf("m_stg%d" % i, [128, 4096]) for i in range(2)]
        xr = P.sbuf("m_xr", [128, 8, 512]); hh = P.sbuf("m_h", [128, 8, 512], BF16)
        o_n = P.sbuf("m_on", [128, 4, 512], BF16); o_r = P.sbuf("m_or", [128, 4, 512], BF16); o_s = P.sbuf("m_os", [64, 8, 512], BF16)
        sig = [P.sbuf("m_sig%d" % i, [128, 512]) for i in range(3)]
        ypre = P.sbuf("m_ypre", [128, 8, 512], BF16); yacc = P.sbuf("m_yacc", [128, 512]); ytmp = P.sbuf("m_ytmp", [128, 512])
        sq = [P.sbuf("m_sq%d" % i, [128, 512]) for i in range(2)]
        stat = [P.sbuf("m_st%d" % i, [128, 512]) for i in range(3)]
        load_w_bf16(wg[:], w_in[l, :, 0:3072].rearrange("(k p) m -> p k m", p=128), stg, 128, 0, "m_wg")
        load_w_bf16(wbn[:], w_branch[l, 0].rearrange("(k p) m -> p k m", p=128), stg, 128, 0, "m_wbn")
        load_w_bf16(wbr[:], w_branch[l, 1].rearrange("(k p) m -> p k m", p=128), stg, 128, 0, "m_wbr")
        load_w_bf16(wbs[:], w_branch[l, 2].rearrange("(g c) m -> c g m", c=64), stg, 64, 0, "m_wbs")
        load_w_bf16(wo[:], w_out[l].rearrange("(k p) m -> p k m", p=128), stg, 128, 0, "m_wo")
        ona_v = ona_d.rearrange("(k p) t -> p k t", p=128); orw_v = orw_d.rearrange("(k p) t -> p k t", p=128)
        osgu_v = osgu_d.rearrange("(g c) t -> c g t", c=64)
        pr = 0
        for gi, (t0, n) in enumerate(GROUPS):
            c = 1 if gi == 0 else 0
            P.dma("sync", xr[:, :, :n], xT_v[:, :, t0:t0 + n], reads=[("xT", gi)], writes=["m_xr"])
            P.dma("sync", o_n[:, :, :n], ona_v[:, :, t0:t0 + n], reads=[("ona", gi)], writes=["m_on"])
            P.dma("sync", o_r[:, :, :n], orw_v[:, :, t0:t0 + n], reads=[("orw", gi)], writes=["m_or"])
            P.dma("sync", o_s[:, :, :n], osgu_v[:, :, t0:t0 + n], reads=[("osgu", gi)], writes=["m_os"])
            for k in range(8):
                P.op("vector" if k % 2 == 0 else "gpsimd", lambda e, k=k, c=c: e.tensor_scalar(
                    out=hh[:, k, :n], in0=xr[:, k, :n], scalar1=mod1[:, l, 8 + k, c:c + 1], scalar2=mod[:, l, k, c:c + 1],
                    op0=ALU.mult, op1=ALU.add), reads=["m_xr", "mod", "mod1"], writes=["m_h"])
            for m in range(8):
                for i in range(3):
                    pg, pb = PS[2 * (pr % 3)], PS[2 * (pr % 3) + 1]
                    kg, kb_ = ("ps", 2 * (pr % 3)), ("ps", 2 * (pr % 3) + 1)
                    pr += 1
                    mc = i * 8 + m
                    for k in range(8):
                        P.op("tensor", lambda e, k=k, mc=mc, pg=pg: e.matmul(pg[:, :n], lhsT=wg[:, k, mc * 128:(mc + 1) * 128], rhs=hh[:, k, :n],
                                                                             start=(k == 0), stop=(k == 7)), reads=["m_wg", "m_h"], writes=[kg])
                    P.op("scalar", lambda e, i=i, pg=pg: e.activation(out=sig[i][:, :n], in_=pg[:, :n], func=AF.Sigmoid),
                         reads=[kg], writes=[("m_sig", i)])
                    if i < 2:
                        wb_, ob_, ok_ = (wbn, o_n, "m_on") if i == 0 else (wbr, o_r, "m_or")
                        for k in range(4):
                            P.op("tensor", lambda e, k=k, m=m, pb=pb, wb_=wb_, ob_=ob_: e.matmul(
                                pb[:, :n], lhsT=wb_[:, k, m * 128:(m + 1) * 128], rhs=ob_[:, k, :n], start=(k == 0), stop=(k == 3)),
                                reads=["m_wbn", "m_wbr", ok_], writes=[kb_])
                    else:
                        for g in range(8):
                            P.op("tensor", lambda e, g=g, m=m, pb=pb: e.matmul(
                                pb[:, :n], lhsT=wbs[:, g, m * 128:(m + 1) * 128], rhs=o_s[:, g, :n], start=(g == 0), stop=(g == 7)),
                                reads=["m_wbs", "m_os"], writes=[kb_])
                    if i == 0:
                        P.op("vector", lambda e, pb=pb: e.tensor_tensor(out=yacc[:, :n], in0=pb[:, :n], in1=sig[0][:, :n], op=ALU.mult),
                             reads=[kb_, ("m_sig", 0)], writes=["m_yacc"])
                    else:
                        P.op("vector", lambda e, pb=pb, i=i: e.tensor_tensor(out=ytmp[:, :n], in0=pb[:, :n], in1=sig[i][:, :n], op=ALU.mult),
                             reads=[kb_, ("m_sig", i)], writes=["m_ytmp"])
                        if i == 1:
                            P.op("gpsimd", lambda e: e.tensor_tensor(out=yacc[:, :n], in0=yacc[:, :n], in1=ytmp[:, :n], op=ALU.add),
                                 reads=["m_yacc", "m_ytmp"], writes=["m_yacc"])
                        else:
                            P.op("gpsimd", lambda e, m=m: e.tensor_tensor(out=ypre[:, m, :n], in0=yacc[:, :n], in1=ytmp[:, :n], op=ALU.add),
                                 reads=["m_yacc", "m_ytmp"], writes=["m_ypre"])
            for m in range(8):
                pg = PS[2 + m % 4]; kg = ("ps", 2 + m % 4)
                for k in range(8):
                    P.op("tensor", lambda e, k=k, m=m, pg=pg: e.matmul(pg[:, :n], lhsT=wo[:, k, m * 128:(m + 1) * 128], rhs=ypre[:, k, :n],
                                                                         start=(k == 0), stop=(k == 7)), reads=["m_wo", "m_ypre"], writes=[kg])
                P.op("scalar", lambda e, m=m, pg=pg, c=c: e.activation(out=ytmp[:, :n], in_=pg[:, :n], func=AF.Copy,
                                                                      scale=mod[:, l, 16 + m, c:c + 1]), reads=[kg, "mod"], writes=["m_ytmp"])
                P.op("vector", lambda e, m=m: e.scalar_tensor_tensor(out=xr[:, m, :n], in0=xr[:, m, :n], scalar=ALPHA, in1=ytmp[:, :n],
                                                                    op0=ALU.mult, op1=ALU.add), reads=["m_xr", "m_ytmp"], writes=["m_xr"])
            layer_norm_T(xr, n, l, 0, 1, xr, "m_xr", "m_x1", sq, stat)
            P.dma("gpsimd", xT_v[:, :, t0:t0 + n], xr[:, :, :n], reads=["m_xr"] + [("m_x1", k) for k in range(8)], writes=[("xT", gi)])
        if "x1" in dbg and l == 0:
            d = dbg_out("x1", [D, T])
            P.dma("sync", d, xT, reads=[("xT", g) for g in range(9)], writes=["OUT_dbgx1"])

    FG = [(t0, 256) for t0 in range(0, T, 256)]

    def stage_ffn(l):
        wgu = P.sbuf("f_wgu", [128, 8, 2 * D_FF], BF16)
        wd = P.sbuf("f_wd", [128, 22, 1024], BF16)
        stg = [P.sbuf("f_stg%d" % i, [128, 2048]) for i in range(2)]
        xr = P.sbuf("f_xr", [128, 8, 256]); ff = P.sbuf("f_f", [128, 8, 256], BF16)
        act = P.sbuf("f_a", [128, 22, 256], BF16)
        sgt = [P.sbuf("f_sg%d" % i, [128, 256]) for i in range(2)]
        ytmp = P.sbuf("f_ytmp", [128, 256])
        sq = [P.sbuf("f_sq%d" % i, [128, 256]) for i in range(2)]
        stat = [P.sbuf("f_st%d" % i, [128, 256]) for i in range(3)]

        def load2(dst_view, src_view, tag):
            a, b = dst_view.shape[1], dst_view.shape[2]
            i = 0
            for a0 in range(a):
                for b0 in range(0, b, 2048):
                    b1 = min(b, b0 + 2048)
                    st = stg[i % 2]
                    P.dma("sync", st[:, 0:b1 - b0], src_view[:, a0, b0:b1], writes=[("fstg", i % 2)])
                    P.op("gpsimd" if i % 2 == 0 else "vector", lambda e, st=st, a0=a0, b0=b0, b1=b1: e.tensor_copy(
                        out=dst_view[:, a0, b0:b1], in_=st[:, 0:b1 - b0]), reads=[("fstg", i % 2)], writes=[tag])
                    i += 1
        load2(wgu[:], w_gu[l].rearrange("(k p) m -> p k m", p=128), "f_wgu")
        load2(wd[:], w_down[l].rearrange("(k p) m -> p k m", p=128), "f_wd")
        for fi, (t0, n) in enumerate(FG):
            gi = 0 if t0 < 256 else 1 + (t0 - 256) // 512
            c = 1 if fi == 0 else 0
            P.dma("sync", xr[:, :, :n], xT_v[:, :, t0:t0 + n], reads=[("xT", gi)], writes=["f_xr"])
            for k in range(8):
                P.op("vector" if k % 2 == 0 else "gpsimd", lambda e, k=k, c=c: e.tensor_scalar(
                    out=ff[:, k, :n], in0=xr[:, k, :n], scalar1=mod1[:, l, 32 + k, c:c + 1], scalar2=mod[:, l, 24 + k, c:c + 1],
                    op0=ALU.mult, op1=ALU.add), reads=["f_xr", "mod", "mod1"], writes=["f_f"])
            for j in range(22):
                x2 = j % 2
                pg, pu = PS[2 + 2 * x2], PS[3 + 2 * x2]
                kg, ku = ("ps", 2 + 2 * x2), ("ps", 3 + 2 * x2)
                for k in range(8):
                    P.op("tensor", lambda e, k=k, j=j, pg=pg: e.matmul(pg[:, :n], lhsT=wgu[:, k, j * 128:(j + 1) * 128], rhs=ff[:, k, :n],
                                                                         start=(k == 0), stop=(k == 7)), reads=["f_wgu", "f_f"], writes=[kg])
                for k in range(8):
                    P.op("tensor", lambda e, k=k, j=j, pu=pu: e.matmul(pu[:, :n], lhsT=wgu[:, k, D_FF + j * 128:D_FF + (j + 1) * 128],
                                                                         rhs=ff[:, k, :n], start=(k == 0), stop=(k == 7)),
                         reads=["f_wgu", "f_f"], writes=[ku])
                P.op("scalar", lambda e, pg=pg, x2=x2: e.activation(out=sgt[x2][:, :n], in_=pg[:, :n], func=AF.Silu),
                     reads=[kg], writes=[("f_sg", x2)])
                P.op("vector", lambda e, pu=pu, x2=x2, j=j: e.tensor_tensor(out=act[:, j, :n], in0=pu[:, :n], in1=sgt[x2][:, :n], op=ALU.mult),
                     reads=[ku, ("f_sg", x2)], writes=["f_a"])
            for m in range(8):
                pg = PS[2 + m % 4]; kg = ("ps", 2 + m % 4)
                for j in range(22):
                    P.op("tensor", lambda e, j=j, m=m, pg=pg: e.matmul(pg[:, :n], lhsT=wd[:, j, m * 128:(m + 1) * 128], rhs=act[:, j, :n],
                                                                         start=(j == 0), stop=(j == 21)), reads=["f_wd", "f_a"], writes=[kg])
                P.op("scalar", lambda e, m=m, pg=pg, c=c: e.activation(out=ytmp[:, :n], in_=pg[:, :n], func=AF.Copy,
                                                                      scale=mod[:, l, 40 + m, c:c + 1]), reads=[kg, "mod"], writes=["f_ytmp"])
                P.op("vector", lambda e, m=m: e.scalar_tensor_tensor(out=xr[:, m, :n], in0=xr[:, m, :n], scalar=ALPHA, in1=ytmp[:, :n],
                                                                    op0=ALU.mult, op1=ALU.add), reads=["f_xr", "f_ytmp"], writes=["f_xr"])
            layer_norm_T(xr, n, l, 2, 3, xr, "f_xr", "f_x2", sq, stat)
            P.dma("gpsimd", xT_v[:, :, t0:t0 + n], xr[:, :, :n], reads=["f_xr"] + [("f_x2", k) for k in range(8)], writes=[("xT", gi)])
        if "x2" in dbg and l == 0:
            d = dbg_out("x2", [D, T])
            P.dma("sync", d, xT, reads=[("xT", g) for g in range(9)], writes=["OUT_dbgx2"])

    def stage_final():
        xg_ = [P.sbuf("o_xg%d" % i, [128, 8, 128]) for i in range(2)]
        ot = [P.sbuf("o_t%d" % i, [128, 1024]) for i in range(2)]
        for t in range(2, NT):
            s_ = t % 2
            gi = 1 + (t - 2) // 4
            P.dma("sync", xg_[s_][:], xT_v[:, :, t * 128:(t + 1) * 128], reads=[("xT", gi)], writes=[("o_xg", s_)])
            for half in range(2):
                pb = 2 + 2 * s_ + half
                for kk in range(4):
                    k = half * 4 + kk
                    P.op("tensor", lambda e, s_=s_, k=k, kk=kk, pb=pb: e.transpose(
                        out=PS[pb][:, kk * 128:(kk + 1) * 128], in_=xg_[s_][:, k, :], identity=ident[:]),
                        reads=[("o_xg", s_), "ident"], writes=[("ps", pb)])
                if half == 0:
                    P.op("scalar", lambda e, s_=s_, pb=pb: e.copy(out=ot[s_][:, 0:512], in_=PS[pb][:]), reads=[("ps", pb)], writes=[("o_t", s_, 0)])
                else:
                    P.op("vector", lambda e, s_=s_, pb=pb: e.tensor_copy(out=ot[s_][:, 512:1024], in_=PS[pb][:]), reads=[("ps", pb)],
                         writes=[("o_t", s_, 1)])
            P.dma("gpsimd", out_d[(t - 2) * 128:(t - 1) * 128, :], ot[s_][:], reads=[("o_t", s_, 0), ("o_t", s_, 1)], writes=["OUT_%d" % t])

    rw_mu_d = P.dram("rw_mu", [128, NL, 2, 15], F32, "ExternalInput")
    rw_w0a0_d = P.dram("rw_w0a0", [128, NL, 2, 2, 4], F32, "ExternalInput")
    rw_w2_d = P.dram("rw_w2", [NL, 128, 512], F32, "ExternalInput")
    rw_a2_d = P.dram("rw_a2", [NL, 128, 512], F32, "ExternalInput")
    rw_g2_d = P.dram("rw_g2", [NL, 128, 512], F32, "ExternalInput")
    rw_vec_d = P.dram("rw_vec", [128, NL, 5, 4], F32, "ExternalInput")
    mask64_d = P.dram("mask64", [64, 4, 64], F32, "ExternalInput")
    bones_d = P.dram("bones", [128, 128], F32, "ExternalInput")
    g_d = P.dram("rw_g", [512, T], F32)
    bv_d = P.dram("rw_bv", [512, T], F32)
    NCH = T // 64
    summ_d = P.dram("rw_summ", [2, NCH, 64, 2056], F32)
    etot_d = P.dram("rw_etot", [2, NCH, 512], F32)
    y_d = P.dram("rw_y", [2, T, 512], F32)
    CDEC = float(np.exp(-0.5))
    psctr = [0]

    def psn():
        i = psctr[0] % 6
        psctr[0] += 1
        return PS[i], ("ps", i)

    def stage_rwkv(l):
        import os
        RW_NG = int(os.environ.get("RW_NGROUPS", "99")); RW_CH = int(os.environ.get("RW_CHUNKS", "1")); RW_PH = int(os.environ.get("RW_PHASES", "3"))
        RW_RND = int(os.environ.get("RW_ROUNDS", "6")); RW_ST = int(os.environ.get("RW_STEPS", "99"))
        NG = 128
        NC = NG // 64
        prw_v = prw.rearrange("(k p) t -> p k t", p=128)
        g_v = g_d.rearrange("(k p) t -> p k t", p=128)
        bv_v = bv_d.rearrange("(k p) t -> p k t", p=128)
        grp512 = lambda tok: 0 if tok < 256 else 1 + (tok - 256) // 512
        mu = P.sbuf("rw_mu", [128, 2, 15]); c0 = P.sbuf("rw_c0", [128, 15])
        w0a0 = P.sbuf("rw_w0a0", [128, 2, 2, 4]); w2s = P.sbuf("rw_w2", [128, 512]); a2s = P.sbuf("rw_a2", [128, 512])
        g2s = P.sbuf("rw_g2", [128, 512]); vec = P.sbuf("rw_vec", [128, 5, 4]); omka = P.sbuf("rw_omka", [128, 4])
        mask = P.sbuf("rw_mask", [64, 4, 64]); bones = P.sbuf("rw_bones", [128, 128]); eps12 = P.sbuf("rw_eps12", [128, 1])
        eps_gn = P.sbuf("rw_epsgn", [128, 1]); rmask = P.sbuf("rw_rmask", [128, 4 * NC, 64])
        P.dma("sync", mu[:], rw_mu_d[:, l], writes=["rw_par"]); P.dma("sync", w0a0[:], rw_w0a0_d[:, l], writes=["rw_par"])
        P.dma("sync", w2s[:], rw_w2_d[l], writes=["rw_par"]); P.dma("sync", a2s[:], rw_a2_d[l], writes=["rw_par"])
        P.dma("sync", g2s[:], rw_g2_d[l], writes=["rw_par"]); P.dma("sync", vec[:], rw_vec_d[:, l], writes=["rw_par"])
        P.dma("sync", mask[:], mask64_d, writes=["rw_par"]); P.dma("sync", bones[:], bones_d, writes=["rw_par"])
        P.op("vector", lambda e: e.tensor_tensor(out=c0[:], in0=mu[:, 0, :], in1=mu[:, 1, :], op=ALU.add), reads=["rw_par"], writes=["rw_c0"])
        P.op("vector", lambda e: e.tensor_scalar(out=c0[:], in0=c0[:], scalar1=-1.0, scalar2=1.0, op0=ALU.mult, op1=ALU.add),
             reads=["rw_c0"], writes=["rw_c0"])
        P.op("vector", lambda e: e.tensor_scalar(out=omka[:], in0=vec[:, 1, :], scalar1=-1.0, scalar2=1.0, op0=ALU.mult, op1=ALU.add),
             reads=["rw_par"], writes=["rw_omka"])
        P.op("vector", lambda e: e.memset(eps12[:], 1e-12), writes=["rw_eps"])
        P.op("vector", lambda e: e.memset(eps_gn[:], 64e-5), writes=["rw_eps"])
        P.op("vector", lambda e: e.memset(rmask[:], 1.0), writes=["rw_rmask"])
        P.op("vector", lambda e: e.memset(rmask[:, :, 0:1], 0.0), reads=["rw_rmask"], writes=["rw_rmask"])
        m1 = P.mark()
        pin = P.sbuf("rw_pin", [128, 15, NG + 2]); psh = P.sbuf("rw_psh", [128, 15, NG])
        sgw = [P.sbuf("rw_sgw%d" % d, [128, 4, NG]) for d in range(2)]
        aa = [P.sbuf("rw_a%d" % d, [128, 4, NG]) for d in range(2)]
        kd = [P.sbuf("rw_kd%d" % d, [128, 4, NG]) for d in range(2)]
        kk = P.sbuf("rw_kk", [128, 4, NG]); tA = P.sbuf("rw_tA", [128, 4, NG]); tB = P.sbuf("rw_tB", [128, 4, NG])
        Lp = P.sbuf("rw_Lp", [128, 4, NG]); Li = P.sbuf("rw_Li", [128, 4, NG]); Le = P.sbuf("rw_Le", [128, 4, NG])
        rt = P.sbuf("rw_rt", [128, 4, NG], F32R); at = P.sbuf("rw_at", [128, 4, NG], F32R); kt = P.sbuf("rw_kt", [128, 4, NG], F32R)
        bt = P.sbuf("rw_bt", [128, 4, NG], F32R); Kh = P.sbuf("rw_Kh", [128, 4, NG]); Bh = P.sbuf("rw_Bh", [128, 4, NG])
        etot = P.sbuf("rw_etot", [128, 4, NC], F32R); gst = P.sbuf("rw_gst", [128, 4, NG])
        mk = {nm: [P.sbuf("rw_%s%s" % (nm, eo), [128, 4, NG], F32R) for eo in "EO"] for nm in ("at", "kt", "bt")}
        zsrc = P.sbuf("rw_zsrc", [128, 1024])
        P.op("vector", lambda e: e.memset(zsrc[:], 0.0), writes=["rw_zsrc"])
        Vt = [P.sbuf("rw_Vt%d" % c, [128, 512], F32R) for c in range(NC)]
        UT = []
        zl = [(Vt[c], ("rw_Vt", c)) for c in range(NC)]
        for u_ in range(2):
            ut = dict(KhT=P.sbuf("rw_KhT%d" % u_, [128, 512], F32R), BhT=P.sbuf("rw_BhT%d" % u_, [128, 512], F32R),
                      Mka=P.sbuf("rw_Mka%d" % u_, [128, 8, 64], F32R), Mkr=P.sbuf("rw_Mkr%d" % u_, [128, 8, 64], F32R),
                      Mbr=P.sbuf("rw_Mbr%d" % u_, [128, 8, 64], F32R),
                      Nn=[P.sbuf("rw_N%d_%d" % (u_, i), [128, 8, 64], F32R) for i in range(2)],
                      NTr=[P.sbuf("rw_NT%d_%d" % (u_, i), [128, 8, 64], F32R) for i in range(2)],
                      Z=P.sbuf("rw_Z%d" % u_, [128, 8, 128], F32R), Zn=P.sbuf("rw_Zn%d" % u_, [128, 8, 128], F32R),
                      summ=P.sbuf("rw_summ%d" % u_, [64, 2056]))
            UT.append(ut)
            zl += [(ut["KhT"], ("rw_KhT", u_)), (ut["BhT"], ("rw_BhT", u_)), (ut["Mka"], ("rw_Mka", u_)), (ut["Mkr"], ("rw_Mkr", u_)),
                   (ut["Mbr"], ("rw_Mbr", u_)), (ut["Nn"][0], ("rw_N", u_, 0)), (ut["Nn"][1], ("rw_N", u_, 1)), (ut["NTr"][0], ("rw_NT", u_, 0)),
                   (ut["NTr"][1], ("rw_NT", u_, 1)), (ut["Z"], ("rw_Z", u_)), (ut["Zn"], ("rw_Zn", u_))]
        for tl, key in zl:
            nfree = int(np.prod(tl.shape[1:]))
            src_ = zsrc[64:128, 0:nfree] if len(tl.shape) == 2 else zsrc[64:128, 0:nfree].rearrange("p (a b) -> p a b", a=tl.shape[1])
            P.op("vector", lambda e, tl=tl, src_=src_: e.tensor_copy(out=tl[64:128], in_=src_), reads=["rw_zsrc"],
                 writes=[key] + ([("rw_Z2", key[1])] if key[0] == "rw_Z" else []))
        identr = P.sbuf("rw_identr", [128, 128], F32R)
        P.op("vector", lambda e: e.tensor_copy(out=identr[:], in_=ident[:]), reads=["ident"], writes=["rw_identr"])
        r_ = psh[:, 0:4, :]; k_ = psh[:, 4:8, :]; v_ = psh[:, 8:12, :]
        TA = [("rw_tA", fc) for fc in range(4)]; TB = [("rw_tB", fc) for fc in range(4)]
        for gx in range(min(T // NG, RW_NG)):
            t0 = gx * NG
            isctx = t0 < 256
            rdk = sorted({grp512(max(t0 - 1, 0)), grp512(t0), grp512(min(t0 + NG, T - 1))})
            rdk = [("prw", "fm", g) for g in rdk]
            hasL = not (t0 == 0 or t0 == 256)
            hasR = not (t0 + NG == 256 or t0 + NG == T)
            lo = t0 - 1 if hasL else t0
            hi = t0 + NG + 1 if hasR else t0 + NG
            P.dma("sync", pin[:, :, lo - (t0 - 1):hi - (t0 - 1)], prw_v[:, :, lo:hi], reads=rdk, writes=["rw_pin"])
            if not hasL:
                P.op("gpsimd", lambda e: e.memset(pin[:, :, 0:1], 0.0), reads=["rw_pin"], writes=["rw_pinL"])
            if not hasR:
                P.op("gpsimd", lambda e: e.memset(pin[:, :, NG + 1:NG + 2], 0.0), reads=["rw_pin"], writes=["rw_pinR"])
            pk = ["rw_pin", "rw_pinL", "rw_pinR"]
            for ch in range(15):
                P.op("gpsimd", lambda e, ch=ch: e.tensor_scalar(out=psh[:, ch, :], in0=pin[:, ch, 1:NG + 1], scalar1=c0[:, ch:ch + 1],
                                                               scalar2=None, op0=ALU.mult), reads=pk + ["rw_c0"], writes=[("rw_psh", ch)])
                P.op("vector", lambda e, ch=ch: e.scalar_tensor_tensor(out=psh[:, ch, :], in0=pin[:, ch, 0:NG], scalar=mu[:, 0, ch:ch + 1],
                                                                      in1=psh[:, ch, :], op0=ALU.mult, op1=ALU.add),
                     reads=pk + ["rw_par", ("rw_psh", ch)], writes=[("rw_psh", ch)])
                P.op("vector", lambda e, ch=ch: e.scalar_tensor_tensor(out=psh[:, ch, :], in0=pin[:, ch, 2:NG + 2], scalar=mu[:, 1, ch:ch + 1],
                                                                      in1=psh[:, ch, :], op0=ALU.mult, op1=ALU.add),
                     reads=pk + ["rw_par", ("rw_psh", ch)], writes=[("rw_psh", ch)])
            RK = [("rw_psh", c) for c in range(0, 4)]; KK = [("rw_psh", c) for c in range(4, 8)]; VK = [("rw_psh", c) for c in range(8, 12)]
            P.op("scalar", lambda e: e.activation(out=psh[:, 12, :], in_=psh[:, 12, :], func=AF.Tanh), reads=[("rw_psh", 12)], writes=[("rw_psh", 12)])
            P.op("scalar", lambda e: e.activation(out=psh[:, 14, :], in_=psh[:, 14, :], func=AF.Sigmoid), reads=[("rw_psh", 14)],
                 writes=[("rw_psh", 14)])
            for which, (wsrc, srcch, dst) in enumerate(((w2s, 12, sgw), (a2s, 13, aa))):
                for d in range(2):
                    for fc in range(4):
                        ps, pk_ = psn()
                        P.op("tensor", lambda e, d=d, fc=fc, ps=ps, wsrc=wsrc, srcch=srcch: e.matmul(
                            ps[:, 0:NG], lhsT=wsrc[d * 64:(d + 1) * 64, fc * 128:(fc + 1) * 128], rhs=psh[d * 64:(d + 1) * 64, srcch, :],
                            start=True, stop=True), reads=["rw_par", ("rw_psh", srcch)], writes=[pk_])
                        P.op("scalar", lambda e, d=d, fc=fc, ps=ps, dst=dst, which=which: e.activation(
                            out=dst[d][:, fc, :], in_=ps[:, 0:NG], func=AF.Sigmoid, bias=w0a0[:, which, d, fc:fc + 1], scale=1.0),
                            reads=[pk_, "rw_par"], writes=[("rw_sa", which, d)])
            for fc in range(4):
                ps, pk_ = psn()
                P.op("tensor", lambda e, fc=fc, ps=ps: e.matmul(ps[:, 0:NG], lhsT=g2s[:, fc * 128:(fc + 1) * 128], rhs=psh[:, 14, :],
                                                              start=True, stop=True), reads=["rw_par", ("rw_psh", 14)], writes=[pk_])
                P.op("scalar", lambda e, fc=fc, ps=ps: e.copy(out=gst[:, fc, :], in_=ps[:, 0:NG]), reads=[pk_], writes=["rw_gst"])
            P.dma("gpsimd", g_v[:, :, t0:t0 + NG], gst[:], reads=["rw_gst"], writes=[("rw_g", grp512(t0))])
            for fc in range(4):
                P.op("gpsimd", lambda e, fc=fc: e.tensor_scalar(out=kk[:, fc, :], in0=psh[:, 4 + fc, :], scalar1=vec[:, 0, fc:fc + 1], scalar2=None,
                                                               op0=ALU.mult), reads=KK + ["rw_par"], writes=[("rw_kk", fc)])
                P.op("scalar", lambda e, fc=fc: e.activation(out=tA[:, fc, :], in_=kk[:, fc, :], func=AF.Square), reads=[("rw_kk", fc)],
                     writes=[("rw_tA", fc)])
                ps, pk_ = psn()
                P.op("tensor", lambda e, fc=fc, ps=ps: e.matmul(ps[:, 0:NG], lhsT=bones[:], rhs=tA[:, fc, :], start=True, stop=True),
                     reads=["rw_par", ("rw_tA", fc)], writes=[pk_])
                P.op("scalar", lambda e, fc=fc, ps=ps: e.activation(out=tB[:, fc, :], in_=ps[:, 0:NG], func=AF.Sqrt, bias=eps12[:, 0:1], scale=1.0),
                     reads=[pk_, "rw_eps"], writes=[("rw_tB", fc)])
                P.op("vector", lambda e, fc=fc: e.reciprocal(out=tB[:, fc, :], in_=tB[:, fc, :]), reads=[("rw_tB", fc)], writes=[("rw_tB", fc)])
                P.op("vector", lambda e, fc=fc: e.tensor_tensor(out=kk[:, fc, :], in0=kk[:, fc, :], in1=tB[:, fc, :], op=ALU.mult),
                     reads=[("rw_kk", fc), ("rw_tB", fc)], writes=[("rw_kk", fc)])
            KKN = [("rw_kk", fc) for fc in range(4)]
            for d in range(2):
                for fc in range(4):
                    P.op("gpsimd", lambda e, d=d, fc=fc: e.tensor_scalar(out=kd[d][:, fc, :], in0=aa[d][:, fc, :], scalar1=vec[:, 1, fc:fc + 1],
                                                                        scalar2=omka[:, fc:fc + 1], op0=ALU.mult, op1=ALU.add),
                         reads=[("rw_sa", 1, d), "rw_par", "rw_omka"], writes=[("rw_kd", d)])
                P.op("gpsimd", lambda e, d=d: e.tensor_tensor(out=kd[d][:], in0=kd[d][:], in1=k_, op=ALU.mult),
                     reads=[("rw_kd", d)] + KK, writes=[("rw_kd", d)])
                P.op("vector", lambda e, d=d: e.tensor_tensor(out=aa[d][:], in0=aa[d][:], in1=kk[:], op=ALU.mult),
                     reads=[("rw_sa", 1, d), ("rw_kd", d)] + KKN, writes=[("rw_sa", 1, d)])
            P.op("gpsimd", lambda e: e.tensor_tensor(out=tA[:], in0=kd[0][:], in1=kd[1][:], op=ALU.add),
                 reads=[("rw_kd", 0), ("rw_kd", 1)], writes=TA)
            P.op("gpsimd", lambda e: e.tensor_tensor(out=tA[:], in0=tA[:], in1=r_, op=ALU.mult), reads=TA + RK, writes=TA)
            for fc in range(4):
                P.op("gpsimd", lambda e, fc=fc: e.tensor_scalar(out=tA[:, fc, :], in0=tA[:, fc, :], scalar1=vec[:, 2, fc:fc + 1], scalar2=None,
                                                               op0=ALU.mult), reads=[("rw_tA", fc), "rw_par"], writes=[("rw_tA", fc)])
                ps, pk_ = psn()
                P.op("tensor", lambda e, fc=fc, ps=ps: e.matmul(ps[:, 0:NG], lhsT=bones[:], rhs=tA[:, fc, :], start=True, stop=True),
                     reads=["rw_par", ("rw_tA", fc)], writes=[pk_])
                P.op("vector", lambda e, fc=fc, ps=ps: e.tensor_tensor(out=gst[:, fc, :], in0=ps[:, 0:NG], in1=psh[:, 8 + fc, :], op=ALU.mult),
                     reads=[pk_, "rw_gst"] + VK, writes=["rw_gst"])
            P.dma("gpsimd", bv_v[:, :, t0:t0 + NG], gst[:], reads=["rw_gst"], writes=[("rw_bv", grp512(t0))])
            for c in range(NC):
                ps, pk_ = psn()
                for fc in range(4):
                    P.op("tensor", lambda e, c=c, fc=fc, ps=ps: e.transpose(out=ps[0:64, fc * 128:(fc + 1) * 128],
                                                                          in_=psh[:, 8 + fc, c * 64:(c + 1) * 64], identity=ident[:]),
                         reads=VK + ["ident"], writes=[pk_])
                P.op("scalar", lambda e, c=c, ps=ps: e.copy(out=Vt[c][0:64, :], in_=ps[0:64, :]), reads=[pk_], writes=[("rw_Vt", c)])
            for d in range(2):
                P.op("vector", lambda e, d=d: e.tensor_tensor_scan(out=Lp[:].rearrange("p a b -> p (a b)"),
                                                                  data0=rmask[:].rearrange("p a b -> p (a b)"),
                                                                  data1=sgw[d][:].rearrange("p a b -> p (a b)"), initial=0.0,
                                                                  op0=ALU.mult, op1=ALU.add),
                     reads=[("rw_sa", 0, d), "rw_rmask"], writes=["rw_Lp"])
                Lp3 = Lp[:].rearrange("p a (c t) -> p (a c) t", t=64)
                Li3 = Li[:].rearrange("p a (c t) -> p (a c) t", t=64); Le3 = Le[:].rearrange("p a (c t) -> p (a c) t", t=64)
                tot_b = Lp3[:, :, 63:64].to_broadcast([128, 4 * NC, 64])
                if d == 0:
                    P.op("gpsimd", lambda e: e.tensor_copy(out=Li[:], in_=Lp[:]), reads=["rw_Lp"], writes=["rw_Li"])
                    P.op("vector", lambda e, d=d: e.tensor_tensor(out=Le[:], in0=Lp[:], in1=sgw[d][:], op=ALU.subtract),
                         reads=["rw_Lp", ("rw_sa", 0, d)], writes=["rw_Le"])
                else:
                    P.op("vector", lambda e, d=d: e.tensor_tensor(out=Le[:], in0=Lp[:], in1=sgw[d][:], op=ALU.subtract),
                         reads=["rw_Lp", ("rw_sa", 0, d)], writes=["rw_Le"])
                    P.op("vector", lambda e: e.tensor_tensor(out=Li3, in0=tot_b, in1=Le3, op=ALU.subtract), reads=["rw_Lp", "rw_Le"],
                         writes=["rw_Li"])
                    P.op("vector", lambda e: e.tensor_tensor(out=Le3, in0=tot_b, in1=Lp3, op=ALU.subtract), reads=["rw_Lp", "rw_Li"],
                         writes=["rw_Le"])
                P.op("scalar", lambda e: e.activation(out=etot[:].rearrange("p a c -> p (a c)"), in_=Lp3[:, :, 63], func=AF.Exp, scale=-CDEC),
                     reads=["rw_Lp"], writes=["rw_etot"])
                P.op("scalar", lambda e: e.activation(out=tA[:], in_=Li[:], func=AF.Exp, scale=-CDEC), reads=["rw_Li"], writes=TA)
                P.op("vector", lambda e: e.tensor_tensor(out=rt[:], in0=r_, in1=tA[:], op=ALU.mult), reads=RK + TA, writes=["rw_rt"])
                P.op("scalar", lambda e: e.activation(out=tB[:], in_=Le[:], func=AF.Exp, scale=-CDEC), reads=["rw_Le"], writes=TB)
                P.op("vector", lambda e: e.tensor_tensor(out=at[:], in0=kk[:], in1=tB[:], op=ALU.mult), reads=KKN + TB, writes=["rw_at"])
                P.op("scalar", lambda e: e.activation(out=tA[:], in_=Li[:], func=AF.Exp, scale=CDEC), reads=["rw_Li"], writes=TA)
                P.op("vector", lambda e, d=d: e.tensor_tensor(out=kt[:], in0=kd[d][:], in1=tA[:], op=ALU.mult), reads=[("rw_kd", d)] + TA, writes=["rw_kt"])
                P.op("vector", lambda e, d=d: e.tensor_tensor(out=bt[:], in0=aa[d][:], in1=tA[:], op=ALU.mult), reads=[("rw_sa", 1, d)] + TA, writes=["rw_bt"])
                et_b = etot[:].bitcast(F32).rearrange("p a c -> p (a c)").unsqueeze(2).to_broadcast([128, 4 * NC, 64])
                P.op("vector", lambda e: e.tensor_tensor(out=Kh[:].rearrange("p a (c t) -> p (a c) t", t=64),
                                                         in0=kt[:].bitcast(F32).rearrange("p a (c t) -> p (a c) t", t=64), in1=et_b, op=ALU.mult),
                     reads=["rw_kt", "rw_etot"], writes=["rw_Kh"])
                P.op("gpsimd", lambda e: e.tensor_tensor(out=Bh[:].rearrange("p a (c t) -> p (a c) t", t=64),
                                                         in0=bt[:].bitcast(F32).rearrange("p a (c t) -> p (a c) t", t=64), in1=et_b, op=ALU.mult),
                     reads=["rw_bt", "rw_etot"], writes=["rw_Bh"])
                for xi, (nm, X) in enumerate((("at", at), ("kt", kt), ("bt", bt))):
                    for eo in range(2):
                        if (xi + eo) % 2 == 0:
                            P.op("scalar", lambda e, nm=nm, X=X, eo=eo: e.activation(
                                out=mk[nm][eo][:], in_=X[:].bitcast(F32), func=AF.Copy, scale=bones[:, eo * 64:eo * 64 + 1]),
                                reads=["rw_" + nm, "rw_par"], writes=[("rw_mk", nm, eo)])
                        else:
                            P.op("vector", lambda e, nm=nm, X=X, eo=eo: e.tensor_scalar(
                                out=mk[nm][eo][:], in0=X[:].bitcast(F32), scalar1=bones[:, eo * 64:eo * 64 + 1], scalar2=None, op0=ALU.mult),
                                reads=["rw_" + nm, "rw_par"], writes=[("rw_mk", nm, eo)])
                cg0 = gx * NC
                mS = 0 if d == 0 else 2
                mST = 2 if d == 0 else 0
                mI = 1 if d == 0 else 3
                def chunk_gen(c, cg, u):
                    sl = slice(c * 64, (c + 1) * 64)
                    KhT, BhT, Mka, Mkr, Mbr, Nn, NTr, Z, Zn = (UT[u][k_] for k_ in ("KhT", "BhT", "Mka", "Mkr", "Mbr", "Nn", "NTr", "Z", "Zn"))
                    sm = UT[u]["summ"]; smk = ("rw_summ", u)
                    yield
                    hd = lambda X, h, sl=sl: X[:, h // 2, sl]
                    hdL = lambda nm, h, sl=sl: mk[nm][h % 2][:, h // 2, sl]
                    for src, skey, dst, dkey in ((at[:].bitcast(F32), "rw_at", None, ("rw_Z", u)), (Kh[:], "rw_Kh", KhT, ("rw_KhT", u)),
                                                 (Bh[:], "rw_Bh", BhT, ("rw_BhT", u))):
                        ps, pk_ = psn()
                        for fc in range(4):
                            P.op("tensor", lambda e, fc=fc, ps=ps, src=src, sl=sl: e.transpose(out=ps[0:64, fc * 128:(fc + 1) * 128],
                                                                                            in_=src[:, fc, sl], identity=ident[:]),
                                 reads=[skey, "ident"], writes=[pk_])
                        if dst is None:
                            P.op("scalar", lambda e, ps=ps: e.copy(out=Z[0:64, :, 0:64], in_=ps[0:64, :].rearrange("p (h k) -> p h k", h=8)),
                                 reads=[pk_], writes=[("rw_Z", u)])
                        else:
                            P.op("scalar", lambda e, ps=ps, dst=dst: e.copy(out=dst[0:64, :], in_=ps[0:64, :]), reads=[pk_], writes=[dkey])
                    if RW_ST <= 1:
                        return
                    yield
                    specs = (("bt", at, "MKbt", "rw_at", Nn[0], ("rw_N", u, 0), mS), ("at", bt, "MKat", "rw_bt", NTr[0], ("rw_NT", u, 0), mST),
                             ("kt", at, "MKkt", "rw_at", Mka, ("rw_Mka", u), mS), ("kt", rt, "MKkt", "rw_rt", Mkr, ("rw_Mkr", u), mI),
                             ("bt", rt, "MKbt", "rw_rt", Mbr, ("rw_Mbr", u), mI))
                    for si, (L_, R_, lk, rk_, dst, dkey, mi) in enumerate(specs):
                        ps, pk_ = psn()
                        for h in range(8):
                            P.op("tensor", lambda e, h=h, ps=ps, la=hdL(L_, h), ra=hd(R_, h): e.matmul(ps[0:64, h * 64:(h + 1) * 64], lhsT=la, rhs=ra,
                                                                                                     start=True, stop=True), reads=[("rw_mk", lk[2:], 0), ("rw_mk", lk[2:], 1), rk_], writes=[pk_])
                        P.op("vector" if si % 2 == 0 else "gpsimd" if False else "vector", lambda e, ps=ps, dst=dst, mi=mi: e.tensor_tensor(
                            out=dst[0:64], in0=ps[0:64, :].rearrange("p (h i) -> p h i", h=8), in1=mask[:, mi:mi + 1, :].to_broadcast([64, 8, 64]),
                            op=ALU.mult), reads=[pk_, "rw_par"], writes=[dkey])
                    if RW_ST <= 2:
                        return
                    yield
                    ps, pk_ = psn()
                    for h in range(8):
                        P.op("tensor", lambda e, h=h, ps=ps, c=c: e.matmul(ps[0:64, h * 64:(h + 1) * 64], lhsT=Mka[:, h, :],
                                                                          rhs=Vt[c][:, h * 64:(h + 1) * 64], start=True, stop=True),
                             reads=[("rw_Mka", u), ("rw_Vt", c)], writes=[pk_])
                    P.op("scalar", lambda e, ps=ps: e.copy(out=Z[0:64, :, 64:128], in_=ps[0:64, :].rearrange("p (h k) -> p h k", h=8)),
                         reads=[pk_], writes=[("rw_Z2", u)])
                    if RW_ST <= 3:
                        return
                    yield
                    cur = 0
                    for rnd in range(RW_RND):
                        N_, NT_ = Nn[cur], NTr[cur]
                        nk, ntk = ("rw_N", u, cur), ("rw_NT", u, cur)
                        pz = [psn(), psn()]
                        for h in range(8):
                            ps, pk_ = pz[h // 4]
                            P.op("tensor", lambda e, h=h, ps=ps, N_=N_: e.matmul(ps[0:64, (h % 4) * 128:(h % 4 + 1) * 128], lhsT=N_[:, h, :],
                                                                                rhs=Z[:, h, :], start=True, stop=True),
                                 reads=[nk, ("rw_Z", u), ("rw_Z2", u)], writes=[pk_])
                        for half in range(2):
                            ps, pk_ = pz[half]
                            P.op("vector", lambda e, half=half, ps=ps, rnd=rnd: e.tensor_tensor(
                                out=Z[0:64, half * 4:(half + 1) * 4, :], in0=Z[0:64, half * 4:(half + 1) * 4, :].bitcast(F32),
                                in1=ps[0:64, :].rearrange("p (h k) -> p h k", h=4), op=(ALU.subtract if rnd == 0 else ALU.add)),
                                reads=[pk_, ("rw_Z", u), ("rw_Z2", u)], writes=[("rw_Z", u), ("rw_Z2", u)])
                        yield
                        if rnd < 5:
                            nxt = 1 - cur
                            ps, pk_ = psn()
                            for h in range(8):
                                P.op("tensor", lambda e, h=h, ps=ps, N_=N_, NT_=NT_: e.matmul(ps[0:64, h * 64:(h + 1) * 64], lhsT=NT_[:, h, :],
                                                                                             rhs=N_[:, h, :], start=True, stop=True),
                                     reads=[nk, ntk], writes=[pk_])
                            P.op("scalar", lambda e, ps=ps, nxt=nxt: e.copy(out=Nn[nxt][0:64], in_=ps[0:64, :].rearrange("p (h k) -> p h k", h=8)),
                                 reads=[pk_], writes=[("rw_N", u, nxt)])
                            if rnd < 4:
                                ps, pk_ = psn()
                                for h in range(8):
                                    P.op("tensor", lambda e, h=h, ps=ps, N_=N_, NT_=NT_: e.matmul(ps[0:64, h * 64:(h + 1) * 64], lhsT=N_[:, h, :],
                                                                                                 rhs=NT_[:, h, :], start=True, stop=True),
                                         reads=[nk, ntk], writes=[pk_])
                                P.op("gpsimd" if False else "scalar", lambda e, ps=ps, nxt=nxt: e.copy(
                                    out=NTr[nxt][0:64], in_=ps[0:64, :].rearrange("p (h k) -> p h k", h=8)), reads=[pk_], writes=[("rw_NT", u, nxt)])
                            cur = nxt
                        yield
                    P.op("scalar", lambda e: e.activation(out=Zn[0:64], in_=Z[0:64].bitcast(F32), func=AF.Copy, scale=-1.0),
                         reads=[("rw_Z", u), ("rw_Z2", u)], writes=[("rw_Zn", u)])
                    if RW_ST <= 4:
                        return
                    yield
                    ps, pk_ = psn()
                    for h in range(8):
                        P.op("tensor", lambda e, h=h, ps=ps: e.matmul(ps[0:64, h * 64:(h + 1) * 64], lhsT=Zn[:, h, 0:64],
                                                                     rhs=BhT[:, h * 64:(h + 1) * 64], start=True, stop=True),
                             reads=[("rw_Zn", u), ("rw_BhT", u)], writes=[pk_])
                    P.op("scalar", lambda e, ps=ps, sm=sm: e.copy(out=sm[:, 0:512], in_=ps[0:64, :]), reads=[pk_], writes=[(smk, 0)])
                    if RW_ST <= 5:
                        return
                    yield
                    ps, pk_ = psn()
                    for h in range(8):
                        P.op("tensor", lambda e, h=h, ps=ps, c=c: e.matmul(ps[0:64, h * 64:(h + 1) * 64], lhsT=KhT[:, h * 64:(h + 1) * 64],
                                                                          rhs=Vt[c][:, h * 64:(h + 1) * 64], start=True, stop=False),
                             reads=[("rw_KhT", u), ("rw_Vt", c)], writes=[pk_])
                        P.op("tensor", lambda e, h=h, ps=ps: e.matmul(ps[0:64, h * 64:(h + 1) * 64], lhsT=BhT[:, h * 64:(h + 1) * 64],
                                                                     rhs=Zn[:, h, 64:128], start=False, stop=True),
                             reads=[("rw_BhT", u), ("rw_Zn", u)], writes=[pk_])
                    P.op("vector", lambda e, ps=ps, sm=sm: e.tensor_copy(out=sm[:, 512:1024], in_=ps[0:64, :]), reads=[pk_], writes=[(smk, 1)])
                    if RW_ST <= 6:
                        return
                    yield
                    ps, pk_ = psn()
                    for h in range(8):
                        p0 = (h % 2) * 64
                        P.op("tensor", lambda e, h=h, ps=ps, p0=p0, ra=hd(rt, h): e.matmul(ps[0:64, h * 64:(h + 1) * 64],
                                                                                          lhsT=identr[:, p0:p0 + 64], rhs=ra,
                                                                                          start=True, stop=False),
                             reads=["rw_identr", "rw_rt"], writes=[pk_])
                        P.op("tensor", lambda e, h=h, ps=ps: e.matmul(ps[0:64, h * 64:(h + 1) * 64], lhsT=Zn[:, h, 0:64], rhs=Mbr[:, h, :],
                                                                     start=False, stop=True), reads=[("rw_Zn", u), ("rw_Mbr", u)], writes=[pk_])
                    P.op("scalar", lambda e, ps=ps, sm=sm: e.copy(out=sm[:, 1024:1536], in_=ps[0:64, :]), reads=[pk_], writes=[(smk, 2)])
                    if RW_ST <= 7:
                        return
                    yield
                    ps, pk_ = psn()
                    for h in range(8):
                        P.op("tensor", lambda e, h=h, ps=ps, c=c: e.matmul(ps[0:64, h * 64:(h + 1) * 64], lhsT=Mkr[:, h, :],
                                                                          rhs=Vt[c][:, h * 64:(h + 1) * 64], start=True, stop=False),
                             reads=[("rw_Mkr", u), ("rw_Vt", c)], writes=[pk_])
                        P.op("tensor", lambda e, h=h, ps=ps: e.matmul(ps[0:64, h * 64:(h + 1) * 64], lhsT=Mbr[:, h, :], rhs=Zn[:, h, 64:128],
                                                                     start=False, stop=True), reads=[("rw_Mbr", u), ("rw_Zn", u)], writes=[pk_])
                    P.op("vector", lambda e, ps=ps, sm=sm: e.tensor_copy(out=sm[:, 1536:2048], in_=ps[0:64, :]), reads=[pk_], writes=[(smk, 3)])
                    if RW_ST <= 8:
                        return
                    yield
                    ps, pk_ = psn()
                    for h in range(8):
                        p0 = (h % 2) * 64
                        P.op("tensor", lambda e, h=h, ps=ps, p0=p0, c=c: e.matmul(ps[0:64, h:h + 1], lhsT=ident[:, p0:p0 + 64],
                                                                                 rhs=etot[:].bitcast(F32)[:, h // 2, c:c + 1], start=True, stop=True),
                             reads=["ident", "rw_etot"], writes=[pk_])
                    P.op("vector", lambda e, ps=ps, sm=sm: e.tensor_copy(out=sm[:, 2048:2056], in_=ps[0:64, 0:8]), reads=[pk_], writes=[(smk, 4)])
                    P.dma("gpsimd", summ_d[d, cg], sm[:], reads=[(smk, i) for i in range(5)], writes=[("rw_summd", d, cg)])
                gens = [chunk_gen(c, cg0 + c, c % 2) for c in range(NC if RW_CH else 0)]
                while gens:
                    for g_ in list(gens):
                        try:
                            next(g_)
                        except StopIteration:
                            gens.remove(g_)
        if "rwprep" in dbg and l == 0:
            for nm, src in (("g", g_d), ("bv", bv_d)):
                d_ = dbg_out("rw_" + nm, [512, T])
                P.dma("sync", d_, src, reads=[("rw_" + nm, i) for i in range(9)], writes=["OUT_dbgrw" + nm])
        P.barrier(); P.release(m1)
        if RW_PH < 2:
            return
        ST = [P.sbuf("rw_ST%d" % i, [64, 8, 64]) for i in range(2)]
        sm2 = [P.sbuf("rw_sm2_%d" % i, [64, 2056]) for i in range(3)]
        yt = [P.sbuf("rw_yt%d" % i, [64, 512]) for i in range(2)]
        stt = P.sbuf("rw_stt", [64, 8, 64])
        step = 0
        for d in range(2):
            order = list(range(NCH)) if d == 0 else [3, 2, 1, 0] + list(range(NCH - 1, 3, -1))
            cur = 0
            P.op("vector", lambda e: e.memset(ST[0][:], 0.0), reads=[("rw_ST", 0)], writes=[("rw_ST", 0)])
            for cg in order:
                b3 = step % 3
                b2 = step % 2
                step += 1
                P.dma("sync", sm2[b3][:], summ_d[d, cg], reads=[], writes=[("rw_sm2", b3)])
                S_, Sn_ = ST[cur], ST[1 - cur]
                psy, pyk = psn(); pss, psk = psn()
                for h in range(8):
                    P.op("tensor", lambda e, h=h, psy=psy, S_=S_, b3=b3: e.matmul(psy[0:64, h * 64:(h + 1) * 64],
                                                                               lhsT=sm2[b3][:, 1024 + h * 64:1024 + (h + 1) * 64], rhs=S_[:, h, :],
                                                                               start=True, stop=True),
                         reads=[("rw_sm2", b3), ("rw_ST", cur)], writes=[pyk])
                for h in range(8):
                    P.op("tensor", lambda e, h=h, pss=pss, S_=S_, b3=b3: e.matmul(pss[0:64, h * 64:(h + 1) * 64],
                                                                               lhsT=sm2[b3][:, h * 64:(h + 1) * 64], rhs=S_[:, h, :],
                                                                               start=True, stop=True),
                         reads=[("rw_sm2", b3), ("rw_ST", cur)], writes=[psk])
                P.op("gpsimd", lambda e, S_=S_, b3=b3: e.tensor_tensor(out=stt[:], in0=S_[:], in1=sm2[b3][:, 2048:2056].unsqueeze(2).to_broadcast([64, 8, 64]),
                                                                      op=ALU.mult), reads=[("rw_ST", cur), ("rw_sm2", b3)], writes=["rw_stt"])
                P.op("gpsimd", lambda e, b3=b3: e.tensor_tensor(out=stt[:], in0=stt[:], in1=sm2[b3][:, 512:1024].rearrange("p (h k) -> p h k", h=8),
                                                               op=ALU.add), reads=["rw_stt", ("rw_sm2", b3)], writes=["rw_stt"])
                P.op("vector", lambda e, pss=pss, Sn_=Sn_: e.tensor_tensor(out=Sn_[:], in0=pss[0:64, :].rearrange("p (h k) -> p h k", h=8),
                                                                          in1=stt[:], op=ALU.add), reads=[psk, "rw_stt"], writes=[("rw_ST", 1 - cur)])
                P.op("vector", lambda e, psy=psy, b2=b2, b3=b3: e.tensor_tensor(out=yt[b2][:], in0=psy[0:64, :], in1=sm2[b3][:, 1536:2048], op=ALU.add),
                     reads=[pyk, ("rw_sm2", b3)], writes=[("rw_yt", b2)])
                P.dma("gpsimd", y_d[d, cg * 64:(cg + 1) * 64, :], yt[b2][:], reads=[("rw_yt", b2)], writes=[("rw_yd", d, cg // 2)])
                cur = 1 - cur
        if "rwy" in dbg and l == 0:
            d_ = dbg_out("rw_y", [2, T, 512])
            P.dma("sync", d_, y_d, reads=[("rw_yd", d, t) for d in range(2) for t in range(NT)], writes=["OUT_dbgrwy"])
        P.barrier(); P.release(m1)
        if RW_PH < 3:
            return
        y0 = [P.sbuf("rw_y0_%d" % i, [128, 512]) for i in range(2)]; y1 = [P.sbuf("rw_y1_%d" % i, [128, 512]) for i in range(2)]
        sqt = [P.sbuf("rw_sq%d" % i, [128, 512]) for i in range(2)]
        st8 = [P.sbuf("rw_st8_%d" % i, [128, 4, 8]) for i in range(2)]
        bvt = [P.sbuf("rw_bvt%d" % i, [128, 4, 128]) for i in range(2)]; gt = [P.sbuf("rw_gt%d" % i, [128, 4, 128]) for i in range(2)]
        of = [P.sbuf("rw_of%d" % i, [128, 4, 128]) for i in range(2)]; ob = [P.sbuf("rw_ob%d" % i, [128, 4, 128], BF16) for i in range(2)]
        orw_v = orw_d.rearrange("(k p) t -> p k t", p=128)
        for t in range(NT):
            s_ = t % 2
            gi = 0 if t < 2 else 1 + (t - 2) // 4
            P.dma("sync", y0[s_][:], y_d[0, t * 128:(t + 1) * 128, :], reads=[("rw_yd", 0, t)], writes=[("rw_y0", s_)])
            P.dma("sync", y1[s_][:], y_d[1, t * 128:(t + 1) * 128, :], reads=[("rw_yd", 1, t)], writes=[("rw_y1", s_)])
            P.dma("sync", bvt[s_][:], bv_v[:, :, t * 128:(t + 1) * 128], reads=[("rw_bv", i) for i in range(9)], writes=[("rw_bvt", s_)])
            P.dma("sync", gt[s_][:], g_v[:, :, t * 128:(t + 1) * 128], reads=[("rw_g", i) for i in range(9)], writes=[("rw_gt", s_)])
            ys = y0[s_]; st = st8[s_]
            P.op("gpsimd", lambda e, s_=s_: e.tensor_tensor(out=y0[s_][:], in0=y0[s_][:], in1=y1[s_][:], op=ALU.add),
                 reads=[("rw_y0", s_), ("rw_y1", s_)], writes=[("rw_y0", s_)])
            y3 = ys[:].rearrange("p (h n) -> p h n", h=8)
            P.op("vector", lambda e, st=st, y3=y3: e.tensor_reduce(out=st[:, 0, :], in_=y3, axis=AX.X, op=ALU.add), reads=[("rw_y0", s_)],
                 writes=[("rw_st8", s_)])
            P.op("scalar", lambda e, s_=s_, ys=ys: e.activation(out=sqt[s_][:], in_=ys[:], func=AF.Square), reads=[("rw_y0", s_)],
                 writes=[("rw_sq", s_)])
            P.op("vector", lambda e, st=st, s_=s_: e.tensor_reduce(out=st[:, 1, :], in_=sqt[s_][:].rearrange("p (h n) -> p h n", h=8), axis=AX.X,
                                                                  op=ALU.add), reads=[("rw_sq", s_), ("rw_st8", s_)], writes=[("rw_st8", s_)])
            P.op("vector", lambda e, st=st: e.tensor_scalar(out=st[:, 0, :], in0=st[:, 0, :], scalar1=1.0 / 64, scalar2=None, op0=ALU.mult),
                 reads=[("rw_st8", s_)], writes=[("rw_st8", s_)])
            P.op("vector", lambda e, st=st: e.tensor_tensor(out=st[:, 2, :], in0=st[:, 0, :], in1=st[:, 0, :], op=ALU.mult),
                 reads=[("rw_st8", s_)], writes=[("rw_st8", s_)])
            P.op("vector", lambda e, st=st: e.scalar_tensor_tensor(out=st[:, 2, :], in0=st[:, 1, :], scalar=1.0 / 64, in1=st[:, 2, :],
                                                                  op0=ALU.mult, op1=ALU.subtract), reads=[("rw_st8", s_)], writes=[("rw_st8", s_)])
            P.op("scalar", lambda e, st=st: e.activation(out=st[:, 3, :], in_=st[:, 2, :], func=AF.Sqrt, bias=eps_gn[:, 0:1], scale=1.0),
                 reads=[("rw_st8", s_), "rw_eps"], writes=[("rw_st8", s_)])
            P.op("vector", lambda e, st=st: e.reciprocal(out=st[:, 3, :], in_=st[:, 3, :]), reads=[("rw_st8", s_)], writes=[("rw_st8", s_)])
            P.op("vector", lambda e, st=st, y3=y3: e.tensor_tensor(out=y3, in0=y3, in1=st[:, 0, :].unsqueeze(2).to_broadcast([128, 8, 64]),
                                                                  op=ALU.subtract), reads=[("rw_y0", s_), ("rw_st8", s_), ("rw_sq", s_)],
                 writes=[("rw_y0", s_)])
            P.op("vector", lambda e, st=st, y3=y3: e.tensor_tensor(out=y3, in0=y3, in1=st[:, 3, :].unsqueeze(2).to_broadcast([128, 8, 64]),
                                                                  op=ALU.mult), reads=[("rw_y0", s_), ("rw_st8", s_)], writes=[("rw_y0", s_)])
            ps, pk_ = psn()
            for fc in range(4):
                P.op("tensor", lambda e, fc=fc, ps=ps, ys=ys: e.transpose(out=ps[:, fc * 128:(fc + 1) * 128], in_=ys[:, fc * 128:(fc + 1) * 128],
                                                                        identity=ident[:]), reads=[("rw_y0", s_), "ident"], writes=[pk_])
            for fc in range(4):
                P.op("vector", lambda e, fc=fc, ps=ps, s_=s_: e.tensor_scalar(out=of[s_][:, fc, :], in0=ps[:, fc * 128:(fc + 1) * 128],
                                                                            scalar1=vec[:, 3, fc:fc + 1], scalar2=vec[:, 4, fc:fc + 1],
                                                                            op0=ALU.mult, op1=ALU.add),
                     reads=[pk_, "rw_par"], writes=[("rw_of", s_, fc)])
            P.op("gpsimd", lambda e, s_=s_: e.tensor_tensor(out=of[s_][:], in0=of[s_][:], in1=bvt[s_][:], op=ALU.add),
                 reads=[("rw_of", s_, fc) for fc in range(4)] + [("rw_bvt", s_)], writes=[("rw_of2", s_)])
            P.op("gpsimd", lambda e, s_=s_: e.tensor_tensor(out=ob[s_][:], in0=of[s_][:], in1=gt[s_][:], op=ALU.mult),
                 reads=[("rw_of2", s_), ("rw_gt", s_)], writes=[("rw_ob", s_)])
            P.dma("gpsimd", orw_v[:, :, t * 128:(t + 1) * 128], ob[s_][:], reads=[("rw_ob", s_)], writes=[("orw", gi)])
        if "orw" in dbg and l == 0:
            d_ = dbg_out("orw", [512, T], BF16)
            P.dma("sync", d_, orw_d, reads=[("orw", g) for g in range(9)], writes=["OUT_dbgorw"])

    xT_v = xT.rearrange("(k p) t -> p k t", p=128)
    def stage_inproj(l):
        hT = P.sbuf("hT", [128, 8, T], BF16)
        xg = [P.sbuf("xg%d" % i, [128, 8, 512]) for i in range(2)]
        wblk = [P.sbuf("wblk%d" % i, [128, 8, 512]) for i in range(2)]
        wbf = [P.sbuf("wbf%d" % i, [128, 8, 512], BF16) for i in range(2)]
        ost = [P.sbuf("ost%d" % i, [128, 512]) for i in range(4)]
        ostb = [P.sbuf("ostb%d" % i, [128, 512], BF16) for i in range(4)]
        for gi, (t0, n) in enumerate(GROUPS):
            s = gi % 2
            c = 1 if gi == 0 else 0
            P.dma("sync", xg[s][:, :, :n], xT_v[:, :, t0:t0 + n], reads=[("xT", gi)], writes=[("xg", s)])
            for k in range(8):
                eng = "vector" if k % 2 == 0 else "gpsimd"
                P.op(eng, lambda e, s=s, k=k, c=c, l=l, t0=t0, n=n: e.tensor_scalar(
                    out=hT[:, k, t0:t0 + n], in0=xg[s][:, k, :n], scalar1=mod1[:, l, 8 + k, c:c + 1],
                    scalar2=mod[:, l, k, c:c + 1], op0=ALU.mult, op1=ALU.add),
                    reads=[("xg", s), "mod", "mod1"], writes=[("hT", gi)])
        if "hT" in dbg and l == 0:
            d = dbg_out("hT", [128, 8 * T], BF16)
            P.dma("sync", d, hT[:].rearrange("p k t -> p (k t)"), reads=[("hT", g) for g in range(9)], writes=["OUT_dbghT"])
        blocks = [(3072, 512, "fm_bf", qT, 0), (3584, 512, "fm_bf", kT, 0), (4096, 512, "tm_bf", vtok, 0),
                  (4608, 512, "fm", prw, 0), (5120, 512, "fm", prw, 512), (5632, 512, "fm", prw, 1024),
                  (6144, 384, "fm", prw, 1536), (6528, 512, "fm", sguU, 0), (7040, 512, "tm", sguV, 0)]
        nev = 0
        for bi, (c0, ncol, kind, dst, r0) in enumerate(blocks):
            s = bi % 2
            P.dma("sync", wblk[s][:, :, :ncol], w_in[l, :, c0:c0 + ncol].rearrange("(k p) m -> p k m", p=128),
                  writes=[("wblk", s)])
            for k in range(8):
                eng = "gpsimd" if k % 2 == 0 else "vector"
                P.op(eng, lambda e, s=s, k=k, ncol=ncol: e.tensor_copy(out=wbf[s][:, k, :ncol], in_=wblk[s][:, k, :ncol]),
                     reads=[("wblk", s)], writes=[("wbf", s)])
            if kind.startswith("fm"):
                for gi, (t0, n) in enumerate(GROUPS):
                    for mi in range(ncol // 128):
                        pb = 2 + nev % 4
                        for k in range(8):
                            P.op("tensor", lambda e, s=s, k=k, mi=mi, t0=t0, n=n, pb=pb: e.matmul(
                                PS[pb][:, :n], lhsT=wbf[s][:, k, mi * 128:(mi + 1) * 128], rhs=hT[:, k, t0:t0 + n],
                                start=(k == 0), stop=(k == 7)), reads=[("wbf", s), ("hT", gi)], writes=[("ps", pb)])
                        so = nev % 4
                        o_t = ostb[so] if kind == "fm_bf" else ost[so]
                        okey = ("ostb", so) if kind == "fm_bf" else ("ost", so)
                        if nev % 2 == 0:
                            P.op("scalar", lambda e, o_t=o_t, pb=pb, n=n: e.copy(out=o_t[:, :n], in_=PS[pb][:, :n]),
                                 reads=[("ps", pb)], writes=[okey])
                        else:
                            P.op("vector", lambda e, o_t=o_t, pb=pb, n=n: e.tensor_copy(out=o_t[:, :n], in_=PS[pb][:, :n]),
                                 reads=[("ps", pb)], writes=[okey])
                        rr = r0 + mi * 128
                        P.dma("gpsimd", dst[rr:rr + 128, t0:t0 + n], o_t[:, :n], reads=[okey], writes=[(dst.name, "fm", gi)])
                        nev += 1
            else:
                for t in range(NT):
                    pb = 2 + nev % 4
                    gi = 0 if t < 2 else 1 + (t - 2) // 4
                    for k in range(8):
                        P.op("tensor", lambda e, s=s, k=k, t=t, pb=pb: e.matmul(
                            PS[pb][:, :], lhsT=hT[:, k, t * 128:(t + 1) * 128], rhs=wbf[s][:, k, :],
                            start=(k == 0), stop=(k == 7)), reads=[("wbf", s), ("hT", gi)], writes=[("ps", pb)])
                    so = nev % 4
                    o_t = ostb[so] if kind == "tm_bf" else ost[so]
                    okey = ("ostb", so) if kind == "tm_bf" else ("ost", so)
                    if nev % 2 == 0:
                        P.op("scalar", lambda e, o_t=o_t, pb=pb: e.copy(out=o_t[:], in_=PS[pb][:]), reads=[("ps", pb)], writes=[okey])
                    else:
                        P.op("vector", lambda e, o_t=o_t, pb=pb: e.tensor_copy(out=o_t[:], in_=PS[pb][:]), reads=[("ps", pb)], writes=[okey])
                    P.dma("gpsimd", dst[t * 128:(t + 1) * 128, :], o_t[:], reads=[okey], writes=[(dst.name, "tm", t)])
                    nev += 1
        if "p" in dbg and l == 0:
            for nm, src, shp, dt in (("qT", qT, [512, T], BF16), ("kT", kT, [512, T], BF16), ("vtok", vtok, [T, 512], BF16),
                                     ("prw", prw, [1920, T], F32), ("sguU", sguU, [512, T], F32), ("sguV", sguV, [T, 512], F32)):
                d = dbg_out(nm, shp, dt)
                rk = [(src.name, "fm", g) for g in range(9)] + [(src.name, "tm", t) for t in range(NT)]
                P.dma("sync", d, src, reads=rk, writes=["OUT_dbg" + nm])
    for l in range(n_layers):
        stage_inproj(l)
        if stop == "s1":
            return P, dbg_t
        P.barrier(); P.release(m0)
        stage_sgu(l)
        if stop == "s2":
            return P, dbg_t
        P.barrier(); P.release(m0)
        stage_na(l)
        if stop == "s3":
            return P, dbg_t
        P.barrier(); P.release(m0)
        stage_rwkv(l)
        if stop == "s4":
            return P, dbg_t
        P.barrier(); P.release(m0)
        stage_merge(l)
        if stop == "s5":
            return P, dbg_t
        P.barrier(); P.release(m0)
        stage_ffn(l)
        if stop == "s6":
            return P, dbg_t
        P.barrier(new_epoch=True); P.release(m0)
    if mode == "full":
        stage_final()
    else:
        for gi, (t0, n) in enumerate(GROUPS):
            P.dma("sync", xT_out[:, t0:t0 + n], xT[:, t0:t0 + n], reads=[("xT", gi)], writes=["OUT_x%d" % gi])
    return P, dbg_t


def host_inputs(inputs, b, l0=0, nl=DEPTH, xT=None):
    m = {}
    if xT is None:
        x = np.asarray(inputs["x"], np.float32)
        ctx = np.asarray(inputs["ctx"], np.float32)
        m["xin"] = np.ascontiguousarray(np.concatenate([ctx[b], x[b]], axis=0))
    else:
        m["xT_in"] = xT
    m["ccT"] = np.ascontiguousarray(np.stack([np.asarray(inputs["c"], np.float32)[b], np.asarray(inputs["c_ctx"], np.float32)], axis=1))
    m["ident"] = np.eye(128, dtype=np.float32)
    f = lambda k: np.asarray(inputs[k], np.float32)[l0:l0 + nl]
    m["w_ada"] = f("w_ada")
    m["b_adaT"] = np.ascontiguousarray(f("b_ada").reshape(nl, 48, 128).transpose(0, 2, 1))
    m["w_in"] = f("w_in")
    m["sgu_lng"] = np.ascontiguousarray(np.broadcast_to(f("sgu_ln_g")[:, None, :], (nl, 128, 512)))
    m["sgu_lnb"] = np.ascontiguousarray(np.broadcast_to(f("sgu_ln_b")[:, None, :], (nl, 128, 512)))
    m["sgu_wT"] = np.ascontiguousarray(f("sgu_w").transpose(0, 3, 1, 2))
    m["sgu_bB"] = np.ascontiguousarray(np.broadcast_to(f("sgu_b")[:, None, :, :], (nl, 64, 8, 128)))
    m["na_bias"] = na_bias_layout(f("na_rpb"))
    fmN = lambda a, n: a.reshape(a.shape[0], n, 128).transpose(2, 0, 1)
    m["rw_mu"] = np.ascontiguousarray(np.stack([fmN(f("rwkv_mu_prev"), 15), fmN(f("rwkv_mu_next"), 15)], axis=2))
    w0 = f("rwkv_w0").reshape(nl, 2, 4, 128).transpose(3, 0, 1, 2); a0 = f("rwkv_a0").reshape(nl, 2, 4, 128).transpose(3, 0, 1, 2)
    m["rw_w0a0"] = np.ascontiguousarray(np.stack([w0, a0], axis=2))
    m["rw_w2"] = np.ascontiguousarray(f("rwkv_w2").reshape(nl, 128, 512)); m["rw_a2"] = np.ascontiguousarray(f("rwkv_a2").reshape(nl, 128, 512))
    m["rw_g2"] = f("rwkv_g2")
    m["rw_vec"] = np.ascontiguousarray(np.stack([fmN(f(k).reshape(nl, 512), 4) for k in
                                                 ("rwkv_k_k", "rwkv_k_a", "rwkv_r_k", "rwkv_gn_g", "rwkv_gn_b")], axis=2))
    jj, ii = np.meshgrid(np.arange(64), np.arange(64), indexing="ij")
    m["mask64"] = np.ascontiguousarray(np.stack([jj < ii, jj <= ii, jj > ii, jj >= ii], axis=1).astype(np.float32))
    bo = np.zeros((128, 128), np.float32); bo[:64, :64] = 1; bo[64:, 64:] = 1
    m["bones"] = bo
    m["w_branch"] = f("w_branch"); m["w_out"] = f("w_out"); m["ffn_w_gu"] = f("ffn_w_gu"); m["ffn_w_down"] = f("ffn_w_down")
    fm8 = lambda a: a.reshape(nl, 8, 128).transpose(2, 0, 1)
    m["lnp"] = np.ascontiguousarray(np.stack([fm8(f("ln1_g")), fm8(f("ln1_b")), fm8(f("ln2_g")), fm8(f("ln2_b"))], axis=2))
    return m


_NA_IDX = None


def na_bias_layout(rpb):
    global _NA_IDX
    if _NA_IDX is None:
        ridx = np.zeros((5, 128, 896), np.int64); cidx = np.zeros((5, 128, 896), np.int64); valid = np.zeros((5, 128, 896), bool)
        zero = np.zeros((5, 128, 896), bool)
        for pi, r in enumerate((0, 2, 4, 60, 62)):
            kb = min(max(r - 4, 0), 54)
            for qi in range(128):
                qr, c = r + qi // 64, qi % 64
                row0 = min(max(qr - 4, 0), 56); col0 = min(max(c - 8, 0), 48)
                for j in range(10):
                    kr = kb + j
                    if not (row0 <= kr < row0 + 8):
                        continue
                    for kc in range(col0, col0 + 16):
                        ridx[pi, qi, j * 64 + kc] = kr - qr + 7; cidx[pi, qi, j * 64 + kc] = kc - c + 15; valid[pi, qi, j * 64 + kc] = True
            zero[pi, :, 640:] = True
        _NA_IDX = (ridx, cidx, valid, zero)
    ridx, cidx, valid, zero = _NA_IDX
    g = rpb[:, :, ridx, cidx]
    g = np.where(valid[None, None], g, np.float32(-30000.0))
    g = np.where(zero[None, None], np.float32(0.0), g)
    return np.ascontiguousarray(g.transpose(0, 2, 3, 1, 4)).astype(np.float32)


_CACHE = {}


def _host_params(inputs, l0, nl):
    key = (id(inputs.get("w_in")), l0, nl)
    if key not in _CACHE:
        m = host_inputs(inputs, 0, l0, nl, xT=np.zeros((1,), np.float32))
        m.pop("xT_in"); m.pop("ccT")
        _CACHE[key] = m
    return _CACHE[key]


def kernel(**inputs):
    P, _ = build(n_layers=DEPTH, mode="full")
    nc = P.finalize()
    par = dict(host_inputs(inputs, 0))
    par.pop("xin"); par.pop("ccT")
    in_maps = []
    for core in range(8):
        m = dict(par)
        hb = host_inputs_x(inputs, core // 2)
        m.update(hb)
        in_maps.append(m)
    res = run_bass_kernel_spmd(nc, in_maps, core_ids=list(range(8)))
    return np.stack([res.results[2 * b]["out"] for b in range(4)], axis=0).astype(np.float32)


def host_inputs_x(inputs, b):
    x = np.asarray(inputs["x"], np.float32); ctx = np.asarray(inputs["ctx"], np.float32)
    return {"xin": np.ascontiguousarray(np.concatenate([ctx[b], x[b]], axis=0)),
            "ccT": np.ascontiguousarray(np.stack([np.asarray(inputs["c"], np.float32)[b], np.asarray(inputs["c_ctx"], np.float32)], axis=1))}
```
